# Optimizing a Trainium2 kernel written in Bass

```python
import math
import jax, jax.numpy as jnp
from jax import lax
import numpy as np

D_MODEL = 2048
BATCH = 2
SEQ = 8192
DEPTH = 4

HEAD_DIM = 64
Q_CHUNK = 128
A_CONFIGS = ((128, 1), (512, 4), (2048, 16))
N_A_GROUPS = len(A_CONFIGS)
A_HEADS_PER_GROUP = 4
A_HEADS = N_A_GROUPS * A_HEADS_PER_GROUP
A_BLOCK = 128
B_HEADS = 8
MOBA_BLOCK = 256
MOBA_TOPK = 3
C_KV_HEADS = 3
C_GROUP = 4
C_HEADS = C_KV_HEADS * C_GROUP
CMP_STRIDE = 16
CMP_LEN = 2 * CMP_STRIDE
CMP_HIDDEN = 128
SLC_BLOCK = 64
SLC_TOPN = 16
WIN = 512
FORCE_SCORE = 1e9
N_HEADS = A_HEADS + B_HEADS + C_HEADS
MIX_WIDTH = N_HEADS * HEAD_DIM
A_WIDTH = 3 * A_HEADS * HEAD_DIM
B_WIDTH = 3 * B_HEADS * HEAD_DIM
C_Q_WIDTH = C_HEADS * HEAD_DIM
C_KV_WIDTH = 6 * C_KV_HEADS * HEAD_DIM
C_GATE_WIDTH = 3 * C_HEADS
IN_WIDTH = A_WIDTH + B_WIDTH + C_Q_WIDTH + C_KV_WIDTH + C_GATE_WIDTH
D_FF = 5632
CONV_WIDTH = 3
N_BUCKETS = 32
T5_MAX_DIST = 2048
EPS = 1e-6
NEG = -1e30
Q_SCALE = HEAD_DIM ** -0.5

kernel_name = "hybrid_dilated_moba_nsa_convffn"


def rmsnorm(x, g):
    x32 = x.astype(jnp.float32)
    y = x32 * lax.rsqrt(jnp.mean(x32 * x32, axis=-1, keepdims=True) + EPS)
    return (y * g.astype(jnp.float32)).astype(x.dtype)


def t5_bucket(dist):
    n = jnp.maximum(dist, 0)
    max_exact = N_BUCKETS // 2
    nf = jnp.maximum(n, 1).astype(jnp.float32)
    large = max_exact + (jnp.log(nf / max_exact) / math.log(T5_MAX_DIST / max_exact)
                         * (N_BUCKETS - max_exact)).astype(jnp.int32)
    large = jnp.minimum(large, N_BUCKETS - 1)
    return jnp.where(n < max_exact, n, large)


def masked_softmax(s, mask):
    s32 = jnp.where(mask, s.astype(jnp.float32), NEG)
    m = jnp.max(s32, axis=-1, keepdims=True)
    e = jnp.where(mask, jnp.exp(s32 - m), 0.0)
    den = jnp.sum(e, axis=-1, keepdims=True)
    p = e / jnp.maximum(den, 1e-30)
    lse = m[..., 0] + jnp.log(jnp.maximum(den[..., 0], 1e-30))
    return p, lse


def dilated_group(q, k, v, window, dilation, tbl):
    B, S, H, dh = q.shape
    L = S // dilation
    nq = -(-L // A_BLOCK)
    Lp = nq * A_BLOCK
    nback = window // dilation

    def to_class(t):
        t = t.reshape(B, L, dilation, H, dh).transpose(0, 2, 3, 1, 4)
        return jnp.pad(t, ((0, 0), (0, 0), (0, 0), (0, Lp - L), (0, 0)))

    def band(t):
        tp = jnp.pad(t, ((0, 0), (0, 0), (0, 0), (A_BLOCK, 0), (0, 0)))
        prev = tp[:, :, :, :Lp].reshape(B, dilation, H, nq, A_BLOCK, dh)
        cur = t.reshape(B, dilation, H, nq, A_BLOCK, dh)
        return jnp.concatenate([prev, cur], axis=4)

    qb = to_class(q).reshape(B, dilation, H, nq, A_BLOCK, dh)
    kb = band(to_class(k))
    vb = band(to_class(v))
    a = jnp.arange(A_BLOCK)[:, None]
    bk = jnp.arange(2 * A_BLOCK)[None, :]
    rel = a + A_BLOCK - bk
    key_idx = jnp.arange(nq)[:, None, None] * A_BLOCK - A_BLOCK + bk[None]
    mask = (rel >= 0)[None] & (rel <= nback)[None] & (key_idx >= 0)
    bias = tbl[:, t5_bucket(rel * dilation)]
    s = jnp.einsum('bdhnqe,bdhnke->bdhnqk', qb, kb) + bias[:, None]
    p, lse = masked_softmax(s, mask)
    o = jnp.einsum('bdhnqk,bdhnke->bdhnqe', p.astype(v.dtype), vb)
    o = o.reshape(B, dilation, H, Lp, dh)[:, :, :, :L].transpose(0, 3, 1, 2, 4).reshape(B, S, H, dh)
    lse = lse.reshape(B, dilation, H, Lp)[:, :, :, :L].transpose(0, 3, 1, 2).reshape(B, S, H)
    return o, lse


def dilated_mixer(pA, tbl_a):
    B, S = pA.shape[:2]
    outs, lses = [], []
    for g, (w, d) in enumerate(A_CONFIGS):
        o, lse = dilated_group(pA[:, :, g, 0] * Q_SCALE, pA[:, :, g, 1], pA[:, :, g, 2], w, d,
                               tbl_a[g * A_HEADS_PER_GROUP:(g + 1) * A_HEADS_PER_GROUP])
        outs.append(o)
        lses.append(lse)
    alpha = jax.nn.softmax(jnp.stack(lses, axis=2), axis=2)
    o = jnp.stack(outs, axis=2) * alpha[..., None].astype(outs[0].dtype)
    return o.reshape(B, S, A_HEADS * HEAD_DIM)


def moba_mixer(q, k, v, tbl):
    B, S, H, dh = q.shape
    q, k, v = (t.transpose(0, 2, 1, 3) for t in (q, k, v))
    nb = -(-S // MOBA_BLOCK)
    Sp = nb * MOBA_BLOCK
    kp = jnp.pad(k, ((0, 0), (0, 0), (0, Sp - S), (0, 0)))
    vp = jnp.pad(v, ((0, 0), (0, 0), (0, Sp - S), (0, 0)))
    kb = kp.reshape(B, H, nb, MOBA_BLOCK, dh)
    vb = vp.reshape(B, H, nb, MOBA_BLOCK, dh)
    kmean = jnp.mean(kb.astype(jnp.float32), axis=3).astype(k.dtype)
    topk = min(MOBA_TOPK, nb)
    ar_b = jnp.arange(B)[:, None, None, None]
    ar_h = jnp.arange(H)[None, :, None, None]
    blk_off = jnp.arange(MOBA_BLOCK)

    def chunk(c):
        start = c * Q_CHUNK
        t = start + jnp.arange(Q_CHUNK)
        ob = start // MOBA_BLOCK
        qc = lax.dynamic_slice_in_dim(q, start, Q_CHUNK, axis=2)
        gate = jnp.einsum('bhqe,bhne->bhqn', qc, kmean).astype(jnp.float32)
        past = jnp.arange(nb) < ob
        gate = jnp.where(past, gate, NEG)
        _, idx = lax.top_k(gate, topk)
        sel_ok = idx < ob
        ks = kb[ar_b, ar_h, idx].reshape(B, H, Q_CHUNK, topk * MOBA_BLOCK, dh)
        vs = vb[ar_b, ar_h, idx].reshape(B, H, Q_CHUNK, topk * MOBA_BLOCK, dh)
        pos_s = (idx[..., None] * MOBA_BLOCK + blk_off).reshape(B, H, Q_CHUNK, topk * MOBA_BLOCK)
        mask_s = jnp.repeat(sel_ok, MOBA_BLOCK, axis=-1)
        ko = lax.dynamic_slice_in_dim(kp, ob * MOBA_BLOCK, MOBA_BLOCK, axis=2)
        vo = lax.dynamic_slice_in_dim(vp, ob * MOBA_BLOCK, MOBA_BLOCK, axis=2)
        pos_o = ob * MOBA_BLOCK + blk_off
        mask_o = jnp.broadcast_to(pos_o[None, :] <= t[:, None], (B, H, Q_CHUNK, MOBA_BLOCK))
        s_s = (jnp.einsum('bhqe,bhqke->bhqk', qc, ks)
               + tbl[ar_h, t5_bucket(t[None, None, :, None] - pos_s)])
        s_o = (jnp.einsum('bhqe,bhke->bhqk', qc, ko)
               + tbl[:, t5_bucket(t[:, None] - pos_o[None, :])][None])
        p, _ = masked_softmax(jnp.concatenate([s_s, s_o], -1), jnp.concatenate([mask_s, mask_o], -1))
        p = p.astype(v.dtype)
        n_sel = topk * MOBA_BLOCK
        return (jnp.einsum('bhqk,bhqke->bhqe', p[..., :n_sel], vs)
                + jnp.einsum('bhqk,bhke->bhqe', p[..., n_sel:], vo))

    outs = lax.map(chunk, jnp.arange(S // Q_CHUNK))
    return outs.transpose(1, 0, 3, 2, 4).reshape(B, S, H * dh)


def nsa_mixer(q, kvC, gates, cmp_w1, cmp_w2, cmp_pe, tbl):
    B, S, _, dh = q.shape
    KV, G = C_KV_HEADS, C_GROUP
    q = q.reshape(B, S, KV, G, dh).transpose(0, 2, 3, 1, 4)
    k_cmp, v_cmp, k_slc, v_slc, k_win, v_win = (kvC[:, :, i].transpose(0, 2, 1, 3) for i in range(6))
    gates = gates.reshape(B, S, KV, G, 3).transpose(0, 2, 3, 1, 4)
    n_cmp = S // CMP_STRIDE - 1
    n_slc = S // SLC_BLOCK
    n_sel = min(SLC_TOPN, n_slc)

    def compress(t, w1, w2, pe):
        sub = t.reshape(B, KV, S // CMP_STRIDE, CMP_STRIDE, dh)
        blocks = jnp.concatenate([sub[:, :, :-1], sub[:, :, 1:]], axis=3) + pe
        return jax.nn.gelu(blocks.reshape(B, KV, n_cmp, CMP_LEN * dh) @ w1) @ w2

    kc = compress(k_cmp, cmp_w1[0], cmp_w2[0], cmp_pe[0])
    vc = compress(v_cmp, cmp_w1[1], cmp_w2[1], cmp_pe[1])
    cmp_end = jnp.arange(n_cmp) * CMP_STRIDE + CMP_LEN - 1
    ci = jnp.arange(n_cmp)[:, None] * CMP_STRIDE
    sj = jnp.arange(n_slc)[None, :] * SLC_BLOCK
    overlap = ((ci < sj + SLC_BLOCK) & (ci + CMP_LEN > sj)).astype(jnp.float32)
    ksb = k_slc.reshape(B, KV, n_slc, SLC_BLOCK, dh)
    vsb = v_slc.reshape(B, KV, n_slc, SLC_BLOCK, dh)
    kwp = jnp.pad(k_win, ((0, 0), (0, 0), (WIN, 0), (0, 0)))
    vwp = jnp.pad(v_win, ((0, 0), (0, 0), (WIN, 0), (0, 0)))
    tbl_kgn = tbl.reshape(KV, G, N_BUCKETS)
    tbl_kng = tbl_kgn.transpose(0, 2, 1)
    ar_b = jnp.arange(B)[:, None, None, None]
    ar_kv = jnp.arange(KV)[None, :, None, None]
    slc_off = jnp.arange(SLC_BLOCK)
    jj = jnp.arange(n_slc)[None, :]

    def chunk(c):
        start = c * Q_CHUNK
        t = start + jnp.arange(Q_CHUNK)
        qc = lax.dynamic_slice_in_dim(q, start, Q_CHUNK, axis=3)
        s_c = jnp.einsum('bkgqe,bkne->bkgqn', qc, kc)
        p_c, _ = masked_softmax(s_c, cmp_end[None, :] <= t[:, None])
        o_c = jnp.einsum('bkgqn,bkne->bkgqe', p_c.astype(v_cmp.dtype), vc)
        imp = jnp.einsum('bkgqn,nj->bkqj', p_c, overlap)
        own = (t // SLC_BLOCK)[:, None]
        valid = jj <= own
        forced = (jj == 0) | (jj == own) | (jj == own - 1)
        score = jnp.where(valid, jnp.where(forced, FORCE_SCORE, imp), -1.0)
        _, idx = lax.top_k(score, n_sel)
        ks = ksb[ar_b, ar_kv, idx].reshape(B, KV, Q_CHUNK, n_sel * SLC_BLOCK, dh)
        vs = vsb[ar_b, ar_kv, idx].reshape(B, KV, Q_CHUNK, n_sel * SLC_BLOCK, dh)
        pos = (idx[..., None] * SLC_BLOCK + slc_off).reshape(B, KV, Q_CHUNK, n_sel * SLC_BLOCK)
        dist = t[None, None, :, None] - pos
        bias_s = tbl_kng[ar_kv, t5_bucket(dist)].transpose(0, 1, 4, 2, 3)
        s_s = jnp.einsum('bkgqe,bkqse->bkgqs', qc, ks) + bias_s
        p_s, _ = masked_softmax(s_s, (dist >= 0)[:, :, None])
        o_s = jnp.einsum('bkgqs,bkqse->bkgqe', p_s.astype(v_slc.dtype), vs)
        kw = lax.dynamic_slice_in_dim(kwp, start, Q_CHUNK + WIN, axis=2)
        vw = lax.dynamic_slice_in_dim(vwp, start, Q_CHUNK + WIN, axis=2)
        pos_w = start - WIN + jnp.arange(Q_CHUNK + WIN)
        dist_w = t[:, None] - pos_w[None, :]
        mask_w = (dist_w >= 0) & (dist_w < WIN) & (pos_w[None, :] >= 0)
        s_w = jnp.einsum('bkgqe,bkse->bkgqs', qc, kw) + tbl_kgn[:, :, t5_bucket(dist_w)]
        p_w, _ = masked_softmax(s_w, mask_w)
        o_w = jnp.einsum('bkgqs,bkse->bkgqe', p_w.astype(v_win.dtype), vw)
        g = lax.dynamic_slice_in_dim(gates, start, Q_CHUNK, axis=3).astype(o_c.dtype)
        return g[..., 0:1] * o_c + g[..., 1:2] * o_s + g[..., 2:3] * o_w

    outs = lax.map(chunk, jnp.arange(S // Q_CHUNK))
    return outs.transpose(1, 0, 4, 2, 3, 5).reshape(B, S, C_HEADS * dh)


def conv_ffn(h, w_up, conv_w, conv_b, w_down):
    S = h.shape[1]
    u = h @ w_up
    up = jnp.pad(u, ((0, 0), (CONV_WIDTH - 1, 0), (0, 0)))
    acc = conv_b
    for j in range(CONV_WIDTH):
        acc = acc + conv_w[j] * up[:, CONV_WIDTH - 1 - j:CONV_WIDTH - 1 - j + S]
    a, g = jnp.split(acc, 2, axis=-1)
    return (jax.nn.silu(g) * a) @ w_down


def setup_inputs(seed: int = 0) -> dict:
    key = jax.random.key(seed)
    ks = jax.random.split(key, 14)
    nrm = jax.random.normal
    f32 = jnp.float32
    return {
        "x": nrm(ks[0], (BATCH, SEQ, D_MODEL), f32),
        "rel_table": 0.5 * nrm(ks[1], (N_HEADS, N_BUCKETS), f32),
        "w_in": nrm(ks[2], (DEPTH, D_MODEL, IN_WIDTH), f32) * D_MODEL ** -0.5,
        "w_out": nrm(ks[3], (DEPTH, MIX_WIDTH, D_MODEL), f32) * MIX_WIDTH ** -0.5,
        "cmp_w1": nrm(ks[4], (DEPTH, 2, CMP_LEN * HEAD_DIM, CMP_HIDDEN), f32) * (CMP_LEN * HEAD_DIM) ** -0.5,
        "cmp_w2": nrm(ks[5], (DEPTH, 2, CMP_HIDDEN, HEAD_DIM), f32) * CMP_HIDDEN ** -0.5,
        "cmp_pe": 0.5 * nrm(ks[6], (DEPTH, 2, CMP_LEN, HEAD_DIM), f32),
        "norm_attn": 1.0 + 0.05 * nrm(ks[7], (DEPTH, D_MODEL), f32),
        "norm_mlp": 1.0 + 0.05 * nrm(ks[8], (DEPTH, D_MODEL), f32),
        "w_up": nrm(ks[9], (DEPTH, D_MODEL, 2 * D_FF), f32) * D_MODEL ** -0.5,
        "conv_w": nrm(ks[10], (DEPTH, CONV_WIDTH, 2 * D_FF), f32) * CONV_WIDTH ** -0.5,
        "conv_b": 0.02 * nrm(ks[11], (DEPTH, 2 * D_FF), f32),
        "w_down": nrm(ks[12], (DEPTH, D_FF, D_MODEL), f32) * D_FF ** -0.5,
        "norm_final": 1.0 + 0.05 * nrm(ks[13], (D_MODEL,), f32),
    }


def reference(x, rel_table, w_in, w_out, cmp_w1, cmp_w2, cmp_pe, norm_attn, norm_mlp,
              w_up, conv_w, conv_b, w_down, norm_final):
    B, S, _ = x.shape
    tbl_a = rel_table[:A_HEADS]
    tbl_b = rel_table[A_HEADS:A_HEADS + B_HEADS]
    tbl_c = rel_table[A_HEADS + B_HEADS:]
    for l in range(DEPTH):
        h = rmsnorm(x, norm_attn[l])
        proj = h @ w_in[l]
        pA = proj[..., :A_WIDTH].reshape(B, S, N_A_GROUPS, 3, A_HEADS_PER_GROUP, HEAD_DIM)
        pB = proj[..., A_WIDTH:A_WIDTH + B_WIDTH].reshape(B, S, 3, B_HEADS, HEAD_DIM)
        pC = proj[..., A_WIDTH + B_WIDTH:]
        o_a = dilated_mixer(pA, tbl_a)
        o_b = moba_mixer(pB[:, :, 0] * Q_SCALE, pB[:, :, 1], pB[:, :, 2], tbl_b)
        q_c = pC[..., :C_Q_WIDTH].reshape(B, S, C_HEADS, HEAD_DIM) * Q_SCALE
        kv_c = pC[..., C_Q_WIDTH:C_Q_WIDTH + C_KV_WIDTH].reshape(B, S, 6, C_KV_HEADS, HEAD_DIM)
        g_c = jax.nn.sigmoid(pC[..., C_Q_WIDTH + C_KV_WIDTH:].reshape(B, S, C_HEADS, 3))
        o_c = nsa_mixer(q_c, kv_c, g_c, cmp_w1[l], cmp_w2[l], cmp_pe[l], tbl_c)
        x = x + jnp.concatenate([o_a, o_b, o_c], axis=-1) @ w_out[l]
        h = rmsnorm(x, norm_mlp[l])
        x = x + conv_ffn(h, w_up[l], conv_w[l], conv_b[l], w_down[l])
    return rmsnorm(x, norm_final)
```

```python
import contextlib, math
import numpy as np
import concourse.bass as bass
import concourse.mybir as mybir

F32 = mybir.dt.float32
BF16 = mybir.dt.bfloat16
I32 = mybir.dt.int32
AF = mybir.ActivationFunctionType
ALU = mybir.AluOpType
AX = mybir.AxisListType

SEM_CAP = 30000
N_DMA_SEMS = 8


class _Op:
    __slots__ = ("eng", "fn", "deps", "is_dma", "sig", "has_dependents", "idx", "dsem_prev", "is_cc", "inc")

    def __init__(self, eng, fn, is_dma):
        self.eng = eng
        self.fn = fn
        self.is_dma = is_dma
        self.deps = []
        self.sig = None
        self.has_dependents = False
        self.dsem_prev = None
        self.is_cc = False
        self.inc = 1


class Prog:
    ENGS = ("pe", "act", "dve", "pool", "sp")

    def __init__(self, nc):
        self.nc = nc
        self.ops = []
        self.last_writer = {}
        self.readers = {}

    def _add(self, eng, fn, reads, writes, is_dma):
        op = _Op(eng, fn, is_dma)
        deps = {}
        for r in reads:
            w = self.last_writer.get(r)
            if w is not None:
                deps[id(w)] = w
        for r in writes:
            w = self.last_writer.get(r)
            if w is not None:
                deps[id(w)] = w
            for rd in self.readers.get(r, ()):
                deps[id(rd)] = rd
        for r in reads:
            self.readers.setdefault(r, []).append(op)
        for r in writes:
            self.last_writer[r] = op
            self.readers[r] = []
        for d in deps.values():
            if d is op:
                continue
            if (not is_dma) and (not d.is_dma) and d.eng == eng and eng == "pe":
                continue
            op.deps.append(d)
            d.has_dependents = True
        self.ops.append(op)
        return op

    def op(self, eng, fn, reads=(), writes=()):
        return self._add(eng, fn, reads, writes, False)

    def dma(self, eng, fn, reads=(), writes=()):
        return self._add(eng, fn, reads, writes, True)

    def emit(self, final_wait_ops=()):
        nc = self.nc
        engs = {"pe": nc.tensor, "act": nc.scalar, "dve": nc.vector, "pool": nc.gpsimd, "sp": nc.sync}
        import contextlib
        with contextlib.ExitStack() as st:
            sem_lists = {e: [] for e in self.ENGS}
            counts = {e: 0 for e in self.ENGS}

            def new_sem(name):
                return st.enter_context(nc.semaphore(name))

            dma_sems = {}
            dma_state = {}
            for e in self.ENGS:
                dma_sems[e] = None
            for op in self.ops:
                if op.is_dma:
                    if dma_sems[op.eng] is None:
                        dma_sems[op.eng] = [new_sem(f"d_{op.eng}_{i}") for i in range(N_DMA_SEMS)]
                        dma_state[op.eng] = {"rr": 0, "cnt": [0] * N_DMA_SEMS}
                    stt = dma_state[op.eng]
                    i = stt["rr"]
                    stt["rr"] = (i + 1) % N_DMA_SEMS
                    prev = stt["cnt"][i]
                    stt["cnt"][i] = prev + 16
                    op.sig = (dma_sems[op.eng][i], prev + 16)
                    op.dsem_prev = (dma_sems[op.eng][i], prev) if prev > 0 else None
                elif op.has_dependents:
                    e = op.eng
                    if not sem_lists[e] or counts[e] >= SEM_CAP:
                        sem_lists[e].append(new_sem(f"c_{e}_{len(sem_lists[e])}"))
                        counts[e] = 0
                    counts[e] += 1
                    op.sig = (sem_lists[e][-1], counts[e])
            streams = {e: [] for e in self.ENGS}
            waited = {e: {} for e in self.ENGS}
            for op in self.ops:
                e = op.eng
                waits = []
                need = []
                if op.dsem_prev is not None:
                    need.append(op.dsem_prev)
                for d in op.deps:
                    need.append(d.sig)
                for (sem, val) in need:
                    k = id(sem)
                    if waited[e].get(k, 0) >= val:
                        continue
                    waited[e][k] = val
                    waits.append((sem, val))
                streams[e].append((waits, op))
            finals = [o.sig for o in final_wait_ops]
            blk = st.enter_context(nc.Block())

            def make(e):
                def body(engine):
                    for waits, op in streams[e]:
                        for (sem, val) in waits:
                            engine.wait_ge(sem, val)
                        ins = op.fn()
                        if op.sig is not None:
                            ins.then_inc(op.sig[0], 16 if op.is_dma else 1)
                    if e == "sp":
                        for (sem, val) in finals:
                            engine.wait_ge(sem, val)
                return body

            blk.tensor(make("pe"))
            blk.scalar(make("act"))
            blk.vector(make("dve"))
            blk.gpsimd(make("pool"))
            blk.sync(make("sp"))
        return self


class Ctx:
    ENGS = ("pe", "act", "dve", "pool", "sp")

    def __init__(self, nc, stack):
        self.nc = nc
        self.stack = stack
        self.sem_lists = {e: [] for e in self.ENGS}
        self.counts = {e: 0 for e in self.ENGS}
        self.dma_sems = {e: None for e in self.ENGS}
        self.dma_state = {}
        self.cc_sem = None
        self.cc_count = 0
        self.waited = {e: {} for e in self.ENGS}
        self.barrier_sigs = []
        self.nsem = 0

    def new_sem(self, name):
        self.nsem += 1
        return self.stack.enter_context(self.nc.semaphore(name))


class PProg(Prog):
    def __init__(self, ctx):
        super().__init__(ctx.nc)
        self.ctx = ctx

    def cc(self, fn, reads=(), writes=()):
        op = self._add("pool", fn, reads, writes, True)
        op.is_cc = True
        return op

    def emit(self, final_wait_ops=()):
        nc, ctx = self.nc, self.ctx
        engs = self.ENGS
        last_op = {e: None for e in engs}
        for op in self.ops:
            if not op.is_dma:
                last_op[op.eng] = op
        for e in engs:
            if last_op[e] is not None:
                last_op[e].has_dependents = True
        for op in self.ops:
            if getattr(op, "is_cc", False):
                if ctx.cc_sem is None:
                    ctx.cc_sem = ctx.new_sem("ccs")
                ctx.cc_count += 1
                op.sig = (ctx.cc_sem, ctx.cc_count)
                op.inc = 1
            elif op.is_dma:
                if ctx.dma_sems[op.eng] is None:
                    ctx.dma_sems[op.eng] = [ctx.new_sem(f"d_{op.eng}_{i}") for i in range(N_DMA_SEMS)]
                    ctx.dma_state[op.eng] = {"rr": 0, "cnt": [0] * N_DMA_SEMS}
                stt = ctx.dma_state[op.eng]
                i = stt["rr"]
                stt["rr"] = (i + 1) % N_DMA_SEMS
                prev = stt["cnt"][i]
                stt["cnt"][i] = prev + 16
                op.sig = (ctx.dma_sems[op.eng][i], prev + 16)
                op.dsem_prev = (ctx.dma_sems[op.eng][i], prev) if prev > 0 else None
                op.inc = 16
            elif op.has_dependents:
                e = op.eng
                if not ctx.sem_lists[e] or ctx.counts[e] >= SEM_CAP:
                    ctx.sem_lists[e].append(ctx.new_sem(f"c_{e}_{len(ctx.sem_lists[e])}"))
                    ctx.counts[e] = 0
                ctx.counts[e] += 1
                op.sig = (ctx.sem_lists[e][-1], ctx.counts[e])
                op.inc = 1
        streams = {e: [] for e in engs}
        start_waits = {e: [] for e in engs}
        for e in engs:
            for (sem, val) in ctx.barrier_sigs:
                k = id(sem)
                if ctx.waited[e].get(k, 0) >= val:
                    continue
                ctx.waited[e][k] = val
                start_waits[e].append((sem, val))
        for op in self.ops:
            e = op.eng
            waits = []
            need = []
            if op.dsem_prev is not None:
                need.append(op.dsem_prev)
            for d in op.deps:
                need.append(d.sig)
            for (sem, val) in need:
                k = id(sem)
                if ctx.waited[e].get(k, 0) >= val:
                    continue
                ctx.waited[e][k] = val
                waits.append((sem, val))
            streams[e].append((waits, op))
        sigs = []
        for e in engs:
            if last_op[e] is not None:
                sigs.append(last_op[e].sig)
            if ctx.dma_sems[e] is not None:
                for i, sem in enumerate(ctx.dma_sems[e]):
                    c = ctx.dma_state[e]["cnt"][i]
                    if c > 0:
                        sigs.append((sem, c))
        if ctx.cc_sem is not None and ctx.cc_count > 0:
            sigs.append((ctx.cc_sem, ctx.cc_count))
        ctx.barrier_sigs = sigs
        finals = [o.sig for o in final_wait_ops]
        with nc.Block() as blk:
            def make(e):
                def body(engine):
                    for (sem, val) in start_waits[e]:
                        engine.wait_ge(sem, val)
                    for waits, op in streams[e]:
                        for (sem, val) in waits:
                            engine.wait_ge(sem, val)
                        ins = op.fn()
                        if op.sig is not None:
                            if op.inc == 1 and getattr(op, "is_cc", False):
                                ins.then_inc(op.sig[0])
                            else:
                                ins.then_inc(op.sig[0], op.inc)
                    if e == "sp":
                        for (sem, val) in finals:
                            engine.wait_ge(sem, val)
                return body
            blk.tensor(make("pe"))
            blk.scalar(make("act"))
            blk.vector(make("dve"))
            blk.gpsimd(make("pool"))
            blk.sync(make("sp"))
        return self

import contextlib
import numpy as np

D = 2048
NT = 2048
IN_W = 5796
EPS = 1e-6

def a_col(g, part, hg):
    return g * 768 + part * 256 + hg * 64
B0 = 2304
C0 = 3840
CKV = C0 + 768
CG = CKV + 1152


def t_chunks():
    ch = []
    for g in range(3):
        for pair in range(2):
            ch.append((a_col(g, 0, 2 * pair), 128, [("qT", g * 4 + 2 * pair, 0, 64), ("qT", g * 4 + 2 * pair + 1, 64, 64)], 0.125))
        for pair in range(2):
            ch.append((a_col(g, 1, 2 * pair), 128, [("kT", g * 4 + 2 * pair, 0, 64), ("kT", g * 4 + 2 * pair + 1, 64, 64)], 1.0))
    for pair in range(4):
        ch.append((B0 + pair * 128, 128, [("qT", 12 + 2 * pair, 0, 64), ("qT", 13 + 2 * pair, 64, 64)], 0.125))
    for pair in range(4):
        ch.append((B0 + 512 + pair * 128, 128, [("kT", 12 + 2 * pair, 0, 64), ("kT", 13 + 2 * pair, 64, 64)], 1.0))
    for pair in range(6):
        ch.append((C0 + pair * 128, 128, [("qT", 20 + 2 * pair, 0, 64), ("qT", 21 + 2 * pair, 64, 64)], 0.125))
    for pair in range(3):
        ch.append((CKV + pair * 128, 128, [("cmpT", 2 * pair, 0, 64), ("cmpT", 2 * pair + 1, 64, 64)], 1.0))
    ch.append((CKV + 384, 128, [("kT", 20, 0, 64), ("kT", 21, 64, 64)], 1.0))
    ch.append((CKV + 384 + 128, 64, [("kT", 22, 0, 64)], 1.0))
    ch.append((CKV + 768, 128, [("kT", 23, 0, 64), ("kT", 24, 64, 64)], 1.0))
    ch.append((CKV + 768 + 128, 64, [("kT", 25, 0, 64)], 1.0))
    return ch


def n_chunks():
    ch = []
    for g in range(3):
        ch.append((a_col(g, 2, 0), 256, g * 256))
    ch.append((B0 + 1024, 256, 768))
    ch.append((B0 + 1024 + 256, 256, 1024))
    ch.append((CKV + 576, 192, 1280))
    ch.append((CKV + 960, 192, 1472))
    return ch


def build_p1():
    nc = bass.Bass("TRN2", target_bir_lowering=False)
    xT = nc.dram_tensor("xT", [D, NT], F32, kind="ExternalInput").ap()
    gn = nc.dram_tensor("gn", [128, 16], F32, kind="ExternalInput").ap()
    w = nc.dram_tensor("w", [D, IN_W], F32, kind="ExternalInput").ap()
    outs = {
        "qT": nc.dram_tensor("qT", [32, 64, NT], BF16, kind="ExternalOutput").ap(),
        "kT": nc.dram_tensor("kT", [26, 64, NT], BF16, kind="ExternalOutput").ap(),
        "cmpT": nc.dram_tensor("cmpT", [6, 64, NT], BF16, kind="ExternalOutput").ap(),
    }
    vO = nc.dram_tensor("v", [NT, 1664], BF16, kind="ExternalOutput").ap()
    gT = nc.dram_tensor("gT", [36, NT], F32, kind="ExternalOutput").ap()
    with contextlib.ExitStack() as st:
        T = lambda name, shape, dt: st.enter_context(nc.sbuf_tensor("s_" + name, shape, dt))
        PS = lambda name, shape, dt: st.enter_context(nc.psum_tensor("p_" + name, shape, dt))
        p = Prog(nc)
        emit_p1(nc, p, T, PS, xT, gn, w, outs, vO, gT)
        p.emit(final_wait_ops=p.final_ops)
    return nc


def emit_p1(nc, p, T, PS, xT, gn, w, outs, vO, gT, vdst=None):
    p.final_ops = getattr(p, "final_ops", [])
    hT = T("hT", [128, 16, NT], BF16)
    gsb = T("gsb", [128, 16], F32)
    ones = T("ones", [128, 128], F32)
    xs = [T(f"xs{i}", [128, 16, 512], F32) for i in range(2)]
    sq = [T(f"sq{i}", [128, 512], F32) for i in range(2)]
    rstd = T("rstd", [128, 512], F32)
    wst = [T(f"wst{i}", [128, 16, 256], F32) for i in range(2)]
    wbf = [T(f"wbf{i}", [128, 16, 256], BF16) for i in range(2)]
    ost = [T(f"ost{i}", [128, NT], BF16) for i in range(2)]
    gst = T("gst", [36, NT], F32)
    vst = [T(f"vst{i}", [128, 256], BF16) for i in range(3)]
    pss = PS("pss", [128, 512], F32)
    pacc = [PS(f"pacc{i}", [128, 512], F32) for i in range(3)]

    p.dma("sp", lambda: nc.sync.dma_start(out=gsb[:], in_=gn), writes=["gsb"])
    p.op("pool", lambda: nc.gpsimd.memset(ones[:], 1.0), writes=["ones"])
    xv = xT.rearrange("(k p) t -> p k t", p=128)
    for m in range(4):
        s = m % 2
        p.dma("sp", lambda m=m, s=s: nc.sync.dma_start(out=xs[s][:], in_=xv[:, :, m * 512:(m + 1) * 512]), writes=[f"xs{s}"])
        for k in range(16):
            q = k % 2
            p.op("act", lambda s=s, k=k, q=q: nc.scalar.activation(out=sq[q][:], in_=xs[s][:, k, :], func=AF.Square),
                 reads=[f"xs{s}"], writes=[f"sq{q}"])
            p.op("pe", lambda q=q, k=k: nc.tensor.matmul(pss[:], lhsT=ones[:], rhs=sq[q][:], start=(k == 0), stop=(k == 15)),
                 reads=["ones", f"sq{q}"], writes=["pss"])
        p.op("act", lambda: nc.scalar.activation(out=rstd[:], in_=pss[:], func=AF.Sqrt, scale=1.0 / D, bias=EPS),
             reads=["pss"], writes=["rstd"])
        p.op("dve", lambda: nc.vector.reciprocal(out=rstd[:], in_=rstd[:]), reads=["rstd"], writes=["rstd"])
        for k in range(16):
            eng = "dve" if k % 2 == 0 else "pool"
            E = nc.vector if eng == "dve" else nc.gpsimd
            if eng == "dve":
                p.op("dve", lambda s=s, k=k, m=m: nc.vector.scalar_tensor_tensor(
                    out=hT[:, k, m * 512:(m + 1) * 512], in0=xs[s][:, k, :], scalar=gsb[:, k:k + 1], in1=rstd[:],
                    op0=ALU.mult, op1=ALU.mult), reads=[f"xs{s}", "gsb", "rstd"], writes=[f"hT{m}"])
            else:
                p.op("dve", lambda s=s, k=k, m=m: nc.vector.scalar_tensor_tensor(
                    out=hT[:, k, m * 512:(m + 1) * 512], in0=xs[s][:, k, :], scalar=gsb[:, k:k + 1], in1=rstd[:],
                    op0=ALU.mult, op1=ALU.mult), reads=[f"xs{s}", "gsb", "rstd"], writes=[f"hT{m}"])
    hT_all = [f"hT{m}" for m in range(4)]

    wcount = [0]

    def load_w(c0, ncols):
        s = wcount[0] % 2
        wcount[0] += 1
        src = w[:, c0:c0 + ncols].rearrange("(k p) n -> p k n", p=128)
        p.dma("sp", lambda: nc.sync.dma_start(out=wst[s][:, :, 0:ncols], in_=src), writes=[f"wst{s}"])
        h = 8
        p.op("act", lambda: nc.scalar.copy(out=wbf[s][:, 0:h, 0:ncols], in_=wst[s][:, 0:h, 0:ncols]),
             reads=[f"wst{s}"], writes=[f"wbfa{s}"])
        p.op("pool", lambda: nc.gpsimd.tensor_copy(out=wbf[s][:, h:16, 0:ncols], in_=wst[s][:, h:16, 0:ncols]),
             reads=[f"wst{s}"], writes=[f"wbfb{s}"])
        return s

    tch = t_chunks() + [(CG, 36, [("gT", 0, 0, 36)], 1.0)]
    acc_i = [0]
    for ci, (c0, ncols, dests, scale) in enumerate(tch):
        s = load_w(c0, ncols)
        o = ci % 2
        is_gate = dests[0][0] == "gT"
        for m in range(4):
            a = acc_i[0] % 3
            acc_i[0] += 1
            for k in range(16):
                p.op("pe", lambda a=a, s=s, k=k, m=m, ncols=ncols: nc.tensor.matmul(
                    pacc[a][0:ncols, :], lhsT=wbf[s][:, k, 0:ncols], rhs=hT[:, k, m * 512:(m + 1) * 512],
                    start=(k == 0), stop=(k == 15)),
                    reads=[f"wbfa{s}", f"wbfb{s}", f"hT{m}"], writes=[f"pacc{a}"])
            if is_gate:
                p.op("act", lambda a=a, m=m: nc.scalar.activation(out=gst[:, m * 512:(m + 1) * 512], in_=pacc[a][0:36, :], func=AF.Sigmoid),
                     reads=[f"pacc{a}"], writes=["gst"])
            elif m % 2 == 0:
                p.op("act", lambda a=a, m=m, o=o, ncols=ncols, scale=scale: nc.scalar.activation(
                    out=ost[o][0:ncols, m * 512:(m + 1) * 512], in_=pacc[a][0:ncols, :], func=AF.Copy, scale=scale),
                    reads=[f"pacc{a}"], writes=[f"ost{o}"])
            else:
                p.op("dve", lambda a=a, m=m, o=o, ncols=ncols, scale=scale: nc.vector.tensor_scalar(
                    out=ost[o][0:ncols, m * 512:(m + 1) * 512], in0=pacc[a][0:ncols, :], scalar1=scale, scalar2=None, op0=ALU.mult),
                    reads=[f"pacc{a}"], writes=[f"ost{o}"])
        if is_gate:
            p.final_ops.append(p.dma("pool", lambda: nc.gpsimd.dma_start(out=gT, in_=gst[:]), reads=["gst"]))
        else:
            for (dn, dh, r0, nr) in dests:
                p.final_ops.append(p.dma("pool", lambda dn=dn, dh=dh, r0=r0, nr=nr, o=o: nc.gpsimd.dma_start(
                    out=outs[dn][dh], in_=ost[o][r0:r0 + nr, :]), reads=[f"ost{o}"]))

    vi = [0]
    for (c0, ncols, vc0) in n_chunks():
        s = load_w(c0, ncols)
        for ts in range(16):
            a = acc_i[0] % 3
            acc_i[0] += 1
            m = ts // 4
            for k in range(16):
                p.op("pe", lambda a=a, s=s, k=k, ts=ts, ncols=ncols: nc.tensor.matmul(
                    pacc[a][:, 0:ncols], lhsT=hT[:, k, ts * 128:(ts + 1) * 128], rhs=wbf[s][:, k, 0:ncols],
                    start=(k == 0), stop=(k == 15)),
                    reads=[f"wbfa{s}", f"wbfb{s}", f"hT{m}"], writes=[f"pacc{a}"])
            vs = vi[0] % 3
            vi[0] += 1
            if ts % 2 == 0:
                p.op("act", lambda a=a, vs=vs, ncols=ncols: nc.scalar.copy(out=vst[vs][:, 0:ncols], in_=pacc[a][:, 0:ncols]),
                     reads=[f"pacc{a}"], writes=[f"vst{vs}"])
            else:
                p.op("dve", lambda a=a, vs=vs, ncols=ncols: nc.vector.tensor_copy(out=vst[vs][:, 0:ncols], in_=pacc[a][:, 0:ncols]),
                     reads=[f"pacc{a}"], writes=[f"vst{vs}"])
            if vdst is None:
                p.final_ops.append(p.dma("pool", lambda ts=ts, vs=vs, vc0=vc0, ncols=ncols: nc.gpsimd.dma_start(
                    out=vO[ts * 128:(ts + 1) * 128, vc0:vc0 + ncols], in_=vst[vs][:, 0:ncols]), reads=[f"vst{vs}"]))
            else:
                p.final_ops.append(p.dma("pool", lambda ts=ts, vs=vs, vc0=vc0, ncols=ncols: nc.gpsimd.dma_start(
                    out=vdst(ts, vc0, ncols), in_=vst[vs][:, 0:ncols].rearrange("p (h e) -> p h e", e=64)), reads=[f"vst{vs}"]))


import contextlib, math
import numpy as np

S = 8192
NT = 2048
BW = 4480
GL = BW + 128
NEGM = -30000.0
A_CFG = ((128, 1), (512, 4), (2048, 16))
U16 = mybir.dt.uint16


def t5_bucket_np(dist):
    n = np.maximum(dist, 0)
    nf = np.maximum(n, 1).astype(np.float32)
    large = 16 + (np.log(nf / np.float32(16)) / np.float32(math.log(128.0)) * np.float32(16)).astype(np.int32)
    large = np.minimum(large, 31)
    return np.where(n < 16, n, large)


def band_vectors(rel_table, j):
    v = np.arange(GL)
    dist = v + 512 * j - 2047
    bk = t5_bucket_np(dist)
    G = np.empty((44, GL), np.float32)
    cb = np.empty((44,), np.float32)
    for b in range(44):
        if b < 12:
            h = b
            W, d = A_CFG[b // 4]
            ok = (dist >= 0) & (dist <= W) & (dist % d == 0)
        elif b < 20:
            h = b
            ok = dist >= 0
        elif b < 32:
            h = b
            ok = dist >= 0
        else:
            h = 20 + (b - 32)
            ok = (dist >= 0) & (dist < 512)
        G[b] = np.where(ok, rel_table[h, bk], np.float32(NEGM))
        cb[b] = rel_table[h, 31]
    return G, cb


def core_tokens(j):
    return np.concatenate([np.arange(512 * (4 * m + j), 512 * (4 * m + j) + 512) for m in range(4)])


def moba_consts(j):
    t = core_tokens(j)
    ob = t // 256
    n = np.arange(32)[None, :]
    neg = np.where(n >= ob[:, None], np.float32(-1e30), np.float32(0)).astype(np.float32)
    own = (n >= ob[:, None]).astype(np.float32)
    f = lambda a: np.ascontiguousarray(a.reshape(16, 128, 32).transpose(1, 0, 2))
    return f(neg), f(own)


def eb_const():
    k = np.arange(S)
    return (k[None, :] // 256 == np.arange(32)[:, None]).astype(np.float32)


def ec_const():
    c = np.arange(S)
    key = (c // 128) * 128 + 127 - (c % 128)
    return (key[None, :] // 64 == np.arange(128)[:, None]).astype(np.float32)


def rev_blocks(a, axis):
    a = np.moveaxis(a, axis, -1)
    sh = a.shape
    a = a.reshape(sh[:-1] + (sh[-1] // 128, 128))[..., ::-1].reshape(sh)
    return np.moveaxis(a, -1, axis)


def nsa_consts(j):
    t = core_tokens(j)
    own = (t // 64)[:, None]
    jb = np.arange(128)[None, :]
    valid = jb <= own
    forced = (jb == 0) | (jb == own) | (jb == own - 1)
    am = (valid & ~forced).astype(np.float32)
    ba = np.where(valid, np.where(forced, np.float32(1e9), np.float32(0)), np.float32(-1)).astype(np.float32)
    f = lambda a: np.ascontiguousarray(a.reshape(16, 128, 128).transpose(1, 0, 2))
    pp = np.arange(128)[:, None]
    q = np.arange(512)[None, :]
    cmA = np.where(16 * pp + 31 <= 512 * j + q, np.float32(0), np.float32(NEGM)).astype(np.float32)
    cmB = np.where(16 * pp + 31 - 2048 <= 512 * j + q, np.float32(0), np.float32(NEGM)).astype(np.float32)
    return f(am), f(ba), cmA, cmB


def ovl_const():
    i = np.arange(512)[:, None]
    jb = np.arange(128)[None, :]
    ov = ((16 * i < 64 * jb + 64) & (16 * i + 32 > 64 * jb) & (i < 511)).astype(np.float32)
    return np.ascontiguousarray(ov.reshape(4, 128, 128).transpose(1, 0, 2))


def selg_const():
    sg = np.zeros((36, 36, 64), np.float32)
    for r in range(36):
        sg[r, r, :] = 1.0
    return sg

class P2:
    def __init__(self, nc, p, T, PS, dr, heads_A=(0, 1, 2, 3), heads_B=tuple(range(8)), kvs_C=(0, 1, 2), fused=False):
        self.nc, self.p, self.T, self.PS, self.dr = nc, p, T, PS, dr
        self.fused = fused
        self.heads_A, self.heads_B, self.kvs_C = heads_A, heads_B, kvs_C
        self.units = []
        self.alloc()

    def alloc(self):
        T, PS = self.T, self.PS
        self.kTall = T("kTall", [128, S], BF16)
        self.qTall = T("qTall", [128, NT], BF16)
        self.kTb = [self.kTall[0:64], self.kTall[64:128]]
        self.qTb = [self.qTall[0:64], self.qTall[64:128]]
        self.Vb = [T(f"Vb{i}", [128, 64, 128], BF16) for i in range(2)]
        self.band1 = T("band1", [128, BW], F32)
        self.bandb = [self.band1, self.band1]
        self.cbb = T("cbb", [128, 44], F32)
        self.Eb = T("Eb", [128, S], BF16)
        self.MTb = [T(f"MTb{i}", [128, NT], BF16) for i in range(2)]
        self.sb = [T(f"sb{i}", [128, 512], F32) for i in range(2)]
        self.PT = [T(f"PT{i}", [128, 512], BF16) for i in range(3)]
        self.nd = T("nd", [128, 12, 512], F32)
        self.rden = T("rden", [64, 512], F32)
        self.dsum = T("dsum", [128, 512], F32)
        self.ost = [T(f"ost{i}", [64, 512], BF16) for i in range(3)]
        self.identb = T("identb", [128, 128], BF16)
        self.identf = T("identf", [128, 128], F32)
        self.negm = T("negm", [128, 16, 32], F32)
        self.ownm = T("ownm", [128, 16, 32], F32)
        self.km = T("km", [128, 32], F32)
        self.kmb = T("kmb", [128, 32], BF16)
        self.gm = T("gm", [128, 16, 32], F32)
        self.m8 = T("m8", [128, 16, 8], F32)
        self.selt = T("selt", [128, 16, 32], F32)
        self.Mq = T("Mq", [128, 16, 32], BF16)
        self.qcm = [T(f"qcm{i}", [64, 512], BF16) for i in range(3)]
        self.ef = [T(f"ef{i}", [128, 512], F32) for i in range(4)]
        self.pcb = [T(f"pcb{i}", [128, 512], BF16) for i in range(4)]
        self.w1b = T("w1b", [64, 32, 128], BF16)
        self.w2f = T("w2f", [128, 2, 64], F32)
        self.w2b = T("w2b", [128, 2, 64], BF16)
        self.peTf = T("peTf", [64, 2, 32, 2], F32)
        self.peTb = T("peTb", [64, 2, 32, 2], BF16)
        self.kcTb = T("kcTb", [64, 512], BF16)
        self.vcb = T("vcb", [128, 4, 64], BF16)
        self.scr = [T(f"scr{i}", [128, 512], F32) for i in range(2)]
        self.cbias = T("cbias", [128, 1], F32)
        self.hid = T("hid", [128, 512], BF16)
        self.amb = T("amb", [128, 4, 128], F32)
        self.bab = T("bab", [128, 4, 128], F32)
        self.cmA = T("cmA", [128, 512], F32)
        self.cmB = T("cmB", [128, 512], F32)
        self.ovl = T("ovl", [128, 4, 128], F32)
        self.sel3 = T("sel3", [36, 3, 64], F32)
        self.gTs = T("gTs", [36, NT], F32)
        self.impS = T("impS", [128, 512], F32)
        self.score = T("score", [128, 4, 128], F32)
        self.sc2 = T("sc2", [128, 128], F32)
        self.m16 = T("m16", [128, 4, 16], F32)
        self.Msel = T("Msel", [128, 4, 128], BF16)
        self.onesf = T("onesf", [128, 128], F32)
        self.tmpf = T("tmpf", [64, 512], F32)
        self.ocs = T("ocs", [64, 512], F32)
        self.pS = [PS(f"pS{i}", [128, 512], F32) for i in range(3)]
        self.pacc = [PS(f"pacc{i}", [128, 512], F32) for i in range(2)]
        self.pm = [PS(f"pm{i}", [128, 512], F32) for i in range(2)]
        self.pmb = PS("pmb", [128, 1024], BF16)
        if self.fused:
            self.hst = [T(f"hst{i}", [128, 512], F32) for i in range(2)]
            self.Jf = T("Jf", [128, 128], F32)
        self.ost_i = 0
        self.slot = 0
        self.acc_i = 0
        self.out_ops = []

    def setup(self):
        nc, p = self.nc, self.p
        p.op("pool", lambda: nc.gpsimd.memset(self.identf[:], 1.0), writes=["identf"])
        p.op("pool", lambda: nc.gpsimd.affine_select(out=self.identf[:], in_=self.identf[:], pattern=[[-1, 128]],
                                                     compare_op=ALU.is_equal, fill=0.0, base=0, channel_multiplier=1),
             reads=["identf"], writes=["identf"])
        p.op("dve", lambda: nc.vector.tensor_copy(out=self.identb[:], in_=self.identf[:]), reads=["identf"], writes=["identb"])
        for i in range(2):
            p.op("pool", lambda i=i: nc.gpsimd.memset(self.Vb[i][:, :, 64:128], 1.0), writes=[f"Vones{i}"])
        p.dma("sp", lambda: nc.sync.dma_start(out=self.cbb[:], in_=self.dr["cb"]), writes=["cbb"])
        if self.fused:
            p.op("pool", lambda: nc.gpsimd.memset(self.Jf[:], 1.0), writes=["Jf"])
            p.op("pool", lambda: nc.gpsimd.affine_select(out=self.Jf[:], in_=self.Jf[:], pattern=[[1, 128]],
                                                         compare_op=ALU.is_equal, fill=0.0, base=-127, channel_multiplier=1),
                 reads=["Jf"], writes=["Jf"])

    def load_kT_gathered(self, dst, chunks, row0, res):
        nc, p = self.nc, self.p
        src2d = None
        for (r0, nr, ap) in chunks:
            if r0 <= row0 < r0 + nr:
                src2d, row0 = ap, row0 - r0
                break
        nrows = src2d.shape[0] // 4
        for m in range(4):
            src = bass.AP(tensor=src2d.tensor, offset=src2d[row0:row0 + 1, m * 512:m * 512 + 1].offset,
                          ap=[[2048, 64], [nrows * 2048, 4], [1, 512]])
            d = dst[:, m * 2048:(m + 1) * 2048].rearrange("e (r i) -> e r i", i=512)
            p.dma("sp", lambda src=src, d=d: nc.sync.dma_start(out=d, in_=src), reads=["gathered"], writes=[res])

    def load_v_gathered(self, s, ki):
        nc, p = self.nc, self.p
        vg, lrow, nr = None, 0, 0
        for (r0, nr_, ap) in self.dr["v_g"]:
            if r0 <= ki * 128 < r0 + nr_:
                vg, lrow, nr = ap, ki * 128 - r0, nr_
                break
        for m in range(4):
            for r in range(4):
                src = bass.AP(tensor=vg.tensor, offset=vg[r * nr + lrow:r * nr + lrow + 1, m * 256:m * 256 + 1].offset,
                              ap=[[1024, 128], [64, 4], [1, 64]])
                d = self.Vb[s][:, m * 16 + r * 4:m * 16 + r * 4 + 4, 0:64]
                p.dma("sp", lambda src=src, d=d: nc.sync.dma_start(out=d, in_=src), reads=["gathered"], writes=[f"Vb{s}"])

    def load_band_flipped(self, bi):
        nc, p = self.nc, self.p
        g = self.dr["G"]
        nch = (BW + 511) // 512
        for ch in range(nch):
            w_ = min(512, BW - ch * 512)
            hs = ch % 2
            src = bass.AP(tensor=g.tensor, offset=g[bi:bi + 1, ch * 512:ch * 512 + 1].offset, ap=[[1, 128], [1, w_]])
            p.dma("sp", lambda src=src, hs=hs, w_=w_: nc.sync.dma_start(out=self.hst[hs][:, 0:w_], in_=src), writes=[f"hst{hs}"])
            pb = self.pm[ch % 2]
            p.op("pe", lambda hs=hs, w_=w_, pb=pb: nc.tensor.matmul(pb[:, 0:w_], lhsT=self.Jf[:], rhs=self.hst[hs][:, 0:w_], start=True, stop=True),
                 reads=["Jf", f"hst{hs}"], writes=[f"pm{ch % 2}"])
            p.op("act", lambda ch=ch, w_=w_, pb=pb: nc.scalar.copy(out=self.band1[:, ch * 512:ch * 512 + w_], in_=pb[:, 0:w_]),
                 reads=[f"pm{ch % 2}"], writes=["bandb"])

    def load_head(self, qi, ki, bi):
        nc, p, dr = self.nc, self.p, self.dr
        s = self.slot % 2
        self.slot += 1
        if self.fused:
            p.dma("sp", lambda: nc.sync.dma_start(out=self.qTb[s], in_=dr["qT"][qi]), reads=["qT_s"], writes=[f"qTb{s}"])
            self.load_kT_gathered(self.kTb[s], dr["kT_g"], ki * 64, f"kTb{s}")
            self.load_v_gathered(s, ki)
            self.load_band_flipped(bi)
            return s
        p.dma("sp", lambda: nc.sync.dma_start(out=self.qTb[s], in_=dr["qT"][qi]), writes=[f"qTb{s}"])
        p.dma("sp", lambda: nc.sync.dma_start(out=self.kTb[s], in_=dr["kTf"][ki]), writes=[f"kTb{s}"])
        p.dma("sp", lambda: nc.sync.dma_start(out=self.Vb[s][:, :, 0:64], in_=dr["vf"][ki]), writes=[f"Vb{s}"])
        g = dr["G"]
        src = bass.AP(tensor=g.tensor, offset=g[bi:bi + 1, 0:1].offset, ap=[[1, 128], [1, BW]])
        p.dma("sp", lambda: nc.sync.dma_start(out=self.bandb[s][:], in_=src), writes=["bandb"])
        return s

    def add_units(self, s, m, kts, near_lo, bi, acc, mask=None, post=None, pre=None):
        n = len(kts)
        for idx, kt in enumerate(kts):
            self.units.append(dict(s=s, m=m, kt=kt, near=(kt >= near_lo), bi=bi, acc=acc, first=(idx == 0), last=(idx == n - 1),
                                   mask=mask, post=post if idx == n - 1 else None, pre=pre if idx == 0 else None))

    def flush_units(self, LA=2):
        nc, p = self.nc, self.p
        U = self.units
        n = len(U)

        def qk(i):
            u = U[i]
            if u["pre"] is not None:
                u["pre"]()
            b = i % 3
            s, m, kt = u["s"], u["m"], u["kt"]
            rd = [f"kTb{s}", f"qTb{s}"]
            if u["mask"] is None:
                p.op("pe", lambda: nc.tensor.matmul(self.pS[b][:], lhsT=self.kTb[s][:, kt * 128:(kt + 1) * 128],
                                                    rhs=self.qTb[s][:, m * 512:(m + 1) * 512], start=True, stop=True),
                     reads=rd, writes=[f"pS{b}"])
            else:
                nr, ms = u["mask"]
                p.op("pe", lambda: nc.tensor.matmul(self.pS[b][:], lhsT=self.kTb[s][:, kt * 128:(kt + 1) * 128],
                                                    rhs=self.qTb[s][:, m * 512:(m + 1) * 512], start=True, stop=False),
                     reads=rd, writes=[f"pS{b}"])
                p.op("pe", lambda: nc.tensor.matmul(self.pS[b][:], lhsT=self.Eb[0:nr, kt * 128:(kt + 1) * 128],
                                                    rhs=self.MTb[ms][0:nr, m * 512:(m + 1) * 512], start=False, stop=True),
                     reads=["Eb", f"MTb{ms}"], writes=[f"pS{b}"])

        def rest(i):
            u = U[i]
            b = i % 3
            s, m, kt, bi, acc = u["s"], u["m"], u["kt"], u["bi"], u["acc"]
            pt = i % 3
            if u["near"]:
                sbi = i % 2
                u0 = 2048 * m - 128 * kt + 1920
                assert 0 <= u0 and u0 + 512 <= BW, (m, kt, u0)
                p.op("dve", lambda: nc.vector.tensor_tensor(out=self.sb[sbi][:], in0=self.pS[b][:], in1=self.bandb[s][:, u0:u0 + 512], op=ALU.add),
                     reads=[f"pS{b}", "bandb"], writes=[f"sb{sbi}"])
                p.op("act", lambda: nc.scalar.activation(out=self.PT[pt][:], in_=self.sb[sbi][:], func=AF.Exp),
                     reads=[f"sb{sbi}"], writes=[f"PT{pt}"])
            else:
                p.op("act", lambda: nc.scalar.activation(out=self.PT[pt][:], in_=self.pS[b][:], func=AF.Exp, bias=self.cbb[:, bi:bi + 1]),
                     reads=[f"pS{b}", "cbb"], writes=[f"PT{pt}"])
            p.op("pe", lambda: nc.tensor.matmul(self.pacc[acc][:], lhsT=self.Vb[s][:, kt, :], rhs=self.PT[pt][:],
                                                start=u["first"], stop=u["last"]),
                 reads=[f"Vb{s}", f"Vones{s}", f"PT{pt}"], writes=[f"pacc{acc}"])
            if u["post"] is not None:
                u["post"]()

        for i in range(n + LA):
            if i < n:
                qk(i)
            if i - LA >= 0:
                rest(i - LA)
        self.units = []

    def write_out(self, head_feat, m, num_ap, rden_ap):
        nc, p = self.nc, self.p
        o = self.ost_i % 3
        self.ost_i += 1
        num, nres = num_ap
        rd, rres = rden_ap
        p.op("dve", lambda: nc.vector.tensor_tensor(out=self.ost[o][:], in0=num, in1=rd, op=ALU.mult),
             reads=[nres, rres], writes=[f"ost{o}"])
        dst = self.dr["OT"][head_feat * 64:(head_feat + 1) * 64, m * 512:(m + 1) * 512]
        self.out_ops.append(p.dma("pool", lambda: nc.gpsimd.dma_start(out=dst, in_=self.ost[o][:]), reads=[f"ost{o}"]))

    def mixer_A(self):
        nc, p = self.nc, self.p
        for hg in self.heads_A:
            for g in range(3):
                W, d = A_CFG[g]
                h = g * 4 + hg
                s = self.load_head(h, h, h)
                for m in range(4):
                    lo = max(0, (2048 * m - W) // 128)
                    kts = list(range(lo, 16 * m + 16))
                    acc = self.acc_i % 2
                    self.acc_i += 1

                    def post(g=g, m=m, acc=acc):
                        p.op("act", lambda: nc.scalar.copy(out=self.nd[:, g * 4 + m, :], in_=self.pacc[acc][:]),
                             reads=[f"pacc{acc}"], writes=[f"nd{g}_{m}"])
                    self.add_units(s, m, kts, 0, h, acc, post=post)
                self.flush_units()
            for m in range(4):
                p.op("pool", lambda m=m: nc.gpsimd.tensor_tensor(out=self.dsum[64:128, :], in0=self.nd[64:128, 0 * 4 + m, :], in1=self.nd[64:128, 1 * 4 + m, :], op=ALU.add),
                     reads=[f"nd0_{m}", f"nd1_{m}"], writes=["dsum"])
                p.op("pool", lambda m=m: nc.gpsimd.tensor_tensor(out=self.dsum[64:128, :], in0=self.dsum[64:128, :], in1=self.nd[64:128, 2 * 4 + m, :], op=ALU.add),
                     reads=["dsum", f"nd2_{m}"], writes=["dsum"])
                p.op("dve", lambda: nc.vector.reciprocal(out=self.rden[:], in_=self.dsum[64:128, :]), reads=["dsum"], writes=["rden"])
                for g in range(3):
                    self.write_out(g * 4 + hg, m, (self.nd[0:64, g * 4 + m, :], f"nd{g}_{m}"), (self.rden[:], "rden"))

    def moba_prologue(self, s, ms):
        nc, p = self.nc, self.p
        p.op("dve", lambda: nc.vector.tensor_reduce(out=self.km[64 * s:64 * s + 64, :], in_=self.kTb[s].rearrange("e (n k) -> e n k", k=256), axis=AX.X, op=ALU.add),
             reads=[f"kTb{s}"], writes=["km"])
        p.op("dve", lambda: nc.vector.tensor_scalar(out=self.kmb[64 * s:64 * s + 64, :], in0=self.km[64 * s:64 * s + 64, :], scalar1=1.0 / 256, scalar2=None, op0=ALU.mult),
             reads=["km"], writes=["kmb"])
        pg = self.pm[0]
        for qs in range(16):
            p.op("pe", lambda qs=qs: nc.tensor.matmul(pg[:, qs * 32:(qs + 1) * 32], lhsT=self.qTb[s][:, qs * 128:(qs + 1) * 128], rhs=self.kmb[64 * s:64 * s + 64, :], start=True, stop=True),
                 reads=[f"qTb{s}", "kmb"], writes=["pm0"])
        p.op("dve", lambda: nc.vector.tensor_tensor(out=self.gm[:], in0=pg[:].rearrange("p (a b) -> p a b", b=32), in1=self.negm[:], op=ALU.add),
             reads=["pm0", "negm"], writes=["gm"])
        for qs in range(16):
            p.op("dve", lambda qs=qs: nc.vector.max(out=self.m8[:, qs, :], in_=self.gm[:, qs, :]), reads=["gm"], writes=["m8"])
        for qs in range(16):
            p.op("dve", lambda qs=qs: nc.vector.tensor_scalar(out=self.selt[:, qs, :], in0=self.gm[:, qs, :], scalar1=self.m8[:, qs, 2:3], scalar2=None, op0=ALU.is_ge),
                 reads=["gm", "m8"], writes=["selt"])
        p.op("dve", lambda: nc.vector.tensor_tensor(out=self.selt[:], in0=self.selt[:], in1=self.ownm[:], op=ALU.max), reads=["selt", "ownm"], writes=["selt"])
        p.op("dve", lambda: nc.vector.tensor_scalar(out=self.Mq[:], in0=self.selt[:], scalar1=-1.0, scalar2=-NEGM, op0=ALU.add, op1=ALU.mult),
             reads=["selt"], writes=["Mq"])
        for half in range(2):
            for q8 in range(8):
                qs = half * 8 + q8
                p.op("pe", lambda qs=qs, q8=q8: nc.tensor.transpose(out=self.pmb[0:32, q8 * 128:(q8 + 1) * 128], in_=self.Mq[:, qs, :], identity=self.identb[:]),
                     reads=["Mq", "identb"], writes=["pmb"])
            p.op("act", lambda half=half: nc.scalar.copy(out=self.MTb[ms][0:32, half * 1024:(half + 1) * 1024], in_=self.pmb[0:32, :]),
                 reads=["pmb"], writes=[f"MTb{ms}"])

    def mixer_B(self):
        nc, p, dr = self.nc, self.p, self.dr
        p.dma("sp", lambda: nc.sync.dma_start(out=self.negm[:], in_=dr["negm"]), writes=["negm"])
        p.dma("sp", lambda: nc.sync.dma_start(out=self.ownm[:], in_=dr["ownm"]), writes=["ownm"])
        p.dma("sp", lambda: nc.sync.dma_start(out=self.Eb[0:32, :], in_=dr["EB"]), writes=["Eb"])
        for hb in self.heads_B:
            h = 12 + hb
            s = self.load_head(h, h, h)
            ms = hb % 2
            self.moba_prologue(s, ms)
            for m in range(4):
                kts = list(range(0, 16 * m + 16))
                acc = self.acc_i % 2
                self.acc_i += 1

                def post(h=h, m=m, acc=acc):
                    p.op("dve", lambda: nc.vector.reciprocal(out=self.rden[:], in_=self.pacc[acc][64:128, :]), reads=[f"pacc{acc}"], writes=["rden"])
                    self.write_out(h, m, (self.pacc[acc][0:64, :], f"pacc{acc}"), (self.rden[:], "rden"))
                self.add_units(s, m, kts, 16 * m - 12, h, acc, mask=(32, ms), post=post)
            self.flush_units()


    def compress(self, kv, t):
        nc, p, dr = self.nc, self.p, self.dr
        s = 0
        if self.fused:
            self.load_kT_gathered(self.kTb[s], dr["cmpT_g"], (t * 3 + kv) * 64, f"kTb{s}")
        else:
            p.dma("sp", lambda: nc.sync.dma_start(out=self.kTb[s], in_=dr["cmpTf"][t * 3 + kv]), writes=[f"kTb{s}"])
        stg = self.band1[0:64, 0:4096].rearrange("e (l j) -> e l j", j=128)
        p.dma("sp", lambda: nc.sync.dma_start(out=stg, in_=dr["w1"][t]), writes=["bandb"])
        p.op("act", lambda: nc.scalar.copy(out=self.w1b[:], in_=stg), reads=["bandb"], writes=["w1b"])
        ph = self.pm[0]
        base = self.kTb[s]
        for l in range(32):
            rhs = bass.AP(tensor=base.tensor, offset=base.offset + l, ap=[list(base.ap[0]), [16, 511]])
            p.op("pe", lambda l=l, rhs=rhs: nc.tensor.matmul(ph[:, 0:511], lhsT=self.w1b[:, l, :], rhs=rhs, start=(l == 0), stop=(l == 31)),
                 reads=["w1b", f"kTb{s}"], writes=["pm0"])
        pc = self.pm[1]
        for l in range(32):
            p.op("pe", lambda l=l: nc.tensor.matmul(pc[:, 0:2], lhsT=self.w1b[:, l, :], rhs=self.peTb[:, t, l, :], start=(l == 0), stop=(l == 31)),
                 reads=["w1b", "peTb"], writes=["pm1"])
        p.op("dve", lambda: nc.vector.tensor_copy(out=self.cbias[:], in_=pc[:, 0:1]), reads=["pm1"], writes=["cbias"])
        x, y = self.scr[0], self.scr[1]
        p.op("act", lambda: nc.scalar.activation(out=x[:, 0:511], in_=ph[:, 0:511], func=AF.Identity, bias=self.cbias[:, 0:1]),
             reads=["pm0", "cbias"], writes=["scr0"])
        p.op("dve", lambda: nc.vector.tensor_tensor(out=y[:, 0:511], in0=x[:, 0:511], in1=x[:, 0:511], op=ALU.mult), reads=["scr0"], writes=["scr1"])
        p.op("dve", lambda: nc.vector.tensor_scalar(out=y[:, 0:511], in0=y[:, 0:511], scalar1=0.044715, scalar2=1.0, op0=ALU.mult, op1=ALU.add), reads=["scr1"], writes=["scr1"])
        p.op("dve", lambda: nc.vector.tensor_tensor(out=y[:, 0:511], in0=y[:, 0:511], in1=x[:, 0:511], op=ALU.mult), reads=["scr0", "scr1"], writes=["scr1"])
        p.op("act", lambda: nc.scalar.activation(out=y[:, 0:511], in_=y[:, 0:511], func=AF.Tanh, scale=0.7978845608028654), reads=["scr1"], writes=["scr1"])
        p.op("dve", lambda: nc.vector.scalar_tensor_tensor(out=y[:, 0:511], in0=y[:, 0:511], scalar=1.0, in1=x[:, 0:511], op0=ALU.add, op1=ALU.mult), reads=["scr0", "scr1"], writes=["scr1"])
        p.op("dve", lambda: nc.vector.tensor_scalar(out=self.hid[:, 0:511], in0=y[:, 0:511], scalar1=0.5, scalar2=None, op0=ALU.mult), reads=["scr1"], writes=["hid"])
        if t == 0:
            pk = self.pS[0]
            p.op("pe", lambda: nc.tensor.matmul(pk[0:64, :], lhsT=self.w2b[:, 0, :], rhs=self.hid[:], start=True, stop=True),
                 reads=["w2b", "hid"], writes=["pS0"])
            p.op("act", lambda: nc.scalar.copy(out=self.kcTb[:], in_=pk[0:64, :]), reads=["pS0"], writes=["kcTb"])
        else:
            pv = self.pS[1]
            for it in range(4):
                p.op("pe", lambda it=it: nc.tensor.matmul(pv[:, it * 64:(it + 1) * 64], lhsT=self.hid[:, it * 128:(it + 1) * 128], rhs=self.w2b[:, 1, :], start=True, stop=True),
                     reads=["w2b", "hid"], writes=["pS1"])
            p.op("act", lambda: nc.scalar.copy(out=self.vcb[:], in_=pv[:, 0:256].rearrange("p (a b) -> p a b", b=64)), reads=["pS1"], writes=["vcb"])

    def cmp_stage(self, kv, m):
        nc, p, dr = self.nc, self.p, self.dr
        pden, poc, pgt, pimp, ptr = self.pS[0], self.pS[1], self.pS[2], self.pacc[0], self.pacc[1]
        nit = min(m, 3) + 1
        for gq in range(4):
            hc = kv * 4 + gq
            qs = self.qc_i % 3
            self.qc_i += 1
            p.dma("sp", lambda qs=qs, hc=hc: nc.sync.dma_start(out=self.qcm[qs][:], in_=dr["qT"][20 + hc][:, m * 512:(m + 1) * 512]), writes=[f"qcm{qs}"])
            p.dma("sp", lambda hc=hc: nc.sync.dma_start(out=self.sel3[:], in_=dr["selg"][:, 3 * hc:3 * hc + 3, :]), writes=["sel3"])
            for it in range(nit):
                ps = self.pm[it % 2]
                psn = f"pm{it % 2}"
                p.op("pe", lambda it=it, ps=ps, qs=qs: nc.tensor.matmul(ps[:], lhsT=self.kcTb[:, it * 128:(it + 1) * 128], rhs=self.qcm[qs][:], start=True, stop=True),
                     reads=["kcTb", f"qcm{qs}"], writes=[psn])
                if it >= m - 1:
                    cm, cmn = (self.cmA, "cmA") if it == m else (self.cmB, "cmB")
                    sbi = it % 2
                    p.op("dve", lambda ps=ps, cm=cm, sbi=sbi: nc.vector.tensor_tensor(out=self.sb[sbi][:], in0=ps[:], in1=cm[:], op=ALU.add),
                         reads=[psn, cmn], writes=[f"sb{sbi}"])
                    p.op("act", lambda it=it, sbi=sbi: nc.scalar.activation(out=self.ef[it][:], in_=self.sb[sbi][:], func=AF.Exp), reads=[f"sb{sbi}"], writes=[f"ef{it}"])
                else:
                    p.op("act", lambda it=it, ps=ps: nc.scalar.activation(out=self.ef[it][:], in_=ps[:], func=AF.Exp), reads=[psn], writes=[f"ef{it}"])
                p.op("pe", lambda it=it: nc.tensor.matmul(pden[:], lhsT=self.onesf[:], rhs=self.ef[it][:], start=(it == 0), stop=(it == nit - 1)),
                     reads=["onesf", f"ef{it}"], writes=["pS0"])
            rd = self.scr[0]
            p.op("dve", lambda: nc.vector.tensor_scalar(out=rd[:], in0=pden[:], scalar1=1e-30, scalar2=None, op0=ALU.max), reads=["pS0"], writes=["scr0"])
            p.op("dve", lambda: nc.vector.reciprocal(out=rd[:], in_=rd[:]), reads=["scr0"], writes=["scr0"])
            for it in range(nit):
                p.op("dve", lambda it=it: nc.vector.tensor_tensor(out=self.ef[it][:], in0=self.ef[it][:], in1=rd[:], op=ALU.mult), reads=[f"ef{it}", "scr0"], writes=[f"ef{it}"])
                p.op("pool", lambda it=it: nc.gpsimd.tensor_copy(out=self.pcb[it][:], in_=self.ef[it][:]), reads=[f"ef{it}"], writes=[f"pcb{it}"])
                p.op("pe", lambda it=it, gq=gq: nc.tensor.matmul(pimp[:], lhsT=self.ovl[:, it, :], rhs=self.ef[it][:], start=(gq == 0 and it == 0), stop=(gq == 3 and it == nit - 1)),
                     reads=["ovl", f"ef{it}"], writes=["pacc0"])
            for it in range(nit):
                p.op("pe", lambda it=it: nc.tensor.matmul(poc[0:64, :], lhsT=self.vcb[:, it, :], rhs=self.pcb[it][:], start=(it == 0), stop=(it == nit - 1)),
                     reads=["vcb", f"pcb{it}"], writes=["pS1"])
            p.op("pe", lambda: nc.tensor.matmul(pgt[0:64, :], lhsT=self.sel3[:, 0, :], rhs=self.gTs[:, m * 512:(m + 1) * 512], start=True, stop=True),
                 reads=["sel3", "gTs"], writes=["pS2"])
            p.op("act", lambda: nc.scalar.copy(out=self.ocs[:], in_=poc[0:64, :]), reads=["pS1"], writes=["ocs"])
            half, idx = gq // 2, (gq % 2) * 4 + m
            if half == 0:
                p.op("dve", lambda idx=idx: nc.vector.tensor_tensor(out=self.nd[0:64, idx, :], in0=self.ocs[:], in1=pgt[0:64, :], op=ALU.mult),
                     reads=["ocs", "pS2"], writes=[f"oc{gq}_{m}"])
            else:
                p.op("dve", lambda: nc.vector.tensor_tensor(out=self.tmpf[:], in0=self.ocs[:], in1=pgt[0:64, :], op=ALU.mult),
                     reads=["ocs", "pS2"], writes=["tmpf"])
                p.op("dve", lambda idx=idx: nc.vector.tensor_copy(out=self.nd[64:128, idx, :], in_=self.tmpf[:]), reads=["tmpf"], writes=[f"oc{gq}_{m}"])
        p.dma("sp", lambda: nc.sync.dma_start(out=self.amb[:], in_=dr["AM"][:, 4 * m:4 * m + 4, :]), writes=["amb"])
        p.dma("sp", lambda: nc.sync.dma_start(out=self.bab[:], in_=dr["BA"][:, 4 * m:4 * m + 4, :]), writes=["bab"])
        p.op("act", lambda: nc.scalar.copy(out=self.impS[:], in_=pimp[:]), reads=["pacc0"], writes=["impS"])
        for qs in range(4):
            p.op("pe", lambda qs=qs: nc.tensor.transpose(out=ptr[:, qs * 128:(qs + 1) * 128], in_=self.impS[:, qs * 128:(qs + 1) * 128], identity=self.identf[:]),
                 reads=["impS", "identf"], writes=["pacc1"])
        p.op("dve", lambda: nc.vector.tensor_tensor(out=self.score[:], in0=ptr[:].rearrange("p (a b) -> p a b", b=128), in1=self.amb[:], op=ALU.mult),
             reads=["pacc1", "amb"], writes=["score"])
        p.op("dve", lambda: nc.vector.tensor_tensor(out=self.score[:], in0=self.score[:], in1=self.bab[:], op=ALU.add), reads=["score", "bab"], writes=["score"])
        for qs in range(4):
            p.op("dve", lambda qs=qs: nc.vector.max(out=self.m16[:, qs, 0:8], in_=self.score[:, qs, :]), reads=["score"], writes=["m16"])
            p.op("dve", lambda qs=qs: nc.vector.match_replace(out=self.sc2[:], in_to_replace=self.m16[:, qs, 0:8], in_values=self.score[:, qs, :], imm_value=-1e30),
                 reads=["score", "m16"], writes=["sc2"])
            p.op("dve", lambda qs=qs: nc.vector.max(out=self.m16[:, qs, 8:16], in_=self.sc2[:]), reads=["sc2"], writes=["m16"])
            p.op("dve", lambda qs=qs: nc.vector.tensor_scalar(out=self.score[:, qs, :], in0=self.score[:, qs, :], scalar1=self.m16[:, qs, 15:16], scalar2=None, op0=ALU.is_ge),
                 reads=["score", "m16"], writes=["score"])
        p.op("dve", lambda: nc.vector.tensor_scalar(out=self.Msel[:], in0=self.score[:], scalar1=-1.0, scalar2=-NEGM, op0=ALU.add, op1=ALU.mult), reads=["score"], writes=["Msel"])
        ms = kv % 2
        for qs in range(4):
            p.op("pe", lambda qs=qs: nc.tensor.transpose(out=self.pmb[:, qs * 128:(qs + 1) * 128], in_=self.Msel[:, qs, :], identity=self.identb[:]),
                 reads=["Msel", "identb"], writes=["pmb"])
        p.op("act", lambda: nc.scalar.copy(out=self.MTb[ms][:, m * 512:(m + 1) * 512], in_=self.pmb[:, 0:512]), reads=["pmb"], writes=[f"MTb{ms}"])

    def mixer_C(self):
        nc, p, dr = self.nc, self.p, self.dr
        self.qc_i = 0
        p.dma("sp", lambda: nc.sync.dma_start(out=self.Eb[:], in_=dr["EC"]), writes=["Eb"])
        for nm in ("cmA", "cmB", "ovl", "gTs"):
            p.dma("sp", lambda nm=nm: nc.sync.dma_start(out=getattr(self, nm)[:], in_=dr[nm]), writes=[nm])
        p.dma("sp", lambda: nc.sync.dma_start(out=self.w2f[:], in_=dr["w2"]), writes=["w2f"])
        p.dma("sp", lambda: nc.sync.dma_start(out=self.peTf[:], in_=dr["peT"]), writes=["peTf"])
        p.op("dve", lambda: nc.vector.tensor_copy(out=self.w2b[:], in_=self.w2f[:]), reads=["w2f"], writes=["w2b"])
        p.op("dve", lambda: nc.vector.tensor_copy(out=self.peTb[:], in_=self.peTf[:]), reads=["peTf"], writes=["peTb"])
        p.op("pool", lambda: nc.gpsimd.memset(self.onesf[:], 1.0), writes=["onesf"])
        p.op("pool", lambda: nc.gpsimd.memset(self.hid[:], 0.0), writes=["hid"])
        for kv in self.kvs_C:
            self.compress(kv, 0)
            self.compress(kv, 1)
            for m in range(4):
                self.cmp_stage(kv, m)
            ms = kv % 2
            for gq in range(4):
                hc = kv * 4 + gq
                s = self.load_head(20 + hc, 20 + kv, 20 + hc)
                p.dma("sp", lambda hc=hc: nc.sync.dma_start(out=self.sel3[:], in_=dr["selg"][:, 3 * hc:3 * hc + 3, :]), writes=["sel3"])
                for m in range(4):
                    kts = list(range(0, 16 * m + 16))
                    acc = self.acc_i % 2
                    self.acc_i += 1

                    def post(gq=gq, m=m, acc=acc):
                        pg = self.pm[0]
                        p.op("dve", lambda: nc.vector.reciprocal(out=self.rden[:], in_=self.pacc[acc][64:128, :]), reads=[f"pacc{acc}"], writes=["rden"])
                        p.op("pe", lambda: nc.tensor.matmul(pg[0:64, :], lhsT=self.sel3[:, 1, :], rhs=self.gTs[:, m * 512:(m + 1) * 512], start=True, stop=True),
                             reads=["sel3", "gTs"], writes=["pm0"])
                        p.op("dve", lambda: nc.vector.tensor_tensor(out=self.tmpf[:], in0=self.pacc[acc][0:64, :], in1=self.rden[:], op=ALU.mult),
                             reads=[f"pacc{acc}", "rden"], writes=["tmpf"])
                        p.op("dve", lambda: nc.vector.tensor_tensor(out=self.nd[0:64, 8 + m, :], in0=self.tmpf[:], in1=pg[0:64, :], op=ALU.mult),
                             reads=["tmpf", "pm0"], writes=[f"res{m}"])
                        half, idx = gq // 2, (gq % 2) * 4 + m
                        if half == 0:
                            p.op("pool", lambda: nc.gpsimd.tensor_tensor(out=self.nd[0:64, 8 + m, :], in0=self.nd[0:64, 8 + m, :], in1=self.nd[0:64, idx, :], op=ALU.add),
                                 reads=[f"res{m}", f"oc{gq}_{m}"], writes=[f"res{m}"])
                        else:
                            p.op("dve", lambda: nc.vector.tensor_copy(out=self.tmpf[:], in_=self.nd[64:128, idx, :]), reads=[f"oc{gq}_{m}"], writes=["tmpf"])
                            p.op("pool", lambda: nc.gpsimd.tensor_tensor(out=self.nd[0:64, 8 + m, :], in0=self.nd[0:64, 8 + m, :], in1=self.tmpf[:], op=ALU.add),
                                 reads=[f"res{m}", "tmpf"], writes=[f"res{m}"])
                    self.add_units(s, m, kts, 16 * m - 12, 20 + hc, acc, mask=(128, ms), post=post)
                self.flush_units()
                s = self.load_head(20 + hc, 23 + kv, 32 + hc)
                for m in range(4):
                    kts = list(range(max(0, 16 * m - 4), 16 * m + 16))
                    acc = self.acc_i % 2
                    self.acc_i += 1

                    def post(hc=hc, m=m, acc=acc):
                        pg = self.pm[1]
                        p.op("dve", lambda: nc.vector.reciprocal(out=self.rden[:], in_=self.pacc[acc][64:128, :]), reads=[f"pacc{acc}"], writes=["rden"])
                        p.op("pe", lambda: nc.tensor.matmul(pg[0:64, :], lhsT=self.sel3[:, 2, :], rhs=self.gTs[:, m * 512:(m + 1) * 512], start=True, stop=True),
                             reads=["sel3", "gTs"], writes=["pm1"])
                        p.op("dve", lambda: nc.vector.tensor_tensor(out=self.tmpf[:], in0=self.pacc[acc][0:64, :], in1=self.rden[:], op=ALU.mult),
                             reads=[f"pacc{acc}", "rden"], writes=["tmpf"])
                        p.op("dve", lambda: nc.vector.tensor_tensor(out=self.tmpf[:], in0=self.tmpf[:], in1=pg[0:64, :], op=ALU.mult),
                             reads=["tmpf", "pm1"], writes=["tmpf"])
                        o = self.ost_i % 3
                        self.ost_i += 1
                        p.op("dve", lambda: nc.vector.tensor_tensor(out=self.ost[o][:], in0=self.tmpf[:], in1=self.nd[0:64, 8 + m, :], op=ALU.add),
                             reads=["tmpf", f"res{m}"], writes=[f"ost{o}"])
                        dst = self.dr["OT"][(20 + hc) * 64:(21 + hc) * 64, m * 512:(m + 1) * 512]
                        self.out_ops.append(p.dma("pool", lambda: nc.gpsimd.dma_start(out=dst, in_=self.ost[o][:]), reads=[f"ost{o}"]))
                    self.add_units(s, m, kts, 0, 32 + hc, acc, post=post)
                self.flush_units()


def dram_p2(nc):
    dr = {}
    I = lambda name, shape, dt: nc.dram_tensor(name, shape, dt, kind="ExternalInput").ap()
    dr["qT"] = I("qT", [32, 64, NT], BF16)
    dr["kTf"] = I("kTf", [26, 64, S], BF16)
    dr["vf"] = I("vf", [26, 128, 64, 64], BF16)
    dr["G"] = I("G", [44, GL], F32)
    dr["cb"] = I("cb", [128, 44], F32)
    dr["negm"] = I("negm", [128, 16, 32], F32)
    dr["ownm"] = I("ownm", [128, 16, 32], F32)
    dr["EB"] = I("EB", [32, S], BF16)
    dr["cmpTf"] = I("cmpTf", [6, 64, S], BF16)
    dr["w1"] = I("w1", [2, 64, 32, 128], F32)
    dr["w2"] = I("w2", [128, 2, 64], F32)
    dr["peT"] = I("peT", [64, 2, 32, 2], F32)
    dr["EC"] = I("EC", [128, S], BF16)
    dr["AM"] = I("AM", [128, 16, 128], F32)
    dr["BA"] = I("BA", [128, 16, 128], F32)
    dr["cmA"] = I("cmA", [128, 512], F32)
    dr["cmB"] = I("cmB", [128, 512], F32)
    dr["ovl"] = I("ovl", [128, 4, 128], F32)
    dr["selg"] = I("selg", [36, 36, 64], F32)
    dr["gTs"] = I("gTs", [36, NT], F32)
    dr["OT"] = nc.dram_tensor("OT", [2048, NT], BF16, kind="ExternalOutput").ap()
    return dr


def build_p2(**kw):
    nc = bass.Bass("TRN2", target_bir_lowering=False)
    dr = dram_p2(nc)
    with contextlib.ExitStack() as st:
        T = lambda name, shape, dt: st.enter_context(nc.sbuf_tensor("s_" + name, shape, dt))
        PS = lambda name, shape, dt: st.enter_context(nc.psum_tensor("p_" + name, shape, dt))
        p = Prog(nc)
        P = P2(nc, p, T, PS, dr, **kw)
        P.setup()
        P.mixer_A()
        P.mixer_B()
        P.mixer_C()
        p.emit(final_wait_ops=P.out_ops)
    return nc

import contextlib
import numpy as np

D = 2048
NT = 2048
DFF = 5632
EPS = 1e-6
TW = 514


def build_p3a():
    nc = bass.Bass("TRN2", target_bir_lowering=False)
    I = lambda name, shape, dt: nc.dram_tensor(name, shape, dt, kind="ExternalInput").ap()
    xT = I("xT", [D, 4, TW], F32)
    OT = I("OT", [D, 4, TW], BF16)
    w = I("w", [D, D], F32)
    gn = I("gn", [128, 16], F32)
    x1T = nc.dram_tensor("x1T", [D, 4, TW], F32, kind="ExternalOutput").ap()
    hT = nc.dram_tensor("hT", [128, 16, 4, TW], BF16, kind="ExternalOutput").ap()
    with contextlib.ExitStack() as st:
        T = lambda name, shape, dt: st.enter_context(nc.sbuf_tensor("s_" + name, shape, dt))
        PS = lambda name, shape, dt: st.enter_context(nc.psum_tensor("p_" + name, shape, dt))
        p = Prog(nc)
        outs = []
        OTb = T("OTb", [128, 16, 4, TW], BF16)
        gsb = T("gsb", [128, 16], F32)
        ones = T("ones", [128, 128], F32)
        wst = [T(f"wst{i}", [128, 16, 128], F32) for i in range(2)]
        wbf = [T(f"wbf{i}", [128, 16, 128], BF16) for i in range(2)]
        xc = [T(f"xc{i}", [128, 4, TW], F32) for i in range(2)]
        x1c = [T(f"x1c{i}", [128, 4, TW], F32) for i in range(2)]
        sqt = T("sqt", [128, 4, TW], F32)
        accsq = T("accsq", [128, 4, TW], F32)
        rstd = T("rstd", [128, 4, TW], F32)
        hc = [T(f"hc{i}", [128, 4, TW], BF16) for i in range(2)]
        pacc = [PS(f"pacc{i}", [128, 512], F32) for i in range(4)]
        ph = PS("ph", [128, 512], F32)
        pss = [PS(f"pss{i}", [128, 512], F32) for i in range(2)]

        p.dma("sp", lambda: nc.sync.dma_start(out=gsb[:], in_=gn), writes=["gsb"])
        p.op("pool", lambda: nc.gpsimd.memset(ones[:], 1.0), writes=["ones"])
        p.op("pool", lambda: nc.gpsimd.memset(accsq[:], 0.0), writes=["accsq"])
        OTv = OT.rearrange("(k p) m t -> p k m t", p=128)
        for k4 in range(4):
            p.dma("sp", lambda k4=k4: nc.sync.dma_start(out=OTb[:, k4 * 4:(k4 + 1) * 4], in_=OTv[:, k4 * 4:(k4 + 1) * 4]), writes=[f"OTb{k4}"])
        OTres = [f"OTb{k4}" for k4 in range(4)]
        ai = 0
        for c in range(16):
            s = c % 2
            src = w[:, c * 128:(c + 1) * 128].rearrange("(k p) n -> p k n", p=128)
            p.dma("sp", lambda s=s, src=src: nc.sync.dma_start(out=wst[s][:], in_=src), writes=[f"wst{s}"])
            p.op("act", lambda s=s: nc.scalar.copy(out=wbf[s][:, 0:8], in_=wst[s][:, 0:8]), reads=[f"wst{s}"], writes=[f"wbfa{s}"])
            p.op("pool", lambda s=s: nc.gpsimd.tensor_copy(out=wbf[s][:, 8:16], in_=wst[s][:, 8:16]), reads=[f"wst{s}"], writes=[f"wbfb{s}"])
            p.dma("sp", lambda s=s, c=c: nc.sync.dma_start(out=xc[s][:], in_=xT[c * 128:(c + 1) * 128]), writes=[f"xc{s}"])
            for m in range(4):
                a = ai % 4
                ai += 1
                for k in range(16):
                    p.op("pe", lambda a=a, s=s, k=k, m=m: nc.tensor.matmul(pacc[a][:], lhsT=wbf[s][:, k, :], rhs=OTb[:, k, m, 2:TW], start=(k == 0), stop=(k == 15)),
                         reads=[f"wbfa{s}", f"wbfb{s}"] + OTres, writes=[f"pacc{a}"])
                p.op("dve", lambda a=a, s=s, m=m: nc.vector.tensor_tensor(out=x1c[s][:, m, 2:TW], in0=pacc[a][:], in1=xc[s][:, m, 2:TW], op=ALU.add),
                     reads=[f"pacc{a}", f"xc{s}"], writes=[f"x1c{s}"])
            for k in range(16):
                p.op("pe", lambda s=s, k=k: nc.tensor.matmul(ph[:, 0:8], lhsT=wbf[s][:, k, :], rhs=OTb[:, k, :, 0:2], start=(k == 0), stop=(k == 15)),
                     reads=[f"wbfa{s}", f"wbfb{s}"] + OTres, writes=["ph"])
            p.op("dve", lambda s=s: nc.vector.tensor_tensor(out=x1c[s][:, :, 0:2], in0=ph[:, 0:8].rearrange("p (m h) -> p m h", h=2), in1=xc[s][:, :, 0:2], op=ALU.add),
                 reads=["ph", f"xc{s}"], writes=[f"x1c{s}"])
            outs.append(p.dma("pool", lambda s=s, c=c: nc.gpsimd.dma_start(out=x1T[c * 128:(c + 1) * 128], in_=x1c[s][:]), reads=[f"x1c{s}"], writes=[f"x1T{c}"]))
            p.op("act", lambda s=s: nc.scalar.activation(out=sqt[:], in_=x1c[s][:], func=AF.Square), reads=[f"x1c{s}"], writes=["sqt"])
            p.op("pool", lambda: nc.gpsimd.tensor_tensor(out=accsq[:], in0=accsq[:], in1=sqt[:], op=ALU.add), reads=["sqt", "accsq"], writes=["accsq"])
        for m in range(4):
            q = m % 2
            p.op("pe", lambda q=q, m=m: nc.tensor.matmul(pss[q][:], lhsT=ones[:], rhs=accsq[:, m, 2:TW], start=True, stop=True), reads=["ones", "accsq"], writes=[f"pss{q}"])
            p.op("act", lambda q=q, m=m: nc.scalar.activation(out=rstd[:, m, 2:TW], in_=pss[q][:], func=AF.Sqrt, scale=1.0 / D, bias=EPS), reads=[f"pss{q}"], writes=["rstd"])
        p.op("pe", lambda: nc.tensor.matmul(ph[:, 0:8], lhsT=ones[:], rhs=accsq[:, :, 0:2], start=True, stop=True), reads=["ones", "accsq"], writes=["ph"])
        p.op("act", lambda: nc.scalar.activation(out=rstd[:, :, 0:2], in_=ph[:, 0:8].rearrange("p (m h) -> p m h", h=2), func=AF.Sqrt, scale=1.0 / D, bias=EPS), reads=["ph"], writes=["rstd"])
        p.op("dve", lambda: nc.vector.reciprocal(out=rstd[:], in_=rstd[:]), reads=["rstd"], writes=["rstd"])
        for c in range(16):
            s = c % 2
            p.dma("sp", lambda s=s, c=c: nc.sync.dma_start(out=x1c[s][:], in_=x1T[c * 128:(c + 1) * 128]), reads=[f"x1T{c}"], writes=[f"x1c{s}"])
            p.op("dve", lambda s=s, c=c: nc.vector.scalar_tensor_tensor(out=hc[s][:], in0=x1c[s][:], scalar=gsb[:, c:c + 1], in1=rstd[:], op0=ALU.mult, op1=ALU.mult),
                 reads=[f"x1c{s}", "gsb", "rstd"], writes=[f"hc{s}"])
            outs.append(p.dma("pool", lambda s=s, c=c: nc.gpsimd.dma_start(out=hT[:, c], in_=hc[s][:]), reads=[f"hc{s}"]))
        p.emit(final_wait_ops=outs)
    return nc


def emit_p3b(nc, p, T, PS, hT, x1get, wu, wd, cw, cbv, x2T):
    outs = []
    hTh = T("hTh", [128, 16, 2, TW], BF16)
    actT = T("actT", [128, 44, 1024], BF16)
    wst = [T(f"wst{i}", [128, 4096], F32) for i in range(2)]
    wbf = [T(f"wbf{i}", [128, 4096], BF16) for i in range(2)]
    cws = T("cws", [128, 88, 3], F32)
    cbs = T("cbs", [128, 88], F32)
    ua = [T(f"ua{i}", [128, TW], F32) for i in range(2)]
    ug = [T(f"ug{i}", [128, TW], F32) for i in range(2)]
    ya = [T(f"ya{i}", [128, 512], F32) for i in range(2)]
    yg = [T(f"yg{i}", [128, 512], F32) for i in range(2)]
    sg = [T(f"sg{i}", [128, 512], F32) for i in range(2)]
    x1c = [T(f"x1c{i}", [128, 2, 512], F32) for i in range(2)]
    pa = [PS(f"pa{i}", [128, 512], F32) for i in range(2)]
    pg = [PS(f"pg{i}", [128, 512], F32) for i in range(2)]
    ph = PS("ph", [128, 512], F32)
    pd = [PS(f"pd{i}", [128, 512], F32) for i in range(2)]
    p.dma("sp", lambda: nc.sync.dma_start(out=cws[:], in_=cw), writes=["cws"])
    p.dma("sp", lambda: nc.sync.dma_start(out=cbs[:], in_=cbv), writes=["cbs"])
    wi = 0
    ui = 0
    for half in range(2):
        p.dma("sp", lambda half=half: nc.sync.dma_start(out=hTh[:], in_=hT[:, :, 2 * half:2 * half + 2, :]), writes=["hTh"])
        for c in range(44):
            s = wi % 2
            wi += 1
            wv = wst[s][:].rearrange("p (k n) -> p k n", n=256)
            wb = wbf[s][:].rearrange("p (k n) -> p k n", n=256)
            for part in range(2):
                col0 = part * DFF + c * 128
                src = wu[:, col0:col0 + 128].rearrange("(k p) n -> p k n", p=128)
                p.dma("sp", lambda wv=wv, src=src, part=part: nc.sync.dma_start(out=wv[:, :, part * 128:(part + 1) * 128], in_=src), writes=[f"wst{s}_{part}"])
            p.op("act", lambda wv=wv, wb=wb: nc.scalar.copy(out=wb[:, 0:6], in_=wv[:, 0:6]), reads=[f"wst{s}_0", f"wst{s}_1"], writes=[f"wbfa{s}"])
            p.op("pool", lambda wv=wv, wb=wb: nc.gpsimd.tensor_copy(out=wb[:, 6:16], in_=wv[:, 6:16]), reads=[f"wst{s}_0", f"wst{s}_1"], writes=[f"wbfb{s}"])
            wres = [f"wbfa{s}", f"wbfb{s}"]
            for k in range(16):
                p.op("pe", lambda k=k, wb=wb: nc.tensor.matmul(ph[:, 0:4], lhsT=wb[:, k, 0:128], rhs=hTh[:, k, :, 0:2], start=(k == 0), stop=(k == 15)),
                     reads=wres + ["hTh"], writes=["ph"])
            for k in range(16):
                p.op("pe", lambda k=k, wb=wb: nc.tensor.matmul(ph[:, 4:8], lhsT=wb[:, k, 128:256], rhs=hTh[:, k, :, 0:2], start=(k == 0), stop=(k == 15)),
                     reads=wres + ["hTh"], writes=["ph"])
            for tt in range(2):
                u = ui % 2
                ui += 1
                for k in range(16):
                    p.op("pe", lambda u=u, k=k, tt=tt, wb=wb: nc.tensor.matmul(pa[u][:], lhsT=wb[:, k, 0:128], rhs=hTh[:, k, tt, 2:TW], start=(k == 0), stop=(k == 15)),
                         reads=wres + ["hTh"], writes=[f"pa{u}"])
                for k in range(16):
                    p.op("pe", lambda u=u, k=k, tt=tt, wb=wb: nc.tensor.matmul(pg[u][:], lhsT=wb[:, k, 128:256], rhs=hTh[:, k, tt, 2:TW], start=(k == 0), stop=(k == 15)),
                         reads=wres + ["hTh"], writes=[f"pg{u}"])
                p.op("act", lambda u=u: nc.scalar.copy(out=ua[u][:, 2:TW], in_=pa[u][:]), reads=[f"pa{u}"], writes=[f"ua{u}"])
                p.op("act", lambda u=u: nc.scalar.copy(out=ug[u][:, 2:TW], in_=pg[u][:]), reads=[f"pg{u}"], writes=[f"ug{u}"])
                p.op("act", lambda u=u, tt=tt: nc.scalar.copy(out=ua[u][:, 0:2], in_=ph[:, 2 * tt:2 * tt + 2]), reads=["ph"], writes=[f"ua{u}"])
                p.op("act", lambda u=u, tt=tt: nc.scalar.copy(out=ug[u][:, 0:2], in_=ph[:, 4 + 2 * tt:6 + 2 * tt]), reads=["ph"], writes=[f"ug{u}"])
                for (ub, yb, nm, ch) in ((ua, ya, "a", c), (ug, yg, "g", 44 + c)):
                    p.op("dve", lambda u=u, ub=ub, yb=yb, ch=ch: nc.vector.tensor_scalar(out=yb[u][:], in0=ub[u][:, 2:TW], scalar1=cws[:, ch, 0:1], scalar2=cbs[:, ch:ch + 1], op0=ALU.mult, op1=ALU.add),
                         reads=[f"u{nm}{u}", "cws", "cbs"], writes=[f"y{nm}{u}"])
                    p.op("dve", lambda u=u, ub=ub, yb=yb, ch=ch: nc.vector.scalar_tensor_tensor(out=yb[u][:], in0=ub[u][:, 1:TW - 1], scalar=cws[:, ch, 1:2], in1=yb[u][:], op0=ALU.mult, op1=ALU.add),
                         reads=[f"u{nm}{u}", "cws", f"y{nm}{u}"], writes=[f"y{nm}{u}"])
                    p.op("dve", lambda u=u, ub=ub, yb=yb, ch=ch: nc.vector.scalar_tensor_tensor(out=yb[u][:], in0=ub[u][:, 0:TW - 2], scalar=cws[:, ch, 2:3], in1=yb[u][:], op0=ALU.mult, op1=ALU.add),
                         reads=[f"u{nm}{u}", "cws", f"y{nm}{u}"], writes=[f"y{nm}{u}"])
                p.op("act", lambda u=u: nc.scalar.activation(out=sg[u][:], in_=yg[u][:], func=AF.Silu), reads=[f"yg{u}"], writes=[f"sg{u}"])
                p.op("pool", lambda u=u, c=c, tt=tt: nc.gpsimd.tensor_tensor(out=actT[:, c, tt * 512:(tt + 1) * 512], in0=sg[u][:], in1=ya[u][:], op=ALU.mult),
                     reads=[f"sg{u}", f"ya{u}"], writes=[f"actT{c}"])
        actres = [f"actT{c}" for c in range(44)]
        for cc in range(16):
            for piece in range(2):
                s = wi % 2
                wi += 1
                wv = wst[s][:, 0:22 * 128].rearrange("p (k n) -> p k n", n=128)
                wb = wbf[s][:, 0:22 * 128].rearrange("p (k n) -> p k n", n=128)
                src = wd[piece * 2816:(piece + 1) * 2816, cc * 128:(cc + 1) * 128].rearrange("(k p) n -> p k n", p=128)
                p.dma("sp", lambda wv=wv, src=src: nc.sync.dma_start(out=wv, in_=src), writes=[f"wst{s}_0", f"wst{s}_1"])
                p.op("act", lambda wv=wv, wb=wb: nc.scalar.copy(out=wb[:, 0:9], in_=wv[:, 0:9]), reads=[f"wst{s}_0", f"wst{s}_1"], writes=[f"wbfa{s}"])
                p.op("pool", lambda wv=wv, wb=wb: nc.gpsimd.tensor_copy(out=wb[:, 9:22], in_=wv[:, 9:22]), reads=[f"wst{s}_0", f"wst{s}_1"], writes=[f"wbfb{s}"])
                for tt in range(2):
                    for k in range(22):
                        c = piece * 22 + k
                        p.op("pe", lambda wb=wb, k=k, c=c, tt=tt: nc.tensor.matmul(pd[tt][:], lhsT=wb[:, k, :], rhs=actT[:, c, tt * 512:(tt + 1) * 512], start=(c == 0), stop=(c == 43)),
                             reads=[f"wbfa{s}", f"wbfb{s}", f"actT{c}"], writes=[f"pd{tt}"])
            xs = cc % 2
            p.dma("sp", lambda xs=xs, cc=cc, half=half: nc.sync.dma_start(out=x1c[xs][:], in_=x1get(cc, half)), writes=[f"x1c{xs}"])
            for tt in range(2):
                p.op("dve", lambda xs=xs, tt=tt: nc.vector.tensor_tensor(out=x1c[xs][:, tt, :], in0=pd[tt][:], in1=x1c[xs][:, tt, :], op=ALU.add),
                     reads=[f"pd{tt}", f"x1c{xs}"], writes=[f"x1c{xs}"])
            dst = x2T[cc * 128:(cc + 1) * 128, half * 1024:(half + 1) * 1024].rearrange("p (t n) -> p t n", n=512)
            outs.append(p.dma("pool", lambda xs=xs, dst=dst: nc.gpsimd.dma_start(out=dst, in_=x1c[xs][:]), reads=[f"x1c{xs}"]))
    return outs


def build_p3b():
    nc = bass.Bass("TRN2", target_bir_lowering=False)
    I = lambda name, shape, dt: nc.dram_tensor(name, shape, dt, kind="ExternalInput").ap()
    hT = I("hT", [128, 16, 4, TW], BF16)
    x1T = I("x1T", [D, 4, TW], F32)
    wu = I("wu", [D, 2 * DFF], F32)
    wd = I("wd", [DFF, D], F32)
    cw = I("cw", [128, 88, 3], F32)
    cbv = I("cbv", [128, 88], F32)
    x2T = nc.dram_tensor("x2T", [D, NT], F32, kind="ExternalOutput").ap()
    with contextlib.ExitStack() as st:
        T = lambda name, shape, dt: st.enter_context(nc.sbuf_tensor("s_" + name, shape, dt))
        PS = lambda name, shape, dt: st.enter_context(nc.psum_tensor("p_" + name, shape, dt))
        p = Prog(nc)
        x1get = lambda cc, half: x1T[cc * 128:(cc + 1) * 128, 2 * half:2 * half + 2, 2:TW]
        outs = emit_p3b(nc, p, T, PS, hT, x1get, wu, wd, cw, cbv, x2T)
        p.emit(final_wait_ops=outs)
    return nc


def emit_kf(nc, p, T, PS, xT, gn, yT):
    outs = []
    gsb = T("gsb", [128, 16], F32)
    ones = T("ones", [128, 128], F32)
    xs = [T(f"xs{i}", [128, 16, 512], F32) for i in range(2)]
    ys = [T(f"ys{i}", [128, 16, 512], F32) for i in range(2)]
    sq = [T(f"sq{i}", [128, 512], F32) for i in range(2)]
    rstd = T("rstd", [128, 512], F32)
    pss = PS("pss", [128, 512], F32)
    p.dma("sp", lambda: nc.sync.dma_start(out=gsb[:], in_=gn), writes=["gsb"])
    p.op("pool", lambda: nc.gpsimd.memset(ones[:], 1.0), writes=["ones"])
    xv = xT.rearrange("(k p) t -> p k t", p=128)
    yv = yT.rearrange("(k p) t -> p k t", p=128)
    for m in range(4):
        s = m % 2
        p.dma("sp", lambda m=m, s=s: nc.sync.dma_start(out=xs[s][:], in_=xv[:, :, m * 512:(m + 1) * 512]), writes=[f"xs{s}"])
        for k in range(16):
            q = k % 2
            p.op("act", lambda s=s, k=k, q=q: nc.scalar.activation(out=sq[q][:], in_=xs[s][:, k, :], func=AF.Square), reads=[f"xs{s}"], writes=[f"sq{q}"])
            p.op("pe", lambda q=q, k=k: nc.tensor.matmul(pss[:], lhsT=ones[:], rhs=sq[q][:], start=(k == 0), stop=(k == 15)), reads=["ones", f"sq{q}"], writes=["pss"])
        p.op("act", lambda: nc.scalar.activation(out=rstd[:], in_=pss[:], func=AF.Sqrt, scale=1.0 / D, bias=EPS), reads=["pss"], writes=["rstd"])
        p.op("dve", lambda: nc.vector.reciprocal(out=rstd[:], in_=rstd[:]), reads=["rstd"], writes=["rstd"])
        for k in range(16):
            p.op("dve", lambda s=s, k=k: nc.vector.scalar_tensor_tensor(out=ys[s][:, k, :], in0=xs[s][:, k, :], scalar=gsb[:, k:k + 1], in1=rstd[:], op0=ALU.mult, op1=ALU.mult),
                 reads=[f"xs{s}", "gsb", "rstd"], writes=[f"ys{s}"])
        outs.append(p.dma("pool", lambda m=m, s=s: nc.gpsimd.dma_start(out=yv[:, :, m * 512:(m + 1) * 512], in_=ys[s][:]), reads=[f"ys{s}"]))
    return outs


def build_kf():
    nc = bass.Bass("TRN2", target_bir_lowering=False)
    xT = nc.dram_tensor("xT", [D, NT], F32, kind="ExternalInput").ap()
    gn = nc.dram_tensor("gn", [128, 16], F32, kind="ExternalInput").ap()
    yT = nc.dram_tensor("yT", [D, NT], F32, kind="ExternalOutput").ap()
    with contextlib.ExitStack() as st:
        T = lambda name, shape, dt: st.enter_context(nc.sbuf_tensor("s_" + name, shape, dt))
        PS = lambda name, shape, dt: st.enter_context(nc.psum_tensor("p_" + name, shape, dt))
        p = Prog(nc)
        outs = emit_kf(nc, p, T, PS, xT, gn, yT)
        p.emit(final_wait_ops=outs)
    return nc

import contextlib, math
import numpy as np

DEPTH = 4
RG = [[0, 1, 2, 3], [4, 5, 6, 7]]


def ec_const_nat():
    c = np.arange(S)
    return (c[None, :] // 64 == np.arange(128)[:, None]).astype(np.float32)


def halo_coef(j):
    co = np.zeros((128, 5), np.float32)
    if j >= 1:
        co[:, j - 1] = 1.0
    else:
        co[:, 4] = 1.0
    return co


def emit_p3a_f(nc, p, T, PS, xT, OT, w, gn, x1T, hT, ht_s):
    OTb = T("OTb", [128, 16, NT], BF16)
    gsb = T("gsb", [128, 16], F32)
    ones = T("ones", [128, 128], F32)
    wst = [T(f"wst{i}", [128, 16, 128], F32) for i in range(2)]
    wbf = [T(f"wbf{i}", [128, 16, 128], BF16) for i in range(2)]
    xc = [T(f"xc{i}", [128, NT], F32) for i in range(2)]
    x1c = [T(f"x1c{i}", [128, NT], F32) for i in range(2)]
    sqt = T("sqt", [128, NT], F32)
    accsq = T("accsq", [128, NT], F32)
    rstd = T("rstd", [128, NT], F32)
    hc = [T(f"hc{i}", [128, NT], BF16) for i in range(2)]
    pacc = [PS(f"pacc{i}", [128, 512], F32) for i in range(4)]
    pss = [PS(f"pss{i}", [128, 512], F32) for i in range(2)]
    p.dma("sp", lambda: nc.sync.dma_start(out=gsb[:], in_=gn), writes=["gsb"])
    p.op("pool", lambda: nc.gpsimd.memset(ones[:], 1.0), writes=["ones"])
    p.op("pool", lambda: nc.gpsimd.memset(accsq[:], 0.0), writes=["accsq"])
    OTv = OT.rearrange("(k p) t -> p k t", p=128)
    for k4 in range(4):
        p.dma("sp", lambda k4=k4: nc.sync.dma_start(out=OTb[:, k4 * 4:(k4 + 1) * 4], in_=OTv[:, k4 * 4:(k4 + 1) * 4]), writes=[f"OTb{k4}"])
    OTres = [f"OTb{k4}" for k4 in range(4)]
    ai = 0
    for c in range(16):
        s = c % 2
        src = w[:, c * 128:(c + 1) * 128].rearrange("(k p) n -> p k n", p=128)
        p.dma("sp", lambda s=s, src=src: nc.sync.dma_start(out=wst[s][:], in_=src), writes=[f"wst{s}"])
        p.op("act", lambda s=s: nc.scalar.copy(out=wbf[s][:, 0:8], in_=wst[s][:, 0:8]), reads=[f"wst{s}"], writes=[f"wbfa{s}"])
        p.op("pool", lambda s=s: nc.gpsimd.tensor_copy(out=wbf[s][:, 8:16], in_=wst[s][:, 8:16]), reads=[f"wst{s}"], writes=[f"wbfb{s}"])
        p.dma("sp", lambda s=s, c=c: nc.sync.dma_start(out=xc[s][:], in_=xT[c * 128:(c + 1) * 128]), writes=[f"xc{s}"])
        for m in range(4):
            a = ai % 4
            ai += 1
            for k in range(16):
                p.op("pe", lambda a=a, s=s, k=k, m=m: nc.tensor.matmul(pacc[a][:], lhsT=wbf[s][:, k, :], rhs=OTb[:, k, m * 512:(m + 1) * 512], start=(k == 0), stop=(k == 15)),
                     reads=[f"wbfa{s}", f"wbfb{s}"] + OTres, writes=[f"pacc{a}"])
            p.op("dve", lambda a=a, s=s, m=m: nc.vector.tensor_tensor(out=x1c[s][:, m * 512:(m + 1) * 512], in0=pacc[a][:], in1=xc[s][:, m * 512:(m + 1) * 512], op=ALU.add),
                 reads=[f"pacc{a}", f"xc{s}"], writes=[f"x1c{s}"])
        p.dma("pool", lambda s=s, c=c: nc.gpsimd.dma_start(out=x1T[c * 128:(c + 1) * 128], in_=x1c[s][:]), reads=[f"x1c{s}"], writes=[f"x1T{c}"])
        p.op("act", lambda s=s: nc.scalar.activation(out=sqt[:], in_=x1c[s][:], func=AF.Square), reads=[f"x1c{s}"], writes=["sqt"])
        p.op("pool", lambda: nc.gpsimd.tensor_tensor(out=accsq[:], in0=accsq[:], in1=sqt[:], op=ALU.add), reads=["sqt", "accsq"], writes=["accsq"])
    for m in range(4):
        q = m % 2
        p.op("pe", lambda q=q, m=m: nc.tensor.matmul(pss[q][:], lhsT=ones[:], rhs=accsq[:, m * 512:(m + 1) * 512], start=True, stop=True), reads=["ones", "accsq"], writes=[f"pss{q}"])
        p.op("act", lambda q=q, m=m: nc.scalar.activation(out=rstd[:, m * 512:(m + 1) * 512], in_=pss[q][:], func=AF.Sqrt, scale=1.0 / D, bias=EPS), reads=[f"pss{q}"], writes=["rstd"])
    p.op("dve", lambda: nc.vector.reciprocal(out=rstd[:], in_=rstd[:]), reads=["rstd"], writes=["rstd"])
    for c in range(16):
        s = c % 2
        p.dma("sp", lambda s=s, c=c: nc.sync.dma_start(out=x1c[s][:], in_=x1T[c * 128:(c + 1) * 128]), reads=[f"x1T{c}"], writes=[f"x1c{s}"])
        p.op("dve", lambda s=s, c=c: nc.vector.scalar_tensor_tensor(out=hc[s][:], in0=x1c[s][:], scalar=gsb[:, c:c + 1], in1=rstd[:], op0=ALU.mult, op1=ALU.mult),
             reads=[f"x1c{s}", "gsb", "rstd"], writes=[f"hc{s}"])
        p.dma("pool", lambda s=s, c=c: nc.gpsimd.dma_start(out=hT[:, c, :, 2:TW], in_=hc[s][:].rearrange("p (m t) -> p m t", t=512)), reads=[f"hc{s}"])
        p.dma("pool", lambda s=s, c=c: nc.gpsimd.dma_start(out=ht_s[c * 128:(c + 1) * 128, :].rearrange("p (m h) -> p m h", h=2),
                                                            in_=hc[s][:].rearrange("p (m t) -> p m t", t=512)[:, :, 510:512]), reads=[f"hc{s}"])


def emit_halo(nc, p, T, PS, ht_g, hco_d, hT):
    Hg = T("Hg", [128, 4, 16, 8], BF16)
    hco = T("hco", [128, 5], F32)
    acc = T("hacc", [128, 4, 16, 2], F32)
    hb = T("hb", [128, 4, 16, 2], BF16)
    p.dma("sp", lambda: nc.sync.dma_start(out=Hg[:], in_=ht_g.rearrange("(r k p) c -> p r k c", r=4, p=128)), writes=["Hg"])
    p.dma("sp", lambda: nc.sync.dma_start(out=hco[:], in_=hco_d), writes=["hco"])
    for m in range(4):
        p.op("dve", lambda m=m: nc.vector.tensor_scalar(out=acc[:, m], in0=Hg[:, 0, :, 2 * m:2 * m + 2], scalar1=hco[:, 0:1], scalar2=None, op0=ALU.mult),
             reads=["Hg", "hco"], writes=[f"hacc{m}"])
        for r in range(1, 4):
            p.op("dve", lambda m=m, r=r: nc.vector.scalar_tensor_tensor(out=acc[:, m], in0=Hg[:, r, :, 2 * m:2 * m + 2], scalar=hco[:, r:r + 1], in1=acc[:, m], op0=ALU.mult, op1=ALU.add),
                 reads=["Hg", "hco", f"hacc{m}"], writes=[f"hacc{m}"])
        if m >= 1:
            p.op("dve", lambda m=m: nc.vector.scalar_tensor_tensor(out=acc[:, m], in0=Hg[:, 3, :, 2 * m - 2:2 * m], scalar=hco[:, 4:5], in1=acc[:, m], op0=ALU.mult, op1=ALU.add),
                 reads=["Hg", "hco", f"hacc{m}"], writes=[f"hacc{m}"])
        p.op("dve", lambda m=m: nc.vector.tensor_copy(out=hb[:, m], in_=acc[:, m]), reads=[f"hacc{m}"], writes=[f"hb{m}"])
        p.dma("sp", lambda m=m: nc.sync.dma_start(out=hT[:, :, m, 0:2], in_=hb[:, m]), reads=[f"hb{m}"])


def build_fused(depth=DEPTH, debug=False, stop=10**9):
    nc = bass.Bass("TRN2", target_bir_lowering=False)
    I = lambda name, shape, dt: nc.dram_tensor(name, shape, dt, kind="ExternalInput").ap()
    N = lambda name, shape, dt, **kw: nc.dram_tensor(name, shape, dt, kind="Internal", **kw).ap()
    xT0 = I("xT0", [D, NT], F32)
    w_in = I("w_in", [depth, D, IN_W], F32)
    w_out = I("w_out", [depth, D, D], F32)
    w_up = I("w_up", [depth, D, 2 * DFF], F32)
    w_down = I("w_down", [depth, DFF, D], F32)
    gn_attn = I("gn_attn", [depth, 128, 16], F32)
    gn_mlp = I("gn_mlp", [depth, 128, 16], F32)
    gn_fin = I("gn_fin", [128, 16], F32)
    w1d = I("w1", [depth, 2, 64, 32, 128], F32)
    w2d = I("w2", [depth, 128, 2, 64], F32)
    peTd = I("peT", [depth, 64, 2, 32, 2], F32)
    cwd = I("cw", [depth, 128, 88, 3], F32)
    cbd = I("cbv", [depth, 128, 88], F32)
    hco_d = I("hco", [128, 5], F32)
    consts = {}
    for nm, shape, dt in (("G", [44, GL], F32), ("cb", [128, 44], F32), ("negm", [128, 16, 32], F32), ("ownm", [128, 16, 32], F32),
                          ("EB", [32, S], BF16), ("EC", [128, S], BF16), ("AM", [128, 16, 128], F32), ("BA", [128, 16, 128], F32),
                          ("cmA", [128, 512], F32), ("cmB", [128, 512], F32), ("ovl", [128, 4, 128], F32), ("selg", [36, 36, 64], F32)):
        consts[nm] = I(nm, shape, dt)
    yT = nc.dram_tensor("yT", [D, NT], F32, kind="ExternalOutput").ap()
    qT_s = N("qT_s", [32 * 64, NT], BF16)
    kT_s = N("kT_s", [26 * 64, NT], BF16, addr_space="Local")
    cmpT_s = N("cmpT_s", [6 * 64, NT], BF16, addr_space="Local")
    v_s = N("v_s", [26 * 128, 1024], BF16, addr_space="Local")
    def chunked(name, total, step, cols):
        out = []
        r0 = 0
        while r0 < total:
            nr = min(step, total - r0)
            out.append((r0, nr, N(f"{name}_{r0}", [4 * nr, cols], BF16, addr_space="Local")))
            r0 += nr
        return out
    kT_g = chunked("kT_g", 26 * 64, 256, NT)
    cmpT_g = chunked("cmpT_g", 6 * 64, 256, NT)
    v_g = chunked("v_g", 26 * 128, 512, 1024)
    gT_s = N("gT_s", [36, NT], F32)
    N2 = (lambda name, shape, dt: nc.dram_tensor(name, shape, dt, kind="ExternalOutput").ap()) if debug else N
    OT_s = N2("OT_s", [D, NT], BF16)
    x1T_s = N("x1T_s", [D, NT], F32)
    hT_s = N2("hT_s", [128, 16, 4, TW], BF16)
    ht_s = N("ht_s", [D, 8], BF16, addr_space="Local")
    ht_g = N("ht_g", [4 * D, 8], BF16, addr_space="Local")
    x2T_s = N2("x2T_s", [D, NT], F32)

    phase = [0]
    with contextlib.ExitStack() as top:
        ctx = Ctx(nc, top)

        def run_phase(fn, final=False):
            ph = phase[0]
            phase[0] += 1
            if ph >= stop and not final:
                return
            with contextlib.ExitStack() as st:
                T = lambda name, shape, dt: st.enter_context(nc.sbuf_tensor(f"s{ph}_" + name, shape, dt))
                PS = lambda name, shape, dt: st.enter_context(nc.psum_tensor(f"p{ph}_" + name, shape, dt))
                p = PProg(ctx)
                fin = fn(p, T, PS)
                p.emit(final_wait_ops=fin if final else ())

        xcur = xT0
        for l in range(depth):
            def ph_p1(p, T, PS, l=l, xcur=xcur):
                outs = {"qT": qT_s.rearrange("(h e) t -> h e t", e=64), "kT": kT_s.rearrange("(h e) t -> h e t", e=64),
                        "cmpT": cmpT_s.rearrange("(h e) t -> h e t", e=64)}
                v4 = v_s.rearrange("(h p) (ts e) -> h p ts e", p=128, e=64)

                def vdst(ts, vc0, ncols):
                    h0, nh = vc0 // 64, ncols // 64
                    return v4[h0:h0 + nh, :, ts, :].rearrange("h p e -> p h e")
                emit_p1(nc, p, T, PS, xcur, gn_attn[l], w_in[l], outs, None, gT_s, vdst=vdst)
                return ()
            run_phase(ph_p1)

            def ph_cc1(p, T, PS):
                for (a, chs) in ((kT_s, kT_g), (cmpT_s, cmpT_g), (v_s, v_g)):
                    for (r0, nr, b) in chs:
                        p.cc(lambda a=a, b=b, r0=r0, nr=nr: nc.gpsimd.collective_compute("AllGather", ALU.bypass, replica_groups=RG, ins=[a[r0:r0 + nr]], outs=[b]))
                return ()
            run_phase(ph_cc1)

            def ph_p2(p, T, PS, l=l):
                dr = dict(consts)
                dr.update({"qT": qT_s.rearrange("(h e) t -> h e t", e=64), "kT_g": kT_g, "cmpT_g": cmpT_g, "v_g": v_g, "gTs": gT_s,
                           "w1": w1d[l], "w2": w2d[l], "peT": peTd[l], "OT": OT_s})
                P = P2(nc, p, T, PS, dr, fused=True)
                P.setup()
                P.mixer_A()
                P.mixer_B()
                P.mixer_C()
                return ()
            run_phase(ph_p2)

            def ph_p3a(p, T, PS, l=l, xcur=xcur):
                emit_p3a_f(nc, p, T, PS, xcur, OT_s, w_out[l], gn_mlp[l], x1T_s, hT_s, ht_s)
                return ()
            run_phase(ph_p3a)

            def ph_cc2(p, T, PS):
                p.cc(lambda: nc.gpsimd.collective_compute("AllGather", ALU.bypass, replica_groups=RG, ins=[ht_s], outs=[ht_g]))
                return ()
            run_phase(ph_cc2)

            def ph_halo(p, T, PS):
                emit_halo(nc, p, T, PS, ht_g, hco_d, hT_s)
                return ()
            run_phase(ph_halo)

            def ph_p3b(p, T, PS, l=l):
                x1get = lambda cc, half: x1T_s[cc * 128:(cc + 1) * 128, half * 1024:(half + 1) * 1024].rearrange("p (t n) -> p t n", n=512)
                emit_p3b(nc, p, T, PS, hT_s, x1get, w_up[l], w_down[l], cwd[l], cbd[l], x2T_s)
                return ()
            run_phase(ph_p3b)
            xcur = x2T_s

        def ph_kf(p, T, PS):
            return emit_kf(nc, p, T, PS, x2T_s, gn_fin, yT)
        run_phase(ph_kf, final=True)
    return nc


def fused_host_inputs(x, rel_table, w_in, w_out, cmp_w1, cmp_w2, cmp_pe, norm_attn, norm_mlp, w_up, conv_w, conv_b, w_down, norm_final):
    import ml_dtypes
    bf = ml_dtypes.bfloat16
    f32 = np.float32
    A = lambda a: np.ascontiguousarray(np.asarray(a, f32))
    x = np.asarray(x, f32)
    rel_table = np.asarray(rel_table, f32)
    L = np.asarray(w_in).shape[0]
    shared = {
        "w_in": A(w_in), "w_out": A(w_out), "w_up": A(w_up), "w_down": A(w_down),
        "gn_attn": A(np.asarray(norm_attn, f32).reshape(L, 16, 128).transpose(0, 2, 1)),
        "gn_mlp": A(np.asarray(norm_mlp, f32).reshape(L, 16, 128).transpose(0, 2, 1)),
        "gn_fin": A(np.asarray(norm_final, f32).reshape(16, 128).T),
        "w1": A(np.asarray(cmp_w1, f32).reshape(L, 2, 32, 64, 128).transpose(0, 1, 3, 2, 4)),
        "w2": A(np.asarray(cmp_w2, f32).transpose(0, 2, 1, 3)),
        "peT": A(np.repeat(np.asarray(cmp_pe, f32).transpose(0, 3, 1, 2)[..., None], 2, axis=-1)),
        "cw": A(np.asarray(conv_w, f32).transpose(0, 2, 1).reshape(L, 88, 128, 3).transpose(0, 2, 1, 3)),
        "cbv": A(np.asarray(conv_b, f32).reshape(L, 88, 128).transpose(0, 2, 1)),
        "EB": eb_const().astype(bf), "EC": ec_const_nat().astype(bf), "ovl": ovl_const(), "selg": selg_const(),
    }
    per_j = []
    for j in range(4):
        G, cb = band_vectors(rel_table, j)
        negm, ownm = moba_consts(j)
        AM, BA, cmA, cmB = nsa_consts(j)
        per_j.append({"G": G, "cb": np.ascontiguousarray(np.broadcast_to(cb[None, :], (128, 44))), "negm": negm, "ownm": ownm,
                      "AM": AM, "BA": BA, "cmA": cmA, "cmB": cmB, "hco": halo_coef(j)})
    in_maps = []
    for c in range(8):
        b, j = c // 4, c % 4
        d = dict(shared)
        d.update(per_j[j])
        d["xT0"] = np.ascontiguousarray(x[b, core_tokens(j)].T)
        in_maps.append(d)
    return in_maps


from concourse.bass_utils import run_bass_kernel_spmd

_FUSED = {}


def kernel(x, rel_table, w_in, w_out, cmp_w1, cmp_w2, cmp_pe, norm_attn, norm_mlp,
           w_up, conv_w, conv_b, w_down, norm_final):
    if "nc" not in _FUSED:
        _FUSED["nc"] = build_fused()
    nc = _FUSED["nc"]
    in_maps = fused_host_inputs(x, rel_table, w_in, w_out, cmp_w1, cmp_w2, cmp_pe, norm_attn, norm_mlp,
                                w_up, conv_w, conv_b, w_down, norm_final)
    res = run_bass_kernel_spmd(nc, in_maps, core_ids=list(range(8)))
    out = np.empty((2, S, 2048), np.float32)
    for c in range(8):
        b, j = c // 4, c % 4
        out[b, core_tokens(j)] = np.asarray(res.results[c]["yT"]).T
    return out
```

```python
import contextlib, math
import numpy as np
import concourse.bass as bass
import concourse.mybir as mybir

F32 = mybir.dt.float32
BF16 = mybir.dt.bfloat16
I32 = mybir.dt.int32
AF = mybir.ActivationFunctionType
ALU = mybir.AluOpType
AX = mybir.AxisListType

SEM_CAP = 30000
N_DMA_SEMS = 8


class _Op:
    __slots__ = ("eng", "fn", "deps", "is_dma", "sig", "has_dependents", "idx", "dsem_prev", "is_cc", "inc")

    def __init__(self, eng, fn, is_dma):
        self.eng = eng
        self.fn = fn
        self.is_dma = is_dma
        self.deps = []
        self.sig = None
        self.has_dependents = False
        self.dsem_prev = None
        self.is_cc = False
        self.inc = 1


class Prog:
    ENGS = ("pe", "act", "dve", "pool", "sp")

    def __init__(self, nc):
        self.nc = nc
        self.ops = []
        self.last_writer = {}
        self.readers = {}

    def _add(self, eng, fn, reads, writes, is_dma):
        op = _Op(eng, fn, is_dma)
        deps = {}
        for r in reads:
            w = self.last_writer.get(r)
            if w is not None:
                deps[id(w)] = w
        for r in writes:
            w = self.last_writer.get(r)
            if w is not None:
                deps[id(w)] = w
            for rd in self.readers.get(r, ()):
                deps[id(rd)] = rd
        for r in reads:
            self.readers.setdefault(r, []).append(op)
        for r in writes:
            self.last_writer[r] = op
            self.readers[r] = []
        for d in deps.values():
            if d is op:
                continue
            if (not is_dma) and (not d.is_dma) and d.eng == eng and eng == "pe":
                continue
            op.deps.append(d)
            d.has_dependents = True
        self.ops.append(op)
        return op

    def op(self, eng, fn, reads=(), writes=()):
        return self._add(eng, fn, reads, writes, False)

    def dma(self, eng, fn, reads=(), writes=()):
        return self._add(eng, fn, reads, writes, True)

    def emit(self, final_wait_ops=()):
        nc = self.nc
        engs = {"pe": nc.tensor, "act": nc.scalar, "dve": nc.vector, "pool": nc.gpsimd, "sp": nc.sync}
        import contextlib
        with contextlib.ExitStack() as st:
            sem_lists = {e: [] for e in self.ENGS}
            counts = {e: 0 for e in self.ENGS}

            def new_sem(name):
                return st.enter_context(nc.semaphore(name))

            dma_sems = {}
            dma_state = {}
            for e in self.ENGS:
                dma_sems[e] = None
            for op in self.ops:
                if op.is_dma:
                    if dma_sems[op.eng] is None:
                        dma_sems[op.eng] = [new_sem(f"d_{op.eng}_{i}") for i in range(N_DMA_SEMS)]
                        dma_state[op.eng] = {"rr": 0, "cnt": [0] * N_DMA_SEMS}
                    stt = dma_state[op.eng]
                    i = stt["rr"]
                    stt["rr"] = (i + 1) % N_DMA_SEMS
                    prev = stt["cnt"][i]
                    stt["cnt"][i] = prev + 16
                    op.sig = (dma_sems[op.eng][i], prev + 16)
                    op.dsem_prev = (dma_sems[op.eng][i], prev) if prev > 0 else None
                elif op.has_dependents:
                    e = op.eng
                    if not sem_lists[e] or counts[e] >= SEM_CAP:
                        sem_lists[e].append(new_sem(f"c_{e}_{len(sem_lists[e])}"))
                        counts[e] = 0
                    counts[e] += 1
                    op.sig = (sem_lists[e][-1], counts[e])
            streams = {e: [] for e in self.ENGS}
            waited = {e: {} for e in self.ENGS}
            for op in self.ops:
                e = op.eng
                waits = []
                need = []
                if op.dsem_prev is not None:
                    need.append(op.dsem_prev)
                for d in op.deps:
                    need.append(d.sig)
                for (sem, val) in need:
                    k = id(sem)
                    if waited[e].get(k, 0) >= val:
                        continue
                    waited[e][k] = val
                    waits.append((sem, val))
                streams[e].append((waits, op))
            finals = [o.sig for o in final_wait_ops]
            blk = st.enter_context(nc.Block())

            def make(e):
                def body(engine):
                    for waits, op in streams[e]:
                        for (sem, val) in waits:
                            engine.wait_ge(sem, val)
                        ins = op.fn()
                        if op.sig is not None:
                            ins.then_inc(op.sig[0], 16 if op.is_dma else 1)
                    if e == "sp":
                        for (sem, val) in finals:
                            engine.wait_ge(sem, val)
                return body

            blk.tensor(make("pe"))
            blk.scalar(make("act"))
            blk.vector(make("dve"))
            blk.gpsimd(make("pool"))
            blk.sync(make("sp"))
        return self


class Ctx:
    ENGS = ("pe", "act", "dve", "pool", "sp")

    def __init__(self, nc, stack):
        self.nc = nc
        self.stack = stack
        self.sem_lists = {e: [] for e in self.ENGS}
        self.counts = {e: 0 for e in self.ENGS}
        self.dma_sems = {e: None for e in self.ENGS}
        self.dma_state = {}
        self.cc_sem = None
        self.cc_count = 0
        self.waited = {e: {} for e in self.ENGS}
        self.barrier_sigs = []
        self.nsem = 0

    def new_sem(self, name):
        self.nsem += 1
        return self.stack.enter_context(self.nc.semaphore(name))


class PProg(Prog):
    def __init__(self, ctx):
        super().__init__(ctx.nc)
        self.ctx = ctx

    def cc(self, fn, reads=(), writes=()):
        op = self._add("pool", fn, reads, writes, True)
        op.is_cc = True
        return op

    def emit(self, final_wait_ops=()):
        nc, ctx = self.nc, self.ctx
        engs = self.ENGS
        last_op = {e: None for e in engs}
        for op in self.ops:
            if not op.is_dma:
                last_op[op.eng] = op
        for e in engs:
            if last_op[e] is not None:
                last_op[e].has_dependents = True
        for op in self.ops:
            if getattr(op, "is_cc", False):
                if ctx.cc_sem is None:
                    ctx.cc_sem = ctx.new_sem("ccs")
                ctx.cc_count += 1
                op.sig = (ctx.cc_sem, ctx.cc_count)
                op.inc = 1
            elif op.is_dma:
                if ctx.dma_sems[op.eng] is None:
                    ctx.dma_sems[op.eng] = [ctx.new_sem(f"d_{op.eng}_{i}") for i in range(N_DMA_SEMS)]
                    ctx.dma_state[op.eng] = {"rr": 0, "cnt": [0] * N_DMA_SEMS}
                stt = ctx.dma_state[op.eng]
                i = stt["rr"]
                stt["rr"] = (i + 1) % N_DMA_SEMS
                prev = stt["cnt"][i]
                stt["cnt"][i] = prev + 16
                op.sig = (ctx.dma_sems[op.eng][i], prev + 16)
                op.dsem_prev = (ctx.dma_sems[op.eng][i], prev) if prev > 0 else None
                op.inc = 16
            elif op.has_dependents:
                e = op.eng
                if not ctx.sem_lists[e] or ctx.counts[e] >= SEM_CAP:
                    ctx.sem_lists[e].append(ctx.new_sem(f"c_{e}_{len(ctx.sem_lists[e])}"))
                    ctx.counts[e] = 0
                ctx.counts[e] += 1
                op.sig = (ctx.sem_lists[e][-1], ctx.counts[e])
                op.inc = 1
        streams = {e: [] for e in engs}
        start_waits = {e: [] for e in engs}
        for e in engs:
            for (sem, val) in ctx.barrier_sigs:
                k = id(sem)
                if ctx.waited[e].get(k, 0) >= val:
                    continue
                ctx.waited[e][k] = val
                start_waits[e].append((sem, val))
        for op in self.ops:
            e = op.eng
            waits = []
            need = []
            if op.dsem_prev is not None:
                need.append(op.dsem_prev)
            for d in op.deps:
                need.append(d.sig)
            for (sem, val) in need:
                k = id(sem)
                if ctx.waited[e].get(k, 0) >= val:
                    continue
                ctx.waited[e][k] = val
                waits.append((sem, val))
            streams[e].append((waits, op))
        sigs = []
        for e in engs:
            if last_op[e] is not None:
                sigs.append(last_op[e].sig)
            if ctx.dma_sems[e] is not None:
                for i, sem in enumerate(ctx.dma_sems[e]):
                    c = ctx.dma_state[e]["cnt"][i]
                    if c > 0:
                        sigs.append((sem, c))
        if ctx.cc_sem is not None and ctx.cc_count > 0:
            sigs.append((ctx.cc_sem, ctx.cc_count))
        ctx.barrier_sigs = sigs
        finals = [o.sig for o in final_wait_ops]
        with nc.Block() as blk:
            def make(e):
                def body(engine):
                    for (sem, val) in start_waits[e]:
                        engine.wait_ge(sem, val)
                    for waits, op in streams[e]:
                        for (sem, val) in waits:
                            engine.wait_ge(sem, val)
                        ins = op.fn()
                        if op.sig is not None:
                            if op.inc == 1 and getattr(op, "is_cc", False):
                                ins.then_inc(op.sig[0])
                            else:
                                ins.then_inc(op.sig[0], op.inc)
                    if e == "sp":
                        for (sem, val) in finals:
                            engine.wait_ge(sem, val)
                return body
            blk.tensor(make("pe"))
            blk.scalar(make("act"))
            blk.vector(make("dve"))
            blk.gpsimd(make("pool"))
            blk.sync(make("sp"))
        return self

import contextlib
import numpy as np

D = 2048
NT = 2048
IN_W = 5796
EPS = 1e-6

def a_col(g, part, hg):
    return g * 768 + part * 256 + hg * 64
B0 = 2304
C0 = 3840
CKV = C0 + 768
CG = CKV + 1152


def t_chunks():
    ch = []
    for g in range(3):
        for pair in range(2):
            ch.append((a_col(g, 0, 2 * pair), 128, [("qT", g * 4 + 2 * pair, 0, 64), ("qT", g * 4 + 2 * pair + 1, 64, 64)], 0.125))
        for pair in range(2):
            ch.append((a_col(g, 1, 2 * pair), 128, [("kT", g * 4 + 2 * pair, 0, 64), ("kT", g * 4 + 2 * pair + 1, 64, 64)], 1.0))
    for pair in range(4):
        ch.append((B0 + pair * 128, 128, [("qT", 12 + 2 * pair, 0, 64), ("qT", 13 + 2 * pair, 64, 64)], 0.125))
    for pair in range(4):
        ch.append((B0 + 512 + pair * 128, 128, [("kT", 12 + 2 * pair, 0, 64), ("kT", 13 + 2 * pair, 64, 64)], 1.0))
    for pair in range(6):
        ch.append((C0 + pair * 128, 128, [("qT", 20 + 2 * pair, 0, 64), ("qT", 21 + 2 * pair, 64, 64)], 0.125))
    for pair in range(3):
        ch.append((CKV + pair * 128, 128, [("cmpT", 2 * pair, 0, 64), ("cmpT", 2 * pair + 1, 64, 64)], 1.0))
    ch.append((CKV + 384, 128, [("kT", 20, 0, 64), ("kT", 21, 64, 64)], 1.0))
    ch.append((CKV + 384 + 128, 64, [("kT", 22, 0, 64)], 1.0))
    ch.append((CKV + 768, 128, [("kT", 23, 0, 64), ("kT", 24, 64, 64)], 1.0))
    ch.append((CKV + 768 + 128, 64, [("kT", 25, 0, 64)], 1.0))
    return ch


def n_chunks():
    ch = []
    for g in range(3):
        ch.append((a_col(g, 2, 0), 256, g * 256))
    ch.append((B0 + 1024, 256, 768))
    ch.append((B0 + 1024 + 256, 256, 1024))
    ch.append((CKV + 576, 192, 1280))
    ch.append((CKV + 960, 192, 1472))
    return ch


def build_p1():
    nc = bass.Bass("TRN2", target_bir_lowering=False)
    xT = nc.dram_tensor("xT", [D, NT], F32, kind="ExternalInput").ap()
    gn = nc.dram_tensor("gn", [128, 16], F32, kind="ExternalInput").ap()
    w = nc.dram_tensor("w", [D, IN_W], F32, kind="ExternalInput").ap()
    outs = {
        "qT": nc.dram_tensor("qT", [32, 64, NT], BF16, kind="ExternalOutput").ap(),
        "kT": nc.dram_tensor("kT", [26, 64, NT], BF16, kind="ExternalOutput").ap(),
        "cmpT": nc.dram_tensor("cmpT", [6, 64, NT], BF16, kind="ExternalOutput").ap(),
    }
    vO = nc.dram_tensor("v", [NT, 1664], BF16, kind="ExternalOutput").ap()
    gT = nc.dram_tensor("gT", [36, NT], F32, kind="ExternalOutput").ap()
    with contextlib.ExitStack() as st:
        T = lambda name, shape, dt: st.enter_context(nc.sbuf_tensor("s_" + name, shape, dt))
        PS = lambda name, shape, dt: st.enter_context(nc.psum_tensor("p_" + name, shape, dt))
        p = Prog(nc)
        emit_p1(nc, p, T, PS, xT, gn, w, outs, vO, gT)
        p.emit(final_wait_ops=p.final_ops)
    return nc


def emit_p1(nc, p, T, PS, xT, gn, w, outs, vO, gT, vdst=None, wsrc=None):
    p.final_ops = getattr(p, "final_ops", [])
    hT = T("hT", [128, 16, NT], BF16)
    gsb = T("gsb", [128, 16], F32)
    ones = T("ones", [128, 128], F32)
    xs = [T(f"xs{i}", [128, 16, 512], F32) for i in range(2)]
    sq = [T(f"sq{i}", [128, 512], F32) for i in range(2)]
    rstd = T("rstd", [128, 512], F32)
    wst = [T(f"wst{i}", [128, 16, 256], F32) for i in range(2)]
    wbf = [T(f"wbf{i}", [128, 16, 256], BF16) for i in range(2)]
    ost = [T(f"ost{i}", [128, NT], BF16) for i in range(2)]
    gst = T("gst", [36, NT], F32)
    vst = [T(f"vst{i}", [128, 256], BF16) for i in range(3)]
    pss = PS("pss", [128, 512], F32)
    pacc = [PS(f"pacc{i}", [128, 512], F32) for i in range(3)]

    p.dma("sp", lambda: nc.sync.dma_start(out=gsb[:], in_=gn), writes=["gsb"])
    p.op("pool", lambda: nc.gpsimd.memset(ones[:], 1.0), writes=["ones"])
    xv = xT.rearrange("(k p) t -> p k t", p=128)
    for m in range(4):
        s = m % 2
        p.dma("sp", lambda m=m, s=s: nc.sync.dma_start(out=xs[s][:], in_=xv[:, :, m * 512:(m + 1) * 512]), writes=[f"xs{s}"])
        for k in range(16):
            q = k % 2
            p.op("act", lambda s=s, k=k, q=q: nc.scalar.activation(out=sq[q][:], in_=xs[s][:, k, :], func=AF.Square),
                 reads=[f"xs{s}"], writes=[f"sq{q}"])
            p.op("pe", lambda q=q, k=k: nc.tensor.matmul(pss[:], lhsT=ones[:], rhs=sq[q][:], start=(k == 0), stop=(k == 15)),
                 reads=["ones", f"sq{q}"], writes=["pss"])
        p.op("act", lambda: nc.scalar.activation(out=rstd[:], in_=pss[:], func=AF.Sqrt, scale=1.0 / D, bias=EPS),
             reads=["pss"], writes=["rstd"])
        p.op("dve", lambda: nc.vector.reciprocal(out=rstd[:], in_=rstd[:]), reads=["rstd"], writes=["rstd"])
        for k in range(16):
            eng = "dve" if k % 2 == 0 else "pool"
            E = nc.vector if eng == "dve" else nc.gpsimd
            if eng == "dve":
                p.op("dve", lambda s=s, k=k, m=m: nc.vector.scalar_tensor_tensor(
                    out=hT[:, k, m * 512:(m + 1) * 512], in0=xs[s][:, k, :], scalar=gsb[:, k:k + 1], in1=rstd[:],
                    op0=ALU.mult, op1=ALU.mult), reads=[f"xs{s}", "gsb", "rstd"], writes=[f"hT{m}"])
            else:
                p.op("dve", lambda s=s, k=k, m=m: nc.vector.scalar_tensor_tensor(
                    out=hT[:, k, m * 512:(m + 1) * 512], in0=xs[s][:, k, :], scalar=gsb[:, k:k + 1], in1=rstd[:],
                    op0=ALU.mult, op1=ALU.mult), reads=[f"xs{s}", "gsb", "rstd"], writes=[f"hT{m}"])
    hT_all = [f"hT{m}" for m in range(4)]

    wcount = [0]

    def load_w(c0, ncols, kind=None, idx=0):
        s = wcount[0] % 2
        wcount[0] += 1
        if wsrc is None:
            src = w[:, c0:c0 + ncols].rearrange("(k p) n -> p k n", p=128)
        else:
            src = wsrc(kind, idx)[:, :, 0:ncols]
        p.dma("sp", lambda: nc.sync.dma_start(out=wst[s][:, :, 0:ncols], in_=src), writes=[f"wst{s}"])
        h = 8
        p.op("act", lambda: nc.scalar.copy(out=wbf[s][:, 0:h, 0:ncols], in_=wst[s][:, 0:h, 0:ncols]),
             reads=[f"wst{s}"], writes=[f"wbfa{s}"])
        p.op("pool", lambda: nc.gpsimd.tensor_copy(out=wbf[s][:, h:16, 0:ncols], in_=wst[s][:, h:16, 0:ncols]),
             reads=[f"wst{s}"], writes=[f"wbfb{s}"])
        return s

    tch = t_chunks() + [(CG, 36, [("gT", 0, 0, 36)], 1.0)]
    acc_i = [0]
    for ci, (c0, ncols, dests, scale) in enumerate(tch):
        s = load_w(c0, ncols, "T", ci)
        o = ci % 2
        is_gate = dests[0][0] == "gT"
        for m in range(4):
            a = acc_i[0] % 3
            acc_i[0] += 1
            for k in range(16):
                p.op("pe", lambda a=a, s=s, k=k, m=m, ncols=ncols: nc.tensor.matmul(
                    pacc[a][0:ncols, :], lhsT=wbf[s][:, k, 0:ncols], rhs=hT[:, k, m * 512:(m + 1) * 512],
                    start=(k == 0), stop=(k == 15)),
                    reads=[f"wbfa{s}", f"wbfb{s}", f"hT{m}"], writes=[f"pacc{a}"])
            if is_gate:
                p.op("act", lambda a=a, m=m: nc.scalar.activation(out=gst[:, m * 512:(m + 1) * 512], in_=pacc[a][0:36, :], func=AF.Sigmoid),
                     reads=[f"pacc{a}"], writes=["gst"])
            elif m % 2 == 0:
                p.op("act", lambda a=a, m=m, o=o, ncols=ncols, scale=scale: nc.scalar.activation(
                    out=ost[o][0:ncols, m * 512:(m + 1) * 512], in_=pacc[a][0:ncols, :], func=AF.Copy, scale=scale),
                    reads=[f"pacc{a}"], writes=[f"ost{o}"])
            else:
                p.op("dve", lambda a=a, m=m, o=o, ncols=ncols, scale=scale: nc.vector.tensor_scalar(
                    out=ost[o][0:ncols, m * 512:(m + 1) * 512], in0=pacc[a][0:ncols, :], scalar1=scale, scalar2=None, op0=ALU.mult),
                    reads=[f"pacc{a}"], writes=[f"ost{o}"])
        if is_gate:
            p.final_ops.append(p.dma("pool", lambda: nc.gpsimd.dma_start(out=gT, in_=gst[:]), reads=["gst"]))
        else:
            for (dn, dh, r0, nr) in dests:
                p.final_ops.append(p.dma("pool", lambda dn=dn, dh=dh, r0=r0, nr=nr, o=o: nc.gpsimd.dma_start(
                    out=outs[dn][dh], in_=ost[o][r0:r0 + nr, :]), reads=[f"ost{o}"]))

    vi = [0]
    for ni, (c0, ncols, vc0) in enumerate(n_chunks()):
        s = load_w(c0, ncols, "N", ni)
        for ts in range(16):
            a = acc_i[0] % 3
            acc_i[0] += 1
            m = ts // 4
            for k in range(16):
                p.op("pe", lambda a=a, s=s, k=k, ts=ts, ncols=ncols: nc.tensor.matmul(
                    pacc[a][:, 0:ncols], lhsT=hT[:, k, ts * 128:(ts + 1) * 128], rhs=wbf[s][:, k, 0:ncols],
                    start=(k == 0), stop=(k == 15)),
                    reads=[f"wbfa{s}", f"wbfb{s}", f"hT{m}"], writes=[f"pacc{a}"])
            vs = vi[0] % 3
            vi[0] += 1
            if ts % 2 == 0:
                p.op("act", lambda a=a, vs=vs, ncols=ncols: nc.scalar.copy(out=vst[vs][:, 0:ncols], in_=pacc[a][:, 0:ncols]),
                     reads=[f"pacc{a}"], writes=[f"vst{vs}"])
            else:
                p.op("dve", lambda a=a, vs=vs, ncols=ncols: nc.vector.tensor_copy(out=vst[vs][:, 0:ncols], in_=pacc[a][:, 0:ncols]),
                     reads=[f"pacc{a}"], writes=[f"vst{vs}"])
            if vdst is None:
                p.final_ops.append(p.dma("pool", lambda ts=ts, vs=vs, vc0=vc0, ncols=ncols: nc.gpsimd.dma_start(
                    out=vO[ts * 128:(ts + 1) * 128, vc0:vc0 + ncols], in_=vst[vs][:, 0:ncols]), reads=[f"vst{vs}"]))
            else:
                p.final_ops.append(p.dma("pool", lambda ts=ts, vs=vs, vc0=vc0, ncols=ncols: nc.gpsimd.dma_start(
                    out=vdst(ts, vc0, ncols), in_=vst[vs][:, 0:ncols].rearrange("p (h e) -> p h e", e=64)), reads=[f"vst{vs}"]))


import contextlib, math
import numpy as np

S = 8192
NT = 2048
BW = 4480
GL = BW + 128
NEGM = -30000.0
A_CFG = ((128, 1), (512, 4), (2048, 16))
U16 = mybir.dt.uint16


def t5_bucket_np(dist):
    n = np.maximum(dist, 0)
    nf = np.maximum(n, 1).astype(np.float32)
    large = 16 + (np.log(nf / np.float32(16)) / np.float32(math.log(128.0)) * np.float32(16)).astype(np.int32)
    large = np.minimum(large, 31)
    return np.where(n < 16, n, large)


def band_vectors(rel_table, j):
    v = np.arange(GL)
    dist = v + 512 * j - 2047
    bk = t5_bucket_np(dist)
    G = np.empty((44, GL), np.float32)
    cb = np.empty((44,), np.float32)
    for b in range(44):
        if b < 12:
            h = b
            W, d = A_CFG[b // 4]
            ok = (dist >= 0) & (dist <= W) & (dist % d == 0)
        elif b < 20:
            h = b
            ok = dist >= 0
        elif b < 32:
            h = b
            ok = dist >= 0
        else:
            h = 20 + (b - 32)
            ok = (dist >= 0) & (dist < 512)
        G[b] = np.where(ok, rel_table[h, bk], np.float32(NEGM))
        cb[b] = rel_table[h, 31]
    return G, cb


def core_tokens(j):
    return np.concatenate([np.arange(512 * (4 * m + j), 512 * (4 * m + j) + 512) for m in range(4)])


def moba_consts(j):
    t = core_tokens(j)
    ob = t // 256
    n = np.arange(32)[None, :]
    neg = np.where(n >= ob[:, None], np.float32(-1e30), np.float32(0)).astype(np.float32)
    own = (n >= ob[:, None]).astype(np.float32)
    f = lambda a: np.ascontiguousarray(a.reshape(16, 128, 32).transpose(1, 0, 2))
    return f(neg), f(own)


def eb_const():
    k = np.arange(S)
    return (k[None, :] // 256 == np.arange(32)[:, None]).astype(np.float32)


def ec_const():
    c = np.arange(S)
    key = (c // 128) * 128 + 127 - (c % 128)
    return (key[None, :] // 64 == np.arange(128)[:, None]).astype(np.float32)


def rev_blocks(a, axis):
    a = np.moveaxis(a, axis, -1)
    sh = a.shape
    a = a.reshape(sh[:-1] + (sh[-1] // 128, 128))[..., ::-1].reshape(sh)
    return np.moveaxis(a, -1, axis)


def nsa_consts(j):
    t = core_tokens(j)
    own = (t // 64)[:, None]
    jb = np.arange(128)[None, :]
    valid = jb <= own
    forced = (jb == 0) | (jb == own) | (jb == own - 1)
    am = (valid & ~forced).astype(np.float32)
    ba = np.where(valid, np.where(forced, np.float32(1e9), np.float32(0)), np.float32(-1)).astype(np.float32)
    f = lambda a: np.ascontiguousarray(a.reshape(16, 128, 128).transpose(1, 0, 2))
    pp = np.arange(128)[:, None]
    q = np.arange(512)[None, :]
    cmA = np.where(16 * pp + 31 <= 512 * j + q, np.float32(0), np.float32(NEGM)).astype(np.float32)
    cmB = np.where(16 * pp + 31 - 2048 <= 512 * j + q, np.float32(0), np.float32(NEGM)).astype(np.float32)
    return f(am), f(ba), cmA, cmB


def ovl_const():
    i = np.arange(512)[:, None]
    jb = np.arange(128)[None, :]
    ov = ((16 * i < 64 * jb + 64) & (16 * i + 32 > 64 * jb) & (i < 511)).astype(np.float32)
    return np.ascontiguousarray(ov.reshape(4, 128, 128).transpose(1, 0, 2))


def selg_const():
    sg = np.zeros((36, 36, 64), np.float32)
    for r in range(36):
        sg[r, r, :] = 1.0
    return sg

class P2:
    def __init__(self, nc, p, T, PS, dr, heads_A=(0, 1, 2, 3), heads_B=tuple(range(8)), kvs_C=(0, 1, 2), fused=False):
        self.nc, self.p, self.T, self.PS, self.dr = nc, p, T, PS, dr
        self.fused = fused
        self.heads_A, self.heads_B, self.kvs_C = heads_A, heads_B, kvs_C
        self.units = []
        self.alloc()

    def alloc(self):
        T, PS = self.T, self.PS
        self.kTall = T("kTall", [128, S], BF16)
        self.qTall = T("qTall", [128, NT], BF16)
        self.kTb = [self.kTall[0:64], self.kTall[64:128]]
        self.qTb = [self.qTall[0:64], self.qTall[64:128]]
        self.Vb = [T(f"Vb{i}", [128, 64, 128], BF16) for i in range(2)]
        self.band1 = T("band1", [128, BW], F32)
        self.bandb = [self.band1, self.band1]
        self.cbb = T("cbb", [128, 44], F32)
        self.Eb = T("Eb", [128, S], BF16)
        self.MTb = [T(f"MTb{i}", [128, NT], BF16) for i in range(2)]
        self.sb = [T(f"sb{i}", [128, 512], F32) for i in range(2)]
        self.PT = [T(f"PT{i}", [128, 512], BF16) for i in range(3)]
        self.nd = T("nd", [128, 12, 512], F32)
        self.rden = T("rden", [64, 512], F32)
        self.dsum = T("dsum", [128, 512], F32)
        self.ost = [T(f"ost{i}", [64, 512], BF16) for i in range(3)]
        self.identb = T("identb", [128, 128], BF16)
        self.identf = T("identf", [128, 128], F32)
        self.negm = T("negm", [128, 16, 32], F32)
        self.ownm = T("ownm", [128, 16, 32], F32)
        self.km = T("km", [128, 32], F32)
        self.kmb = T("kmb", [128, 32], BF16)
        self.gm = T("gm", [128, 16, 32], F32)
        self.m8 = T("m8", [128, 16, 8], F32)
        self.selt = T("selt", [128, 16, 32], F32)
        self.Mq = T("Mq", [128, 16, 32], BF16)
        self.qcm = [T(f"qcm{i}", [64, 512], BF16) for i in range(3)]
        self.ef = [T(f"ef{i}", [128, 512], F32) for i in range(4)]
        self.pcb = [T(f"pcb{i}", [128, 512], BF16) for i in range(4)]
        self.w1b = T("w1b", [64, 32, 128], BF16)
        self.w2f = T("w2f", [128, 2, 64], F32)
        self.w2b = T("w2b", [128, 2, 64], BF16)
        self.peTf = T("peTf", [64, 2, 32, 2], F32)
        self.peTb = T("peTb", [64, 2, 32, 2], BF16)
        self.kcTb = T("kcTb", [64, 512], BF16)
        self.vcb = T("vcb", [128, 4, 64], BF16)
        self.scr = [T(f"scr{i}", [128, 512], F32) for i in range(2)]
        self.cbias = T("cbias", [128, 1], F32)
        self.hid = T("hid", [128, 512], BF16)
        self.amb = T("amb", [128, 4, 128], F32)
        self.bab = T("bab", [128, 4, 128], F32)
        self.cmA = T("cmA", [128, 512], F32)
        self.cmB = T("cmB", [128, 512], F32)
        self.ovl = T("ovl", [128, 4, 128], F32)
        self.sel3 = T("sel3", [36, 3, 64], F32)
        self.gTs = T("gTs", [36, NT], F32)
        self.impS = T("impS", [128, 512], F32)
        self.score = T("score", [128, 4, 128], F32)
        self.sc2 = T("sc2", [128, 128], F32)
        self.m16 = T("m16", [128, 4, 16], F32)
        self.Msel = T("Msel", [128, 4, 128], BF16)
        self.onesf = T("onesf", [128, 128], F32)
        self.tmpf = T("tmpf", [64, 512], F32)
        self.ocs = T("ocs", [64, 512], F32)
        self.pS = [PS(f"pS{i}", [128, 512], F32) for i in range(4)]
        self.pacc = [PS(f"pacc{i}", [128, 512], F32) for i in range(2)]
        self.pm = [PS("pm0", [128, 512], F32), self.pS[3]]
        self.pmn = ["pm0", "pS3"]
        self.pmb = PS("pmb", [128, 1024], BF16)
        if self.fused:
            self.hst = [T(f"hst{i}", [128, 512], F32) for i in range(2)]
            self.Jf = T("Jf", [128, 128], F32)
        self.ost_i = 0
        self.slot = 0
        self.acc_i = 0
        self.out_ops = []

    def setup(self):
        nc, p = self.nc, self.p
        p.op("pool", lambda: nc.gpsimd.memset(self.identf[:], 1.0), writes=["identf"])
        p.op("pool", lambda: nc.gpsimd.affine_select(out=self.identf[:], in_=self.identf[:], pattern=[[-1, 128]],
                                                     compare_op=ALU.is_equal, fill=0.0, base=0, channel_multiplier=1),
             reads=["identf"], writes=["identf"])
        p.op("dve", lambda: nc.vector.tensor_copy(out=self.identb[:], in_=self.identf[:]), reads=["identf"], writes=["identb"])
        for i in range(2):
            p.op("pool", lambda i=i: nc.gpsimd.memset(self.Vb[i][:, :, 64:128], 1.0), writes=[f"Vones{i}"])
        p.dma("sp", lambda: nc.sync.dma_start(out=self.cbb[:], in_=self.dr["cb"]), writes=["cbb"])
        if self.fused:
            p.op("pool", lambda: nc.gpsimd.memset(self.Jf[:], 1.0), writes=["Jf"])
            p.op("pool", lambda: nc.gpsimd.affine_select(out=self.Jf[:], in_=self.Jf[:], pattern=[[1, 128]],
                                                         compare_op=ALU.is_equal, fill=0.0, base=-127, channel_multiplier=1),
                 reads=["Jf"], writes=["Jf"])

    def load_kT_gathered(self, dst, chunks, row0, res):
        nc, p = self.nc, self.p
        src2d = None
        for (r0, nr, ap) in chunks:
            if r0 <= row0 < r0 + nr:
                src2d, row0 = ap, row0 - r0
                break
        nrows = src2d.shape[0] // 4
        for m in range(4):
            src = bass.AP(tensor=src2d.tensor, offset=src2d[row0:row0 + 1, m * 512:m * 512 + 1].offset,
                          ap=[[2048, 64], [nrows * 2048, 4], [1, 512]])
            d = dst[:, m * 2048:(m + 1) * 2048].rearrange("e (r i) -> e r i", i=512)
            p.dma("sp", lambda src=src, d=d: nc.sync.dma_start(out=d, in_=src), reads=["gathered"], writes=[res])

    def load_v_gathered(self, s, ki):
        nc, p = self.nc, self.p
        vg, lrow, nr = None, 0, 0
        for (r0, nr_, ap) in self.dr["v_g"]:
            if r0 <= ki * 128 < r0 + nr_:
                vg, lrow, nr = ap, ki * 128 - r0, nr_
                break
        for m in range(4):
            for r in range(4):
                src = bass.AP(tensor=vg.tensor, offset=vg[r * nr + lrow:r * nr + lrow + 1, m * 256:m * 256 + 1].offset,
                              ap=[[1024, 128], [64, 4], [1, 64]])
                d = self.Vb[s][:, m * 16 + r * 4:m * 16 + r * 4 + 4, 0:64]
                p.dma("sp", lambda src=src, d=d: nc.sync.dma_start(out=d, in_=src), reads=["gathered"], writes=[f"Vb{s}"])

    def load_band_flipped(self, bi):
        nc, p = self.nc, self.p
        g = self.dr["G"]
        nch = (BW + 511) // 512
        for ch in range(nch):
            w_ = min(512, BW - ch * 512)
            hs = ch % 2
            src = bass.AP(tensor=g.tensor, offset=g[bi:bi + 1, ch * 512:ch * 512 + 1].offset, ap=[[1, 128], [1, w_]])
            p.dma("sp", lambda src=src, hs=hs, w_=w_: nc.sync.dma_start(out=self.hst[hs][:, 0:w_], in_=src), writes=[f"hst{hs}"])
            pb = self.pm[ch % 2]
            p.op("pe", lambda hs=hs, w_=w_, pb=pb: nc.tensor.matmul(pb[:, 0:w_], lhsT=self.Jf[:], rhs=self.hst[hs][:, 0:w_], start=True, stop=True),
                 reads=["Jf", f"hst{hs}"], writes=[self.pmn[ch % 2]])
            p.op("act", lambda ch=ch, w_=w_, pb=pb: nc.scalar.copy(out=self.band1[:, ch * 512:ch * 512 + w_], in_=pb[:, 0:w_]),
                 reads=[self.pmn[ch % 2]], writes=["bandb"])

    def load_head(self, qi, ki, bi):
        nc, p, dr = self.nc, self.p, self.dr
        s = self.slot % 2
        self.slot += 1
        if self.fused:
            p.dma("sp", lambda: nc.sync.dma_start(out=self.qTb[s], in_=dr["qT"][qi]), reads=["qT_s"], writes=[f"qTb{s}"])
            self.load_kT_gathered(self.kTb[s], dr["kT_g"], ki * 64, f"kTb{s}")
            self.load_v_gathered(s, ki)
            self.load_band_flipped(bi)
            return s
        p.dma("sp", lambda: nc.sync.dma_start(out=self.qTb[s], in_=dr["qT"][qi]), writes=[f"qTb{s}"])
        p.dma("sp", lambda: nc.sync.dma_start(out=self.kTb[s], in_=dr["kTf"][ki]), writes=[f"kTb{s}"])
        p.dma("sp", lambda: nc.sync.dma_start(out=self.Vb[s][:, :, 0:64], in_=dr["vf"][ki]), writes=[f"Vb{s}"])
        g = dr["G"]
        src = bass.AP(tensor=g.tensor, offset=g[bi:bi + 1, 0:1].offset, ap=[[1, 128], [1, BW]])
        p.dma("sp", lambda: nc.sync.dma_start(out=self.bandb[s][:], in_=src), writes=["bandb"])
        return s

    def add_units(self, s, m, kts, near_lo, bi, acc, mask=None, post=None, pre=None):
        n = len(kts)
        for idx, kt in enumerate(kts):
            self.units.append(dict(s=s, m=m, kt=kt, near=(kt >= near_lo), bi=bi, acc=acc, first=(idx == 0), last=(idx == n - 1),
                                   mask=mask, post=post if idx == n - 1 else None, pre=pre if idx == 0 else None))

    def flush_units(self, LA=3):
        nc, p = self.nc, self.p
        U = self.units
        n = len(U)

        def qk(i):
            u = U[i]
            if u["pre"] is not None:
                u["pre"]()
            b = i % 4
            s, m, kt = u["s"], u["m"], u["kt"]
            rd = [f"kTb{s}", f"qTb{s}"]
            if u["mask"] is None:
                p.op("pe", lambda: nc.tensor.matmul(self.pS[b][:], lhsT=self.kTb[s][:, kt * 128:(kt + 1) * 128],
                                                    rhs=self.qTb[s][:, m * 512:(m + 1) * 512], start=True, stop=True),
                     reads=rd, writes=[f"pS{b}"])
            else:
                nr, ms = u["mask"]
                p.op("pe", lambda: nc.tensor.matmul(self.pS[b][:], lhsT=self.kTb[s][:, kt * 128:(kt + 1) * 128],
                                                    rhs=self.qTb[s][:, m * 512:(m + 1) * 512], start=True, stop=False),
                     reads=rd, writes=[f"pS{b}"])
                p.op("pe", lambda: nc.tensor.matmul(self.pS[b][:], lhsT=self.Eb[0:nr, kt * 128:(kt + 1) * 128],
                                                    rhs=self.MTb[ms][0:nr, m * 512:(m + 1) * 512], start=False, stop=True),
                     reads=["Eb", f"MTb{ms}"], writes=[f"pS{b}"])

        def rest(i):
            u = U[i]
            b = i % 4
            s, m, kt, bi, acc = u["s"], u["m"], u["kt"], u["bi"], u["acc"]
            pt = i % 3
            if u["near"]:
                sbi = i % 2
                u0 = 2048 * m - 128 * kt + 1920
                assert 0 <= u0 and u0 + 512 <= BW, (m, kt, u0)
                p.op("dve", lambda: nc.vector.tensor_tensor(out=self.sb[sbi][:], in0=self.pS[b][:], in1=self.bandb[s][:, u0:u0 + 512], op=ALU.add),
                     reads=[f"pS{b}", "bandb"], writes=[f"sb{sbi}"])
                p.op("act", lambda: nc.scalar.activation(out=self.PT[pt][:], in_=self.sb[sbi][:], func=AF.Exp),
                     reads=[f"sb{sbi}"], writes=[f"PT{pt}"])
            else:
                p.op("act", lambda: nc.scalar.activation(out=self.PT[pt][:], in_=self.pS[b][:], func=AF.Exp, bias=self.cbb[:, bi:bi + 1]),
                     reads=[f"pS{b}", "cbb"], writes=[f"PT{pt}"])
            p.op("pe", lambda: nc.tensor.matmul(self.pacc[acc][:], lhsT=self.Vb[s][:, kt, :], rhs=self.PT[pt][:],
                                                start=u["first"], stop=u["last"]),
                 reads=[f"Vb{s}", f"Vones{s}", f"PT{pt}"], writes=[f"pacc{acc}"])
            if u["post"] is not None:
                u["post"]()

        for i in range(n + LA):
            if i < n:
                qk(i)
            if i - LA >= 0:
                rest(i - LA)
        self.units = []

    def write_out(self, head_feat, m, num_ap, rden_ap):
        nc, p = self.nc, self.p
        o = self.ost_i % 3
        self.ost_i += 1
        num, nres = num_ap
        rd, rres = rden_ap
        p.op("dve", lambda: nc.vector.tensor_tensor(out=self.ost[o][:], in0=num, in1=rd, op=ALU.mult),
             reads=[nres, rres], writes=[f"ost{o}"])
        dst = self.dr["OT"][head_feat * 64:(head_feat + 1) * 64, m * 512:(m + 1) * 512]
        self.out_ops.append(p.dma("pool", lambda: nc.gpsimd.dma_start(out=dst, in_=self.ost[o][:]), reads=[f"ost{o}"]))

    def mixer_A(self):
        nc, p = self.nc, self.p
        for hg in self.heads_A:
            for g in range(3):
                W, d = A_CFG[g]
                h = g * 4 + hg
                s = self.load_head(h, h, h)
                for m in range(4):
                    lo = max(0, (2048 * m - W) // 128)
                    kts = list(range(lo, 16 * m + 16))
                    acc = self.acc_i % 2
                    self.acc_i += 1

                    def post(g=g, m=m, acc=acc):
                        p.op("act", lambda: nc.scalar.copy(out=self.nd[:, g * 4 + m, :], in_=self.pacc[acc][:]),
                             reads=[f"pacc{acc}"], writes=[f"nd{g}_{m}"])
                    self.add_units(s, m, kts, 0, h, acc, post=post)
                self.flush_units()
            for m in range(4):
                p.op("pool", lambda m=m: nc.gpsimd.tensor_tensor(out=self.dsum[64:128, :], in0=self.nd[64:128, 0 * 4 + m, :], in1=self.nd[64:128, 1 * 4 + m, :], op=ALU.add),
                     reads=[f"nd0_{m}", f"nd1_{m}"], writes=["dsum"])
                p.op("pool", lambda m=m: nc.gpsimd.tensor_tensor(out=self.dsum[64:128, :], in0=self.dsum[64:128, :], in1=self.nd[64:128, 2 * 4 + m, :], op=ALU.add),
                     reads=["dsum", f"nd2_{m}"], writes=["dsum"])
                p.op("dve", lambda: nc.vector.reciprocal(out=self.rden[:], in_=self.dsum[64:128, :]), reads=["dsum"], writes=["rden"])
                for g in range(3):
                    self.write_out(g * 4 + hg, m, (self.nd[0:64, g * 4 + m, :], f"nd{g}_{m}"), (self.rden[:], "rden"))

    def moba_prologue(self, s, ms):
        nc, p = self.nc, self.p
        p.op("dve", lambda: nc.vector.tensor_reduce(out=self.km[64 * s:64 * s + 64, :], in_=self.kTb[s].rearrange("e (n k) -> e n k", k=256), axis=AX.X, op=ALU.add),
             reads=[f"kTb{s}"], writes=["km"])
        p.op("dve", lambda: nc.vector.tensor_scalar(out=self.kmb[64 * s:64 * s + 64, :], in0=self.km[64 * s:64 * s + 64, :], scalar1=1.0 / 256, scalar2=None, op0=ALU.mult),
             reads=["km"], writes=["kmb"])
        pg = self.pm[0]
        for qs in range(16):
            p.op("pe", lambda qs=qs: nc.tensor.matmul(pg[:, qs * 32:(qs + 1) * 32], lhsT=self.qTb[s][:, qs * 128:(qs + 1) * 128], rhs=self.kmb[64 * s:64 * s + 64, :], start=True, stop=True),
                 reads=[f"qTb{s}", "kmb"], writes=["pm0"])
        p.op("dve", lambda: nc.vector.tensor_tensor(out=self.gm[:], in0=pg[:].rearrange("p (a b) -> p a b", b=32), in1=self.negm[:], op=ALU.add),
             reads=["pm0", "negm"], writes=["gm"])
        for qs in range(16):
            p.op("dve", lambda qs=qs: nc.vector.max(out=self.m8[:, qs, :], in_=self.gm[:, qs, :]), reads=["gm"], writes=["m8"])
        for qs in range(16):
            p.op("dve", lambda qs=qs: nc.vector.tensor_scalar(out=self.selt[:, qs, :], in0=self.gm[:, qs, :], scalar1=self.m8[:, qs, 2:3], scalar2=None, op0=ALU.is_ge),
                 reads=["gm", "m8"], writes=["selt"])
        p.op("dve", lambda: nc.vector.tensor_tensor(out=self.selt[:], in0=self.selt[:], in1=self.ownm[:], op=ALU.max), reads=["selt", "ownm"], writes=["selt"])
        p.op("dve", lambda: nc.vector.tensor_scalar(out=self.Mq[:], in0=self.selt[:], scalar1=-1.0, scalar2=-NEGM, op0=ALU.add, op1=ALU.mult),
             reads=["selt"], writes=["Mq"])
        for half in range(2):
            for q8 in range(8):
                qs = half * 8 + q8
                p.op("pe", lambda qs=qs, q8=q8: nc.tensor.transpose(out=self.pmb[0:32, q8 * 128:(q8 + 1) * 128], in_=self.Mq[:, qs, :], identity=self.identb[:]),
                     reads=["Mq", "identb"], writes=["pmb"])
            p.op("act", lambda half=half: nc.scalar.copy(out=self.MTb[ms][0:32, half * 1024:(half + 1) * 1024], in_=self.pmb[0:32, :]),
                 reads=["pmb"], writes=[f"MTb{ms}"])

    def mixer_B(self):
        nc, p, dr = self.nc, self.p, self.dr
        p.dma("sp", lambda: nc.sync.dma_start(out=self.negm[:], in_=dr["negm"]), writes=["negm"])
        p.dma("sp", lambda: nc.sync.dma_start(out=self.ownm[:], in_=dr["ownm"]), writes=["ownm"])
        p.dma("sp", lambda: nc.sync.dma_start(out=self.Eb[0:32, :], in_=dr["EB"]), writes=["Eb"])
        for hb in self.heads_B:
            h = 12 + hb
            s = self.load_head(h, h, h)
            ms = hb % 2
            self.moba_prologue(s, ms)
            for m in range(4):
                kts = list(range(0, 16 * m + 16))
                acc = self.acc_i % 2
                self.acc_i += 1

                def post(h=h, m=m, acc=acc):
                    p.op("dve", lambda: nc.vector.reciprocal(out=self.rden[:], in_=self.pacc[acc][64:128, :]), reads=[f"pacc{acc}"], writes=["rden"])
                    self.write_out(h, m, (self.pacc[acc][0:64, :], f"pacc{acc}"), (self.rden[:], "rden"))
                self.add_units(s, m, kts, 16 * m - 12, h, acc, mask=(32, ms), post=post)
            self.flush_units()


    def compress(self, kv, t):
        nc, p, dr = self.nc, self.p, self.dr
        s = 0
        if self.fused:
            self.load_kT_gathered(self.kTb[s], dr["cmpT_g"], (t * 3 + kv) * 64, f"kTb{s}")
        else:
            p.dma("sp", lambda: nc.sync.dma_start(out=self.kTb[s], in_=dr["cmpTf"][t * 3 + kv]), writes=[f"kTb{s}"])
        stg = self.band1[0:64, 0:4096].rearrange("e (l j) -> e l j", j=128)
        p.dma("sp", lambda: nc.sync.dma_start(out=stg, in_=dr["w1"][t]), writes=["bandb"])
        p.op("act", lambda: nc.scalar.copy(out=self.w1b[:], in_=stg), reads=["bandb"], writes=["w1b"])
        ph = self.pm[0]
        base = self.kTb[s]
        for l in range(32):
            rhs = bass.AP(tensor=base.tensor, offset=base.offset + l, ap=[list(base.ap[0]), [16, 511]])
            p.op("pe", lambda l=l, rhs=rhs: nc.tensor.matmul(ph[:, 0:511], lhsT=self.w1b[:, l, :], rhs=rhs, start=(l == 0), stop=(l == 31)),
                 reads=["w1b", f"kTb{s}"], writes=["pm0"])
        pc = self.pm[1]
        for l in range(32):
            p.op("pe", lambda l=l: nc.tensor.matmul(pc[:, 0:2], lhsT=self.w1b[:, l, :], rhs=self.peTb[:, t, l, :], start=(l == 0), stop=(l == 31)),
                 reads=["w1b", "peTb"], writes=["pS3"])
        p.op("dve", lambda: nc.vector.tensor_copy(out=self.cbias[:], in_=pc[:, 0:1]), reads=["pS3"], writes=["cbias"])
        x, y = self.scr[0], self.scr[1]
        p.op("act", lambda: nc.scalar.activation(out=x[:, 0:511], in_=ph[:, 0:511], func=AF.Identity, bias=self.cbias[:, 0:1]),
             reads=["pm0", "cbias"], writes=["scr0"])
        p.op("dve", lambda: nc.vector.tensor_tensor(out=y[:, 0:511], in0=x[:, 0:511], in1=x[:, 0:511], op=ALU.mult), reads=["scr0"], writes=["scr1"])
        p.op("dve", lambda: nc.vector.tensor_scalar(out=y[:, 0:511], in0=y[:, 0:511], scalar1=0.044715, scalar2=1.0, op0=ALU.mult, op1=ALU.add), reads=["scr1"], writes=["scr1"])
        p.op("dve", lambda: nc.vector.tensor_tensor(out=y[:, 0:511], in0=y[:, 0:511], in1=x[:, 0:511], op=ALU.mult), reads=["scr0", "scr1"], writes=["scr1"])
        p.op("act", lambda: nc.scalar.activation(out=y[:, 0:511], in_=y[:, 0:511], func=AF.Tanh, scale=0.7978845608028654), reads=["scr1"], writes=["scr1"])
        p.op("dve", lambda: nc.vector.scalar_tensor_tensor(out=y[:, 0:511], in0=y[:, 0:511], scalar=1.0, in1=x[:, 0:511], op0=ALU.add, op1=ALU.mult), reads=["scr0", "scr1"], writes=["scr1"])
        p.op("dve", lambda: nc.vector.tensor_scalar(out=self.hid[:, 0:511], in0=y[:, 0:511], scalar1=0.5, scalar2=None, op0=ALU.mult), reads=["scr1"], writes=["hid"])
        if t == 0:
            pk = self.pS[0]
            p.op("pe", lambda: nc.tensor.matmul(pk[0:64, :], lhsT=self.w2b[:, 0, :], rhs=self.hid[:], start=True, stop=True),
                 reads=["w2b", "hid"], writes=["pS0"])
            p.op("act", lambda: nc.scalar.copy(out=self.kcTb[:], in_=pk[0:64, :]), reads=["pS0"], writes=["kcTb"])
        else:
            pv = self.pS[1]
            for it in range(4):
                p.op("pe", lambda it=it: nc.tensor.matmul(pv[:, it * 64:(it + 1) * 64], lhsT=self.hid[:, it * 128:(it + 1) * 128], rhs=self.w2b[:, 1, :], start=True, stop=True),
                     reads=["w2b", "hid"], writes=["pS1"])
            p.op("act", lambda: nc.scalar.copy(out=self.vcb[:], in_=pv[:, 0:256].rearrange("p (a b) -> p a b", b=64)), reads=["pS1"], writes=["vcb"])

    def cmp_stage(self, kv, m):
        nc, p, dr = self.nc, self.p, self.dr
        pden, poc, pgt, pimp, ptr = self.pS[0], self.pS[1], self.pS[2], self.pacc[0], self.pacc[1]
        nit = min(m, 3) + 1
        for gq in range(4):
            hc = kv * 4 + gq
            qs = self.qc_i % 3
            self.qc_i += 1
            p.dma("sp", lambda qs=qs, hc=hc: nc.sync.dma_start(out=self.qcm[qs][:], in_=dr["qT"][20 + hc][:, m * 512:(m + 1) * 512]), writes=[f"qcm{qs}"])
            p.dma("sp", lambda hc=hc: nc.sync.dma_start(out=self.sel3[:], in_=dr["selg"][:, 3 * hc:3 * hc + 3, :]), writes=["sel3"])
            for it in range(nit):
                ps = self.pm[it % 2]
                psn = self.pmn[it % 2]
                p.op("pe", lambda it=it, ps=ps, qs=qs: nc.tensor.matmul(ps[:], lhsT=self.kcTb[:, it * 128:(it + 1) * 128], rhs=self.qcm[qs][:], start=True, stop=True),
                     reads=["kcTb", f"qcm{qs}"], writes=[psn])
                if it >= m - 1:
                    cm, cmn = (self.cmA, "cmA") if it == m else (self.cmB, "cmB")
                    sbi = it % 2
                    p.op("dve", lambda ps=ps, cm=cm, sbi=sbi: nc.vector.tensor_tensor(out=self.sb[sbi][:], in0=ps[:], in1=cm[:], op=ALU.add),
                         reads=[psn, cmn], writes=[f"sb{sbi}"])
                    p.op("act", lambda it=it, sbi=sbi: nc.scalar.activation(out=self.ef[it][:], in_=self.sb[sbi][:], func=AF.Exp), reads=[f"sb{sbi}"], writes=[f"ef{it}"])
                else:
                    p.op("act", lambda it=it, ps=ps: nc.scalar.activation(out=self.ef[it][:], in_=ps[:], func=AF.Exp), reads=[psn], writes=[f"ef{it}"])
                p.op("pe", lambda it=it: nc.tensor.matmul(pden[:], lhsT=self.onesf[:], rhs=self.ef[it][:], start=(it == 0), stop=(it == nit - 1)),
                     reads=["onesf", f"ef{it}"], writes=["pS0"])
            rd = self.scr[0]
            p.op("dve", lambda: nc.vector.tensor_scalar(out=rd[:], in0=pden[:], scalar1=1e-30, scalar2=None, op0=ALU.max), reads=["pS0"], writes=["scr0"])
            p.op("dve", lambda: nc.vector.reciprocal(out=rd[:], in_=rd[:]), reads=["scr0"], writes=["scr0"])
            for it in range(nit):
                p.op("dve", lambda it=it: nc.vector.tensor_tensor(out=self.ef[it][:], in0=self.ef[it][:], in1=rd[:], op=ALU.mult), reads=[f"ef{it}", "scr0"], writes=[f"ef{it}"])
                p.op("pool", lambda it=it: nc.gpsimd.tensor_copy(out=self.pcb[it][:], in_=self.ef[it][:]), reads=[f"ef{it}"], writes=[f"pcb{it}"])
                p.op("pe", lambda it=it, gq=gq: nc.tensor.matmul(pimp[:], lhsT=self.ovl[:, it, :], rhs=self.ef[it][:], start=(gq == 0 and it == 0), stop=(gq == 3 and it == nit - 1)),
                     reads=["ovl", f"ef{it}"], writes=["pacc0"])
            for it in range(nit):
                p.op("pe", lambda it=it: nc.tensor.matmul(poc[0:64, :], lhsT=self.vcb[:, it, :], rhs=self.pcb[it][:], start=(it == 0), stop=(it == nit - 1)),
                     reads=["vcb", f"pcb{it}"], writes=["pS1"])
            p.op("pe", lambda: nc.tensor.matmul(pgt[0:64, :], lhsT=self.sel3[:, 0, :], rhs=self.gTs[:, m * 512:(m + 1) * 512], start=True, stop=True),
                 reads=["sel3", "gTs"], writes=["pS2"])
            p.op("act", lambda: nc.scalar.copy(out=self.ocs[:], in_=poc[0:64, :]), reads=["pS1"], writes=["ocs"])
            half, idx = gq // 2, (gq % 2) * 4 + m
            if half == 0:
                p.op("dve", lambda idx=idx: nc.vector.tensor_tensor(out=self.nd[0:64, idx, :], in0=self.ocs[:], in1=pgt[0:64, :], op=ALU.mult),
                     reads=["ocs", "pS2"], writes=[f"oc{gq}_{m}"])
            else:
                p.op("dve", lambda: nc.vector.tensor_tensor(out=self.tmpf[:], in0=self.ocs[:], in1=pgt[0:64, :], op=ALU.mult),
                     reads=["ocs", "pS2"], writes=["tmpf"])
                p.op("dve", lambda idx=idx: nc.vector.tensor_copy(out=self.nd[64:128, idx, :], in_=self.tmpf[:]), reads=["tmpf"], writes=[f"oc{gq}_{m}"])
        p.dma("sp", lambda: nc.sync.dma_start(out=self.amb[:], in_=dr["AM"][:, 4 * m:4 * m + 4, :]), writes=["amb"])
        p.dma("sp", lambda: nc.sync.dma_start(out=self.bab[:], in_=dr["BA"][:, 4 * m:4 * m + 4, :]), writes=["bab"])
        p.op("act", lambda: nc.scalar.copy(out=self.impS[:], in_=pimp[:]), reads=["pacc0"], writes=["impS"])
        for qs in range(4):
            p.op("pe", lambda qs=qs: nc.tensor.transpose(out=ptr[:, qs * 128:(qs + 1) * 128], in_=self.impS[:, qs * 128:(qs + 1) * 128], identity=self.identf[:]),
                 reads=["impS", "identf"], writes=["pacc1"])
        p.op("dve", lambda: nc.vector.tensor_tensor(out=self.score[:], in0=ptr[:].rearrange("p (a b) -> p a b", b=128), in1=self.amb[:], op=ALU.mult),
             reads=["pacc1", "amb"], writes=["score"])
        p.op("dve", lambda: nc.vector.tensor_tensor(out=self.score[:], in0=self.score[:], in1=self.bab[:], op=ALU.add), reads=["score", "bab"], writes=["score"])
        for qs in range(4):
            p.op("dve", lambda qs=qs: nc.vector.max(out=self.m16[:, qs, 0:8], in_=self.score[:, qs, :]), reads=["score"], writes=["m16"])
            p.op("dve", lambda qs=qs: nc.vector.match_replace(out=self.sc2[:], in_to_replace=self.m16[:, qs, 0:8], in_values=self.score[:, qs, :], imm_value=-1e30),
                 reads=["score", "m16"], writes=["sc2"])
            p.op("dve", lambda qs=qs: nc.vector.max(out=self.m16[:, qs, 8:16], in_=self.sc2[:]), reads=["sc2"], writes=["m16"])
            p.op("dve", lambda qs=qs: nc.vector.tensor_scalar(out=self.score[:, qs, :], in0=self.score[:, qs, :], scalar1=self.m16[:, qs, 15:16], scalar2=None, op0=ALU.is_ge),
                 reads=["score", "m16"], writes=["score"])
        p.op("dve", lambda: nc.vector.tensor_scalar(out=self.Msel[:], in0=self.score[:], scalar1=-1.0, scalar2=-NEGM, op0=ALU.add, op1=ALU.mult), reads=["score"], writes=["Msel"])
        ms = kv % 2
        for qs in range(4):
            p.op("pe", lambda qs=qs: nc.tensor.transpose(out=self.pmb[:, qs * 128:(qs + 1) * 128], in_=self.Msel[:, qs, :], identity=self.identb[:]),
                 reads=["Msel", "identb"], writes=["pmb"])
        p.op("act", lambda: nc.scalar.copy(out=self.MTb[ms][:, m * 512:(m + 1) * 512], in_=self.pmb[:, 0:512]), reads=["pmb"], writes=[f"MTb{ms}"])

    def mixer_C(self):
        nc, p, dr = self.nc, self.p, self.dr
        self.qc_i = 0
        p.dma("sp", lambda: nc.sync.dma_start(out=self.Eb[:], in_=dr["EC"]), writes=["Eb"])
        for nm in ("cmA", "cmB", "ovl", "gTs"):
            p.dma("sp", lambda nm=nm: nc.sync.dma_start(out=getattr(self, nm)[:], in_=dr[nm]), writes=[nm])
        p.dma("sp", lambda: nc.sync.dma_start(out=self.w2f[:], in_=dr["w2"]), writes=["w2f"])
        p.dma("sp", lambda: nc.sync.dma_start(out=self.peTf[:], in_=dr["peT"]), writes=["peTf"])
        p.op("dve", lambda: nc.vector.tensor_copy(out=self.w2b[:], in_=self.w2f[:]), reads=["w2f"], writes=["w2b"])
        p.op("dve", lambda: nc.vector.tensor_copy(out=self.peTb[:], in_=self.peTf[:]), reads=["peTf"], writes=["peTb"])
        p.op("pool", lambda: nc.gpsimd.memset(self.onesf[:], 1.0), writes=["onesf"])
        p.op("pool", lambda: nc.gpsimd.memset(self.hid[:], 0.0), writes=["hid"])
        for kv in self.kvs_C:
            self.compress(kv, 0)
            self.compress(kv, 1)
            for m in range(4):
                self.cmp_stage(kv, m)
            ms = kv % 2
            for gq in range(4):
                hc = kv * 4 + gq
                s = self.load_head(20 + hc, 20 + kv, 20 + hc)
                p.dma("sp", lambda hc=hc: nc.sync.dma_start(out=self.sel3[:], in_=dr["selg"][:, 3 * hc:3 * hc + 3, :]), writes=["sel3"])
                for m in range(4):
                    kts = list(range(0, 16 * m + 16))
                    acc = self.acc_i % 2
                    self.acc_i += 1

                    def post(gq=gq, m=m, acc=acc):
                        pg = self.pm[0]
                        p.op("dve", lambda: nc.vector.reciprocal(out=self.rden[:], in_=self.pacc[acc][64:128, :]), reads=[f"pacc{acc}"], writes=["rden"])
                        p.op("pe", lambda: nc.tensor.matmul(pg[0:64, :], lhsT=self.sel3[:, 1, :], rhs=self.gTs[:, m * 512:(m + 1) * 512], start=True, stop=True),
                             reads=["sel3", "gTs"], writes=["pm0"])
                        p.op("dve", lambda: nc.vector.tensor_tensor(out=self.tmpf[:], in0=self.pacc[acc][0:64, :], in1=self.rden[:], op=ALU.mult),
                             reads=[f"pacc{acc}", "rden"], writes=["tmpf"])
                        p.op("dve", lambda: nc.vector.tensor_tensor(out=self.nd[0:64, 8 + m, :], in0=self.tmpf[:], in1=pg[0:64, :], op=ALU.mult),
                             reads=["tmpf", "pm0"], writes=[f"res{m}"])
                        half, idx = gq // 2, (gq % 2) * 4 + m
                        if half == 0:
                            p.op("pool", lambda: nc.gpsimd.tensor_tensor(out=self.nd[0:64, 8 + m, :], in0=self.nd[0:64, 8 + m, :], in1=self.nd[0:64, idx, :], op=ALU.add),
                                 reads=[f"res{m}", f"oc{gq}_{m}"], writes=[f"res{m}"])
                        else:
                            p.op("dve", lambda: nc.vector.tensor_copy(out=self.tmpf[:], in_=self.nd[64:128, idx, :]), reads=[f"oc{gq}_{m}"], writes=["tmpf"])
                            p.op("pool", lambda: nc.gpsimd.tensor_tensor(out=self.nd[0:64, 8 + m, :], in0=self.nd[0:64, 8 + m, :], in1=self.tmpf[:], op=ALU.add),
                                 reads=[f"res{m}", "tmpf"], writes=[f"res{m}"])
                    self.add_units(s, m, kts, 16 * m - 12, 20 + hc, acc, mask=(128, ms), post=post)
                self.flush_units()
                s = self.load_head(20 + hc, 23 + kv, 32 + hc)
                for m in range(4):
                    kts = list(range(max(0, 16 * m - 4), 16 * m + 16))
                    acc = self.acc_i % 2
                    self.acc_i += 1

                    def post(hc=hc, m=m, acc=acc):
                        pg = self.pm[1]
                        p.op("dve", lambda: nc.vector.reciprocal(out=self.rden[:], in_=self.pacc[acc][64:128, :]), reads=[f"pacc{acc}"], writes=["rden"])
                        p.op("pe", lambda: nc.tensor.matmul(pg[0:64, :], lhsT=self.sel3[:, 2, :], rhs=self.gTs[:, m * 512:(m + 1) * 512], start=True, stop=True),
                             reads=["sel3", "gTs"], writes=["pS3"])
                        p.op("dve", lambda: nc.vector.tensor_tensor(out=self.tmpf[:], in0=self.pacc[acc][0:64, :], in1=self.rden[:], op=ALU.mult),
                             reads=[f"pacc{acc}", "rden"], writes=["tmpf"])
                        p.op("dve", lambda: nc.vector.tensor_tensor(out=self.tmpf[:], in0=self.tmpf[:], in1=pg[0:64, :], op=ALU.mult),
                             reads=["tmpf", "pS3"], writes=["tmpf"])
                        o = self.ost_i % 3
                        self.ost_i += 1
                        p.op("dve", lambda: nc.vector.tensor_tensor(out=self.ost[o][:], in0=self.tmpf[:], in1=self.nd[0:64, 8 + m, :], op=ALU.add),
                             reads=["tmpf", f"res{m}"], writes=[f"ost{o}"])
                        dst = self.dr["OT"][(20 + hc) * 64:(21 + hc) * 64, m * 512:(m + 1) * 512]
                        self.out_ops.append(p.dma("pool", lambda: nc.gpsimd.dma_start(out=dst, in_=self.ost[o][:]), reads=[f"ost{o}"]))
                    self.add_units(s, m, kts, 0, 32 + hc, acc, post=post)
                self.flush_units()


def dram_p2(nc):
    dr = {}
    I = lambda name, shape, dt: nc.dram_tensor(name, shape, dt, kind="ExternalInput").ap()
    dr["qT"] = I("qT", [32, 64, NT], BF16)
    dr["kTf"] = I("kTf", [26, 64, S], BF16)
    dr["vf"] = I("vf", [26, 128, 64, 64], BF16)
    dr["G"] = I("G", [44, GL], F32)
    dr["cb"] = I("cb", [128, 44], F32)
    dr["negm"] = I("negm", [128, 16, 32], F32)
    dr["ownm"] = I("ownm", [128, 16, 32], F32)
    dr["EB"] = I("EB", [32, S], BF16)
    dr["cmpTf"] = I("cmpTf", [6, 64, S], BF16)
    dr["w1"] = I("w1", [2, 64, 32, 128], F32)
    dr["w2"] = I("w2", [128, 2, 64], F32)
    dr["peT"] = I("peT", [64, 2, 32, 2], F32)
    dr["EC"] = I("EC", [128, S], BF16)
    dr["AM"] = I("AM", [128, 16, 128], F32)
    dr["BA"] = I("BA", [128, 16, 128], F32)
    dr["cmA"] = I("cmA", [128, 512], F32)
    dr["cmB"] = I("cmB", [128, 512], F32)
    dr["ovl"] = I("ovl", [128, 4, 128], F32)
    dr["selg"] = I("selg", [36, 36, 64], F32)
    dr["gTs"] = I("gTs", [36, NT], F32)
    dr["OT"] = nc.dram_tensor("OT", [2048, NT], BF16, kind="ExternalOutput").ap()
    return dr


def build_p2(**kw):
    nc = bass.Bass("TRN2", target_bir_lowering=False)
    dr = dram_p2(nc)
    with contextlib.ExitStack() as st:
        T = lambda name, shape, dt: st.enter_context(nc.sbuf_tensor("s_" + name, shape, dt))
        PS = lambda name, shape, dt: st.enter_context(nc.psum_tensor("p_" + name, shape, dt))
        p = Prog(nc)
        P = P2(nc, p, T, PS, dr, **kw)
        P.setup()
        P.mixer_A()
        P.mixer_B()
        P.mixer_C()
        p.emit(final_wait_ops=P.out_ops)
    return nc

import contextlib
import numpy as np

D = 2048
NT = 2048
DFF = 5632
EPS = 1e-6
TW = 514


def build_p3a():
    nc = bass.Bass("TRN2", target_bir_lowering=False)
    I = lambda name, shape, dt: nc.dram_tensor(name, shape, dt, kind="ExternalInput").ap()
    xT = I("xT", [D, 4, TW], F32)
    OT = I("OT", [D, 4, TW], BF16)
    w = I("w", [D, D], F32)
    gn = I("gn", [128, 16], F32)
    x1T = nc.dram_tensor("x1T", [D, 4, TW], F32, kind="ExternalOutput").ap()
    hT = nc.dram_tensor("hT", [128, 16, 4, TW], BF16, kind="ExternalOutput").ap()
    with contextlib.ExitStack() as st:
        T = lambda name, shape, dt: st.enter_context(nc.sbuf_tensor("s_" + name, shape, dt))
        PS = lambda name, shape, dt: st.enter_context(nc.psum_tensor("p_" + name, shape, dt))
        p = Prog(nc)
        outs = []
        OTb = T("OTb", [128, 16, 4, TW], BF16)
        gsb = T("gsb", [128, 16], F32)
        ones = T("ones", [128, 128], F32)
        wst = [T(f"wst{i}", [128, 16, 128], F32) for i in range(2)]
        wbf = [T(f"wbf{i}", [128, 16, 128], BF16) for i in range(2)]
        xc = [T(f"xc{i}", [128, 4, TW], F32) for i in range(2)]
        x1c = [T(f"x1c{i}", [128, 4, TW], F32) for i in range(2)]
        sqt = T("sqt", [128, 4, TW], F32)
        accsq = T("accsq", [128, 4, TW], F32)
        rstd = T("rstd", [128, 4, TW], F32)
        hc = [T(f"hc{i}", [128, 4, TW], BF16) for i in range(2)]
        pacc = [PS(f"pacc{i}", [128, 512], F32) for i in range(4)]
        ph = PS("ph", [128, 512], F32)
        pss = [PS(f"pss{i}", [128, 512], F32) for i in range(2)]

        p.dma("sp", lambda: nc.sync.dma_start(out=gsb[:], in_=gn), writes=["gsb"])
        p.op("pool", lambda: nc.gpsimd.memset(ones[:], 1.0), writes=["ones"])
        p.op("pool", lambda: nc.gpsimd.memset(accsq[:], 0.0), writes=["accsq"])
        OTv = OT.rearrange("(k p) m t -> p k m t", p=128)
        for k4 in range(4):
            p.dma("sp", lambda k4=k4: nc.sync.dma_start(out=OTb[:, k4 * 4:(k4 + 1) * 4], in_=OTv[:, k4 * 4:(k4 + 1) * 4]), writes=[f"OTb{k4}"])
        OTres = [f"OTb{k4}" for k4 in range(4)]
        ai = 0
        for c in range(16):
            s = c % 2
            src = w[:, c * 128:(c + 1) * 128].rearrange("(k p) n -> p k n", p=128)
            p.dma("sp", lambda s=s, src=src: nc.sync.dma_start(out=wst[s][:], in_=src), writes=[f"wst{s}"])
            p.op("act", lambda s=s: nc.scalar.copy(out=wbf[s][:, 0:8], in_=wst[s][:, 0:8]), reads=[f"wst{s}"], writes=[f"wbfa{s}"])
            p.op("pool", lambda s=s: nc.gpsimd.tensor_copy(out=wbf[s][:, 8:16], in_=wst[s][:, 8:16]), reads=[f"wst{s}"], writes=[f"wbfb{s}"])
            p.dma("sp", lambda s=s, c=c: nc.sync.dma_start(out=xc[s][:], in_=xT[c * 128:(c + 1) * 128]), writes=[f"xc{s}"])
            for m in range(4):
                a = ai % 4
                ai += 1
                for k in range(16):
                    p.op("pe", lambda a=a, s=s, k=k, m=m: nc.tensor.matmul(pacc[a][:], lhsT=wbf[s][:, k, :], rhs=OTb[:, k, m, 2:TW], start=(k == 0), stop=(k == 15)),
                         reads=[f"wbfa{s}", f"wbfb{s}"] + OTres, writes=[f"pacc{a}"])
                p.op("dve", lambda a=a, s=s, m=m: nc.vector.tensor_tensor(out=x1c[s][:, m, 2:TW], in0=pacc[a][:], in1=xc[s][:, m, 2:TW], op=ALU.add),
                     reads=[f"pacc{a}", f"xc{s}"], writes=[f"x1c{s}"])
            for k in range(16):
                p.op("pe", lambda s=s, k=k: nc.tensor.matmul(ph[:, 0:8], lhsT=wbf[s][:, k, :], rhs=OTb[:, k, :, 0:2], start=(k == 0), stop=(k == 15)),
                     reads=[f"wbfa{s}", f"wbfb{s}"] + OTres, writes=["ph"])
            p.op("dve", lambda s=s: nc.vector.tensor_tensor(out=x1c[s][:, :, 0:2], in0=ph[:, 0:8].rearrange("p (m h) -> p m h", h=2), in1=xc[s][:, :, 0:2], op=ALU.add),
                 reads=["ph", f"xc{s}"], writes=[f"x1c{s}"])
            outs.append(p.dma("pool", lambda s=s, c=c: nc.gpsimd.dma_start(out=x1T[c * 128:(c + 1) * 128], in_=x1c[s][:]), reads=[f"x1c{s}"], writes=[f"x1T{c}"]))
            p.op("act", lambda s=s: nc.scalar.activation(out=sqt[:], in_=x1c[s][:], func=AF.Square), reads=[f"x1c{s}"], writes=["sqt"])
            p.op("pool", lambda: nc.gpsimd.tensor_tensor(out=accsq[:], in0=accsq[:], in1=sqt[:], op=ALU.add), reads=["sqt", "accsq"], writes=["accsq"])
        for m in range(4):
            q = m % 2
            p.op("pe", lambda q=q, m=m: nc.tensor.matmul(pss[q][:], lhsT=ones[:], rhs=accsq[:, m, 2:TW], start=True, stop=True), reads=["ones", "accsq"], writes=[f"pss{q}"])
            p.op("act", lambda q=q, m=m: nc.scalar.activation(out=rstd[:, m, 2:TW], in_=pss[q][:], func=AF.Sqrt, scale=1.0 / D, bias=EPS), reads=[f"pss{q}"], writes=["rstd"])
        p.op("pe", lambda: nc.tensor.matmul(ph[:, 0:8], lhsT=ones[:], rhs=accsq[:, :, 0:2], start=True, stop=True), reads=["ones", "accsq"], writes=["ph"])
        p.op("act", lambda: nc.scalar.activation(out=rstd[:, :, 0:2], in_=ph[:, 0:8].rearrange("p (m h) -> p m h", h=2), func=AF.Sqrt, scale=1.0 / D, bias=EPS), reads=["ph"], writes=["rstd"])
        p.op("dve", lambda: nc.vector.reciprocal(out=rstd[:], in_=rstd[:]), reads=["rstd"], writes=["rstd"])
        for c in range(16):
            s = c % 2
            p.dma("sp", lambda s=s, c=c: nc.sync.dma_start(out=x1c[s][:], in_=x1T[c * 128:(c + 1) * 128]), reads=[f"x1T{c}"], writes=[f"x1c{s}"])
            p.op("dve", lambda s=s, c=c: nc.vector.scalar_tensor_tensor(out=hc[s][:], in0=x1c[s][:], scalar=gsb[:, c:c + 1], in1=rstd[:], op0=ALU.mult, op1=ALU.mult),
                 reads=[f"x1c{s}", "gsb", "rstd"], writes=[f"hc{s}"])
            outs.append(p.dma("pool", lambda s=s, c=c: nc.gpsimd.dma_start(out=hT[:, c], in_=hc[s][:]), reads=[f"hc{s}"]))
        p.emit(final_wait_ops=outs)
    return nc


def emit_p3b(nc, p, T, PS, hT, x1get, wu, wd, cw, cbv, x2T, relayout=False):
    outs = []
    hTh = T("hTh", [128, 16, 2, TW], BF16)
    actT = T("actT", [128, 44, 1024], BF16)
    wst = [T(f"wst{i}", [128, 4096], F32) for i in range(2)]
    wbf = [T(f"wbf{i}", [128, 4096], BF16) for i in range(2)]
    cws = T("cws", [128, 88, 3], F32)
    cbs = T("cbs", [128, 88], F32)
    ua = [T(f"ua{i}", [128, TW], F32) for i in range(2)]
    ug = [T(f"ug{i}", [128, TW], F32) for i in range(2)]
    ya = [T(f"ya{i}", [128, 512], F32) for i in range(2)]
    yg = [T(f"yg{i}", [128, 512], F32) for i in range(2)]
    sg = [T(f"sg{i}", [128, 512], F32) for i in range(2)]
    x1c = [T(f"x1c{i}", [128, 2, 512], F32) for i in range(2)]
    pa = [PS(f"pa{i}", [128, 512], F32) for i in range(2)]
    pg = [PS(f"pg{i}", [128, 512], F32) for i in range(2)]
    ph = PS("ph", [128, 512], F32)
    pd = [PS(f"pd{i}", [128, 512], F32) for i in range(2)]
    p.dma("sp", lambda: nc.sync.dma_start(out=cws[:], in_=cw), writes=["cws"])
    p.dma("sp", lambda: nc.sync.dma_start(out=cbs[:], in_=cbv), writes=["cbs"])
    wi = 0
    ui = 0
    for half in range(2):
        p.dma("sp", lambda half=half: nc.sync.dma_start(out=hTh[:], in_=hT[:, :, 2 * half:2 * half + 2, :]), writes=["hTh"])
        for c in range(44):
            s = wi % 2
            wi += 1
            wv = wst[s][:].rearrange("p (k n) -> p k n", n=256)
            wb = wbf[s][:].rearrange("p (k n) -> p k n", n=256)
            if relayout:
                srcp = wu[c].rearrange("p (k n) -> p k n", n=256)
                p.dma("sp", lambda wv=wv, srcp=srcp: nc.sync.dma_start(out=wv, in_=srcp), writes=[f"wst{s}_0", f"wst{s}_1"])
            else:
                for part in range(2):
                    col0 = part * DFF + c * 128
                    src = wu[:, col0:col0 + 128].rearrange("(k p) n -> p k n", p=128)
                    p.dma("sp", lambda wv=wv, src=src, part=part: nc.sync.dma_start(out=wv[:, :, part * 128:(part + 1) * 128], in_=src), writes=[f"wst{s}_{part}"])
            p.op("act", lambda wv=wv, wb=wb: nc.scalar.copy(out=wb[:, 0:6], in_=wv[:, 0:6]), reads=[f"wst{s}_0", f"wst{s}_1"], writes=[f"wbfa{s}"])
            p.op("pool", lambda wv=wv, wb=wb: nc.gpsimd.tensor_copy(out=wb[:, 6:16], in_=wv[:, 6:16]), reads=[f"wst{s}_0", f"wst{s}_1"], writes=[f"wbfb{s}"])
            wres = [f"wbfa{s}", f"wbfb{s}"]
            for k in range(16):
                p.op("pe", lambda k=k, wb=wb: nc.tensor.matmul(ph[:, 0:4], lhsT=wb[:, k, 0:128], rhs=hTh[:, k, :, 0:2], start=(k == 0), stop=(k == 15)),
                     reads=wres + ["hTh"], writes=["ph"])
            for k in range(16):
                p.op("pe", lambda k=k, wb=wb: nc.tensor.matmul(ph[:, 4:8], lhsT=wb[:, k, 128:256], rhs=hTh[:, k, :, 0:2], start=(k == 0), stop=(k == 15)),
                     reads=wres + ["hTh"], writes=["ph"])
            for tt in range(2):
                u = ui % 2
                ui += 1
                for k in range(16):
                    p.op("pe", lambda u=u, k=k, tt=tt, wb=wb: nc.tensor.matmul(pa[u][:], lhsT=wb[:, k, 0:128], rhs=hTh[:, k, tt, 2:TW], start=(k == 0), stop=(k == 15)),
                         reads=wres + ["hTh"], writes=[f"pa{u}"])
                for k in range(16):
                    p.op("pe", lambda u=u, k=k, tt=tt, wb=wb: nc.tensor.matmul(pg[u][:], lhsT=wb[:, k, 128:256], rhs=hTh[:, k, tt, 2:TW], start=(k == 0), stop=(k == 15)),
                         reads=wres + ["hTh"], writes=[f"pg{u}"])
                p.op("act", lambda u=u: nc.scalar.copy(out=ua[u][:, 2:TW], in_=pa[u][:]), reads=[f"pa{u}"], writes=[f"ua{u}"])
                p.op("act", lambda u=u: nc.scalar.copy(out=ug[u][:, 2:TW], in_=pg[u][:]), reads=[f"pg{u}"], writes=[f"ug{u}"])
                p.op("act", lambda u=u, tt=tt: nc.scalar.copy(out=ua[u][:, 0:2], in_=ph[:, 2 * tt:2 * tt + 2]), reads=["ph"], writes=[f"ua{u}"])
                p.op("act", lambda u=u, tt=tt: nc.scalar.copy(out=ug[u][:, 0:2], in_=ph[:, 4 + 2 * tt:6 + 2 * tt]), reads=["ph"], writes=[f"ug{u}"])
                for (ub, yb, nm, ch) in ((ua, ya, "a", c), (ug, yg, "g", 44 + c)):
                    p.op("dve", lambda u=u, ub=ub, yb=yb, ch=ch: nc.vector.tensor_scalar(out=yb[u][:], in0=ub[u][:, 2:TW], scalar1=cws[:, ch, 0:1], scalar2=cbs[:, ch:ch + 1], op0=ALU.mult, op1=ALU.add),
                         reads=[f"u{nm}{u}", "cws", "cbs"], writes=[f"y{nm}{u}"])
                    p.op("dve", lambda u=u, ub=ub, yb=yb, ch=ch: nc.vector.scalar_tensor_tensor(out=yb[u][:], in0=ub[u][:, 1:TW - 1], scalar=cws[:, ch, 1:2], in1=yb[u][:], op0=ALU.mult, op1=ALU.add),
                         reads=[f"u{nm}{u}", "cws", f"y{nm}{u}"], writes=[f"y{nm}{u}"])
                    p.op("dve", lambda u=u, ub=ub, yb=yb, ch=ch: nc.vector.scalar_tensor_tensor(out=yb[u][:], in0=ub[u][:, 0:TW - 2], scalar=cws[:, ch, 2:3], in1=yb[u][:], op0=ALU.mult, op1=ALU.add),
                         reads=[f"u{nm}{u}", "cws", f"y{nm}{u}"], writes=[f"y{nm}{u}"])
                p.op("act", lambda u=u: nc.scalar.activation(out=sg[u][:], in_=yg[u][:], func=AF.Silu), reads=[f"yg{u}"], writes=[f"sg{u}"])
                p.op("pool", lambda u=u, c=c, tt=tt: nc.gpsimd.tensor_tensor(out=actT[:, c, tt * 512:(tt + 1) * 512], in0=sg[u][:], in1=ya[u][:], op=ALU.mult),
                     reads=[f"sg{u}", f"ya{u}"], writes=[f"actT{c}"])
        actres = [f"actT{c}" for c in range(44)]
        for cc in range(16):
            for piece in range(2):
                s = wi % 2
                wi += 1
                wv = wst[s][:, 0:22 * 128].rearrange("p (k n) -> p k n", n=128)
                wb = wbf[s][:, 0:22 * 128].rearrange("p (k n) -> p k n", n=128)
                if relayout:
                    src = wd[cc, piece].rearrange("p (k n) -> p k n", n=128)
                else:
                    src = wd[piece * 2816:(piece + 1) * 2816, cc * 128:(cc + 1) * 128].rearrange("(k p) n -> p k n", p=128)
                p.dma("sp", lambda wv=wv, src=src: nc.sync.dma_start(out=wv, in_=src), writes=[f"wst{s}_0", f"wst{s}_1"])
                p.op("act", lambda wv=wv, wb=wb: nc.scalar.copy(out=wb[:, 0:9], in_=wv[:, 0:9]), reads=[f"wst{s}_0", f"wst{s}_1"], writes=[f"wbfa{s}"])
                p.op("pool", lambda wv=wv, wb=wb: nc.gpsimd.tensor_copy(out=wb[:, 9:22], in_=wv[:, 9:22]), reads=[f"wst{s}_0", f"wst{s}_1"], writes=[f"wbfb{s}"])
                for tt in range(2):
                    for k in range(22):
                        c = piece * 22 + k
                        p.op("pe", lambda wb=wb, k=k, c=c, tt=tt: nc.tensor.matmul(pd[tt][:], lhsT=wb[:, k, :], rhs=actT[:, c, tt * 512:(tt + 1) * 512], start=(c == 0), stop=(c == 43)),
                             reads=[f"wbfa{s}", f"wbfb{s}", f"actT{c}"], writes=[f"pd{tt}"])
            xs = cc % 2
            p.dma("sp", lambda xs=xs, cc=cc, half=half: nc.sync.dma_start(out=x1c[xs][:], in_=x1get(cc, half)), writes=[f"x1c{xs}"])
            for tt in range(2):
                p.op("dve", lambda xs=xs, tt=tt: nc.vector.tensor_tensor(out=x1c[xs][:, tt, :], in0=pd[tt][:], in1=x1c[xs][:, tt, :], op=ALU.add),
                     reads=[f"pd{tt}", f"x1c{xs}"], writes=[f"x1c{xs}"])
            dst = x2T[cc * 128:(cc + 1) * 128, half * 1024:(half + 1) * 1024].rearrange("p (t n) -> p t n", n=512)
            outs.append(p.dma("pool", lambda xs=xs, dst=dst: nc.gpsimd.dma_start(out=dst, in_=x1c[xs][:]), reads=[f"x1c{xs}"]))
    return outs


def build_p3b():
    nc = bass.Bass("TRN2", target_bir_lowering=False)
    I = lambda name, shape, dt: nc.dram_tensor(name, shape, dt, kind="ExternalInput").ap()
    hT = I("hT", [128, 16, 4, TW], BF16)
    x1T = I("x1T", [D, 4, TW], F32)
    wu = I("wu", [D, 2 * DFF], F32)
    wd = I("wd", [DFF, D], F32)
    cw = I("cw", [128, 88, 3], F32)
    cbv = I("cbv", [128, 88], F32)
    x2T = nc.dram_tensor("x2T", [D, NT], F32, kind="ExternalOutput").ap()
    with contextlib.ExitStack() as st:
        T = lambda name, shape, dt: st.enter_context(nc.sbuf_tensor("s_" + name, shape, dt))
        PS = lambda name, shape, dt: st.enter_context(nc.psum_tensor("p_" + name, shape, dt))
        p = Prog(nc)
        x1get = lambda cc, half: x1T[cc * 128:(cc + 1) * 128, 2 * half:2 * half + 2, 2:TW]
        outs = emit_p3b(nc, p, T, PS, hT, x1get, wu, wd, cw, cbv, x2T)
        p.emit(final_wait_ops=outs)
    return nc


def emit_kf(nc, p, T, PS, xT, gn, yT):
    outs = []
    gsb = T("gsb", [128, 16], F32)
    ones = T("ones", [128, 128], F32)
    xs = [T(f"xs{i}", [128, 16, 512], F32) for i in range(2)]
    ys = [T(f"ys{i}", [128, 16, 512], F32) for i in range(2)]
    sq = [T(f"sq{i}", [128, 512], F32) for i in range(2)]
    rstd = T("rstd", [128, 512], F32)
    pss = PS("pss", [128, 512], F32)
    p.dma("sp", lambda: nc.sync.dma_start(out=gsb[:], in_=gn), writes=["gsb"])
    p.op("pool", lambda: nc.gpsimd.memset(ones[:], 1.0), writes=["ones"])
    xv = xT.rearrange("(k p) t -> p k t", p=128)
    yv = yT.rearrange("(k p) t -> p k t", p=128)
    for m in range(4):
        s = m % 2
        p.dma("sp", lambda m=m, s=s: nc.sync.dma_start(out=xs[s][:], in_=xv[:, :, m * 512:(m + 1) * 512]), writes=[f"xs{s}"])
        for k in range(16):
            q = k % 2
            p.op("act", lambda s=s, k=k, q=q: nc.scalar.activation(out=sq[q][:], in_=xs[s][:, k, :], func=AF.Square), reads=[f"xs{s}"], writes=[f"sq{q}"])
            p.op("pe", lambda q=q, k=k: nc.tensor.matmul(pss[:], lhsT=ones[:], rhs=sq[q][:], start=(k == 0), stop=(k == 15)), reads=["ones", f"sq{q}"], writes=["pss"])
        p.op("act", lambda: nc.scalar.activation(out=rstd[:], in_=pss[:], func=AF.Sqrt, scale=1.0 / D, bias=EPS), reads=["pss"], writes=["rstd"])
        p.op("dve", lambda: nc.vector.reciprocal(out=rstd[:], in_=rstd[:]), reads=["rstd"], writes=["rstd"])
        for k in range(16):
            p.op("dve", lambda s=s, k=k: nc.vector.scalar_tensor_tensor(out=ys[s][:, k, :], in0=xs[s][:, k, :], scalar=gsb[:, k:k + 1], in1=rstd[:], op0=ALU.mult, op1=ALU.mult),
                 reads=[f"xs{s}", "gsb", "rstd"], writes=[f"ys{s}"])
        outs.append(p.dma("pool", lambda m=m, s=s: nc.gpsimd.dma_start(out=yv[:, :, m * 512:(m + 1) * 512], in_=ys[s][:]), reads=[f"ys{s}"]))
    return outs


def build_kf():
    nc = bass.Bass("TRN2", target_bir_lowering=False)
    xT = nc.dram_tensor("xT", [D, NT], F32, kind="ExternalInput").ap()
    gn = nc.dram_tensor("gn", [128, 16], F32, kind="ExternalInput").ap()
    yT = nc.dram_tensor("yT", [D, NT], F32, kind="ExternalOutput").ap()
    with contextlib.ExitStack() as st:
        T = lambda name, shape, dt: st.enter_context(nc.sbuf_tensor("s_" + name, shape, dt))
        PS = lambda name, shape, dt: st.enter_context(nc.psum_tensor("p_" + name, shape, dt))
        p = Prog(nc)
        outs = emit_kf(nc, p, T, PS, xT, gn, yT)
        p.emit(final_wait_ops=outs)
    return nc

import contextlib, math
import numpy as np

DEPTH = 4
RG = [[0, 1, 2, 3], [4, 5, 6, 7]]


def ec_const_nat():
    c = np.arange(S)
    return (c[None, :] // 64 == np.arange(128)[:, None]).astype(np.float32)


def halo_coef(j):
    co = np.zeros((128, 5), np.float32)
    if j >= 1:
        co[:, j - 1] = 1.0
    else:
        co[:, 4] = 1.0
    return co


def emit_p3a_f(nc, p, T, PS, xT, OT, w, gn, x1T, hT, ht_s):
    OTb = T("OTb", [128, 16, NT], BF16)
    gsb = T("gsb", [128, 16], F32)
    ones = T("ones", [128, 128], F32)
    wst = [T(f"wst{i}", [128, 16, 128], F32) for i in range(2)]
    wbf = [T(f"wbf{i}", [128, 16, 128], BF16) for i in range(2)]
    xc = [T(f"xc{i}", [128, NT], F32) for i in range(2)]
    x1c = [T(f"x1c{i}", [128, NT], F32) for i in range(2)]
    sqt = T("sqt", [128, NT], F32)
    accsq = T("accsq", [128, NT], F32)
    rstd = T("rstd", [128, NT], F32)
    hc = [T(f"hc{i}", [128, NT], BF16) for i in range(2)]
    pacc = [PS(f"pacc{i}", [128, 512], F32) for i in range(4)]
    pss = [PS(f"pss{i}", [128, 512], F32) for i in range(2)]
    p.dma("sp", lambda: nc.sync.dma_start(out=gsb[:], in_=gn), writes=["gsb"])
    p.op("pool", lambda: nc.gpsimd.memset(ones[:], 1.0), writes=["ones"])
    p.op("pool", lambda: nc.gpsimd.memset(accsq[:], 0.0), writes=["accsq"])
    OTv = OT.rearrange("(k p) t -> p k t", p=128)
    for k4 in range(4):
        p.dma("sp", lambda k4=k4: nc.sync.dma_start(out=OTb[:, k4 * 4:(k4 + 1) * 4], in_=OTv[:, k4 * 4:(k4 + 1) * 4]), writes=[f"OTb{k4}"])
    OTres = [f"OTb{k4}" for k4 in range(4)]
    ai = 0
    for c in range(16):
        s = c % 2
        src = w[c].rearrange("p (k n) -> p k n", n=128)
        p.dma("sp", lambda s=s, src=src: nc.sync.dma_start(out=wst[s][:], in_=src), writes=[f"wst{s}"])
        p.op("act", lambda s=s: nc.scalar.copy(out=wbf[s][:, 0:8], in_=wst[s][:, 0:8]), reads=[f"wst{s}"], writes=[f"wbfa{s}"])
        p.op("pool", lambda s=s: nc.gpsimd.tensor_copy(out=wbf[s][:, 8:16], in_=wst[s][:, 8:16]), reads=[f"wst{s}"], writes=[f"wbfb{s}"])
        p.dma("sp", lambda s=s, c=c: nc.sync.dma_start(out=xc[s][:], in_=xT[c * 128:(c + 1) * 128]), writes=[f"xc{s}"])
        for m in range(4):
            a = ai % 4
            ai += 1
            for k in range(16):
                p.op("pe", lambda a=a, s=s, k=k, m=m: nc.tensor.matmul(pacc[a][:], lhsT=wbf[s][:, k, :], rhs=OTb[:, k, m * 512:(m + 1) * 512], start=(k == 0), stop=(k == 15)),
                     reads=[f"wbfa{s}", f"wbfb{s}"] + OTres, writes=[f"pacc{a}"])
            p.op("dve", lambda a=a, s=s, m=m: nc.vector.tensor_tensor(out=x1c[s][:, m * 512:(m + 1) * 512], in0=pacc[a][:], in1=xc[s][:, m * 512:(m + 1) * 512], op=ALU.add),
                 reads=[f"pacc{a}", f"xc{s}"], writes=[f"x1c{s}"])
        p.dma("pool", lambda s=s, c=c: nc.gpsimd.dma_start(out=x1T[c * 128:(c + 1) * 128], in_=x1c[s][:]), reads=[f"x1c{s}"], writes=[f"x1T{c}"])
        p.op("act", lambda s=s: nc.scalar.activation(out=sqt[:], in_=x1c[s][:], func=AF.Square), reads=[f"x1c{s}"], writes=["sqt"])
        p.op("pool", lambda: nc.gpsimd.tensor_tensor(out=accsq[:], in0=accsq[:], in1=sqt[:], op=ALU.add), reads=["sqt", "accsq"], writes=["accsq"])
    for m in range(4):
        q = m % 2
        p.op("pe", lambda q=q, m=m: nc.tensor.matmul(pss[q][:], lhsT=ones[:], rhs=accsq[:, m * 512:(m + 1) * 512], start=True, stop=True), reads=["ones", "accsq"], writes=[f"pss{q}"])
        p.op("act", lambda q=q, m=m: nc.scalar.activation(out=rstd[:, m * 512:(m + 1) * 512], in_=pss[q][:], func=AF.Sqrt, scale=1.0 / D, bias=EPS), reads=[f"pss{q}"], writes=["rstd"])
    p.op("dve", lambda: nc.vector.reciprocal(out=rstd[:], in_=rstd[:]), reads=["rstd"], writes=["rstd"])
    for c in range(16):
        s = c % 2
        p.dma("sp", lambda s=s, c=c: nc.sync.dma_start(out=x1c[s][:], in_=x1T[c * 128:(c + 1) * 128]), reads=[f"x1T{c}"], writes=[f"x1c{s}"])
        p.op("dve", lambda s=s, c=c: nc.vector.scalar_tensor_tensor(out=hc[s][:], in0=x1c[s][:], scalar=gsb[:, c:c + 1], in1=rstd[:], op0=ALU.mult, op1=ALU.mult),
             reads=[f"x1c{s}", "gsb", "rstd"], writes=[f"hc{s}"])
        p.dma("pool", lambda s=s, c=c: nc.gpsimd.dma_start(out=hT[:, c, :, 2:TW], in_=hc[s][:].rearrange("p (m t) -> p m t", t=512)), reads=[f"hc{s}"])
        p.dma("pool", lambda s=s, c=c: nc.gpsimd.dma_start(out=ht_s[c * 128:(c + 1) * 128, :].rearrange("p (m h) -> p m h", h=2),
                                                            in_=hc[s][:].rearrange("p (m t) -> p m t", t=512)[:, :, 510:512]), reads=[f"hc{s}"])


def emit_halo(nc, p, T, PS, ht_g, hco_d, hT):
    Hg = T("Hg", [128, 4, 16, 8], BF16)
    hco = T("hco", [128, 5], F32)
    acc = T("hacc", [128, 4, 16, 2], F32)
    hb = T("hb", [128, 4, 16, 2], BF16)
    p.dma("sp", lambda: nc.sync.dma_start(out=Hg[:], in_=ht_g.rearrange("(r k p) c -> p r k c", r=4, p=128)), writes=["Hg"])
    p.dma("sp", lambda: nc.sync.dma_start(out=hco[:], in_=hco_d), writes=["hco"])
    for m in range(4):
        p.op("dve", lambda m=m: nc.vector.tensor_scalar(out=acc[:, m], in0=Hg[:, 0, :, 2 * m:2 * m + 2], scalar1=hco[:, 0:1], scalar2=None, op0=ALU.mult),
             reads=["Hg", "hco"], writes=[f"hacc{m}"])
        for r in range(1, 4):
            p.op("dve", lambda m=m, r=r: nc.vector.scalar_tensor_tensor(out=acc[:, m], in0=Hg[:, r, :, 2 * m:2 * m + 2], scalar=hco[:, r:r + 1], in1=acc[:, m], op0=ALU.mult, op1=ALU.add),
                 reads=["Hg", "hco", f"hacc{m}"], writes=[f"hacc{m}"])
        if m >= 1:
            p.op("dve", lambda m=m: nc.vector.scalar_tensor_tensor(out=acc[:, m], in0=Hg[:, 3, :, 2 * m - 2:2 * m], scalar=hco[:, 4:5], in1=acc[:, m], op0=ALU.mult, op1=ALU.add),
                 reads=["Hg", "hco", f"hacc{m}"], writes=[f"hacc{m}"])
        p.op("dve", lambda m=m: nc.vector.tensor_copy(out=hb[:, m], in_=acc[:, m]), reads=[f"hacc{m}"], writes=[f"hb{m}"])
        p.dma("sp", lambda m=m: nc.sync.dma_start(out=hT[:, :, m, 0:2], in_=hb[:, m]), reads=[f"hb{m}"])


def build_fused(depth=DEPTH, debug=False, stop=10**9):
    nc = bass.Bass("TRN2", target_bir_lowering=False)
    I = lambda name, shape, dt: nc.dram_tensor(name, shape, dt, kind="ExternalInput").ap()
    N = lambda name, shape, dt, **kw: nc.dram_tensor(name, shape, dt, kind="Internal", **kw).ap()
    xT0 = I("xT0", [D, NT], F32)
    NTC = len(t_chunks()) + 1
    NNC = len(n_chunks())
    wiT = I("wiT", [depth, NTC, 128, 2048], F32)
    wiN = I("wiN", [depth, NNC, 128, 4096], F32)
    w_out = I("woR", [depth, 16, 128, 2048], F32)
    w_up = I("wuR", [depth, 44, 128, 4096], F32)
    w_down = I("wdR", [depth, 16, 2, 128, 2816], F32)
    gn_attn = I("gn_attn", [depth, 128, 16], F32)
    gn_mlp = I("gn_mlp", [depth, 128, 16], F32)
    gn_fin = I("gn_fin", [128, 16], F32)
    w1d = I("w1", [depth, 2, 64, 32, 128], F32)
    w2d = I("w2", [depth, 128, 2, 64], F32)
    peTd = I("peT", [depth, 64, 2, 32, 2], F32)
    cwd = I("cw", [depth, 128, 88, 3], F32)
    cbd = I("cbv", [depth, 128, 88], F32)
    hco_d = I("hco", [128, 5], F32)
    consts = {}
    for nm, shape, dt in (("G", [44, GL], F32), ("cb", [128, 44], F32), ("negm", [128, 16, 32], F32), ("ownm", [128, 16, 32], F32),
                          ("EB", [32, S], BF16), ("EC", [128, S], BF16), ("AM", [128, 16, 128], F32), ("BA", [128, 16, 128], F32),
                          ("cmA", [128, 512], F32), ("cmB", [128, 512], F32), ("ovl", [128, 4, 128], F32), ("selg", [36, 36, 64], F32)):
        consts[nm] = I(nm, shape, dt)
    yT = nc.dram_tensor("yT", [D, NT], F32, kind="ExternalOutput").ap()
    qT_s = N("qT_s", [32 * 64, NT], BF16)
    kT_s = N("kT_s", [26 * 64, NT], BF16, addr_space="Local")
    cmpT_s = N("cmpT_s", [6 * 64, NT], BF16, addr_space="Local")
    v_s = N("v_s", [26 * 128, 1024], BF16, addr_space="Local")
    def chunked(name, total, step, cols):
        out = []
        r0 = 0
        while r0 < total:
            nr = min(step, total - r0)
            out.append((r0, nr, N(f"{name}_{r0}", [4 * nr, cols], BF16, addr_space="Local")))
            r0 += nr
        return out
    kT_g = chunked("kT_g", 26 * 64, 256, NT)
    cmpT_g = chunked("cmpT_g", 6 * 64, 256, NT)
    v_g = chunked("v_g", 26 * 128, 512, 1024)
    gT_s = N("gT_s", [36, NT], F32)
    N2 = (lambda name, shape, dt: nc.dram_tensor(name, shape, dt, kind="ExternalOutput").ap()) if debug else N
    OT_s = N2("OT_s", [D, NT], BF16)
    x1T_s = N("x1T_s", [D, NT], F32)
    hT_s = N2("hT_s", [128, 16, 4, TW], BF16)
    ht_s = N("ht_s", [D, 8], BF16, addr_space="Local")
    ht_g = N("ht_g", [4 * D, 8], BF16, addr_space="Local")
    x2T_s = N2("x2T_s", [D, NT], F32)

    phase = [0]
    with contextlib.ExitStack() as top:
        ctx = Ctx(nc, top)

        def run_phase(fn, final=False):
            ph = phase[0]
            phase[0] += 1
            if ph >= stop and not final:
                return
            with contextlib.ExitStack() as st:
                T = lambda name, shape, dt: st.enter_context(nc.sbuf_tensor(f"s{ph}_" + name, shape, dt))
                PS = lambda name, shape, dt: st.enter_context(nc.psum_tensor(f"p{ph}_" + name, shape, dt))
                p = PProg(ctx)
                fin = fn(p, T, PS)
                p.emit(final_wait_ops=fin if final else ())

        xcur = xT0
        for l in range(depth):
            def ph_p1(p, T, PS, l=l, xcur=xcur):
                outs = {"qT": qT_s.rearrange("(h e) t -> h e t", e=64), "kT": kT_s.rearrange("(h e) t -> h e t", e=64),
                        "cmpT": cmpT_s.rearrange("(h e) t -> h e t", e=64)}
                v4 = v_s.rearrange("(h p) (ts e) -> h p ts e", p=128, e=64)

                def vdst(ts, vc0, ncols):
                    h0, nh = vc0 // 64, ncols // 64
                    return v4[h0:h0 + nh, :, ts, :].rearrange("h p e -> p h e")
                def wsrc(kind, idx):
                    if kind == "T":
                        return wiT[l, idx].rearrange("p (k n) -> p k n", n=128)
                    return wiN[l, idx].rearrange("p (k n) -> p k n", n=256)
                emit_p1(nc, p, T, PS, xcur, gn_attn[l], None, outs, None, gT_s, vdst=vdst, wsrc=wsrc)
                return ()
            run_phase(ph_p1)

            def ph_cc1(p, T, PS):
                for (a, chs) in ((kT_s, kT_g), (cmpT_s, cmpT_g), (v_s, v_g)):
                    for (r0, nr, b) in chs:
                        p.cc(lambda a=a, b=b, r0=r0, nr=nr: nc.gpsimd.collective_compute("AllGather", ALU.bypass, replica_groups=RG, ins=[a[r0:r0 + nr]], outs=[b]))
                return ()
            run_phase(ph_cc1)

            def ph_p2(p, T, PS, l=l):
                dr = dict(consts)
                dr.update({"qT": qT_s.rearrange("(h e) t -> h e t", e=64), "kT_g": kT_g, "cmpT_g": cmpT_g, "v_g": v_g, "gTs": gT_s,
                           "w1": w1d[l], "w2": w2d[l], "peT": peTd[l], "OT": OT_s})
                P = P2(nc, p, T, PS, dr, fused=True)
                P.setup()
                P.mixer_A()
                P.mixer_B()
                P.mixer_C()
                return ()
            run_phase(ph_p2)

            def ph_p3a(p, T, PS, l=l, xcur=xcur):
                emit_p3a_f(nc, p, T, PS, xcur, OT_s, w_out[l], gn_mlp[l], x1T_s, hT_s, ht_s)
                return ()
            run_phase(ph_p3a)

            def ph_cc2(p, T, PS):
                p.cc(lambda: nc.gpsimd.collective_compute("AllGather", ALU.bypass, replica_groups=RG, ins=[ht_s], outs=[ht_g]))
                return ()
            run_phase(ph_cc2)

            def ph_halo(p, T, PS):
                emit_halo(nc, p, T, PS, ht_g, hco_d, hT_s)
                return ()
            run_phase(ph_halo)

            def ph_p3b(p, T, PS, l=l):
                x1get = lambda cc, half: x1T_s[cc * 128:(cc + 1) * 128, half * 1024:(half + 1) * 1024].rearrange("p (t n) -> p t n", n=512)
                emit_p3b(nc, p, T, PS, hT_s, x1get, w_up[l], w_down[l], cwd[l], cbd[l], x2T_s, relayout=True)
                return ()
            run_phase(ph_p3b)
            xcur = x2T_s

        def ph_kf(p, T, PS):
            return emit_kf(nc, p, T, PS, x2T_s, gn_fin, yT)
        run_phase(ph_kf, final=True)
    return nc


def _relayout_in_T(w_in):
    L = w_in.shape[0]
    tch = t_chunks() + [(CG, 36, None, 1.0)]
    out = np.zeros((L, len(tch), 128, 16, 128), np.float32)
    for ci, ch in enumerate(tch):
        c0, nc_ = ch[0], ch[1]
        out[:, ci, :, :, 0:nc_] = w_in[:, :, c0:c0 + nc_].reshape(L, 16, 128, nc_).transpose(0, 2, 1, 3)
    return out.reshape(L, len(tch), 128, 2048)


def _relayout_in_N(w_in):
    L = w_in.shape[0]
    nch = n_chunks()
    out = np.zeros((L, len(nch), 128, 16, 256), np.float32)
    for ni, (c0, nc_, vc0) in enumerate(nch):
        out[:, ni, :, :, 0:nc_] = w_in[:, :, c0:c0 + nc_].reshape(L, 16, 128, nc_).transpose(0, 2, 1, 3)
    return out.reshape(L, len(nch), 128, 4096)


def fused_host_inputs(x, rel_table, w_in, w_out, cmp_w1, cmp_w2, cmp_pe, norm_attn, norm_mlp, w_up, conv_w, conv_b, w_down, norm_final):
    import ml_dtypes
    bf = ml_dtypes.bfloat16
    f32 = np.float32
    A = lambda a: np.ascontiguousarray(np.asarray(a, f32))
    x = np.asarray(x, f32)
    rel_table = np.asarray(rel_table, f32)
    L = np.asarray(w_in).shape[0]
    shared = {
        "wiT": _relayout_in_T(np.asarray(w_in, f32)), "wiN": _relayout_in_N(np.asarray(w_in, f32)),
        "woR": A(np.asarray(w_out, f32).reshape(L, 16, 128, 16, 128).transpose(0, 3, 2, 1, 4).reshape(L, 16, 128, 2048)),
        "wuR": A(np.asarray(w_up, f32).reshape(L, 16, 128, 2, 44, 128).transpose(0, 4, 2, 1, 3, 5).reshape(L, 44, 128, 4096)),
        "wdR": A(np.asarray(w_down, f32).reshape(L, 2, 22, 128, 16, 128).transpose(0, 4, 1, 3, 2, 5).reshape(L, 16, 2, 128, 2816)),
        "gn_attn": A(np.asarray(norm_attn, f32).reshape(L, 16, 128).transpose(0, 2, 1)),
        "gn_mlp": A(np.asarray(norm_mlp, f32).reshape(L, 16, 128).transpose(0, 2, 1)),
        "gn_fin": A(np.asarray(norm_final, f32).reshape(16, 128).T),
        "w1": A(np.asarray(cmp_w1, f32).reshape(L, 2, 32, 64, 128).transpose(0, 1, 3, 2, 4)),
        "w2": A(np.asarray(cmp_w2, f32).transpose(0, 2, 1, 3)),
        "peT": A(np.repeat(np.asarray(cmp_pe, f32).transpose(0, 3, 1, 2)[..., None], 2, axis=-1)),
        "cw": A(np.asarray(conv_w, f32).transpose(0, 2, 1).reshape(L, 88, 128, 3).transpose(0, 2, 1, 3)),
        "cbv": A(np.asarray(conv_b, f32).reshape(L, 88, 128).transpose(0, 2, 1)),
        "EB": eb_const().astype(bf), "EC": ec_const_nat().astype(bf), "ovl": ovl_const(), "selg": selg_const(),
    }
    per_j = []
    for j in range(4):
        G, cb = band_vectors(rel_table, j)
        negm, ownm = moba_consts(j)
        AM, BA, cmA, cmB = nsa_consts(j)
        per_j.append({"G": G, "cb": np.ascontiguousarray(np.broadcast_to(cb[None, :], (128, 44))), "negm": negm, "ownm": ownm,
                      "AM": AM, "BA": BA, "cmA": cmA, "cmB": cmB, "hco": halo_coef(j)})
    in_maps = []
    for c in range(8):
        b, j = c // 4, c % 4
        d = dict(shared)
        d.update(per_j[j])
        d["xT0"] = np.ascontiguousarray(x[b, core_tokens(j)].T)
        in_maps.append(d)
    return in_maps


from concourse.bass_utils import run_bass_kernel_spmd

_FUSED = {}


def kernel(x, rel_table, w_in, w_out, cmp_w1, cmp_w2, cmp_pe, norm_attn, norm_mlp,
           w_up, conv_w, conv_b, w_down, norm_final):
    if "nc" not in _FUSED:
        _FUSED["nc"] = build_fused()
    nc = _FUSED["nc"]
    in_maps = fused_host_inputs(x, rel_table, w_in, w_out, cmp_w1, cmp_w2, cmp_pe, norm_attn, norm_mlp,
                                w_up, conv_w, conv_b, w_down, norm_final)
    res = run_bass_kernel_spmd(nc, in_maps, core_ids=list(range(8)))
    out = np.empty((2, S, 2048), np.float32)
    for c in range(8):
        b, j = c // 4, c % 4
        out[b, core_tokens(j)] = np.asarray(res.results[c]["yT"]).T
    return out
```

```python
import contextlib, math
import numpy as np
import concourse.bass as bass
import concourse.mybir as mybir

F32 = mybir.dt.float32
BF16 = mybir.dt.bfloat16
I32 = mybir.dt.int32
AF = mybir.ActivationFunctionType
ALU = mybir.AluOpType
AX = mybir.AxisListType

SEM_CAP = 30000
N_DMA_SEMS = 8


class _Op:
    __slots__ = ("eng", "fn", "deps", "is_dma", "sig", "has_dependents", "idx", "dsem_prev", "is_cc", "inc")

    def __init__(self, eng, fn, is_dma):
        self.eng = eng
        self.fn = fn
        self.is_dma = is_dma
        self.deps = []
        self.sig = None
        self.has_dependents = False
        self.dsem_prev = None
        self.is_cc = False
        self.inc = 1


class Prog:
    ENGS = ("pe", "act", "dve", "pool", "sp")

    def __init__(self, nc):
        self.nc = nc
        self.ops = []
        self.last_writer = {}
        self.readers = {}

    def _add(self, eng, fn, reads, writes, is_dma):
        op = _Op(eng, fn, is_dma)
        deps = {}
        for r in reads:
            w = self.last_writer.get(r)
            if w is not None:
                deps[id(w)] = w
        for r in writes:
            w = self.last_writer.get(r)
            if w is not None:
                deps[id(w)] = w
            for rd in self.readers.get(r, ()):
                deps[id(rd)] = rd
        for r in reads:
            self.readers.setdefault(r, []).append(op)
        for r in writes:
            self.last_writer[r] = op
            self.readers[r] = []
        for d in deps.values():
            if d is op:
                continue
            if (not is_dma) and (not d.is_dma) and d.eng == eng and eng == "pe":
                continue
            op.deps.append(d)
            d.has_dependents = True
        self.ops.append(op)
        return op

    def op(self, eng, fn, reads=(), writes=()):
        return self._add(eng, fn, reads, writes, False)

    def dma(self, eng, fn, reads=(), writes=()):
        return self._add(eng, fn, reads, writes, True)

    def emit(self, final_wait_ops=()):
        nc = self.nc
        engs = {"pe": nc.tensor, "act": nc.scalar, "dve": nc.vector, "pool": nc.gpsimd, "sp": nc.sync}
        import contextlib
        with contextlib.ExitStack() as st:
            sem_lists = {e: [] for e in self.ENGS}
            counts = {e: 0 for e in self.ENGS}

            def new_sem(name):
                return st.enter_context(nc.semaphore(name))

            dma_sems = {}
            dma_state = {}
            for e in self.ENGS:
                dma_sems[e] = None
            for op in self.ops:
                if op.is_dma:
                    if dma_sems[op.eng] is None:
                        dma_sems[op.eng] = [new_sem(f"d_{op.eng}_{i}") for i in range(N_DMA_SEMS)]
                        dma_state[op.eng] = {"rr": 0, "cnt": [0] * N_DMA_SEMS}
                    stt = dma_state[op.eng]
                    i = stt["rr"]
                    stt["rr"] = (i + 1) % N_DMA_SEMS
                    prev = stt["cnt"][i]
                    stt["cnt"][i] = prev + 16
                    op.sig = (dma_sems[op.eng][i], prev + 16)
                    op.dsem_prev = (dma_sems[op.eng][i], prev) if prev > 0 else None
                elif op.has_dependents:
                    e = op.eng
                    if not sem_lists[e] or counts[e] >= SEM_CAP:
                        sem_lists[e].append(new_sem(f"c_{e}_{len(sem_lists[e])}"))
                        counts[e] = 0
                    counts[e] += 1
                    op.sig = (sem_lists[e][-1], counts[e])
            streams = {e: [] for e in self.ENGS}
            waited = {e: {} for e in self.ENGS}
            for op in self.ops:
                e = op.eng
                waits = []
                need = []
                if op.dsem_prev is not None:
                    need.append(op.dsem_prev)
                for d in op.deps:
                    need.append(d.sig)
                for (sem, val) in need:
                    k = id(sem)
                    if waited[e].get(k, 0) >= val:
                        continue
                    waited[e][k] = val
                    waits.append((sem, val))
                streams[e].append((waits, op))
            finals = [o.sig for o in final_wait_ops]
            blk = st.enter_context(nc.Block())

            def make(e):
                def body(engine):
                    for waits, op in streams[e]:
                        for (sem, val) in waits:
                            engine.wait_ge(sem, val)
                        ins = op.fn()
                        if op.sig is not None:
                            ins.then_inc(op.sig[0], 16 if op.is_dma else 1)
                    if e == "sp":
                        for (sem, val) in finals:
                            engine.wait_ge(sem, val)
                return body

            blk.tensor(make("pe"))
            blk.scalar(make("act"))
            blk.vector(make("dve"))
            blk.gpsimd(make("pool"))
            blk.sync(make("sp"))
        return self


class Ctx:
    ENGS = ("pe", "act", "dve", "pool", "sp")

    def __init__(self, nc, stack):
        self.nc = nc
        self.stack = stack
        self.sem_lists = {e: [] for e in self.ENGS}
        self.counts = {e: 0 for e in self.ENGS}
        self.dma_sems = {e: None for e in self.ENGS}
        self.dma_state = {}
        self.cc_sem = None
        self.cc_count = 0
        self.waited = {e: {} for e in self.ENGS}
        self.barrier_sigs = []
        self.nsem = 0

    def new_sem(self, name):
        self.nsem += 1
        return self.stack.enter_context(self.nc.semaphore(name))


class PProg(Prog):
    def __init__(self, ctx):
        super().__init__(ctx.nc)
        self.ctx = ctx

    def cc(self, fn, reads=(), writes=()):
        op = self._add("pool", fn, reads, writes, True)
        op.is_cc = True
        return op

    def emit(self, final_wait_ops=()):
        nc, ctx = self.nc, self.ctx
        engs = self.ENGS
        last_op = {e: None for e in engs}
        for op in self.ops:
            if not op.is_dma:
                last_op[op.eng] = op
        for e in engs:
            if last_op[e] is not None:
                last_op[e].has_dependents = True
        for op in self.ops:
            if getattr(op, "is_cc", False):
                if ctx.cc_sem is None:
                    ctx.cc_sem = ctx.new_sem("ccs")
                ctx.cc_count += 1
                op.sig = (ctx.cc_sem, ctx.cc_count)
                op.inc = 1
            elif op.is_dma:
                if ctx.dma_sems[op.eng] is None:
                    ctx.dma_sems[op.eng] = [ctx.new_sem(f"d_{op.eng}_{i}") for i in range(N_DMA_SEMS)]
                    ctx.dma_state[op.eng] = {"rr": 0, "cnt": [0] * N_DMA_SEMS}
                stt = ctx.dma_state[op.eng]
                i = stt["rr"]
                stt["rr"] = (i + 1) % N_DMA_SEMS
                prev = stt["cnt"][i]
                stt["cnt"][i] = prev + 16
                op.sig = (ctx.dma_sems[op.eng][i], prev + 16)
                op.dsem_prev = (ctx.dma_sems[op.eng][i], prev) if prev > 0 else None
                op.inc = 16
            elif op.has_dependents:
                e = op.eng
                if not ctx.sem_lists[e] or ctx.counts[e] >= SEM_CAP:
                    ctx.sem_lists[e].append(ctx.new_sem(f"c_{e}_{len(ctx.sem_lists[e])}"))
                    ctx.counts[e] = 0
                ctx.counts[e] += 1
                op.sig = (ctx.sem_lists[e][-1], ctx.counts[e])
                op.inc = 1
        streams = {e: [] for e in engs}
        start_waits = {e: [] for e in engs}
        for e in engs:
            for (sem, val) in ctx.barrier_sigs:
                k = id(sem)
                if ctx.waited[e].get(k, 0) >= val:
                    continue
                ctx.waited[e][k] = val
                start_waits[e].append((sem, val))
        for op in self.ops:
            e = op.eng
            waits = []
            need = []
            if op.dsem_prev is not None:
                need.append(op.dsem_prev)
            for d in op.deps:
                need.append(d.sig)
            for (sem, val) in need:
                k = id(sem)
                if ctx.waited[e].get(k, 0) >= val:
                    continue
                ctx.waited[e][k] = val
                waits.append((sem, val))
            streams[e].append((waits, op))
        sigs = []
        for e in engs:
            if last_op[e] is not None:
                sigs.append(last_op[e].sig)
            if ctx.dma_sems[e] is not None:
                for i, sem in enumerate(ctx.dma_sems[e]):
                    c = ctx.dma_state[e]["cnt"][i]
                    if c > 0:
                        sigs.append((sem, c))
        if ctx.cc_sem is not None and ctx.cc_count > 0:
            sigs.append((ctx.cc_sem, ctx.cc_count))
        ctx.barrier_sigs = sigs
        finals = [o.sig for o in final_wait_ops]
        with nc.Block() as blk:
            def make(e):
                def body(engine):
                    for (sem, val) in start_waits[e]:
                        engine.wait_ge(sem, val)
                    for waits, op in streams[e]:
                        for (sem, val) in waits:
                            engine.wait_ge(sem, val)
                        ins = op.fn()
                        if op.sig is not None:
                            if op.inc == 1 and getattr(op, "is_cc", False):
                                ins.then_inc(op.sig[0])
                            else:
                                ins.then_inc(op.sig[0], op.inc)
                    if e == "sp":
                        for (sem, val) in finals:
                            engine.wait_ge(sem, val)
                return body
            blk.tensor(make("pe"))
            blk.scalar(make("act"))
            blk.vector(make("dve"))
            blk.gpsimd(make("pool"))
            blk.sync(make("sp"))
        return self

import contextlib
import numpy as np

D = 2048
NT = 2048
IN_W = 5796
EPS = 1e-6

def a_col(g, part, hg):
    return g * 768 + part * 256 + hg * 64
B0 = 2304
C0 = 3840
CKV = C0 + 768
CG = CKV + 1152


def t_chunks():
    ch = []
    for g in range(3):
        for pair in range(2):
            ch.append((a_col(g, 0, 2 * pair), 128, [("qT", g * 4 + 2 * pair, 0, 64), ("qT", g * 4 + 2 * pair + 1, 64, 64)], 0.125))
        for pair in range(2):
            ch.append((a_col(g, 1, 2 * pair), 128, [("kT", g * 4 + 2 * pair, 0, 64), ("kT", g * 4 + 2 * pair + 1, 64, 64)], 1.0))
    for pair in range(4):
        ch.append((B0 + pair * 128, 128, [("qT", 12 + 2 * pair, 0, 64), ("qT", 13 + 2 * pair, 64, 64)], 0.125))
    for pair in range(4):
        ch.append((B0 + 512 + pair * 128, 128, [("kT", 12 + 2 * pair, 0, 64), ("kT", 13 + 2 * pair, 64, 64)], 1.0))
    for pair in range(6):
        ch.append((C0 + pair * 128, 128, [("qT", 20 + 2 * pair, 0, 64), ("qT", 21 + 2 * pair, 64, 64)], 0.125))
    for pair in range(3):
        ch.append((CKV + pair * 128, 128, [("cmpT", 2 * pair, 0, 64), ("cmpT", 2 * pair + 1, 64, 64)], 1.0))
    ch.append((CKV + 384, 128, [("kT", 20, 0, 64), ("kT", 21, 64, 64)], 1.0))
    ch.append((CKV + 384 + 128, 64, [("kT", 22, 0, 64)], 1.0))
    ch.append((CKV + 768, 128, [("kT", 23, 0, 64), ("kT", 24, 64, 64)], 1.0))
    ch.append((CKV + 768 + 128, 64, [("kT", 25, 0, 64)], 1.0))
    return ch


def n_chunks():
    ch = []
    for g in range(3):
        ch.append((a_col(g, 2, 0), 256, g * 256))
    ch.append((B0 + 1024, 256, 768))
    ch.append((B0 + 1024 + 256, 256, 1024))
    ch.append((CKV + 576, 192, 1280))
    ch.append((CKV + 960, 192, 1472))
    return ch


def build_p1():
    nc = bass.Bass("TRN2", target_bir_lowering=False)
    xT = nc.dram_tensor("xT", [D, NT], F32, kind="ExternalInput").ap()
    gn = nc.dram_tensor("gn", [128, 16], F32, kind="ExternalInput").ap()
    w = nc.dram_tensor("w", [D, IN_W], F32, kind="ExternalInput").ap()
    outs = {
        "qT": nc.dram_tensor("qT", [32, 64, NT], BF16, kind="ExternalOutput").ap(),
        "kT": nc.dram_tensor("kT", [26, 64, NT], BF16, kind="ExternalOutput").ap(),
        "cmpT": nc.dram_tensor("cmpT", [6, 64, NT], BF16, kind="ExternalOutput").ap(),
    }
    vO = nc.dram_tensor("v", [NT, 1664], BF16, kind="ExternalOutput").ap()
    gT = nc.dram_tensor("gT", [36, NT], F32, kind="ExternalOutput").ap()
    with contextlib.ExitStack() as st:
        T = lambda name, shape, dt: st.enter_context(nc.sbuf_tensor("s_" + name, shape, dt))
        PS = lambda name, shape, dt: st.enter_context(nc.psum_tensor("p_" + name, shape, dt))
        p = Prog(nc)
        emit_p1(nc, p, T, PS, xT, gn, w, outs, vO, gT)
        p.emit(final_wait_ops=p.final_ops)
    return nc


def emit_p1(nc, p, T, PS, xT, gn, w, outs, vO, gT, vdst=None, wsrc=None):
    p.final_ops = getattr(p, "final_ops", [])
    hT = T("hT", [128, 16, NT], BF16)
    gsb = T("gsb", [128, 16], F32)
    ones = T("ones", [128, 128], F32)
    xs = [T(f"xs{i}", [128, 16, 512], F32) for i in range(2)]
    sq = [T(f"sq{i}", [128, 512], F32) for i in range(2)]
    rstd = T("rstd", [128, 512], F32)
    wst = [T(f"wst{i}", [128, 16, 256], F32) for i in range(2)]
    wbf = [T(f"wbf{i}", [128, 16, 256], BF16) for i in range(2)]
    ost = [T(f"ost{i}", [128, NT], BF16) for i in range(2)]
    gst = T("gst", [36, NT], F32)
    vst = [T(f"vst{i}", [128, 256], BF16) for i in range(3)]
    pss = PS("pss", [128, 512], F32)
    pacc = [PS(f"pacc{i}", [128, 512], F32) for i in range(3)]

    p.dma("sp", lambda: nc.sync.dma_start(out=gsb[:], in_=gn), writes=["gsb"])
    p.op("pool", lambda: nc.gpsimd.memset(ones[:], 1.0), writes=["ones"])
    xv = xT.rearrange("(k p) t -> p k t", p=128)
    for m in range(4):
        s = m % 2
        p.dma("sp", lambda m=m, s=s: nc.sync.dma_start(out=xs[s][:], in_=xv[:, :, m * 512:(m + 1) * 512]), writes=[f"xs{s}"])
        for k in range(16):
            q = k % 2
            p.op("act", lambda s=s, k=k, q=q: nc.scalar.activation(out=sq[q][:], in_=xs[s][:, k, :], func=AF.Square),
                 reads=[f"xs{s}"], writes=[f"sq{q}"])
            p.op("pe", lambda q=q, k=k: nc.tensor.matmul(pss[:], lhsT=ones[:], rhs=sq[q][:], start=(k == 0), stop=(k == 15)),
                 reads=["ones", f"sq{q}"], writes=["pss"])
        p.op("act", lambda: nc.scalar.activation(out=rstd[:], in_=pss[:], func=AF.Sqrt, scale=1.0 / D, bias=EPS),
             reads=["pss"], writes=["rstd"])
        p.op("dve", lambda: nc.vector.reciprocal(out=rstd[:], in_=rstd[:]), reads=["rstd"], writes=["rstd"])
        for k in range(16):
            eng = "dve" if k % 2 == 0 else "pool"
            E = nc.vector if eng == "dve" else nc.gpsimd
            if eng == "dve":
                p.op("dve", lambda s=s, k=k, m=m: nc.vector.scalar_tensor_tensor(
                    out=hT[:, k, m * 512:(m + 1) * 512], in0=xs[s][:, k, :], scalar=gsb[:, k:k + 1], in1=rstd[:],
                    op0=ALU.mult, op1=ALU.mult), reads=[f"xs{s}", "gsb", "rstd"], writes=[f"hT{m}"])
            else:
                p.op("dve", lambda s=s, k=k, m=m: nc.vector.scalar_tensor_tensor(
                    out=hT[:, k, m * 512:(m + 1) * 512], in0=xs[s][:, k, :], scalar=gsb[:, k:k + 1], in1=rstd[:],
                    op0=ALU.mult, op1=ALU.mult), reads=[f"xs{s}", "gsb", "rstd"], writes=[f"hT{m}"])
    hT_all = [f"hT{m}" for m in range(4)]

    wcount = [0]

    def load_w(c0, ncols, kind=None, idx=0):
        s = wcount[0] % 2
        wcount[0] += 1
        if wsrc is None:
            src = w[:, c0:c0 + ncols].rearrange("(k p) n -> p k n", p=128)
        else:
            src = wsrc(kind, idx)[:, :, 0:ncols]
        p.dma("sp", lambda: nc.sync.dma_start(out=wst[s][:, :, 0:ncols], in_=src), writes=[f"wst{s}"])
        h = 8
        p.op("act", lambda: nc.scalar.copy(out=wbf[s][:, 0:h, 0:ncols], in_=wst[s][:, 0:h, 0:ncols]),
             reads=[f"wst{s}"], writes=[f"wbfa{s}"])
        p.op("pool", lambda: nc.gpsimd.tensor_copy(out=wbf[s][:, h:16, 0:ncols], in_=wst[s][:, h:16, 0:ncols]),
             reads=[f"wst{s}"], writes=[f"wbfb{s}"])
        return s

    tch = t_chunks() + [(CG, 36, [("gT", 0, 0, 36)], 1.0)]
    acc_i = [0]
    for ci, (c0, ncols, dests, scale) in enumerate(tch):
        s = load_w(c0, ncols, "T", ci)
        o = ci % 2
        is_gate = dests[0][0] == "gT"
        for m in range(4):
            a = acc_i[0] % 3
            acc_i[0] += 1
            for k in range(16):
                p.op("pe", lambda a=a, s=s, k=k, m=m, ncols=ncols: nc.tensor.matmul(
                    pacc[a][0:ncols, :], lhsT=wbf[s][:, k, 0:ncols], rhs=hT[:, k, m * 512:(m + 1) * 512],
                    start=(k == 0), stop=(k == 15)),
                    reads=[f"wbfa{s}", f"wbfb{s}", f"hT{m}"], writes=[f"pacc{a}"])
            if is_gate:
                p.op("act", lambda a=a, m=m: nc.scalar.activation(out=gst[:, m * 512:(m + 1) * 512], in_=pacc[a][0:36, :], func=AF.Sigmoid),
                     reads=[f"pacc{a}"], writes=["gst"])
            elif m % 2 == 0:
                p.op("act", lambda a=a, m=m, o=o, ncols=ncols, scale=scale: nc.scalar.activation(
                    out=ost[o][0:ncols, m * 512:(m + 1) * 512], in_=pacc[a][0:ncols, :], func=AF.Copy, scale=scale),
                    reads=[f"pacc{a}"], writes=[f"ost{o}"])
            else:
                p.op("dve", lambda a=a, m=m, o=o, ncols=ncols, scale=scale: nc.vector.tensor_scalar(
                    out=ost[o][0:ncols, m * 512:(m + 1) * 512], in0=pacc[a][0:ncols, :], scalar1=scale, scalar2=None, op0=ALU.mult),
                    reads=[f"pacc{a}"], writes=[f"ost{o}"])
        if is_gate:
            p.final_ops.append(p.dma("pool", lambda: nc.gpsimd.dma_start(out=gT, in_=gst[:]), reads=["gst"]))
        else:
            for (dn, dh, r0, nr) in dests:
                p.final_ops.append(p.dma("pool", lambda dn=dn, dh=dh, r0=r0, nr=nr, o=o: nc.gpsimd.dma_start(
                    out=outs[dn][dh], in_=ost[o][r0:r0 + nr, :]), reads=[f"ost{o}"]))

    vi = [0]
    for ni, (c0, ncols, vc0) in enumerate(n_chunks()):
        s = load_w(c0, ncols, "N", ni)
        for ts in range(16):
            a = acc_i[0] % 3
            acc_i[0] += 1
            m = ts // 4
            for k in range(16):
                p.op("pe", lambda a=a, s=s, k=k, ts=ts, ncols=ncols: nc.tensor.matmul(
                    pacc[a][:, 0:ncols], lhsT=hT[:, k, ts * 128:(ts + 1) * 128], rhs=wbf[s][:, k, 0:ncols],
                    start=(k == 0), stop=(k == 15)),
                    reads=[f"wbfa{s}", f"wbfb{s}", f"hT{m}"], writes=[f"pacc{a}"])
            vs = vi[0] % 3
            vi[0] += 1
            if ts % 2 == 0:
                p.op("act", lambda a=a, vs=vs, ncols=ncols: nc.scalar.copy(out=vst[vs][:, 0:ncols], in_=pacc[a][:, 0:ncols]),
                     reads=[f"pacc{a}"], writes=[f"vst{vs}"])
            else:
                p.op("dve", lambda a=a, vs=vs, ncols=ncols: nc.vector.tensor_copy(out=vst[vs][:, 0:ncols], in_=pacc[a][:, 0:ncols]),
                     reads=[f"pacc{a}"], writes=[f"vst{vs}"])
            if vdst is None:
                p.final_ops.append(p.dma("pool", lambda ts=ts, vs=vs, vc0=vc0, ncols=ncols: nc.gpsimd.dma_start(
                    out=vO[ts * 128:(ts + 1) * 128, vc0:vc0 + ncols], in_=vst[vs][:, 0:ncols]), reads=[f"vst{vs}"]))
            else:
                p.final_ops.append(p.dma("pool", lambda ts=ts, vs=vs, vc0=vc0, ncols=ncols: nc.gpsimd.dma_start(
                    out=vdst(ts, vc0, ncols), in_=vst[vs][:, 0:ncols].rearrange("p (h e) -> p h e", e=64)), reads=[f"vst{vs}"]))


import contextlib, math
import numpy as np

S = 8192
NT = 2048
BW = 4480
GL = BW + 128
NEGM = -30000.0
A_CFG = ((128, 1), (512, 4), (2048, 16))
U16 = mybir.dt.uint16


def t5_bucket_np(dist):
    n = np.maximum(dist, 0)
    nf = np.maximum(n, 1).astype(np.float32)
    large = 16 + (np.log(nf / np.float32(16)) / np.float32(math.log(128.0)) * np.float32(16)).astype(np.int32)
    large = np.minimum(large, 31)
    return np.where(n < 16, n, large)


def band_vectors(rel_table, j):
    v = np.arange(GL)
    dist = v + 512 * j - 2047
    bk = t5_bucket_np(dist)
    G = np.empty((44, GL), np.float32)
    cb = np.empty((44,), np.float32)
    for b in range(44):
        if b < 12:
            h = b
            W, d = A_CFG[b // 4]
            ok = (dist >= 0) & (dist <= W) & (dist % d == 0)
        elif b < 20:
            h = b
            ok = dist >= 0
        elif b < 32:
            h = b
            ok = dist >= 0
        else:
            h = 20 + (b - 32)
            ok = (dist >= 0) & (dist < 512)
        G[b] = np.where(ok, rel_table[h, bk], np.float32(NEGM))
        cb[b] = rel_table[h, 31]
    return G, cb


def core_tokens(j):
    return np.concatenate([np.arange(512 * (4 * m + j), 512 * (4 * m + j) + 512) for m in range(4)])


def moba_consts(j):
    t = core_tokens(j)
    ob = t // 256
    n = np.arange(32)[None, :]
    neg = np.where(n >= ob[:, None], np.float32(-1e30), np.float32(0)).astype(np.float32)
    own = (n >= ob[:, None]).astype(np.float32)
    f = lambda a: np.ascontiguousarray(a.reshape(16, 128, 32).transpose(1, 0, 2))
    return f(neg), f(own)


def eb_const():
    k = np.arange(S)
    return (k[None, :] // 256 == np.arange(32)[:, None]).astype(np.float32)


def ec_const():
    c = np.arange(S)
    key = (c // 128) * 128 + 127 - (c % 128)
    return (key[None, :] // 64 == np.arange(128)[:, None]).astype(np.float32)


def rev_blocks(a, axis):
    a = np.moveaxis(a, axis, -1)
    sh = a.shape
    a = a.reshape(sh[:-1] + (sh[-1] // 128, 128))[..., ::-1].reshape(sh)
    return np.moveaxis(a, -1, axis)


def nsa_consts(j):
    t = core_tokens(j)
    own = (t // 64)[:, None]
    jb = np.arange(128)[None, :]
    valid = jb <= own
    forced = (jb == 0) | (jb == own) | (jb == own - 1)
    am = (valid & ~forced).astype(np.float32)
    ba = np.where(valid, np.where(forced, np.float32(1e9), np.float32(0)), np.float32(-1)).astype(np.float32)
    f = lambda a: np.ascontiguousarray(a.reshape(16, 128, 128).transpose(1, 0, 2))
    pp = np.arange(128)[:, None]
    q = np.arange(512)[None, :]
    cmA = np.where(16 * pp + 31 <= 512 * j + q, np.float32(0), np.float32(NEGM)).astype(np.float32)
    cmB = np.where(16 * pp + 31 - 2048 <= 512 * j + q, np.float32(0), np.float32(NEGM)).astype(np.float32)
    return f(am), f(ba), cmA, cmB


def ovl_const():
    i = np.arange(512)[:, None]
    jb = np.arange(128)[None, :]
    ov = ((16 * i < 64 * jb + 64) & (16 * i + 32 > 64 * jb) & (i < 511)).astype(np.float32)
    return np.ascontiguousarray(ov.reshape(4, 128, 128).transpose(1, 0, 2))


def selg_const():
    sg = np.zeros((36, 36, 64), np.float32)
    for r in range(36):
        sg[r, r, :] = 1.0
    return sg

class P2:
    def __init__(self, nc, p, T, PS, dr, heads_A=(0, 1, 2, 3), heads_B=tuple(range(8)), kvs_C=(0, 1, 2), fused=False):
        self.nc, self.p, self.T, self.PS, self.dr = nc, p, T, PS, dr
        self.fused = fused
        self.heads_A, self.heads_B, self.kvs_C = heads_A, heads_B, kvs_C
        self.units = []
        self.alloc()

    def alloc(self):
        T, PS = self.T, self.PS
        self.kTall = T("kTall", [128, S], BF16)
        self.qTz = [T(f"qTz{i}", [128, NT], BF16) for i in range(2)]
        self.kTb = [self.kTall[0:64], self.kTall[64:128]]
        self.qTb = [self.qTz[0][0:64], self.qTz[1][64:128]]
        self.Vb = [T(f"Vb{i}", [128, 64, 128], BF16) for i in range(2)]
        self.band1 = T("band1", [128, BW], F32)
        self.bandb = [self.band1, self.band1]
        self.cbb = T("cbb", [128, 44], F32)
        self.Eb = T("Eb", [128, S], BF16)
        self.MTb = [T(f"MTb{i}", [128, NT], BF16) for i in range(2)]
        self.sb = [T(f"sb{i}", [128, 512], F32) for i in range(2)]
        self.PT = [T(f"PT{i}", [128, 512], BF16) for i in range(3)]
        self.dsum = self.sb[1]
        self.nd = T("nd", [128, 12, 512], F32)
        self.rden = T("rden", [64, 512], F32)
        self.ost = [T(f"ost{i}", [64, 512], BF16) for i in range(3)]
        self.identb = T("identb", [128, 128], BF16)
        self.identf = T("identf", [128, 128], F32)
        self.negm = T("negm", [128, 16, 32], F32)
        self.ownm = T("ownm", [128, 16, 32], F32)
        self.km = T("km", [128, 32], F32)
        self.kmb = T("kmb", [128, 32], BF16)
        self.gm = T("gm", [128, 16, 32], F32)
        self.m8 = T("m8", [128, 16, 8], F32)
        self.selt = T("selt", [128, 16, 32], F32)
        self.Mq = T("Mq", [128, 16, 32], BF16)
        self.qcm = [T(f"qcm{i}", [64, 512], BF16) for i in range(3)]
        self.ef = [T(f"ef{i}", [128, 512], F32) for i in range(4)]
        self.pcb = [T(f"pcb{i}", [128, 512], BF16) for i in range(4)]
        self.w1b = T("w1b", [64, 32, 128], BF16)
        self.w2f = T("w2f", [128, 2, 64], F32)
        self.w2b = T("w2b", [128, 2, 64], BF16)
        self.peTf = T("peTf", [64, 2, 32, 2], F32)
        self.peTb = T("peTb", [64, 2, 32, 2], BF16)
        self.kcTb = T("kcTb", [64, 512], BF16)
        self.vcb = T("vcb", [128, 4, 64], BF16)
        self.scr = [T(f"scr{i}", [128, 512], F32) for i in range(2)]
        self.ocs = self.scr[1][0:64]
        self.cbias = T("cbias", [128, 1], F32)
        self.hid = T("hid", [128, 512], BF16)
        self.amb = T("amb", [128, 4, 128], F32)
        self.bab = T("bab", [128, 4, 128], F32)
        self.cmA = T("cmA", [128, 512], F32)
        self.cmB = T("cmB", [128, 512], F32)
        self.ovl = T("ovl", [128, 4, 128], F32)
        self.sel3 = T("sel3", [36, 3, 64], F32)
        self.gTs = T("gTs", [36, NT], F32)
        self.impS = T("impS", [128, 512], F32)
        self.score = T("score", [128, 4, 128], F32)
        self.sc2 = T("sc2", [128, 128], F32)
        self.m16 = T("m16", [128, 4, 16], F32)
        self.Msel = T("Msel", [128, 4, 128], BF16)
        self.onesf = T("onesf", [128, 128], F32)
        self.tmpf = T("tmpf", [64, 512], F32)
        self.pS = [PS(f"pS{i}", [128, 512], F32) for i in range(4)]
        self.pacc = [PS(f"pacc{i}", [128, 512], F32) for i in range(2)]
        self.pm = [PS("pm0", [128, 512], F32), self.pS[3]]
        self.pmn = ["pm0", "pS3"]
        self.pmb = PS("pmb", [128, 1024], BF16)
        if self.fused:
            self.hst = [T(f"hst{i}", [128, 512], F32) for i in range(2)]
            self.Jf = T("Jf", [128, 128], F32)
        self.ost_i = 0
        self.slot = 0
        self.acc_i = 0
        self.out_ops = []

    def setup(self):
        nc, p = self.nc, self.p
        p.op("pool", lambda: nc.gpsimd.memset(self.identf[:], 1.0), writes=["identf"])
        p.op("pool", lambda: nc.gpsimd.affine_select(out=self.identf[:], in_=self.identf[:], pattern=[[-1, 128]],
                                                     compare_op=ALU.is_equal, fill=0.0, base=0, channel_multiplier=1),
             reads=["identf"], writes=["identf"])
        p.op("dve", lambda: nc.vector.tensor_copy(out=self.identb[:], in_=self.identf[:]), reads=["identf"], writes=["identb"])
        for i in range(2):
            p.op("pool", lambda i=i: nc.gpsimd.memset(self.Vb[i][:, :, 64:128], 1.0), writes=[f"Vones{i}"])
        p.dma("sp", lambda: nc.sync.dma_start(out=self.cbb[:], in_=self.dr["cb"]), writes=["cbb"])
        p.op("pool", lambda: nc.gpsimd.memset(self.kTall[:], 0.0), writes=["kTb0", "kTb1"])
        for i in range(2):
            p.op("pool", lambda i=i: nc.gpsimd.memset(self.qTz[i][:], 0.0), writes=[f"qTb{i}"])
            p.op("pool", lambda i=i: nc.gpsimd.memset(self.MTb[i][:], 0.0), writes=[f"MTb{i}"])
        p.op("pool", lambda: nc.gpsimd.memset(self.Eb[:], 0.0), writes=["Eb"])
        if self.fused:
            p.op("pool", lambda: nc.gpsimd.memset(self.Jf[:], 1.0), writes=["Jf"])
            p.op("pool", lambda: nc.gpsimd.affine_select(out=self.Jf[:], in_=self.Jf[:], pattern=[[1, 128]],
                                                         compare_op=ALU.is_equal, fill=0.0, base=-127, channel_multiplier=1),
                 reads=["Jf"], writes=["Jf"])

    def load_kT_gathered(self, dst, chunks, row0, res):
        nc, p = self.nc, self.p
        src2d = None
        for (r0, nr, ap) in chunks:
            if r0 <= row0 < r0 + nr:
                src2d, row0 = ap, row0 - r0
                break
        nrows = src2d.shape[0] // 4
        for m in range(4):
            src = bass.AP(tensor=src2d.tensor, offset=src2d[row0:row0 + 1, m * 512:m * 512 + 1].offset,
                          ap=[[2048, 64], [nrows * 2048, 4], [1, 512]])
            d = dst[:, m * 2048:(m + 1) * 2048].rearrange("e (r i) -> e r i", i=512)
            p.dma("sp", lambda src=src, d=d: nc.sync.dma_start(out=d, in_=src), reads=["gathered"], writes=[res])

    def load_v_gathered(self, s, ki):
        nc, p = self.nc, self.p
        vg, lrow, nr = None, 0, 0
        for (r0, nr_, ap) in self.dr["v_g"]:
            if r0 <= ki * 128 < r0 + nr_:
                vg, lrow, nr = ap, ki * 128 - r0, nr_
                break
        for m in range(4):
            for r in range(4):
                src = bass.AP(tensor=vg.tensor, offset=vg[r * nr + lrow:r * nr + lrow + 1, m * 256:m * 256 + 1].offset,
                              ap=[[1024, 128], [64, 4], [1, 64]])
                d = self.Vb[s][:, m * 16 + r * 4:m * 16 + r * 4 + 4, 0:64]
                p.dma("sp", lambda src=src, d=d: nc.sync.dma_start(out=d, in_=src), reads=["gathered"], writes=[f"Vb{s}"])

    def load_band_flipped(self, bi):
        nc, p = self.nc, self.p
        g = self.dr["G"]
        nch = (BW + 511) // 512
        for ch in range(nch):
            w_ = min(512, BW - ch * 512)
            hs = ch % 2
            src = bass.AP(tensor=g.tensor, offset=g[bi:bi + 1, ch * 512:ch * 512 + 1].offset, ap=[[1, 128], [1, w_]])
            p.dma("sp", lambda src=src, hs=hs, w_=w_: nc.sync.dma_start(out=self.hst[hs][:, 0:w_], in_=src), writes=[f"hst{hs}"])
            pb = self.pm[ch % 2]
            p.op("pe", lambda hs=hs, w_=w_, pb=pb: nc.tensor.matmul(pb[:, 0:w_], lhsT=self.Jf[:], rhs=self.hst[hs][:, 0:w_], start=True, stop=True),
                 reads=["Jf", f"hst{hs}"], writes=[self.pmn[ch % 2]])
            p.op("act", lambda ch=ch, w_=w_, pb=pb: nc.scalar.copy(out=self.band1[:, ch * 512:ch * 512 + w_], in_=pb[:, 0:w_]),
                 reads=[self.pmn[ch % 2]], writes=["bandb"])

    def load_head(self, qi, ki, bi):
        nc, p, dr = self.nc, self.p, self.dr
        s = self.slot % 2
        self.slot += 1
        if self.fused:
            p.dma("sp", lambda: nc.sync.dma_start(out=self.qTb[s], in_=dr["qT"][qi]), reads=["qT_s"], writes=[f"qTb{s}"])
            self.load_kT_gathered(self.kTb[s], dr["kT_g"], ki * 64, f"kTb{s}")
            self.load_v_gathered(s, ki)
            self.load_band_flipped(bi)
            return s
        p.dma("sp", lambda: nc.sync.dma_start(out=self.qTb[s], in_=dr["qT"][qi]), writes=[f"qTb{s}"])
        p.dma("sp", lambda: nc.sync.dma_start(out=self.kTb[s], in_=dr["kTf"][ki]), writes=[f"kTb{s}"])
        p.dma("sp", lambda: nc.sync.dma_start(out=self.Vb[s][:, :, 0:64], in_=dr["vf"][ki]), writes=[f"Vb{s}"])
        g = dr["G"]
        src = bass.AP(tensor=g.tensor, offset=g[bi:bi + 1, 0:1].offset, ap=[[1, 128], [1, BW]])
        p.dma("sp", lambda: nc.sync.dma_start(out=self.bandb[s][:], in_=src), writes=["bandb"])
        return s

    def add_units(self, s, m, kts, near_lo, bi, acc, mask=None, post=None, pre=None):
        n = len(kts)
        for idx, kt in enumerate(kts):
            self.units.append(dict(s=s, m=m, kt=kt, near=(kt >= near_lo), bi=bi, acc=acc, first=(idx == 0), last=(idx == n - 1),
                                   mask=mask, post=post if idx == n - 1 else None, pre=pre if idx == 0 else None))

    def flush_units(self, LA=3):
        nc, p = self.nc, self.p
        U = self.units
        n = len(U)

        def qk(i):
            u = U[i]
            if u["pre"] is not None:
                u["pre"]()
            b = i % 4
            s, m, kt = u["s"], u["m"], u["kt"]
            rd = [f"kTb{s}", f"qTb{s}"]
            if u["mask"] is None:
                p.op("pe", lambda: nc.tensor.matmul(self.pS[b][:], lhsT=self.kTall[:, kt * 128:(kt + 1) * 128],
                                                    rhs=self.qTz[s][:, m * 512:(m + 1) * 512], start=True, stop=True),
                     reads=rd, writes=[f"pS{b}"])
            else:
                nr, ms = u["mask"]
                p.op("pe", lambda: nc.tensor.matmul(self.pS[b][:], lhsT=self.kTall[:, kt * 128:(kt + 1) * 128],
                                                    rhs=self.qTz[s][:, m * 512:(m + 1) * 512], start=True, stop=False),
                     reads=rd, writes=[f"pS{b}"])
                p.op("pe", lambda: nc.tensor.matmul(self.pS[b][:], lhsT=self.Eb[:, kt * 128:(kt + 1) * 128],
                                                    rhs=self.MTb[ms][:, m * 512:(m + 1) * 512], start=False, stop=True),
                     reads=["Eb", f"MTb{ms}"], writes=[f"pS{b}"])

        def rest(i):
            u = U[i]
            b = i % 4
            s, m, kt, bi, acc = u["s"], u["m"], u["kt"], u["bi"], u["acc"]
            pt = i % 3
            if u["near"]:
                sbi = i % 2
                u0 = 2048 * m - 128 * kt + 1920
                assert 0 <= u0 and u0 + 512 <= BW, (m, kt, u0)
                p.op("dve", lambda: nc.vector.tensor_tensor(out=self.sb[sbi][:], in0=self.pS[b][:], in1=self.bandb[s][:, u0:u0 + 512], op=ALU.add),
                     reads=[f"pS{b}", "bandb"], writes=[f"sb{sbi}"])
                p.op("act", lambda: nc.scalar.activation(out=self.PT[pt][:], in_=self.sb[sbi][:], func=AF.Exp),
                     reads=[f"sb{sbi}"], writes=[f"PT{pt}"])
            else:
                p.op("act", lambda: nc.scalar.activation(out=self.PT[pt][:], in_=self.pS[b][:], func=AF.Exp, bias=self.cbb[:, bi:bi + 1]),
                     reads=[f"pS{b}", "cbb"], writes=[f"PT{pt}"])
            p.op("pe", lambda: nc.tensor.matmul(self.pacc[acc][:], lhsT=self.Vb[s][:, kt, :], rhs=self.PT[pt][:],
                                                start=u["first"], stop=u["last"]),
                 reads=[f"Vb{s}", f"Vones{s}", f"PT{pt}"], writes=[f"pacc{acc}"])
            if u["post"] is not None:
                u["post"]()

        for i in range(n + LA):
            if i < n:
                qk(i)
            if i - LA >= 0:
                rest(i - LA)
        self.units = []

    def write_out(self, head_feat, m, num_ap, rden_ap):
        nc, p = self.nc, self.p
        o = self.ost_i % 3
        self.ost_i += 1
        num, nres = num_ap
        rd, rres = rden_ap
        p.op("dve", lambda: nc.vector.tensor_tensor(out=self.ost[o][:], in0=num, in1=rd, op=ALU.mult),
             reads=[nres, rres], writes=[f"ost{o}"])
        dst = self.dr["OT"][head_feat * 64:(head_feat + 1) * 64, m * 512:(m + 1) * 512]
        self.out_ops.append(p.dma("pool", lambda: nc.gpsimd.dma_start(out=dst, in_=self.ost[o][:]), reads=[f"ost{o}"]))

    def mixer_A(self):
        nc, p = self.nc, self.p
        for hg in self.heads_A:
            for g in range(3):
                W, d = A_CFG[g]
                h = g * 4 + hg
                s = self.load_head(h, h, h)
                for m in range(4):
                    lo = max(0, (2048 * m - W) // 128)
                    kts = list(range(lo, 16 * m + 16))
                    acc = self.acc_i % 2
                    self.acc_i += 1

                    def post(g=g, m=m, acc=acc):
                        p.op("act", lambda: nc.scalar.copy(out=self.nd[:, g * 4 + m, :], in_=self.pacc[acc][:]),
                             reads=[f"pacc{acc}"], writes=[f"nd{g}_{m}"])
                    self.add_units(s, m, kts, 0, h, acc, post=post)
                self.flush_units()
            for m in range(4):
                p.op("pool", lambda m=m: nc.gpsimd.tensor_tensor(out=self.dsum[64:128, :], in0=self.nd[64:128, 0 * 4 + m, :], in1=self.nd[64:128, 1 * 4 + m, :], op=ALU.add),
                     reads=[f"nd0_{m}", f"nd1_{m}"], writes=["sb1"])
                p.op("pool", lambda m=m: nc.gpsimd.tensor_tensor(out=self.dsum[64:128, :], in0=self.dsum[64:128, :], in1=self.nd[64:128, 2 * 4 + m, :], op=ALU.add),
                     reads=["sb1", f"nd2_{m}"], writes=["sb1"])
                p.op("dve", lambda: nc.vector.reciprocal(out=self.rden[:], in_=self.dsum[64:128, :]), reads=["sb1"], writes=["rden"])
                for g in range(3):
                    self.write_out(g * 4 + hg, m, (self.nd[0:64, g * 4 + m, :], f"nd{g}_{m}"), (self.rden[:], "rden"))

    def moba_prologue(self, s, ms):
        nc, p = self.nc, self.p
        p.op("dve", lambda: nc.vector.tensor_reduce(out=self.km[64 * s:64 * s + 64, :], in_=self.kTb[s].rearrange("e (n k) -> e n k", k=256), axis=AX.X, op=ALU.add),
             reads=[f"kTb{s}"], writes=["km"])
        p.op("dve", lambda: nc.vector.tensor_scalar(out=self.kmb[64 * s:64 * s + 64, :], in0=self.km[64 * s:64 * s + 64, :], scalar1=1.0 / 256, scalar2=None, op0=ALU.mult),
             reads=["km"], writes=["kmb"])
        pg = self.pm[0]
        for qs in range(16):
            p.op("pe", lambda qs=qs: nc.tensor.matmul(pg[:, qs * 32:(qs + 1) * 32], lhsT=self.qTb[s][:, qs * 128:(qs + 1) * 128], rhs=self.kmb[64 * s:64 * s + 64, :], start=True, stop=True),
                 reads=[f"qTb{s}", "kmb"], writes=["pm0"])
        p.op("dve", lambda: nc.vector.tensor_tensor(out=self.gm[:], in0=pg[:].rearrange("p (a b) -> p a b", b=32), in1=self.negm[:], op=ALU.add),
             reads=["pm0", "negm"], writes=["gm"])
        for qs in range(16):
            p.op("dve", lambda qs=qs: nc.vector.max(out=self.m8[:, qs, :], in_=self.gm[:, qs, :]), reads=["gm"], writes=["m8"])
        for qs in range(16):
            p.op("dve", lambda qs=qs: nc.vector.tensor_scalar(out=self.selt[:, qs, :], in0=self.gm[:, qs, :], scalar1=self.m8[:, qs, 2:3], scalar2=None, op0=ALU.is_ge),
                 reads=["gm", "m8"], writes=["selt"])
        p.op("dve", lambda: nc.vector.tensor_tensor(out=self.selt[:], in0=self.selt[:], in1=self.ownm[:], op=ALU.max), reads=["selt", "ownm"], writes=["selt"])
        p.op("dve", lambda: nc.vector.tensor_scalar(out=self.Mq[:], in0=self.selt[:], scalar1=-1.0, scalar2=-NEGM, op0=ALU.add, op1=ALU.mult),
             reads=["selt"], writes=["Mq"])
        for half in range(2):
            for q8 in range(8):
                qs = half * 8 + q8
                p.op("pe", lambda qs=qs, q8=q8: nc.tensor.transpose(out=self.pmb[0:32, q8 * 128:(q8 + 1) * 128], in_=self.Mq[:, qs, :], identity=self.identb[:]),
                     reads=["Mq", "identb"], writes=["pmb"])
            p.op("act", lambda half=half: nc.scalar.copy(out=self.MTb[ms][0:32, half * 1024:(half + 1) * 1024], in_=self.pmb[0:32, :]),
                 reads=["pmb"], writes=[f"MTb{ms}"])

    def mixer_B(self):
        nc, p, dr = self.nc, self.p, self.dr
        p.dma("sp", lambda: nc.sync.dma_start(out=self.negm[:], in_=dr["negm"]), writes=["negm"])
        p.dma("sp", lambda: nc.sync.dma_start(out=self.ownm[:], in_=dr["ownm"]), writes=["ownm"])
        p.dma("sp", lambda: nc.sync.dma_start(out=self.Eb[0:32, :], in_=dr["EB"]), writes=["Eb"])
        for hb in self.heads_B:
            h = 12 + hb
            s = self.load_head(h, h, h)
            ms = hb % 2
            self.moba_prologue(s, ms)
            for m in range(4):
                kts = list(range(0, 16 * m + 16))
                acc = self.acc_i % 2
                self.acc_i += 1

                def post(h=h, m=m, acc=acc):
                    p.op("dve", lambda: nc.vector.reciprocal(out=self.rden[:], in_=self.pacc[acc][64:128, :]), reads=[f"pacc{acc}"], writes=["rden"])
                    self.write_out(h, m, (self.pacc[acc][0:64, :], f"pacc{acc}"), (self.rden[:], "rden"))
                self.add_units(s, m, kts, 16 * m - 12, h, acc, mask=(32, ms), post=post)
            self.flush_units()


    def compress(self, kv, t):
        nc, p, dr = self.nc, self.p, self.dr
        s = 0
        if self.fused:
            self.load_kT_gathered(self.kTb[s], dr["cmpT_g"], (t * 3 + kv) * 64, f"kTb{s}")
        else:
            p.dma("sp", lambda: nc.sync.dma_start(out=self.kTb[s], in_=dr["cmpTf"][t * 3 + kv]), writes=[f"kTb{s}"])
        stg = self.band1[0:64, 0:4096].rearrange("e (l j) -> e l j", j=128)
        p.dma("sp", lambda: nc.sync.dma_start(out=stg, in_=dr["w1"][t]), writes=["bandb"])
        p.op("act", lambda: nc.scalar.copy(out=self.w1b[:], in_=stg), reads=["bandb"], writes=["w1b"])
        ph = self.pm[0]
        base = self.kTb[s]
        for l in range(32):
            rhs = bass.AP(tensor=base.tensor, offset=base.offset + l, ap=[list(base.ap[0]), [16, 511]])
            p.op("pe", lambda l=l, rhs=rhs: nc.tensor.matmul(ph[:, 0:511], lhsT=self.w1b[:, l, :], rhs=rhs, start=(l == 0), stop=(l == 31)),
                 reads=["w1b", f"kTb{s}"], writes=["pm0"])
        pc = self.pm[1]
        for l in range(32):
            p.op("pe", lambda l=l: nc.tensor.matmul(pc[:, 0:2], lhsT=self.w1b[:, l, :], rhs=self.peTb[:, t, l, :], start=(l == 0), stop=(l == 31)),
                 reads=["w1b", "peTb"], writes=["pS3"])
        p.op("dve", lambda: nc.vector.tensor_copy(out=self.cbias[:], in_=pc[:, 0:1]), reads=["pS3"], writes=["cbias"])
        x, y = self.scr[0], self.scr[1]
        p.op("act", lambda: nc.scalar.activation(out=x[:, 0:511], in_=ph[:, 0:511], func=AF.Identity, bias=self.cbias[:, 0:1]),
             reads=["pm0", "cbias"], writes=["scr0"])
        p.op("dve", lambda: nc.vector.tensor_tensor(out=y[:, 0:511], in0=x[:, 0:511], in1=x[:, 0:511], op=ALU.mult), reads=["scr0"], writes=["scr1"])
        p.op("dve", lambda: nc.vector.tensor_scalar(out=y[:, 0:511], in0=y[:, 0:511], scalar1=0.044715, scalar2=1.0, op0=ALU.mult, op1=ALU.add), reads=["scr1"], writes=["scr1"])
        p.op("dve", lambda: nc.vector.tensor_tensor(out=y[:, 0:511], in0=y[:, 0:511], in1=x[:, 0:511], op=ALU.mult), reads=["scr0", "scr1"], writes=["scr1"])
        p.op("act", lambda: nc.scalar.activation(out=y[:, 0:511], in_=y[:, 0:511], func=AF.Tanh, scale=0.7978845608028654), reads=["scr1"], writes=["scr1"])
        p.op("dve", lambda: nc.vector.scalar_tensor_tensor(out=y[:, 0:511], in0=y[:, 0:511], scalar=1.0, in1=x[:, 0:511], op0=ALU.add, op1=ALU.mult), reads=["scr0", "scr1"], writes=["scr1"])
        p.op("dve", lambda: nc.vector.tensor_scalar(out=self.hid[:, 0:511], in0=y[:, 0:511], scalar1=0.5, scalar2=None, op0=ALU.mult), reads=["scr1"], writes=["hid"])
        if t == 0:
            pk = self.pS[0]
            p.op("pe", lambda: nc.tensor.matmul(pk[0:64, :], lhsT=self.w2b[:, 0, :], rhs=self.hid[:], start=True, stop=True),
                 reads=["w2b", "hid"], writes=["pS0"])
            p.op("act", lambda: nc.scalar.copy(out=self.kcTb[:], in_=pk[0:64, :]), reads=["pS0"], writes=["kcTb"])
        else:
            pv = self.pS[1]
            for it in range(4):
                p.op("pe", lambda it=it: nc.tensor.matmul(pv[:, it * 64:(it + 1) * 64], lhsT=self.hid[:, it * 128:(it + 1) * 128], rhs=self.w2b[:, 1, :], start=True, stop=True),
                     reads=["w2b", "hid"], writes=["pS1"])
            p.op("act", lambda: nc.scalar.copy(out=self.vcb[:], in_=pv[:, 0:256].rearrange("p (a b) -> p a b", b=64)), reads=["pS1"], writes=["vcb"])

    def cmp_stage(self, kv, m):
        nc, p, dr = self.nc, self.p, self.dr
        pden, poc, pgt, pimp, ptr = self.pS[0], self.pS[1], self.pS[2], self.pacc[0], self.pacc[1]
        nit = min(m, 3) + 1
        for gq in range(4):
            hc = kv * 4 + gq
            qs = self.qc_i % 3
            self.qc_i += 1
            p.dma("sp", lambda qs=qs, hc=hc: nc.sync.dma_start(out=self.qcm[qs][:], in_=dr["qT"][20 + hc][:, m * 512:(m + 1) * 512]), writes=[f"qcm{qs}"])
            p.dma("sp", lambda hc=hc: nc.sync.dma_start(out=self.sel3[:], in_=dr["selg"][:, 3 * hc:3 * hc + 3, :]), writes=["sel3"])
            for it in range(nit):
                ps = self.pm[it % 2]
                psn = self.pmn[it % 2]
                p.op("pe", lambda it=it, ps=ps, qs=qs: nc.tensor.matmul(ps[:], lhsT=self.kcTb[:, it * 128:(it + 1) * 128], rhs=self.qcm[qs][:], start=True, stop=True),
                     reads=["kcTb", f"qcm{qs}"], writes=[psn])
                if it >= m - 1:
                    cm, cmn = (self.cmA, "cmA") if it == m else (self.cmB, "cmB")
                    sbi = it % 2
                    p.op("dve", lambda ps=ps, cm=cm, sbi=sbi: nc.vector.tensor_tensor(out=self.sb[sbi][:], in0=ps[:], in1=cm[:], op=ALU.add),
                         reads=[psn, cmn], writes=[f"sb{sbi}"])
                    p.op("act", lambda it=it, sbi=sbi: nc.scalar.activation(out=self.ef[it][:], in_=self.sb[sbi][:], func=AF.Exp), reads=[f"sb{sbi}"], writes=[f"ef{it}"])
                else:
                    p.op("act", lambda it=it, ps=ps: nc.scalar.activation(out=self.ef[it][:], in_=ps[:], func=AF.Exp), reads=[psn], writes=[f"ef{it}"])
                p.op("pe", lambda it=it: nc.tensor.matmul(pden[:], lhsT=self.onesf[:], rhs=self.ef[it][:], start=(it == 0), stop=(it == nit - 1)),
                     reads=["onesf", f"ef{it}"], writes=["pS0"])
            rd = self.scr[0]
            p.op("dve", lambda: nc.vector.tensor_scalar(out=rd[:], in0=pden[:], scalar1=1e-30, scalar2=None, op0=ALU.max), reads=["pS0"], writes=["scr0"])
            p.op("dve", lambda: nc.vector.reciprocal(out=rd[:], in_=rd[:]), reads=["scr0"], writes=["scr0"])
            for it in range(nit):
                p.op("dve", lambda it=it: nc.vector.tensor_tensor(out=self.ef[it][:], in0=self.ef[it][:], in1=rd[:], op=ALU.mult), reads=[f"ef{it}", "scr0"], writes=[f"ef{it}"])
                p.op("pool", lambda it=it: nc.gpsimd.tensor_copy(out=self.pcb[it][:], in_=self.ef[it][:]), reads=[f"ef{it}"], writes=[f"pcb{it}"])
                p.op("pe", lambda it=it, gq=gq: nc.tensor.matmul(pimp[:], lhsT=self.ovl[:, it, :], rhs=self.ef[it][:], start=(gq == 0 and it == 0), stop=(gq == 3 and it == nit - 1)),
                     reads=["ovl", f"ef{it}"], writes=["pacc0"])
            for it in range(nit):
                p.op("pe", lambda it=it: nc.tensor.matmul(poc[0:64, :], lhsT=self.vcb[:, it, :], rhs=self.pcb[it][:], start=(it == 0), stop=(it == nit - 1)),
                     reads=["vcb", f"pcb{it}"], writes=["pS1"])
            p.op("pe", lambda: nc.tensor.matmul(pgt[0:64, :], lhsT=self.sel3[:, 0, :], rhs=self.gTs[:, m * 512:(m + 1) * 512], start=True, stop=True),
                 reads=["sel3", "gTs"], writes=["pS2"])
            p.op("act", lambda: nc.scalar.copy(out=self.ocs, in_=poc[0:64, :]), reads=["pS1"], writes=["scr1"])
            half, idx = gq // 2, (gq % 2) * 4 + m
            if half == 0:
                p.op("dve", lambda idx=idx: nc.vector.tensor_tensor(out=self.nd[0:64, idx, :], in0=self.ocs, in1=pgt[0:64, :], op=ALU.mult),
                     reads=["scr1", "pS2"], writes=[f"oc{gq}_{m}"])
            else:
                p.op("dve", lambda: nc.vector.tensor_tensor(out=self.tmpf[:], in0=self.ocs, in1=pgt[0:64, :], op=ALU.mult),
                     reads=["scr1", "pS2"], writes=["tmpf"])
                p.op("dve", lambda idx=idx: nc.vector.tensor_copy(out=self.nd[64:128, idx, :], in_=self.tmpf[:]), reads=["tmpf"], writes=[f"oc{gq}_{m}"])
        p.dma("sp", lambda: nc.sync.dma_start(out=self.amb[:], in_=dr["AM"][:, 4 * m:4 * m + 4, :]), writes=["amb"])
        p.dma("sp", lambda: nc.sync.dma_start(out=self.bab[:], in_=dr["BA"][:, 4 * m:4 * m + 4, :]), writes=["bab"])
        p.op("act", lambda: nc.scalar.copy(out=self.impS[:], in_=pimp[:]), reads=["pacc0"], writes=["impS"])
        for qs in range(4):
            p.op("pe", lambda qs=qs: nc.tensor.transpose(out=ptr[:, qs * 128:(qs + 1) * 128], in_=self.impS[:, qs * 128:(qs + 1) * 128], identity=self.identf[:]),
                 reads=["impS", "identf"], writes=["pacc1"])
        p.op("dve", lambda: nc.vector.tensor_tensor(out=self.score[:], in0=ptr[:].rearrange("p (a b) -> p a b", b=128), in1=self.amb[:], op=ALU.mult),
             reads=["pacc1", "amb"], writes=["score"])
        p.op("dve", lambda: nc.vector.tensor_tensor(out=self.score[:], in0=self.score[:], in1=self.bab[:], op=ALU.add), reads=["score", "bab"], writes=["score"])
        for qs in range(4):
            p.op("dve", lambda qs=qs: nc.vector.max(out=self.m16[:, qs, 0:8], in_=self.score[:, qs, :]), reads=["score"], writes=["m16"])
            p.op("dve", lambda qs=qs: nc.vector.match_replace(out=self.sc2[:], in_to_replace=self.m16[:, qs, 0:8], in_values=self.score[:, qs, :], imm_value=-1e30),
                 reads=["score", "m16"], writes=["sc2"])
            p.op("dve", lambda qs=qs: nc.vector.max(out=self.m16[:, qs, 8:16], in_=self.sc2[:]), reads=["sc2"], writes=["m16"])
            p.op("dve", lambda qs=qs: nc.vector.tensor_scalar(out=self.score[:, qs, :], in0=self.score[:, qs, :], scalar1=self.m16[:, qs, 15:16], scalar2=None, op0=ALU.is_ge),
                 reads=["score", "m16"], writes=["score"])
        p.op("dve", lambda: nc.vector.tensor_scalar(out=self.Msel[:], in0=self.score[:], scalar1=-1.0, scalar2=-NEGM, op0=ALU.add, op1=ALU.mult), reads=["score"], writes=["Msel"])
        ms = kv % 2
        for qs in range(4):
            p.op("pe", lambda qs=qs: nc.tensor.transpose(out=self.pmb[:, qs * 128:(qs + 1) * 128], in_=self.Msel[:, qs, :], identity=self.identb[:]),
                 reads=["Msel", "identb"], writes=["pmb"])
        p.op("act", lambda: nc.scalar.copy(out=self.MTb[ms][:, m * 512:(m + 1) * 512], in_=self.pmb[:, 0:512]), reads=["pmb"], writes=[f"MTb{ms}"])

    def mixer_C(self):
        nc, p, dr = self.nc, self.p, self.dr
        self.qc_i = 0
        p.dma("sp", lambda: nc.sync.dma_start(out=self.Eb[:], in_=dr["EC"]), writes=["Eb"])
        for nm in ("cmA", "cmB", "ovl", "gTs"):
            p.dma("sp", lambda nm=nm: nc.sync.dma_start(out=getattr(self, nm)[:], in_=dr[nm]), writes=[nm])
        p.dma("sp", lambda: nc.sync.dma_start(out=self.w2f[:], in_=dr["w2"]), writes=["w2f"])
        p.dma("sp", lambda: nc.sync.dma_start(out=self.peTf[:], in_=dr["peT"]), writes=["peTf"])
        p.op("dve", lambda: nc.vector.tensor_copy(out=self.w2b[:], in_=self.w2f[:]), reads=["w2f"], writes=["w2b"])
        p.op("dve", lambda: nc.vector.tensor_copy(out=self.peTb[:], in_=self.peTf[:]), reads=["peTf"], writes=["peTb"])
        p.op("pool", lambda: nc.gpsimd.memset(self.onesf[:], 1.0), writes=["onesf"])
        p.op("pool", lambda: nc.gpsimd.memset(self.hid[:], 0.0), writes=["hid"])
        for kv in self.kvs_C:
            self.compress(kv, 0)
            self.compress(kv, 1)
            for m in range(4):
                self.cmp_stage(kv, m)
            ms = kv % 2
            for gq in range(4):
                hc = kv * 4 + gq
                s = self.load_head(20 + hc, 20 + kv, 20 + hc)
                p.dma("sp", lambda hc=hc: nc.sync.dma_start(out=self.sel3[:], in_=dr["selg"][:, 3 * hc:3 * hc + 3, :]), writes=["sel3"])
                for m in range(4):
                    kts = list(range(0, 16 * m + 16))
                    acc = self.acc_i % 2
                    self.acc_i += 1

                    def post(gq=gq, m=m, acc=acc):
                        pg = self.pm[0]
                        p.op("dve", lambda: nc.vector.reciprocal(out=self.rden[:], in_=self.pacc[acc][64:128, :]), reads=[f"pacc{acc}"], writes=["rden"])
                        p.op("pe", lambda: nc.tensor.matmul(pg[0:64, :], lhsT=self.sel3[:, 1, :], rhs=self.gTs[:, m * 512:(m + 1) * 512], start=True, stop=True),
                             reads=["sel3", "gTs"], writes=["pm0"])
                        p.op("dve", lambda: nc.vector.tensor_tensor(out=self.tmpf[:], in0=self.pacc[acc][0:64, :], in1=self.rden[:], op=ALU.mult),
                             reads=[f"pacc{acc}", "rden"], writes=["tmpf"])
                        p.op("dve", lambda: nc.vector.tensor_tensor(out=self.nd[0:64, 8 + m, :], in0=self.tmpf[:], in1=pg[0:64, :], op=ALU.mult),
                             reads=["tmpf", "pm0"], writes=[f"res{m}"])
                        half, idx = gq // 2, (gq % 2) * 4 + m
                        if half == 0:
                            p.op("pool", lambda: nc.gpsimd.tensor_tensor(out=self.nd[0:64, 8 + m, :], in0=self.nd[0:64, 8 + m, :], in1=self.nd[0:64, idx, :], op=ALU.add),
                                 reads=[f"res{m}", f"oc{gq}_{m}"], writes=[f"res{m}"])
                        else:
                            p.op("dve", lambda: nc.vector.tensor_copy(out=self.tmpf[:], in_=self.nd[64:128, idx, :]), reads=[f"oc{gq}_{m}"], writes=["tmpf"])
                            p.op("pool", lambda: nc.gpsimd.tensor_tensor(out=self.nd[0:64, 8 + m, :], in0=self.nd[0:64, 8 + m, :], in1=self.tmpf[:], op=ALU.add),
                                 reads=[f"res{m}", "tmpf"], writes=[f"res{m}"])
                    self.add_units(s, m, kts, 16 * m - 12, 20 + hc, acc, mask=(128, ms), post=post)
                self.flush_units()
                s = self.load_head(20 + hc, 23 + kv, 32 + hc)
                for m in range(4):
                    kts = list(range(max(0, 16 * m - 4), 16 * m + 16))
                    acc = self.acc_i % 2
                    self.acc_i += 1

                    def post(hc=hc, m=m, acc=acc):
                        pg = self.pm[1]
                        p.op("dve", lambda: nc.vector.reciprocal(out=self.rden[:], in_=self.pacc[acc][64:128, :]), reads=[f"pacc{acc}"], writes=["rden"])
                        p.op("pe", lambda: nc.tensor.matmul(pg[0:64, :], lhsT=self.sel3[:, 2, :], rhs=self.gTs[:, m * 512:(m + 1) * 512], start=True, stop=True),
                             reads=["sel3", "gTs"], writes=["pS3"])
                        p.op("dve", lambda: nc.vector.tensor_tensor(out=self.tmpf[:], in0=self.pacc[acc][0:64, :], in1=self.rden[:], op=ALU.mult),
                             reads=[f"pacc{acc}", "rden"], writes=["tmpf"])
                        p.op("dve", lambda: nc.vector.tensor_tensor(out=self.tmpf[:], in0=self.tmpf[:], in1=pg[0:64, :], op=ALU.mult),
                             reads=["tmpf", "pS3"], writes=["tmpf"])
                        o = self.ost_i % 3
                        self.ost_i += 1
                        p.op("dve", lambda: nc.vector.tensor_tensor(out=self.ost[o][:], in0=self.tmpf[:], in1=self.nd[0:64, 8 + m, :], op=ALU.add),
                             reads=["tmpf", f"res{m}"], writes=[f"ost{o}"])
                        dst = self.dr["OT"][(20 + hc) * 64:(21 + hc) * 64, m * 512:(m + 1) * 512]
                        self.out_ops.append(p.dma("pool", lambda: nc.gpsimd.dma_start(out=dst, in_=self.ost[o][:]), reads=[f"ost{o}"]))
                    self.add_units(s, m, kts, 0, 32 + hc, acc, post=post)
                self.flush_units()


def dram_p2(nc):
    dr = {}
    I = lambda name, shape, dt: nc.dram_tensor(name, shape, dt, kind="ExternalInput").ap()
    dr["qT"] = I("qT", [32, 64, NT], BF16)
    dr["kTf"] = I("kTf", [26, 64, S], BF16)
    dr["vf"] = I("vf", [26, 128, 64, 64], BF16)
    dr["G"] = I("G", [44, GL], F32)
    dr["cb"] = I("cb", [128, 44], F32)
    dr["negm"] = I("negm", [128, 16, 32], F32)
    dr["ownm"] = I("ownm", [128, 16, 32], F32)
    dr["EB"] = I("EB", [32, S], BF16)
    dr["cmpTf"] = I("cmpTf", [6, 64, S], BF16)
    dr["w1"] = I("w1", [2, 64, 32, 128], F32)
    dr["w2"] = I("w2", [128, 2, 64], F32)
    dr["peT"] = I("peT", [64, 2, 32, 2], F32)
    dr["EC"] = I("EC", [128, S], BF16)
    dr["AM"] = I("AM", [128, 16, 128], F32)
    dr["BA"] = I("BA", [128, 16, 128], F32)
    dr["cmA"] = I("cmA", [128, 512], F32)
    dr["cmB"] = I("cmB", [128, 512], F32)
    dr["ovl"] = I("ovl", [128, 4, 128], F32)
    dr["selg"] = I("selg", [36, 36, 64], F32)
    dr["gTs"] = I("gTs", [36, NT], F32)
    dr["OT"] = nc.dram_tensor("OT", [2048, NT], BF16, kind="ExternalOutput").ap()
    return dr


def build_p2(**kw):
    nc = bass.Bass("TRN2", target_bir_lowering=False)
    dr = dram_p2(nc)
    with contextlib.ExitStack() as st:
        T = lambda name, shape, dt: st.enter_context(nc.sbuf_tensor("s_" + name, shape, dt))
        PS = lambda name, shape, dt: st.enter_context(nc.psum_tensor("p_" + name, shape, dt))
        p = Prog(nc)
        P = P2(nc, p, T, PS, dr, **kw)
        P.setup()
        P.mixer_A()
        P.mixer_B()
        P.mixer_C()
        p.emit(final_wait_ops=P.out_ops)
    return nc

import contextlib
import numpy as np

D = 2048
NT = 2048
DFF = 5632
EPS = 1e-6
TW = 514


def build_p3a():
    nc = bass.Bass("TRN2", target_bir_lowering=False)
    I = lambda name, shape, dt: nc.dram_tensor(name, shape, dt, kind="ExternalInput").ap()
    xT = I("xT", [D, 4, TW], F32)
    OT = I("OT", [D, 4, TW], BF16)
    w = I("w", [D, D], F32)
    gn = I("gn", [128, 16], F32)
    x1T = nc.dram_tensor("x1T", [D, 4, TW], F32, kind="ExternalOutput").ap()
    hT = nc.dram_tensor("hT", [128, 16, 4, TW], BF16, kind="ExternalOutput").ap()
    with contextlib.ExitStack() as st:
        T = lambda name, shape, dt: st.enter_context(nc.sbuf_tensor("s_" + name, shape, dt))
        PS = lambda name, shape, dt: st.enter_context(nc.psum_tensor("p_" + name, shape, dt))
        p = Prog(nc)
        outs = []
        OTb = T("OTb", [128, 16, 4, TW], BF16)
        gsb = T("gsb", [128, 16], F32)
        ones = T("ones", [128, 128], F32)
        wst = [T(f"wst{i}", [128, 16, 128], F32) for i in range(2)]
        wbf = [T(f"wbf{i}", [128, 16, 128], BF16) for i in range(2)]
        xc = [T(f"xc{i}", [128, 4, TW], F32) for i in range(2)]
        x1c = [T(f"x1c{i}", [128, 4, TW], F32) for i in range(2)]
        sqt = T("sqt", [128, 4, TW], F32)
        accsq = T("accsq", [128, 4, TW], F32)
        rstd = T("rstd", [128, 4, TW], F32)
        hc = [T(f"hc{i}", [128, 4, TW], BF16) for i in range(2)]
        pacc = [PS(f"pacc{i}", [128, 512], F32) for i in range(4)]
        ph = PS("ph", [128, 512], F32)
        pss = [PS(f"pss{i}", [128, 512], F32) for i in range(2)]

        p.dma("sp", lambda: nc.sync.dma_start(out=gsb[:], in_=gn), writes=["gsb"])
        p.op("pool", lambda: nc.gpsimd.memset(ones[:], 1.0), writes=["ones"])
        p.op("pool", lambda: nc.gpsimd.memset(accsq[:], 0.0), writes=["accsq"])
        OTv = OT.rearrange("(k p) m t -> p k m t", p=128)
        for k4 in range(4):
            p.dma("sp", lambda k4=k4: nc.sync.dma_start(out=OTb[:, k4 * 4:(k4 + 1) * 4], in_=OTv[:, k4 * 4:(k4 + 1) * 4]), writes=[f"OTb{k4}"])
        OTres = [f"OTb{k4}" for k4 in range(4)]
        ai = 0
        for c in range(16):
            s = c % 2
            src = w[:, c * 128:(c + 1) * 128].rearrange("(k p) n -> p k n", p=128)
            p.dma("sp", lambda s=s, src=src: nc.sync.dma_start(out=wst[s][:], in_=src), writes=[f"wst{s}"])
            p.op("act", lambda s=s: nc.scalar.copy(out=wbf[s][:, 0:8], in_=wst[s][:, 0:8]), reads=[f"wst{s}"], writes=[f"wbfa{s}"])
            p.op("pool", lambda s=s: nc.gpsimd.tensor_copy(out=wbf[s][:, 8:16], in_=wst[s][:, 8:16]), reads=[f"wst{s}"], writes=[f"wbfb{s}"])
            p.dma("sp", lambda s=s, c=c: nc.sync.dma_start(out=xc[s][:], in_=xT[c * 128:(c + 1) * 128]), writes=[f"xc{s}"])
            for m in range(4):
                a = ai % 4
                ai += 1
                for k in range(16):
                    p.op("pe", lambda a=a, s=s, k=k, m=m: nc.tensor.matmul(pacc[a][:], lhsT=wbf[s][:, k, :], rhs=OTb[:, k, m, 2:TW], start=(k == 0), stop=(k == 15)),
                         reads=[f"wbfa{s}", f"wbfb{s}"] + OTres, writes=[f"pacc{a}"])
                p.op("dve", lambda a=a, s=s, m=m: nc.vector.tensor_tensor(out=x1c[s][:, m, 2:TW], in0=pacc[a][:], in1=xc[s][:, m, 2:TW], op=ALU.add),
                     reads=[f"pacc{a}", f"xc{s}"], writes=[f"x1c{s}"])
            for k in range(16):
                p.op("pe", lambda s=s, k=k: nc.tensor.matmul(ph[:, 0:8], lhsT=wbf[s][:, k, :], rhs=OTb[:, k, :, 0:2], start=(k == 0), stop=(k == 15)),
                     reads=[f"wbfa{s}", f"wbfb{s}"] + OTres, writes=["ph"])
            p.op("dve", lambda s=s: nc.vector.tensor_tensor(out=x1c[s][:, :, 0:2], in0=ph[:, 0:8].rearrange("p (m h) -> p m h", h=2), in1=xc[s][:, :, 0:2], op=ALU.add),
                 reads=["ph", f"xc{s}"], writes=[f"x1c{s}"])
            outs.append(p.dma("pool", lambda s=s, c=c: nc.gpsimd.dma_start(out=x1T[c * 128:(c + 1) * 128], in_=x1c[s][:]), reads=[f"x1c{s}"], writes=[f"x1T{c}"]))
            p.op("act", lambda s=s: nc.scalar.activation(out=sqt[:], in_=x1c[s][:], func=AF.Square), reads=[f"x1c{s}"], writes=["sqt"])
            p.op("pool", lambda: nc.gpsimd.tensor_tensor(out=accsq[:], in0=accsq[:], in1=sqt[:], op=ALU.add), reads=["sqt", "accsq"], writes=["accsq"])
        for m in range(4):
            q = m % 2
            p.op("pe", lambda q=q, m=m: nc.tensor.matmul(pss[q][:], lhsT=ones[:], rhs=accsq[:, m, 2:TW], start=True, stop=True), reads=["ones", "accsq"], writes=[f"pss{q}"])
            p.op("act", lambda q=q, m=m: nc.scalar.activation(out=rstd[:, m, 2:TW], in_=pss[q][:], func=AF.Sqrt, scale=1.0 / D, bias=EPS), reads=[f"pss{q}"], writes=["rstd"])
        p.op("pe", lambda: nc.tensor.matmul(ph[:, 0:8], lhsT=ones[:], rhs=accsq[:, :, 0:2], start=True, stop=True), reads=["ones", "accsq"], writes=["ph"])
        p.op("act", lambda: nc.scalar.activation(out=rstd[:, :, 0:2], in_=ph[:, 0:8].rearrange("p (m h) -> p m h", h=2), func=AF.Sqrt, scale=1.0 / D, bias=EPS), reads=["ph"], writes=["rstd"])
        p.op("dve", lambda: nc.vector.reciprocal(out=rstd[:], in_=rstd[:]), reads=["rstd"], writes=["rstd"])
        for c in range(16):
            s = c % 2
            p.dma("sp", lambda s=s, c=c: nc.sync.dma_start(out=x1c[s][:], in_=x1T[c * 128:(c + 1) * 128]), reads=[f"x1T{c}"], writes=[f"x1c{s}"])
            p.op("dve", lambda s=s, c=c: nc.vector.scalar_tensor_tensor(out=hc[s][:], in0=x1c[s][:], scalar=gsb[:, c:c + 1], in1=rstd[:], op0=ALU.mult, op1=ALU.mult),
                 reads=[f"x1c{s}", "gsb", "rstd"], writes=[f"hc{s}"])
            outs.append(p.dma("pool", lambda s=s, c=c: nc.gpsimd.dma_start(out=hT[:, c], in_=hc[s][:]), reads=[f"hc{s}"]))
        p.emit(final_wait_ops=outs)
    return nc


def emit_p3b(nc, p, T, PS, hT, x1get, wu, wd, cw, cbv, x2T, relayout=False):
    outs = []
    hTh = T("hTh", [128, 16, 2, TW], BF16)
    actT = T("actT", [128, 44, 1024], BF16)
    wst = [T(f"wst{i}", [128, 4096], F32) for i in range(2)]
    wbf = [T(f"wbf{i}", [128, 4096], BF16) for i in range(2)]
    cws = T("cws", [128, 88, 3], F32)
    cbs = T("cbs", [128, 88], F32)
    ua = [T(f"ua{i}", [128, TW], F32) for i in range(2)]
    ug = [T(f"ug{i}", [128, TW], F32) for i in range(2)]
    ya = [T(f"ya{i}", [128, 512], F32) for i in range(2)]
    yg = [T(f"yg{i}", [128, 512], F32) for i in range(2)]
    sg = [T(f"sg{i}", [128, 512], F32) for i in range(2)]
    x1c = [T(f"x1c{i}", [128, 2, 512], F32) for i in range(2)]
    pa = [PS(f"pa{i}", [128, 512], F32) for i in range(2)]
    pg = [PS(f"pg{i}", [128, 512], F32) for i in range(2)]
    ph = PS("ph", [128, 512], F32)
    pd = [PS(f"pd{i}", [128, 512], F32) for i in range(2)]
    p.dma("sp", lambda: nc.sync.dma_start(out=cws[:], in_=cw), writes=["cws"])
    p.dma("sp", lambda: nc.sync.dma_start(out=cbs[:], in_=cbv), writes=["cbs"])
    wi = 0
    ui = 0
    for half in range(2):
        p.dma("sp", lambda half=half: nc.sync.dma_start(out=hTh[:], in_=hT[:, :, 2 * half:2 * half + 2, :]), writes=["hTh"])
        for c in range(44):
            s = wi % 2
            wi += 1
            wv = wst[s][:].rearrange("p (k n) -> p k n", n=256)
            wb = wbf[s][:].rearrange("p (k n) -> p k n", n=256)
            if relayout:
                srcp = wu[c].rearrange("p (k n) -> p k n", n=256)
                p.dma("sp", lambda wv=wv, srcp=srcp: nc.sync.dma_start(out=wv, in_=srcp), writes=[f"wst{s}_0", f"wst{s}_1"])
            else:
                for part in range(2):
                    col0 = part * DFF + c * 128
                    src = wu[:, col0:col0 + 128].rearrange("(k p) n -> p k n", p=128)
                    p.dma("sp", lambda wv=wv, src=src, part=part: nc.sync.dma_start(out=wv[:, :, part * 128:(part + 1) * 128], in_=src), writes=[f"wst{s}_{part}"])
            p.op("act", lambda wv=wv, wb=wb: nc.scalar.copy(out=wb[:, 0:6], in_=wv[:, 0:6]), reads=[f"wst{s}_0", f"wst{s}_1"], writes=[f"wbfa{s}"])
            p.op("pool", lambda wv=wv, wb=wb: nc.gpsimd.tensor_copy(out=wb[:, 6:16], in_=wv[:, 6:16]), reads=[f"wst{s}_0", f"wst{s}_1"], writes=[f"wbfb{s}"])
            wres = [f"wbfa{s}", f"wbfb{s}"]
            for k in range(16):
                p.op("pe", lambda k=k, wb=wb: nc.tensor.matmul(ph[:, 0:4], lhsT=wb[:, k, 0:128], rhs=hTh[:, k, :, 0:2], start=(k == 0), stop=(k == 15)),
                     reads=wres + ["hTh"], writes=["ph"])
            for k in range(16):
                p.op("pe", lambda k=k, wb=wb: nc.tensor.matmul(ph[:, 4:8], lhsT=wb[:, k, 128:256], rhs=hTh[:, k, :, 0:2], start=(k == 0), stop=(k == 15)),
                     reads=wres + ["hTh"], writes=["ph"])
            for tt in range(2):
                u = ui % 2
                ui += 1
                for k in range(16):
                    p.op("pe", lambda u=u, k=k, tt=tt, wb=wb: nc.tensor.matmul(pa[u][:], lhsT=wb[:, k, 0:128], rhs=hTh[:, k, tt, 2:TW], start=(k == 0), stop=(k == 15)),
                         reads=wres + ["hTh"], writes=[f"pa{u}"])
                for k in range(16):
                    p.op("pe", lambda u=u, k=k, tt=tt, wb=wb: nc.tensor.matmul(pg[u][:], lhsT=wb[:, k, 128:256], rhs=hTh[:, k, tt, 2:TW], start=(k == 0), stop=(k == 15)),
                         reads=wres + ["hTh"], writes=[f"pg{u}"])
                p.op("act", lambda u=u: nc.scalar.copy(out=ua[u][:, 2:TW], in_=pa[u][:]), reads=[f"pa{u}"], writes=[f"ua{u}"])
                p.op("act", lambda u=u: nc.scalar.copy(out=ug[u][:, 2:TW], in_=pg[u][:]), reads=[f"pg{u}"], writes=[f"ug{u}"])
                p.op("act", lambda u=u, tt=tt: nc.scalar.copy(out=ua[u][:, 0:2], in_=ph[:, 2 * tt:2 * tt + 2]), reads=["ph"], writes=[f"ua{u}"])
                p.op("act", lambda u=u, tt=tt: nc.scalar.copy(out=ug[u][:, 0:2], in_=ph[:, 4 + 2 * tt:6 + 2 * tt]), reads=["ph"], writes=[f"ug{u}"])
                for (ub, yb, nm, ch) in ((ua, ya, "a", c), (ug, yg, "g", 44 + c)):
                    p.op("dve", lambda u=u, ub=ub, yb=yb, ch=ch: nc.vector.tensor_scalar(out=yb[u][:], in0=ub[u][:, 2:TW], scalar1=cws[:, ch, 0:1], scalar2=cbs[:, ch:ch + 1], op0=ALU.mult, op1=ALU.add),
                         reads=[f"u{nm}{u}", "cws", "cbs"], writes=[f"y{nm}{u}"])
                    p.op("dve", lambda u=u, ub=ub, yb=yb, ch=ch: nc.vector.scalar_tensor_tensor(out=yb[u][:], in0=ub[u][:, 1:TW - 1], scalar=cws[:, ch, 1:2], in1=yb[u][:], op0=ALU.mult, op1=ALU.add),
                         reads=[f"u{nm}{u}", "cws", f"y{nm}{u}"], writes=[f"y{nm}{u}"])
                    p.op("dve", lambda u=u, ub=ub, yb=yb, ch=ch: nc.vector.scalar_tensor_tensor(out=yb[u][:], in0=ub[u][:, 0:TW - 2], scalar=cws[:, ch, 2:3], in1=yb[u][:], op0=ALU.mult, op1=ALU.add),
                         reads=[f"u{nm}{u}", "cws", f"y{nm}{u}"], writes=[f"y{nm}{u}"])
                p.op("act", lambda u=u: nc.scalar.activation(out=sg[u][:], in_=yg[u][:], func=AF.Silu), reads=[f"yg{u}"], writes=[f"sg{u}"])
                p.op("pool", lambda u=u, c=c, tt=tt: nc.gpsimd.tensor_tensor(out=actT[:, c, tt * 512:(tt + 1) * 512], in0=sg[u][:], in1=ya[u][:], op=ALU.mult),
                     reads=[f"sg{u}", f"ya{u}"], writes=[f"actT{c}"])
        actres = [f"actT{c}" for c in range(44)]
        for cc in range(16):
            for piece in range(2):
                s = wi % 2
                wi += 1
                wv = wst[s][:, 0:22 * 128].rearrange("p (k n) -> p k n", n=128)
                wb = wbf[s][:, 0:22 * 128].rearrange("p (k n) -> p k n", n=128)
                if relayout:
                    src = wd[cc, piece].rearrange("p (k n) -> p k n", n=128)
                else:
                    src = wd[piece * 2816:(piece + 1) * 2816, cc * 128:(cc + 1) * 128].rearrange("(k p) n -> p k n", p=128)
                p.dma("sp", lambda wv=wv, src=src: nc.sync.dma_start(out=wv, in_=src), writes=[f"wst{s}_0", f"wst{s}_1"])
                p.op("act", lambda wv=wv, wb=wb: nc.scalar.copy(out=wb[:, 0:9], in_=wv[:, 0:9]), reads=[f"wst{s}_0", f"wst{s}_1"], writes=[f"wbfa{s}"])
                p.op("pool", lambda wv=wv, wb=wb: nc.gpsimd.tensor_copy(out=wb[:, 9:22], in_=wv[:, 9:22]), reads=[f"wst{s}_0", f"wst{s}_1"], writes=[f"wbfb{s}"])
                for tt in range(2):
                    for k in range(22):
                        c = piece * 22 + k
                        p.op("pe", lambda wb=wb, k=k, c=c, tt=tt: nc.tensor.matmul(pd[tt][:], lhsT=wb[:, k, :], rhs=actT[:, c, tt * 512:(tt + 1) * 512], start=(c == 0), stop=(c == 43)),
                             reads=[f"wbfa{s}", f"wbfb{s}", f"actT{c}"], writes=[f"pd{tt}"])
            xs = cc % 2
            p.dma("sp", lambda xs=xs, cc=cc, half=half: nc.sync.dma_start(out=x1c[xs][:], in_=x1get(cc, half)), writes=[f"x1c{xs}"])
            for tt in range(2):
                p.op("dve", lambda xs=xs, tt=tt: nc.vector.tensor_tensor(out=x1c[xs][:, tt, :], in0=pd[tt][:], in1=x1c[xs][:, tt, :], op=ALU.add),
                     reads=[f"pd{tt}", f"x1c{xs}"], writes=[f"x1c{xs}"])
            dst = x2T[cc * 128:(cc + 1) * 128, half * 1024:(half + 1) * 1024].rearrange("p (t n) -> p t n", n=512)
            outs.append(p.dma("pool", lambda xs=xs, dst=dst: nc.gpsimd.dma_start(out=dst, in_=x1c[xs][:]), reads=[f"x1c{xs}"]))
    return outs


def build_p3b():
    nc = bass.Bass("TRN2", target_bir_lowering=False)
    I = lambda name, shape, dt: nc.dram_tensor(name, shape, dt, kind="ExternalInput").ap()
    hT = I("hT", [128, 16, 4, TW], BF16)
    x1T = I("x1T", [D, 4, TW], F32)
    wu = I("wu", [D, 2 * DFF], F32)
    wd = I("wd", [DFF, D], F32)
    cw = I("cw", [128, 88, 3], F32)
    cbv = I("cbv", [128, 88], F32)
    x2T = nc.dram_tensor("x2T", [D, NT], F32, kind="ExternalOutput").ap()
    with contextlib.ExitStack() as st:
        T = lambda name, shape, dt: st.enter_context(nc.sbuf_tensor("s_" + name, shape, dt))
        PS = lambda name, shape, dt: st.enter_context(nc.psum_tensor("p_" + name, shape, dt))
        p = Prog(nc)
        x1get = lambda cc, half: x1T[cc * 128:(cc + 1) * 128, 2 * half:2 * half + 2, 2:TW]
        outs = emit_p3b(nc, p, T, PS, hT, x1get, wu, wd, cw, cbv, x2T)
        p.emit(final_wait_ops=outs)
    return nc


def emit_kf(nc, p, T, PS, xT, gn, yT):
    outs = []
    gsb = T("gsb", [128, 16], F32)
    ones = T("ones", [128, 128], F32)
    xs = [T(f"xs{i}", [128, 16, 512], F32) for i in range(2)]
    ys = [T(f"ys{i}", [128, 16, 512], F32) for i in range(2)]
    sq = [T(f"sq{i}", [128, 512], F32) for i in range(2)]
    rstd = T("rstd", [128, 512], F32)
    pss = PS("pss", [128, 512], F32)
    p.dma("sp", lambda: nc.sync.dma_start(out=gsb[:], in_=gn), writes=["gsb"])
    p.op("pool", lambda: nc.gpsimd.memset(ones[:], 1.0), writes=["ones"])
    xv = xT.rearrange("(k p) t -> p k t", p=128)
    yv = yT.rearrange("(k p) t -> p k t", p=128)
    for m in range(4):
        s = m % 2
        p.dma("sp", lambda m=m, s=s: nc.sync.dma_start(out=xs[s][:], in_=xv[:, :, m * 512:(m + 1) * 512]), writes=[f"xs{s}"])
        for k in range(16):
            q = k % 2
            p.op("act", lambda s=s, k=k, q=q: nc.scalar.activation(out=sq[q][:], in_=xs[s][:, k, :], func=AF.Square), reads=[f"xs{s}"], writes=[f"sq{q}"])
            p.op("pe", lambda q=q, k=k: nc.tensor.matmul(pss[:], lhsT=ones[:], rhs=sq[q][:], start=(k == 0), stop=(k == 15)), reads=["ones", f"sq{q}"], writes=["pss"])
        p.op("act", lambda: nc.scalar.activation(out=rstd[:], in_=pss[:], func=AF.Sqrt, scale=1.0 / D, bias=EPS), reads=["pss"], writes=["rstd"])
        p.op("dve", lambda: nc.vector.reciprocal(out=rstd[:], in_=rstd[:]), reads=["rstd"], writes=["rstd"])
        for k in range(16):
            p.op("dve", lambda s=s, k=k: nc.vector.scalar_tensor_tensor(out=ys[s][:, k, :], in0=xs[s][:, k, :], scalar=gsb[:, k:k + 1], in1=rstd[:], op0=ALU.mult, op1=ALU.mult),
                 reads=[f"xs{s}", "gsb", "rstd"], writes=[f"ys{s}"])
        outs.append(p.dma("pool", lambda m=m, s=s: nc.gpsimd.dma_start(out=yv[:, :, m * 512:(m + 1) * 512], in_=ys[s][:]), reads=[f"ys{s}"]))
    return outs


def build_kf():
    nc = bass.Bass("TRN2", target_bir_lowering=False)
    xT = nc.dram_tensor("xT", [D, NT], F32, kind="ExternalInput").ap()
    gn = nc.dram_tensor("gn", [128, 16], F32, kind="ExternalInput").ap()
    yT = nc.dram_tensor("yT", [D, NT], F32, kind="ExternalOutput").ap()
    with contextlib.ExitStack() as st:
        T = lambda name, shape, dt: st.enter_context(nc.sbuf_tensor("s_" + name, shape, dt))
        PS = lambda name, shape, dt: st.enter_context(nc.psum_tensor("p_" + name, shape, dt))
        p = Prog(nc)
        outs = emit_kf(nc, p, T, PS, xT, gn, yT)
        p.emit(final_wait_ops=outs)
    return nc

import contextlib, math
import numpy as np

DEPTH = 4
RG = [[0, 1, 2, 3], [4, 5, 6, 7]]


def ec_const_nat():
    c = np.arange(S)
    return (c[None, :] // 64 == np.arange(128)[:, None]).astype(np.float32)


def halo_coef(j):
    co = np.zeros((128, 5), np.float32)
    if j >= 1:
        co[:, j - 1] = 1.0
    else:
        co[:, 4] = 1.0
    return co


def emit_p3a_f(nc, p, T, PS, xT, OT, w, gn, x1T, hT, ht_s):
    OTb = T("OTb", [128, 16, NT], BF16)
    gsb = T("gsb", [128, 16], F32)
    ones = T("ones", [128, 128], F32)
    wst = [T(f"wst{i}", [128, 16, 128], F32) for i in range(2)]
    wbf = [T(f"wbf{i}", [128, 16, 128], BF16) for i in range(2)]
    xc = [T(f"xc{i}", [128, NT], F32) for i in range(2)]
    x1c = [T(f"x1c{i}", [128, NT], F32) for i in range(2)]
    sqt = T("sqt", [128, NT], F32)
    accsq = T("accsq", [128, NT], F32)
    rstd = T("rstd", [128, NT], F32)
    hc = [T(f"hc{i}", [128, NT], BF16) for i in range(2)]
    pacc = [PS(f"pacc{i}", [128, 512], F32) for i in range(4)]
    pss = [PS(f"pss{i}", [128, 512], F32) for i in range(2)]
    p.dma("sp", lambda: nc.sync.dma_start(out=gsb[:], in_=gn), writes=["gsb"])
    p.op("pool", lambda: nc.gpsimd.memset(ones[:], 1.0), writes=["ones"])
    p.op("pool", lambda: nc.gpsimd.memset(accsq[:], 0.0), writes=["accsq"])
    OTv = OT.rearrange("(k p) t -> p k t", p=128)
    for k4 in range(4):
        p.dma("sp", lambda k4=k4: nc.sync.dma_start(out=OTb[:, k4 * 4:(k4 + 1) * 4], in_=OTv[:, k4 * 4:(k4 + 1) * 4]), writes=[f"OTb{k4}"])
    OTres = [f"OTb{k4}" for k4 in range(4)]
    ai = 0
    for c in range(16):
        s = c % 2
        src = w[c].rearrange("p (k n) -> p k n", n=128)
        p.dma("sp", lambda s=s, src=src: nc.sync.dma_start(out=wst[s][:], in_=src), writes=[f"wst{s}"])
        p.op("act", lambda s=s: nc.scalar.copy(out=wbf[s][:, 0:8], in_=wst[s][:, 0:8]), reads=[f"wst{s}"], writes=[f"wbfa{s}"])
        p.op("pool", lambda s=s: nc.gpsimd.tensor_copy(out=wbf[s][:, 8:16], in_=wst[s][:, 8:16]), reads=[f"wst{s}"], writes=[f"wbfb{s}"])
        p.dma("sp", lambda s=s, c=c: nc.sync.dma_start(out=xc[s][:], in_=xT[c * 128:(c + 1) * 128]), writes=[f"xc{s}"])
        for m in range(4):
            a = ai % 4
            ai += 1
            for k in range(16):
                p.op("pe", lambda a=a, s=s, k=k, m=m: nc.tensor.matmul(pacc[a][:], lhsT=wbf[s][:, k, :], rhs=OTb[:, k, m * 512:(m + 1) * 512], start=(k == 0), stop=(k == 15)),
                     reads=[f"wbfa{s}", f"wbfb{s}"] + OTres, writes=[f"pacc{a}"])
            p.op("dve", lambda a=a, s=s, m=m: nc.vector.tensor_tensor(out=x1c[s][:, m * 512:(m + 1) * 512], in0=pacc[a][:], in1=xc[s][:, m * 512:(m + 1) * 512], op=ALU.add),
                 reads=[f"pacc{a}", f"xc{s}"], writes=[f"x1c{s}"])
        p.dma("pool", lambda s=s, c=c: nc.gpsimd.dma_start(out=x1T[c * 128:(c + 1) * 128], in_=x1c[s][:]), reads=[f"x1c{s}"], writes=[f"x1T{c}"])
        p.op("act", lambda s=s: nc.scalar.activation(out=sqt[:], in_=x1c[s][:], func=AF.Square), reads=[f"x1c{s}"], writes=["sqt"])
        p.op("pool", lambda: nc.gpsimd.tensor_tensor(out=accsq[:], in0=accsq[:], in1=sqt[:], op=ALU.add), reads=["sqt", "accsq"], writes=["accsq"])
    for m in range(4):
        q = m % 2
        p.op("pe", lambda q=q, m=m: nc.tensor.matmul(pss[q][:], lhsT=ones[:], rhs=accsq[:, m * 512:(m + 1) * 512], start=True, stop=True), reads=["ones", "accsq"], writes=[f"pss{q}"])
        p.op("act", lambda q=q, m=m: nc.scalar.activation(out=rstd[:, m * 512:(m + 1) * 512], in_=pss[q][:], func=AF.Sqrt, scale=1.0 / D, bias=EPS), reads=[f"pss{q}"], writes=["rstd"])
    p.op("dve", lambda: nc.vector.reciprocal(out=rstd[:], in_=rstd[:]), reads=["rstd"], writes=["rstd"])
    for c in range(16):
        s = c % 2
        p.dma("sp", lambda s=s, c=c: nc.sync.dma_start(out=x1c[s][:], in_=x1T[c * 128:(c + 1) * 128]), reads=[f"x1T{c}"], writes=[f"x1c{s}"])
        p.op("dve", lambda s=s, c=c: nc.vector.scalar_tensor_tensor(out=hc[s][:], in0=x1c[s][:], scalar=gsb[:, c:c + 1], in1=rstd[:], op0=ALU.mult, op1=ALU.mult),
             reads=[f"x1c{s}", "gsb", "rstd"], writes=[f"hc{s}"])
        p.dma("pool", lambda s=s, c=c: nc.gpsimd.dma_start(out=hT[:, c, :, 2:TW], in_=hc[s][:].rearrange("p (m t) -> p m t", t=512)), reads=[f"hc{s}"])
        p.dma("pool", lambda s=s, c=c: nc.gpsimd.dma_start(out=ht_s[c * 128:(c + 1) * 128, :].rearrange("p (m h) -> p m h", h=2),
                                                            in_=hc[s][:].rearrange("p (m t) -> p m t", t=512)[:, :, 510:512]), reads=[f"hc{s}"])


def emit_halo(nc, p, T, PS, ht_g, hco_d, hT):
    Hg = T("Hg", [128, 4, 16, 8], BF16)
    hco = T("hco", [128, 5], F32)
    acc = T("hacc", [128, 4, 16, 2], F32)
    hb = T("hb", [128, 4, 16, 2], BF16)
    p.dma("sp", lambda: nc.sync.dma_start(out=Hg[:], in_=ht_g.rearrange("(r k p) c -> p r k c", r=4, p=128)), writes=["Hg"])
    p.dma("sp", lambda: nc.sync.dma_start(out=hco[:], in_=hco_d), writes=["hco"])
    for m in range(4):
        p.op("dve", lambda m=m: nc.vector.tensor_scalar(out=acc[:, m], in0=Hg[:, 0, :, 2 * m:2 * m + 2], scalar1=hco[:, 0:1], scalar2=None, op0=ALU.mult),
             reads=["Hg", "hco"], writes=[f"hacc{m}"])
        for r in range(1, 4):
            p.op("dve", lambda m=m, r=r: nc.vector.scalar_tensor_tensor(out=acc[:, m], in0=Hg[:, r, :, 2 * m:2 * m + 2], scalar=hco[:, r:r + 1], in1=acc[:, m], op0=ALU.mult, op1=ALU.add),
                 reads=["Hg", "hco", f"hacc{m}"], writes=[f"hacc{m}"])
        if m >= 1:
            p.op("dve", lambda m=m: nc.vector.scalar_tensor_tensor(out=acc[:, m], in0=Hg[:, 3, :, 2 * m - 2:2 * m], scalar=hco[:, 4:5], in1=acc[:, m], op0=ALU.mult, op1=ALU.add),
                 reads=["Hg", "hco", f"hacc{m}"], writes=[f"hacc{m}"])
        p.op("dve", lambda m=m: nc.vector.tensor_copy(out=hb[:, m], in_=acc[:, m]), reads=[f"hacc{m}"], writes=[f"hb{m}"])
        p.dma("sp", lambda m=m: nc.sync.dma_start(out=hT[:, :, m, 0:2], in_=hb[:, m]), reads=[f"hb{m}"])


def build_fused(depth=DEPTH, debug=False, stop=10**9):
    nc = bass.Bass("TRN2", target_bir_lowering=False)
    I = lambda name, shape, dt: nc.dram_tensor(name, shape, dt, kind="ExternalInput").ap()
    N = lambda name, shape, dt, **kw: nc.dram_tensor(name, shape, dt, kind="Internal", **kw).ap()
    xT0 = I("xT0", [D, NT], F32)
    NTC = len(t_chunks()) + 1
    NNC = len(n_chunks())
    wiT = I("wiT", [depth, NTC, 128, 2048], F32)
    wiN = I("wiN", [depth, NNC, 128, 4096], F32)
    w_out = I("woR", [depth, 16, 128, 2048], F32)
    w_up = I("wuR", [depth, 44, 128, 4096], F32)
    w_down = I("wdR", [depth, 16, 2, 128, 2816], F32)
    gn_attn = I("gn_attn", [depth, 128, 16], F32)
    gn_mlp = I("gn_mlp", [depth, 128, 16], F32)
    gn_fin = I("gn_fin", [128, 16], F32)
    w1d = I("w1", [depth, 2, 64, 32, 128], F32)
    w2d = I("w2", [depth, 128, 2, 64], F32)
    peTd = I("peT", [depth, 64, 2, 32, 2], F32)
    cwd = I("cw", [depth, 128, 88, 3], F32)
    cbd = I("cbv", [depth, 128, 88], F32)
    hco_d = I("hco", [128, 5], F32)
    consts = {}
    for nm, shape, dt in (("G", [44, GL], F32), ("cb", [128, 44], F32), ("negm", [128, 16, 32], F32), ("ownm", [128, 16, 32], F32),
                          ("EB", [32, S], BF16), ("EC", [128, S], BF16), ("AM", [128, 16, 128], F32), ("BA", [128, 16, 128], F32),
                          ("cmA", [128, 512], F32), ("cmB", [128, 512], F32), ("ovl", [128, 4, 128], F32), ("selg", [36, 36, 64], F32)):
        consts[nm] = I(nm, shape, dt)
    yT = nc.dram_tensor("yT", [D, NT], F32, kind="ExternalOutput").ap()
    qT_s = N("qT_s", [32 * 64, NT], BF16)
    kT_s = N("kT_s", [26 * 64, NT], BF16, addr_space="Local")
    cmpT_s = N("cmpT_s", [6 * 64, NT], BF16, addr_space="Local")
    v_s = N("v_s", [26 * 128, 1024], BF16, addr_space="Local")
    def chunked(name, total, step, cols):
        out = []
        r0 = 0
        while r0 < total:
            nr = min(step, total - r0)
            out.append((r0, nr, N(f"{name}_{r0}", [4 * nr, cols], BF16, addr_space="Local")))
            r0 += nr
        return out
    kT_g = chunked("kT_g", 26 * 64, 256, NT)
    cmpT_g = chunked("cmpT_g", 6 * 64, 256, NT)
    v_g = chunked("v_g", 26 * 128, 512, 1024)
    gT_s = N("gT_s", [36, NT], F32)
    N2 = (lambda name, shape, dt: nc.dram_tensor(name, shape, dt, kind="ExternalOutput").ap()) if debug else N
    OT_s = N2("OT_s", [D, NT], BF16)
    x1T_s = N("x1T_s", [D, NT], F32)
    hT_s = N2("hT_s", [128, 16, 4, TW], BF16)
    ht_s = N("ht_s", [D, 8], BF16, addr_space="Local")
    ht_g = N("ht_g", [4 * D, 8], BF16, addr_space="Local")
    x2T_s = N2("x2T_s", [D, NT], F32)

    phase = [0]
    with contextlib.ExitStack() as top:
        ctx = Ctx(nc, top)

        def run_phase(fn, final=False):
            ph = phase[0]
            phase[0] += 1
            if ph >= stop and not final:
                return
            with contextlib.ExitStack() as st:
                T = lambda name, shape, dt: st.enter_context(nc.sbuf_tensor(f"s{ph}_" + name, shape, dt))
                PS = lambda name, shape, dt: st.enter_context(nc.psum_tensor(f"p{ph}_" + name, shape, dt))
                p = PProg(ctx)
                fin = fn(p, T, PS)
                p.emit(final_wait_ops=fin if final else ())

        xcur = xT0
        for l in range(depth):
            def ph_p1(p, T, PS, l=l, xcur=xcur):
                outs = {"qT": qT_s.rearrange("(h e) t -> h e t", e=64), "kT": kT_s.rearrange("(h e) t -> h e t", e=64),
                        "cmpT": cmpT_s.rearrange("(h e) t -> h e t", e=64)}
                v4 = v_s.rearrange("(h p) (ts e) -> h p ts e", p=128, e=64)

                def vdst(ts, vc0, ncols):
                    h0, nh = vc0 // 64, ncols // 64
                    return v4[h0:h0 + nh, :, ts, :].rearrange("h p e -> p h e")
                def wsrc(kind, idx):
                    if kind == "T":
                        return wiT[l, idx].rearrange("p (k n) -> p k n", n=128)
                    return wiN[l, idx].rearrange("p (k n) -> p k n", n=256)
                emit_p1(nc, p, T, PS, xcur, gn_attn[l], None, outs, None, gT_s, vdst=vdst, wsrc=wsrc)
                return ()
            run_phase(ph_p1)

            def ph_cc1(p, T, PS):
                for (a, chs) in ((kT_s, kT_g), (cmpT_s, cmpT_g), (v_s, v_g)):
                    for (r0, nr, b) in chs:
                        p.cc(lambda a=a, b=b, r0=r0, nr=nr: nc.gpsimd.collective_compute("AllGather", ALU.bypass, replica_groups=RG, ins=[a[r0:r0 + nr]], outs=[b]))
                return ()
            run_phase(ph_cc1)

            def ph_p2(p, T, PS, l=l):
                dr = dict(consts)
                dr.update({"qT": qT_s.rearrange("(h e) t -> h e t", e=64), "kT_g": kT_g, "cmpT_g": cmpT_g, "v_g": v_g, "gTs": gT_s,
                           "w1": w1d[l], "w2": w2d[l], "peT": peTd[l], "OT": OT_s})
                P = P2(nc, p, T, PS, dr, fused=True)
                P.setup()
                P.mixer_A()
                P.mixer_B()
                P.mixer_C()
                return ()
            run_phase(ph_p2)

            def ph_p3a(p, T, PS, l=l, xcur=xcur):
                emit_p3a_f(nc, p, T, PS, xcur, OT_s, w_out[l], gn_mlp[l], x1T_s, hT_s, ht_s)
                return ()
            run_phase(ph_p3a)

            def ph_cc2(p, T, PS):
                p.cc(lambda: nc.gpsimd.collective_compute("AllGather", ALU.bypass, replica_groups=RG, ins=[ht_s], outs=[ht_g]))
                return ()
            run_phase(ph_cc2)

            def ph_halo(p, T, PS):
                emit_halo(nc, p, T, PS, ht_g, hco_d, hT_s)
                return ()
            run_phase(ph_halo)

            def ph_p3b(p, T, PS, l=l):
                x1get = lambda cc, half: x1T_s[cc * 128:(cc + 1) * 128, half * 1024:(half + 1) * 1024].rearrange("p (t n) -> p t n", n=512)
                emit_p3b(nc, p, T, PS, hT_s, x1get, w_up[l], w_down[l], cwd[l], cbd[l], x2T_s, relayout=True)
                return ()
            run_phase(ph_p3b)
            xcur = x2T_s

        def ph_kf(p, T, PS):
            return emit_kf(nc, p, T, PS, x2T_s, gn_fin, yT)
        run_phase(ph_kf, final=True)
    return nc


def _relayout_in_T(w_in):
    L = w_in.shape[0]
    tch = t_chunks() + [(CG, 36, None, 1.0)]
    out = np.zeros((L, len(tch), 128, 16, 128), np.float32)
    for ci, ch in enumerate(tch):
        c0, nc_ = ch[0], ch[1]
        out[:, ci, :, :, 0:nc_] = w_in[:, :, c0:c0 + nc_].reshape(L, 16, 128, nc_).transpose(0, 2, 1, 3)
    return out.reshape(L, len(tch), 128, 2048)


def _relayout_in_N(w_in):
    L = w_in.shape[0]
    nch = n_chunks()
    out = np.zeros((L, len(nch), 128, 16, 256), np.float32)
    for ni, (c0, nc_, vc0) in enumerate(nch):
        out[:, ni, :, :, 0:nc_] = w_in[:, :, c0:c0 + nc_].reshape(L, 16, 128, nc_).transpose(0, 2, 1, 3)
    return out.reshape(L, len(nch), 128, 4096)


def fused_host_inputs(x, rel_table, w_in, w_out, cmp_w1, cmp_w2, cmp_pe, norm_attn, norm_mlp, w_up, conv_w, conv_b, w_down, norm_final):
    import ml_dtypes
    bf = ml_dtypes.bfloat16
    f32 = np.float32
    A = lambda a: np.ascontiguousarray(np.asarray(a, f32))
    x = np.asarray(x, f32)
    rel_table = np.asarray(rel_table, f32)
    L = np.asarray(w_in).shape[0]
    shared = {
        "wiT": _relayout_in_T(np.asarray(w_in, f32)), "wiN": _relayout_in_N(np.asarray(w_in, f32)),
        "woR": A(np.asarray(w_out, f32).reshape(L, 16, 128, 16, 128).transpose(0, 3, 2, 1, 4).reshape(L, 16, 128, 2048)),
        "wuR": A(np.asarray(w_up, f32).reshape(L, 16, 128, 2, 44, 128).transpose(0, 4, 2, 1, 3, 5).reshape(L, 44, 128, 4096)),
        "wdR": A(np.asarray(w_down, f32).reshape(L, 2, 22, 128, 16, 128).transpose(0, 4, 1, 3, 2, 5).reshape(L, 16, 2, 128, 2816)),
        "gn_attn": A(np.asarray(norm_attn, f32).reshape(L, 16, 128).transpose(0, 2, 1)),
        "gn_mlp": A(np.asarray(norm_mlp, f32).reshape(L, 16, 128).transpose(0, 2, 1)),
        "gn_fin": A(np.asarray(norm_final, f32).reshape(16, 128).T),
        "w1": A(np.asarray(cmp_w1, f32).reshape(L, 2, 32, 64, 128).transpose(0, 1, 3, 2, 4)),
        "w2": A(np.asarray(cmp_w2, f32).transpose(0, 2, 1, 3)),
        "peT": A(np.repeat(np.asarray(cmp_pe, f32).transpose(0, 3, 1, 2)[..., None], 2, axis=-1)),
        "cw": A(np.asarray(conv_w, f32).transpose(0, 2, 1).reshape(L, 88, 128, 3).transpose(0, 2, 1, 3)),
        "cbv": A(np.asarray(conv_b, f32).reshape(L, 88, 128).transpose(0, 2, 1)),
        "EB": eb_const().astype(bf), "EC": ec_const_nat().astype(bf), "ovl": ovl_const(), "selg": selg_const(),
    }
    per_j = []
    for j in range(4):
        G, cb = band_vectors(rel_table, j)
        negm, ownm = moba_consts(j)
        AM, BA, cmA, cmB = nsa_consts(j)
        per_j.append({"G": G, "cb": np.ascontiguousarray(np.broadcast_to(cb[None, :], (128, 44))), "negm": negm, "ownm": ownm,
                      "AM": AM, "BA": BA, "cmA": cmA, "cmB": cmB, "hco": halo_coef(j)})
    in_maps = []
    for c in range(8):
        b, j = c // 4, c % 4
        d = dict(shared)
        d.update(per_j[j])
        d["xT0"] = np.ascontiguousarray(x[b, core_tokens(j)].T)
        in_maps.append(d)
    return in_maps


from concourse.bass_utils import run_bass_kernel_spmd

_FUSED = {}


def kernel(x, rel_table, w_in, w_out, cmp_w1, cmp_w2, cmp_pe, norm_attn, norm_mlp,
           w_up, conv_w, conv_b, w_down, norm_final):
    if "nc" not in _FUSED:
        _FUSED["nc"] = build_fused()
    nc = _FUSED["nc"]
    in_maps = fused_host_inputs(x, rel_table, w_in, w_out, cmp_w1, cmp_w2, cmp_pe, norm_attn, norm_mlp,
                                w_up, conv_w, conv_b, w_down, norm_final)
    res = run_bass_kernel_spmd(nc, in_maps, core_ids=list(range(8)))
    out = np.empty((2, S, 2048), np.float32)
    for c in range(8):
        b, j = c // 4, c % 4
        out[b, core_tokens(j)] = np.asarray(res.results[c]["yT"]).T
    return out
```

```python
import contextlib, math
import numpy as np
import concourse.bass as bass
import concourse.mybir as mybir

F32 = mybir.dt.float32
BF16 = mybir.dt.bfloat16
I32 = mybir.dt.int32
AF = mybir.ActivationFunctionType
ALU = mybir.AluOpType
AX = mybir.AxisListType

SEM_CAP = 30000
N_DMA_SEMS = 8


class _Op:
    __slots__ = ("eng", "fn", "deps", "is_dma", "sig", "has_dependents", "idx", "dsem_prev", "is_cc", "inc")

    def __init__(self, eng, fn, is_dma):
        self.eng = eng
        self.fn = fn
        self.is_dma = is_dma
        self.deps = []
        self.sig = None
        self.has_dependents = False
        self.dsem_prev = None
        self.is_cc = False
        self.inc = 1


class Prog:
    ENGS = ("pe", "act", "dve", "pool", "sp")

    def __init__(self, nc):
        self.nc = nc
        self.ops = []
        self.last_writer = {}
        self.readers = {}

    def _add(self, eng, fn, reads, writes, is_dma):
        op = _Op(eng, fn, is_dma)
        deps = {}
        for r in reads:
            w = self.last_writer.get(r)
            if w is not None:
                deps[id(w)] = w
        for r in writes:
            w = self.last_writer.get(r)
            if w is not None:
                deps[id(w)] = w
            for rd in self.readers.get(r, ()):
                deps[id(rd)] = rd
        for r in reads:
            self.readers.setdefault(r, []).append(op)
        for r in writes:
            self.last_writer[r] = op
            self.readers[r] = []
        for d in deps.values():
            if d is op:
                continue
            if (not is_dma) and (not d.is_dma) and d.eng == eng and eng == "pe":
                continue
            op.deps.append(d)
            d.has_dependents = True
        self.ops.append(op)
        return op

    def op(self, eng, fn, reads=(), writes=()):
        return self._add(eng, fn, reads, writes, False)

    def dma(self, eng, fn, reads=(), writes=()):
        return self._add(eng, fn, reads, writes, True)

    def emit(self, final_wait_ops=()):
        nc = self.nc
        engs = {"pe": nc.tensor, "act": nc.scalar, "dve": nc.vector, "pool": nc.gpsimd, "sp": nc.sync}
        import contextlib
        with contextlib.ExitStack() as st:
            sem_lists = {e: [] for e in self.ENGS}
            counts = {e: 0 for e in self.ENGS}

            def new_sem(name):
                return st.enter_context(nc.semaphore(name))

            dma_sems = {}
            dma_state = {}
            for e in self.ENGS:
                dma_sems[e] = None
            for op in self.ops:
                if op.is_dma:
                    if dma_sems[op.eng] is None:
                        dma_sems[op.eng] = [new_sem(f"d_{op.eng}_{i}") for i in range(N_DMA_SEMS)]
                        dma_state[op.eng] = {"rr": 0, "cnt": [0] * N_DMA_SEMS}
                    stt = dma_state[op.eng]
                    i = stt["rr"]
                    stt["rr"] = (i + 1) % N_DMA_SEMS
                    prev = stt["cnt"][i]
                    stt["cnt"][i] = prev + 16
                    op.sig = (dma_sems[op.eng][i], prev + 16)
                    op.dsem_prev = (dma_sems[op.eng][i], prev) if prev > 0 else None
                elif op.has_dependents:
                    e = op.eng
                    if not sem_lists[e] or counts[e] >= SEM_CAP:
                        sem_lists[e].append(new_sem(f"c_{e}_{len(sem_lists[e])}"))
                        counts[e] = 0
                    counts[e] += 1
                    op.sig = (sem_lists[e][-1], counts[e])
            streams = {e: [] for e in self.ENGS}
            waited = {e: {} for e in self.ENGS}
            for op in self.ops:
                e = op.eng
                waits = []
                need = []
                if op.dsem_prev is not None:
                    need.append(op.dsem_prev)
                for d in op.deps:
                    need.append(d.sig)
                for (sem, val) in need:
                    k = id(sem)
                    if waited[e].get(k, 0) >= val:
                        continue
                    waited[e][k] = val
                    waits.append((sem, val))
                streams[e].append((waits, op))
            finals = [o.sig for o in final_wait_ops]
            blk = st.enter_context(nc.Block())

            def make(e):
                def body(engine):
                    for waits, op in streams[e]:
                        for (sem, val) in waits:
                            engine.wait_ge(sem, val)
                        ins = op.fn()
                        if op.sig is not None:
                            ins.then_inc(op.sig[0], 16 if op.is_dma else 1)
                    if e == "sp":
                        for (sem, val) in finals:
                            engine.wait_ge(sem, val)
                return body

            blk.tensor(make("pe"))
            blk.scalar(make("act"))
            blk.vector(make("dve"))
            blk.gpsimd(make("pool"))
            blk.sync(make("sp"))
        return self


class Ctx:
    ENGS = ("pe", "act", "dve", "pool", "sp")

    def __init__(self, nc, stack):
        self.nc = nc
        self.stack = stack
        self.sem_lists = {e: [] for e in self.ENGS}
        self.counts = {e: 0 for e in self.ENGS}
        self.dma_sems = {e: None for e in self.ENGS}
        self.dma_state = {}
        self.cc_sem = None
        self.cc_count = 0
        self.waited = {e: {} for e in self.ENGS}
        self.barrier_sigs = []
        self.nsem = 0

    def new_sem(self, name):
        self.nsem += 1
        return self.stack.enter_context(self.nc.semaphore(name))


class PProg(Prog):
    def __init__(self, ctx):
        super().__init__(ctx.nc)
        self.ctx = ctx

    def cc(self, fn, reads=(), writes=()):
        op = self._add("pool", fn, reads, writes, True)
        op.is_cc = True
        return op

    def emit(self, final_wait_ops=()):
        nc, ctx = self.nc, self.ctx
        engs = self.ENGS
        last_op = {e: None for e in engs}
        for op in self.ops:
            if not op.is_dma:
                last_op[op.eng] = op
        for e in engs:
            if last_op[e] is not None:
                last_op[e].has_dependents = True
        for op in self.ops:
            if getattr(op, "is_cc", False):
                if ctx.cc_sem is None:
                    ctx.cc_sem = ctx.new_sem("ccs")
                ctx.cc_count += 1
                op.sig = (ctx.cc_sem, ctx.cc_count)
                op.inc = 1
            elif op.is_dma:
                if ctx.dma_sems[op.eng] is None:
                    ctx.dma_sems[op.eng] = [ctx.new_sem(f"d_{op.eng}_{i}") for i in range(N_DMA_SEMS)]
                    ctx.dma_state[op.eng] = {"rr": 0, "cnt": [0] * N_DMA_SEMS}
                stt = ctx.dma_state[op.eng]
                i = stt["rr"]
                stt["rr"] = (i + 1) % N_DMA_SEMS
                prev = stt["cnt"][i]
                stt["cnt"][i] = prev + 16
                op.sig = (ctx.dma_sems[op.eng][i], prev + 16)
                op.dsem_prev = (ctx.dma_sems[op.eng][i], prev) if prev > 0 else None
                op.inc = 16
            elif op.has_dependents:
                e = op.eng
                if not ctx.sem_lists[e] or ctx.counts[e] >= SEM_CAP:
                    ctx.sem_lists[e].append(ctx.new_sem(f"c_{e}_{len(ctx.sem_lists[e])}"))
                    ctx.counts[e] = 0
                ctx.counts[e] += 1
                op.sig = (ctx.sem_lists[e][-1], ctx.counts[e])
                op.inc = 1
        streams = {e: [] for e in engs}
        start_waits = {e: [] for e in engs}
        for e in engs:
            for (sem, val) in ctx.barrier_sigs:
                k = id(sem)
                if ctx.waited[e].get(k, 0) >= val:
                    continue
                ctx.waited[e][k] = val
                start_waits[e].append((sem, val))
        for op in self.ops:
            e = op.eng
            waits = []
            need = []
            if op.dsem_prev is not None:
                need.append(op.dsem_prev)
            for d in op.deps:
                need.append(d.sig)
            for (sem, val) in need:
                k = id(sem)
                if ctx.waited[e].get(k, 0) >= val:
                    continue
                ctx.waited[e][k] = val
                waits.append((sem, val))
            streams[e].append((waits, op))
        sigs = []
        for e in engs:
            if last_op[e] is not None:
                sigs.append(last_op[e].sig)
            if ctx.dma_sems[e] is not None:
                for i, sem in enumerate(ctx.dma_sems[e]):
                    c = ctx.dma_state[e]["cnt"][i]
                    if c > 0:
                        sigs.append((sem, c))
        if ctx.cc_sem is not None and ctx.cc_count > 0:
            sigs.append((ctx.cc_sem, ctx.cc_count))
        ctx.barrier_sigs = sigs
        finals = [o.sig for o in final_wait_ops]
        with nc.Block() as blk:
            def make(e):
                def body(engine):
                    for (sem, val) in start_waits[e]:
                        engine.wait_ge(sem, val)
                    for waits, op in streams[e]:
                        for (sem, val) in waits:
                            engine.wait_ge(sem, val)
                        ins = op.fn()
                        if op.sig is not None:
                            if op.inc == 1 and getattr(op, "is_cc", False):
                                ins.then_inc(op.sig[0])
                            else:
                                ins.then_inc(op.sig[0], op.inc)
                    if e == "sp":
                        for (sem, val) in finals:
                            engine.wait_ge(sem, val)
                return body
            blk.tensor(make("pe"))
            blk.scalar(make("act"))
            blk.vector(make("dve"))
            blk.gpsimd(make("pool"))
            blk.sync(make("sp"))
        return self

import contextlib
import numpy as np

D = 2048
NT = 2048
IN_W = 5796
EPS = 1e-6

def a_col(g, part, hg):
    return g * 768 + part * 256 + hg * 64
B0 = 2304
C0 = 3840
CKV = C0 + 768
CG = CKV + 1152


def t_chunks():
    ch = []
    for g in range(3):
        for pair in range(2):
            ch.append((a_col(g, 0, 2 * pair), 128, [("qT", g * 4 + 2 * pair, 0, 64), ("qT", g * 4 + 2 * pair + 1, 64, 64)], 0.125))
        for pair in range(2):
            ch.append((a_col(g, 1, 2 * pair), 128, [("kT", g * 4 + 2 * pair, 0, 64), ("kT", g * 4 + 2 * pair + 1, 64, 64)], 1.0))
    for pair in range(4):
        ch.append((B0 + pair * 128, 128, [("qT", 12 + 2 * pair, 0, 64), ("qT", 13 + 2 * pair, 64, 64)], 0.125))
    for pair in range(4):
        ch.append((B0 + 512 + pair * 128, 128, [("kT", 12 + 2 * pair, 0, 64), ("kT", 13 + 2 * pair, 64, 64)], 1.0))
    for pair in range(6):
        ch.append((C0 + pair * 128, 128, [("qT", 20 + 2 * pair, 0, 64), ("qT", 21 + 2 * pair, 64, 64)], 0.125))
    for pair in range(3):
        ch.append((CKV + pair * 128, 128, [("cmpT", 2 * pair, 0, 64), ("cmpT", 2 * pair + 1, 64, 64)], 1.0))
    ch.append((CKV + 384, 128, [("kT", 20, 0, 64), ("kT", 21, 64, 64)], 1.0))
    ch.append((CKV + 384 + 128, 64, [("kT", 22, 0, 64)], 1.0))
    ch.append((CKV + 768, 128, [("kT", 23, 0, 64), ("kT", 24, 64, 64)], 1.0))
    ch.append((CKV + 768 + 128, 64, [("kT", 25, 0, 64)], 1.0))
    return ch


def n_chunks():
    ch = []
    for g in range(3):
        ch.append((a_col(g, 2, 0), 256, g * 256))
    ch.append((B0 + 1024, 256, 768))
    ch.append((B0 + 1024 + 256, 256, 1024))
    ch.append((CKV + 576, 192, 1280))
    ch.append((CKV + 960, 192, 1472))
    return ch


def build_p1():
    nc = bass.Bass("TRN2", target_bir_lowering=False)
    xT = nc.dram_tensor("xT", [D, NT], F32, kind="ExternalInput").ap()
    gn = nc.dram_tensor("gn", [128, 16], F32, kind="ExternalInput").ap()
    w = nc.dram_tensor("w", [D, IN_W], F32, kind="ExternalInput").ap()
    outs = {
        "qT": nc.dram_tensor("qT", [32, 64, NT], BF16, kind="ExternalOutput").ap(),
        "kT": nc.dram_tensor("kT", [26, 64, NT], BF16, kind="ExternalOutput").ap(),
        "cmpT": nc.dram_tensor("cmpT", [6, 64, NT], BF16, kind="ExternalOutput").ap(),
    }
    vO = nc.dram_tensor("v", [NT, 1664], BF16, kind="ExternalOutput").ap()
    gT = nc.dram_tensor("gT", [36, NT], F32, kind="ExternalOutput").ap()
    with contextlib.ExitStack() as st:
        T = lambda name, shape, dt: st.enter_context(nc.sbuf_tensor("s_" + name, shape, dt))
        PS = lambda name, shape, dt: st.enter_context(nc.psum_tensor("p_" + name, shape, dt))
        p = Prog(nc)
        emit_p1(nc, p, T, PS, xT, gn, w, outs, vO, gT)
        p.emit(final_wait_ops=p.final_ops)
    return nc


def emit_p1(nc, p, T, PS, xT, gn, w, outs, vO, gT, vdst=None, wsrc=None):
    p.final_ops = getattr(p, "final_ops", [])
    hT = T("hT", [128, 16, NT], BF16)
    gsb = T("gsb", [128, 16], F32)
    ones = T("ones", [128, 128], F32)
    xs = [T(f"xs{i}", [128, 16, 512], F32) for i in range(2)]
    sq = [T(f"sq{i}", [128, 512], F32) for i in range(2)]
    rstd = T("rstd", [128, 512], F32)
    wst = [T(f"wst{i}", [128, 16, 256], F32) for i in range(2)]
    wbf = [T(f"wbf{i}", [128, 16, 256], BF16) for i in range(2)]
    ost = [T(f"ost{i}", [128, NT], BF16) for i in range(2)]
    gst = T("gst", [36, NT], F32)
    vst = [T(f"vst{i}", [128, 256], BF16) for i in range(3)]
    pss = PS("pss", [128, 512], F32)
    pacc = [PS(f"pacc{i}", [128, 512], F32) for i in range(3)]

    p.dma("sp", lambda: nc.sync.dma_start(out=gsb[:], in_=gn), writes=["gsb"])
    p.op("pool", lambda: nc.gpsimd.memset(ones[:], 1.0), writes=["ones"])
    xv = xT.rearrange("(k p) t -> p k t", p=128)
    for m in range(4):
        s = m % 2
        p.dma("sp", lambda m=m, s=s: nc.sync.dma_start(out=xs[s][:], in_=xv[:, :, m * 512:(m + 1) * 512]), writes=[f"xs{s}"])
        for k in range(16):
            q = k % 2
            p.op("act", lambda s=s, k=k, q=q: nc.scalar.activation(out=sq[q][:], in_=xs[s][:, k, :], func=AF.Square),
                 reads=[f"xs{s}"], writes=[f"sq{q}"])
            p.op("pe", lambda q=q, k=k: nc.tensor.matmul(pss[:], lhsT=ones[:], rhs=sq[q][:], start=(k == 0), stop=(k == 15)),
                 reads=["ones", f"sq{q}"], writes=["pss"])
        p.op("act", lambda: nc.scalar.activation(out=rstd[:], in_=pss[:], func=AF.Sqrt, scale=1.0 / D, bias=EPS),
             reads=["pss"], writes=["rstd"])
        p.op("dve", lambda: nc.vector.reciprocal(out=rstd[:], in_=rstd[:]), reads=["rstd"], writes=["rstd"])
        for k in range(16):
            eng = "dve" if k % 2 == 0 else "pool"
            E = nc.vector if eng == "dve" else nc.gpsimd
            if eng == "dve":
                p.op("dve", lambda s=s, k=k, m=m: nc.vector.scalar_tensor_tensor(
                    out=hT[:, k, m * 512:(m + 1) * 512], in0=xs[s][:, k, :], scalar=gsb[:, k:k + 1], in1=rstd[:],
                    op0=ALU.mult, op1=ALU.mult), reads=[f"xs{s}", "gsb", "rstd"], writes=[f"hT{m}"])
            else:
                p.op("dve", lambda s=s, k=k, m=m: nc.vector.scalar_tensor_tensor(
                    out=hT[:, k, m * 512:(m + 1) * 512], in0=xs[s][:, k, :], scalar=gsb[:, k:k + 1], in1=rstd[:],
                    op0=ALU.mult, op1=ALU.mult), reads=[f"xs{s}", "gsb", "rstd"], writes=[f"hT{m}"])
    hT_all = [f"hT{m}" for m in range(4)]

    wcount = [0]

    def load_w(c0, ncols, kind=None, idx=0):
        s = wcount[0] % 2
        wcount[0] += 1
        if wsrc is None:
            src = w[:, c0:c0 + ncols].rearrange("(k p) n -> p k n", p=128)
        else:
            src = wsrc(kind, idx)[:, :, 0:ncols]
        p.dma("sp", lambda: nc.sync.dma_start(out=wst[s][:, :, 0:ncols], in_=src), writes=[f"wst{s}"])
        h = 12
        p.op("act", lambda: nc.scalar.copy(out=wbf[s][:, 0:h, 0:ncols], in_=wst[s][:, 0:h, 0:ncols]),
             reads=[f"wst{s}"], writes=[f"wbfa{s}"])
        p.op("pool", lambda: nc.gpsimd.tensor_copy(out=wbf[s][:, h:16, 0:ncols], in_=wst[s][:, h:16, 0:ncols]),
             reads=[f"wst{s}"], writes=[f"wbfb{s}"])
        return s

    tch = t_chunks() + [(CG, 36, [("gT", 0, 0, 36)], 1.0)]
    nch = n_chunks()
    wjobs = [(ch[0], ch[1], "T", i) for i, ch in enumerate(tch)] + [(ch[0], ch[1], "N", i) for i, ch in enumerate(nch)]
    wslot = {0: load_w(*wjobs[0])}

    def get_w(j):
        if j + 1 < len(wjobs):
            wslot[j + 1] = load_w(*wjobs[j + 1])
        return wslot[j]
    acc_i = [0]
    for ci, (c0, ncols, dests, scale) in enumerate(tch):
        s = get_w(ci)
        o = ci % 2
        is_gate = dests[0][0] == "gT"
        for m in range(4):
            a = acc_i[0] % 3
            acc_i[0] += 1
            for k in range(16):
                p.op("pe", lambda a=a, s=s, k=k, m=m, ncols=ncols: nc.tensor.matmul(
                    pacc[a][0:ncols, :], lhsT=wbf[s][:, k, 0:ncols], rhs=hT[:, k, m * 512:(m + 1) * 512],
                    start=(k == 0), stop=(k == 15)),
                    reads=[f"wbfa{s}", f"wbfb{s}", f"hT{m}"], writes=[f"pacc{a}"])
            if is_gate:
                p.op("act", lambda a=a, m=m: nc.scalar.activation(out=gst[:, m * 512:(m + 1) * 512], in_=pacc[a][0:36, :], func=AF.Sigmoid),
                     reads=[f"pacc{a}"], writes=["gst"])
            elif m % 2 == 0:
                p.op("act", lambda a=a, m=m, o=o, ncols=ncols, scale=scale: nc.scalar.activation(
                    out=ost[o][0:ncols, m * 512:(m + 1) * 512], in_=pacc[a][0:ncols, :], func=AF.Copy, scale=scale),
                    reads=[f"pacc{a}"], writes=[f"ost{o}"])
            else:
                p.op("dve", lambda a=a, m=m, o=o, ncols=ncols, scale=scale: nc.vector.tensor_scalar(
                    out=ost[o][0:ncols, m * 512:(m + 1) * 512], in0=pacc[a][0:ncols, :], scalar1=scale, scalar2=None, op0=ALU.mult),
                    reads=[f"pacc{a}"], writes=[f"ost{o}"])
        if is_gate:
            p.final_ops.append(p.dma("pool", lambda: nc.gpsimd.dma_start(out=gT, in_=gst[:]), reads=["gst"]))
        else:
            for (dn, dh, r0, nr) in dests:
                p.final_ops.append(p.dma("pool", lambda dn=dn, dh=dh, r0=r0, nr=nr, o=o: nc.gpsimd.dma_start(
                    out=outs[dn][dh], in_=ost[o][r0:r0 + nr, :]), reads=[f"ost{o}"]))

    vi = [0]
    for ni, (c0, ncols, vc0) in enumerate(nch):
        s = get_w(len(tch) + ni)
        for ts in range(16):
            a = acc_i[0] % 3
            acc_i[0] += 1
            m = ts // 4
            for k in range(16):
                p.op("pe", lambda a=a, s=s, k=k, ts=ts, ncols=ncols: nc.tensor.matmul(
                    pacc[a][:, 0:ncols], lhsT=hT[:, k, ts * 128:(ts + 1) * 128], rhs=wbf[s][:, k, 0:ncols],
                    start=(k == 0), stop=(k == 15)),
                    reads=[f"wbfa{s}", f"wbfb{s}", f"hT{m}"], writes=[f"pacc{a}"])
            vs = vi[0] % 3
            vi[0] += 1
            if ts % 2 == 0:
                p.op("act", lambda a=a, vs=vs, ncols=ncols: nc.scalar.copy(out=vst[vs][:, 0:ncols], in_=pacc[a][:, 0:ncols]),
                     reads=[f"pacc{a}"], writes=[f"vst{vs}"])
            else:
                p.op("dve", lambda a=a, vs=vs, ncols=ncols: nc.vector.tensor_copy(out=vst[vs][:, 0:ncols], in_=pacc[a][:, 0:ncols]),
                     reads=[f"pacc{a}"], writes=[f"vst{vs}"])
            if vdst is None:
                p.final_ops.append(p.dma("pool", lambda ts=ts, vs=vs, vc0=vc0, ncols=ncols: nc.gpsimd.dma_start(
                    out=vO[ts * 128:(ts + 1) * 128, vc0:vc0 + ncols], in_=vst[vs][:, 0:ncols]), reads=[f"vst{vs}"]))
            else:
                p.final_ops.append(p.dma("pool", lambda ts=ts, vs=vs, vc0=vc0, ncols=ncols: nc.gpsimd.dma_start(
                    out=vdst(ts, vc0, ncols), in_=vst[vs][:, 0:ncols].rearrange("p (h e) -> p h e", e=64)), reads=[f"vst{vs}"]))


import contextlib, math
import numpy as np

S = 8192
NT = 2048
BW = 4480
GL = BW + 128
NEGM = -30000.0
A_CFG = ((128, 1), (512, 4), (2048, 16))
U16 = mybir.dt.uint16


def t5_bucket_np(dist):
    n = np.maximum(dist, 0)
    nf = np.maximum(n, 1).astype(np.float32)
    large = 16 + (np.log(nf / np.float32(16)) / np.float32(math.log(128.0)) * np.float32(16)).astype(np.int32)
    large = np.minimum(large, 31)
    return np.where(n < 16, n, large)


def band_vectors(rel_table, j):
    v = np.arange(GL)
    dist = v + 512 * j - 2047
    bk = t5_bucket_np(dist)
    G = np.empty((44, GL), np.float32)
    cb = np.empty((44,), np.float32)
    for b in range(44):
        if b < 12:
            h = b
            W, d = A_CFG[b // 4]
            ok = (dist >= 0) & (dist <= W) & (dist % d == 0)
        elif b < 20:
            h = b
            ok = dist >= 0
        elif b < 32:
            h = b
            ok = dist >= 0
        else:
            h = 20 + (b - 32)
            ok = (dist >= 0) & (dist < 512)
        G[b] = np.where(ok, rel_table[h, bk], np.float32(NEGM))
        cb[b] = rel_table[h, 31]
    return G, cb


def core_tokens(j):
    return np.concatenate([np.arange(512 * (4 * m + j), 512 * (4 * m + j) + 512) for m in range(4)])


def moba_consts(j):
    t = core_tokens(j)
    ob = t // 256
    n = np.arange(32)[None, :]
    neg = np.where(n >= ob[:, None], np.float32(-1e30), np.float32(0)).astype(np.float32)
    own = (n >= ob[:, None]).astype(np.float32)
    f = lambda a: np.ascontiguousarray(a.reshape(16, 128, 32).transpose(1, 0, 2))
    return f(neg), f(own)


def eb_const():
    k = np.arange(S)
    return (k[None, :] // 256 == np.arange(32)[:, None]).astype(np.float32)


def ec_const():
    c = np.arange(S)
    key = (c // 128) * 128 + 127 - (c % 128)
    return (key[None, :] // 64 == np.arange(128)[:, None]).astype(np.float32)


def rev_blocks(a, axis):
    a = np.moveaxis(a, axis, -1)
    sh = a.shape
    a = a.reshape(sh[:-1] + (sh[-1] // 128, 128))[..., ::-1].reshape(sh)
    return np.moveaxis(a, -1, axis)


def nsa_consts(j):
    t = core_tokens(j)
    own = (t // 64)[:, None]
    jb = np.arange(128)[None, :]
    valid = jb <= own
    forced = (jb == 0) | (jb == own) | (jb == own - 1)
    am = (valid & ~forced).astype(np.float32)
    ba = np.where(valid, np.where(forced, np.float32(1e9), np.float32(0)), np.float32(-1)).astype(np.float32)
    f = lambda a: np.ascontiguousarray(a.reshape(16, 128, 128).transpose(1, 0, 2))
    pp = np.arange(128)[:, None]
    q = np.arange(512)[None, :]
    cmA = np.where(16 * pp + 31 <= 512 * j + q, np.float32(0), np.float32(NEGM)).astype(np.float32)
    cmB = np.where(16 * pp + 31 - 2048 <= 512 * j + q, np.float32(0), np.float32(NEGM)).astype(np.float32)
    return f(am), f(ba), cmA, cmB


def ovl_const():
    i = np.arange(512)[:, None]
    jb = np.arange(128)[None, :]
    ov = ((16 * i < 64 * jb + 64) & (16 * i + 32 > 64 * jb) & (i < 511)).astype(np.float32)
    return np.ascontiguousarray(ov.reshape(4, 128, 128).transpose(1, 0, 2))


def selg_const():
    sg = np.zeros((36, 36, 64), np.float32)
    for r in range(36):
        sg[r, r, :] = 1.0
    return sg

class P2:
    def __init__(self, nc, p, T, PS, dr, heads_A=(0, 1, 2, 3), heads_B=tuple(range(8)), kvs_C=(0, 1, 2), fused=False):
        self.nc, self.p, self.T, self.PS, self.dr = nc, p, T, PS, dr
        self.fused = fused
        self.heads_A, self.heads_B, self.kvs_C = heads_A, heads_B, kvs_C
        self.units = []
        self.alloc()

    def alloc(self):
        T, PS = self.T, self.PS
        self.kTall = T("kTall", [128, S], BF16)
        self.qTz = [T(f"qTz{i}", [128, NT], BF16) for i in range(2)]
        self.kTb = [self.kTall[0:64], self.kTall[64:128]]
        self.qTb = [self.qTz[0][0:64], self.qTz[1][64:128]]
        self.Vb = [T(f"Vb{i}", [128, 64, 128], BF16) for i in range(2)]
        self.band1 = T("band1", [128, BW], F32)
        self.bandb = [self.band1, self.band1]
        self.cbb = T("cbb", [128, 44], F32)
        self.Eb = T("Eb", [128, S], BF16)
        self.MTb = [T(f"MTb{i}", [128, NT], BF16) for i in range(2)]
        self.sb = [T(f"sb{i}", [128, 512], F32) for i in range(2)]
        self.PT = [T(f"PT{i}", [128, 512], BF16) for i in range(3)]
        self.dsum = self.sb[1]
        self.nd = T("nd", [128, 12, 512], F32)
        self.rden = T("rden", [64, 512], F32)
        self.ost = [T(f"ost{i}", [64, 512], BF16) for i in range(3)]
        self.identb = T("identb", [128, 128], BF16)
        self.identf = T("identf", [128, 128], F32)
        self.negm = T("negm", [128, 16, 32], F32)
        self.ownm = T("ownm", [128, 16, 32], F32)
        self.km = T("km", [128, 32], F32)
        self.kmb = T("kmb", [128, 32], BF16)
        self.gm = T("gm", [128, 16, 32], F32)
        self.m8 = T("m8", [128, 16, 8], F32)
        self.selt = T("selt", [128, 16, 32], F32)
        self.Mq = T("Mq", [128, 16, 32], BF16)
        self.qcm = [T(f"qcm{i}", [64, 512], BF16) for i in range(3)]
        self.ef = [T(f"ef{i}", [128, 512], F32) for i in range(4)]
        self.pcb = [T(f"pcb{i}", [128, 512], BF16) for i in range(4)]
        self.w1b = T("w1b", [64, 32, 128], BF16)
        self.w2f = T("w2f", [128, 2, 64], F32)
        self.w2b = T("w2b", [128, 2, 64], BF16)
        self.peTf = T("peTf", [64, 2, 32, 2], F32)
        self.peTb = T("peTb", [64, 2, 32, 2], BF16)
        self.kcTb = T("kcTb", [64, 512], BF16)
        self.vcb = T("vcb", [128, 4, 64], BF16)
        self.scr = [T(f"scr{i}", [128, 512], F32) for i in range(2)]
        self.ocs = self.scr[1][0:64]
        self.cbias = T("cbias", [128, 1], F32)
        self.hid = T("hid", [128, 512], BF16)
        self.amb = T("amb", [128, 4, 128], F32)
        self.bab = T("bab", [128, 4, 128], F32)
        self.cmA = T("cmA", [128, 512], F32)
        self.cmB = T("cmB", [128, 512], F32)
        self.ovl = T("ovl", [128, 4, 128], F32)
        self.sel3 = T("sel3", [36, 3, 64], F32)
        self.gTs = T("gTs", [36, NT], F32)
        self.impS = T("impS", [128, 512], F32)
        self.score = T("score", [128, 4, 128], F32)
        self.sc2 = T("sc2", [128, 128], F32)
        self.m16 = T("m16", [128, 4, 16], F32)
        self.Msel = T("Msel", [128, 4, 128], BF16)
        self.onesf = T("onesf", [128, 128], F32)
        self.tmpf = T("tmpf", [64, 512], F32)
        self.pS = [PS(f"pS{i}", [128, 512], F32) for i in range(4)]
        self.pacc = [PS(f"pacc{i}", [128, 512], F32) for i in range(2)]
        self.pm = [PS("pm0", [128, 512], F32), self.pS[3]]
        self.pmn = ["pm0", "pS3"]
        self.pmb = PS("pmb", [128, 1024], BF16)
        if self.fused:
            self.hst = [T(f"hst{i}", [128, 512], F32) for i in range(2)]
            self.Jf = T("Jf", [128, 128], F32)
        self.ost_i = 0
        self.slot = 0
        self.acc_i = 0
        self.out_ops = []

    def setup(self):
        nc, p = self.nc, self.p
        p.op("pool", lambda: nc.gpsimd.memset(self.identf[:], 1.0), writes=["identf"])
        p.op("pool", lambda: nc.gpsimd.affine_select(out=self.identf[:], in_=self.identf[:], pattern=[[-1, 128]],
                                                     compare_op=ALU.is_equal, fill=0.0, base=0, channel_multiplier=1),
             reads=["identf"], writes=["identf"])
        p.op("dve", lambda: nc.vector.tensor_copy(out=self.identb[:], in_=self.identf[:]), reads=["identf"], writes=["identb"])
        for i in range(2):
            p.op("pool", lambda i=i: nc.gpsimd.memset(self.Vb[i][:, :, 64:128], 1.0), writes=[f"Vones{i}"])
        p.dma("sp", lambda: nc.sync.dma_start(out=self.cbb[:], in_=self.dr["cb"]), writes=["cbb"])
        p.op("pool", lambda: nc.gpsimd.memset(self.kTall[:], 0.0), writes=["kTb0", "kTb1"])
        for i in range(2):
            p.op("pool", lambda i=i: nc.gpsimd.memset(self.qTz[i][:], 0.0), writes=[f"qTb{i}"])
            p.op("pool", lambda i=i: nc.gpsimd.memset(self.MTb[i][:], 0.0), writes=[f"MTb{i}"])
        p.op("pool", lambda: nc.gpsimd.memset(self.Eb[:], 0.0), writes=["Eb"])
        if self.fused:
            p.op("pool", lambda: nc.gpsimd.memset(self.Jf[:], 1.0), writes=["Jf"])
            p.op("pool", lambda: nc.gpsimd.affine_select(out=self.Jf[:], in_=self.Jf[:], pattern=[[1, 128]],
                                                         compare_op=ALU.is_equal, fill=0.0, base=-127, channel_multiplier=1),
                 reads=["Jf"], writes=["Jf"])

    def load_kT_gathered(self, dst, chunks, row0, res):
        nc, p = self.nc, self.p
        src2d = None
        for (r0, nr, ap) in chunks:
            if r0 <= row0 < r0 + nr:
                src2d, row0 = ap, row0 - r0
                break
        nrows = src2d.shape[0] // 4
        for m in range(4):
            src = bass.AP(tensor=src2d.tensor, offset=src2d[row0:row0 + 1, m * 512:m * 512 + 1].offset,
                          ap=[[2048, 64], [nrows * 2048, 4], [1, 512]])
            d = dst[:, m * 2048:(m + 1) * 2048].rearrange("e (r i) -> e r i", i=512)
            p.dma("sp", lambda src=src, d=d: nc.sync.dma_start(out=d, in_=src), reads=["gathered"], writes=[res])

    def load_v_gathered(self, s, ki):
        nc, p = self.nc, self.p
        vg, lrow, nr = None, 0, 0
        for (r0, nr_, ap) in self.dr["v_g"]:
            if r0 <= ki * 128 < r0 + nr_:
                vg, lrow, nr = ap, ki * 128 - r0, nr_
                break
        for m in range(4):
            for r in range(4):
                src = bass.AP(tensor=vg.tensor, offset=vg[r * nr + lrow:r * nr + lrow + 1, m * 256:m * 256 + 1].offset,
                              ap=[[1024, 128], [64, 4], [1, 64]])
                d = self.Vb[s][:, m * 16 + r * 4:m * 16 + r * 4 + 4, 0:64]
                p.dma("sp", lambda src=src, d=d: nc.sync.dma_start(out=d, in_=src), reads=["gathered"], writes=[f"Vb{s}"])

    def load_band_flipped(self, bi):
        nc, p = self.nc, self.p
        g = self.dr["G"]
        nch = (BW + 511) // 512
        for ch in range(nch):
            w_ = min(512, BW - ch * 512)
            hs = ch % 2
            src = bass.AP(tensor=g.tensor, offset=g[bi:bi + 1, ch * 512:ch * 512 + 1].offset, ap=[[1, 128], [1, w_]])
            p.dma("sp", lambda src=src, hs=hs, w_=w_: nc.sync.dma_start(out=self.hst[hs][:, 0:w_], in_=src), writes=[f"hst{hs}"])
            pb = self.pm[ch % 2]
            p.op("pe", lambda hs=hs, w_=w_, pb=pb: nc.tensor.matmul(pb[:, 0:w_], lhsT=self.Jf[:], rhs=self.hst[hs][:, 0:w_], start=True, stop=True),
                 reads=["Jf", f"hst{hs}"], writes=[self.pmn[ch % 2]])
            p.op("act", lambda ch=ch, w_=w_, pb=pb: nc.scalar.copy(out=self.band1[:, ch * 512:ch * 512 + w_], in_=pb[:, 0:w_]),
                 reads=[self.pmn[ch % 2]], writes=["bandb"])

    def load_head(self, qi, ki, bi):
        nc, p, dr = self.nc, self.p, self.dr
        s = self.slot % 2
        self.slot += 1
        if self.fused:
            p.dma("sp", lambda: nc.sync.dma_start(out=self.qTb[s], in_=dr["qT"][qi]), reads=["qT_s"], writes=[f"qTb{s}"])
            self.load_kT_gathered(self.kTb[s], dr["kT_g"], ki * 64, f"kTb{s}")
            self.load_v_gathered(s, ki)
            self.load_band_flipped(bi)
            return s
        p.dma("sp", lambda: nc.sync.dma_start(out=self.qTb[s], in_=dr["qT"][qi]), writes=[f"qTb{s}"])
        p.dma("sp", lambda: nc.sync.dma_start(out=self.kTb[s], in_=dr["kTf"][ki]), writes=[f"kTb{s}"])
        p.dma("sp", lambda: nc.sync.dma_start(out=self.Vb[s][:, :, 0:64], in_=dr["vf"][ki]), writes=[f"Vb{s}"])
        g = dr["G"]
        src = bass.AP(tensor=g.tensor, offset=g[bi:bi + 1, 0:1].offset, ap=[[1, 128], [1, BW]])
        p.dma("sp", lambda: nc.sync.dma_start(out=self.bandb[s][:], in_=src), writes=["bandb"])
        return s

    def add_units(self, s, m, kts, near_lo, bi, acc, mask=None, post=None, pre=None):
        n = len(kts)
        for idx, kt in enumerate(kts):
            self.units.append(dict(s=s, m=m, kt=kt, near=(kt >= near_lo), bi=bi, acc=acc, first=(idx == 0), last=(idx == n - 1),
                                   mask=mask, post=post if idx == n - 1 else None, pre=pre if idx == 0 else None))

    def flush_units(self, LA=3):
        nc, p = self.nc, self.p
        U = self.units
        n = len(U)

        def qk(i):
            u = U[i]
            if u["pre"] is not None:
                u["pre"]()
            b = i % 4
            s, m, kt = u["s"], u["m"], u["kt"]
            rd = [f"kTb{s}", f"qTb{s}"]
            if u["mask"] is None:
                p.op("pe", lambda: nc.tensor.matmul(self.pS[b][:], lhsT=self.kTall[:, kt * 128:(kt + 1) * 128],
                                                    rhs=self.qTz[s][:, m * 512:(m + 1) * 512], start=True, stop=True),
                     reads=rd, writes=[f"pS{b}"])
            else:
                nr, ms = u["mask"]
                p.op("pe", lambda: nc.tensor.matmul(self.pS[b][:], lhsT=self.kTall[:, kt * 128:(kt + 1) * 128],
                                                    rhs=self.qTz[s][:, m * 512:(m + 1) * 512], start=True, stop=False),
                     reads=rd, writes=[f"pS{b}"])
                p.op("pe", lambda: nc.tensor.matmul(self.pS[b][:], lhsT=self.Eb[:, kt * 128:(kt + 1) * 128],
                                                    rhs=self.MTb[ms][:, m * 512:(m + 1) * 512], start=False, stop=True),
                     reads=["Eb", f"MTb{ms}"], writes=[f"pS{b}"])

        def rest(i):
            u = U[i]
            b = i % 4
            s, m, kt, bi, acc = u["s"], u["m"], u["kt"], u["bi"], u["acc"]
            pt = i % 3
            if u["near"]:
                sbi = i % 2
                u0 = 2048 * m - 128 * kt + 1920
                assert 0 <= u0 and u0 + 512 <= BW, (m, kt, u0)
                p.op("dve", lambda: nc.vector.tensor_tensor(out=self.sb[sbi][:], in0=self.pS[b][:], in1=self.bandb[s][:, u0:u0 + 512], op=ALU.add),
                     reads=[f"pS{b}", "bandb"], writes=[f"sb{sbi}"])
                p.op("act", lambda: nc.scalar.activation(out=self.PT[pt][:], in_=self.sb[sbi][:], func=AF.Exp),
                     reads=[f"sb{sbi}"], writes=[f"PT{pt}"])
            else:
                p.op("act", lambda: nc.scalar.activation(out=self.PT[pt][:], in_=self.pS[b][:], func=AF.Exp, bias=self.cbb[:, bi:bi + 1]),
                     reads=[f"pS{b}", "cbb"], writes=[f"PT{pt}"])
            p.op("pe", lambda: nc.tensor.matmul(self.pacc[acc][:], lhsT=self.Vb[s][:, kt, :], rhs=self.PT[pt][:],
                                                start=u["first"], stop=u["last"]),
                 reads=[f"Vb{s}", f"Vones{s}", f"PT{pt}"], writes=[f"pacc{acc}"])
            if u["post"] is not None:
                u["post"]()

        for i in range(n + LA):
            if i < n:
                qk(i)
            if i - LA >= 0:
                rest(i - LA)
        self.units = []

    def write_out(self, head_feat, m, num_ap, rden_ap):
        nc, p = self.nc, self.p
        o = self.ost_i % 3
        self.ost_i += 1
        num, nres = num_ap
        rd, rres = rden_ap
        p.op("dve", lambda: nc.vector.tensor_tensor(out=self.ost[o][:], in0=num, in1=rd, op=ALU.mult),
             reads=[nres, rres], writes=[f"ost{o}"])
        dst = self.dr["OT"][head_feat * 64:(head_feat + 1) * 64, m * 512:(m + 1) * 512]
        self.out_ops.append(p.dma("pool", lambda: nc.gpsimd.dma_start(out=dst, in_=self.ost[o][:]), reads=[f"ost{o}"]))

    def mixer_A(self):
        nc, p = self.nc, self.p
        for hg in self.heads_A:
            for g in range(3):
                W, d = A_CFG[g]
                h = g * 4 + hg
                s = self.load_head(h, h, h)
                for m in range(4):
                    lo = max(0, (2048 * m - W) // 128)
                    kts = list(range(lo, 16 * m + 16))
                    acc = self.acc_i % 2
                    self.acc_i += 1

                    def post(g=g, m=m, acc=acc):
                        p.op("act", lambda: nc.scalar.copy(out=self.nd[:, g * 4 + m, :], in_=self.pacc[acc][:]),
                             reads=[f"pacc{acc}"], writes=[f"nd{g}_{m}"])
                    self.add_units(s, m, kts, 0, h, acc, post=post)
                self.flush_units()
            for m in range(4):
                p.op("pool", lambda m=m: nc.gpsimd.tensor_tensor(out=self.dsum[64:128, :], in0=self.nd[64:128, 0 * 4 + m, :], in1=self.nd[64:128, 1 * 4 + m, :], op=ALU.add),
                     reads=[f"nd0_{m}", f"nd1_{m}"], writes=["sb1"])
                p.op("pool", lambda m=m: nc.gpsimd.tensor_tensor(out=self.dsum[64:128, :], in0=self.dsum[64:128, :], in1=self.nd[64:128, 2 * 4 + m, :], op=ALU.add),
                     reads=["sb1", f"nd2_{m}"], writes=["sb1"])
                p.op("dve", lambda: nc.vector.reciprocal(out=self.rden[:], in_=self.dsum[64:128, :]), reads=["sb1"], writes=["rden"])
                for g in range(3):
                    self.write_out(g * 4 + hg, m, (self.nd[0:64, g * 4 + m, :], f"nd{g}_{m}"), (self.rden[:], "rden"))

    def moba_prologue(self, s, ms):
        nc, p = self.nc, self.p
        p.op("dve", lambda: nc.vector.tensor_reduce(out=self.km[64 * s:64 * s + 64, :], in_=self.kTb[s].rearrange("e (n k) -> e n k", k=256), axis=AX.X, op=ALU.add),
             reads=[f"kTb{s}"], writes=["km"])
        p.op("dve", lambda: nc.vector.tensor_scalar(out=self.kmb[64 * s:64 * s + 64, :], in0=self.km[64 * s:64 * s + 64, :], scalar1=1.0 / 256, scalar2=None, op0=ALU.mult),
             reads=["km"], writes=["kmb"])
        pg = self.pm[0]
        for qs in range(16):
            p.op("pe", lambda qs=qs: nc.tensor.matmul(pg[:, qs * 32:(qs + 1) * 32], lhsT=self.qTb[s][:, qs * 128:(qs + 1) * 128], rhs=self.kmb[64 * s:64 * s + 64, :], start=True, stop=True),
                 reads=[f"qTb{s}", "kmb"], writes=["pm0"])
        p.op("dve", lambda: nc.vector.tensor_tensor(out=self.gm[:], in0=pg[:].rearrange("p (a b) -> p a b", b=32), in1=self.negm[:], op=ALU.add),
             reads=["pm0", "negm"], writes=["gm"])
        for qs in range(16):
            p.op("dve", lambda qs=qs: nc.vector.max(out=self.m8[:, qs, :], in_=self.gm[:, qs, :]), reads=["gm"], writes=["m8"])
        for qs in range(16):
            p.op("dve", lambda qs=qs: nc.vector.tensor_scalar(out=self.selt[:, qs, :], in0=self.gm[:, qs, :], scalar1=self.m8[:, qs, 2:3], scalar2=None, op0=ALU.is_ge),
                 reads=["gm", "m8"], writes=["selt"])
        p.op("dve", lambda: nc.vector.tensor_tensor(out=self.selt[:], in0=self.selt[:], in1=self.ownm[:], op=ALU.max), reads=["selt", "ownm"], writes=["selt"])
        p.op("dve", lambda: nc.vector.tensor_scalar(out=self.Mq[:], in0=self.selt[:], scalar1=-1.0, scalar2=-NEGM, op0=ALU.add, op1=ALU.mult),
             reads=["selt"], writes=["Mq"])
        for half in range(2):
            for q8 in range(8):
                qs = half * 8 + q8
                p.op("pe", lambda qs=qs, q8=q8: nc.tensor.transpose(out=self.pmb[0:32, q8 * 128:(q8 + 1) * 128], in_=self.Mq[:, qs, :], identity=self.identb[:]),
                     reads=["Mq", "identb"], writes=["pmb"])
            p.op("act", lambda half=half: nc.scalar.copy(out=self.MTb[ms][0:32, half * 1024:(half + 1) * 1024], in_=self.pmb[0:32, :]),
                 reads=["pmb"], writes=[f"MTb{ms}"])

    def mixer_B(self):
        nc, p, dr = self.nc, self.p, self.dr
        p.dma("sp", lambda: nc.sync.dma_start(out=self.negm[:], in_=dr["negm"]), writes=["negm"])
        p.dma("sp", lambda: nc.sync.dma_start(out=self.ownm[:], in_=dr["ownm"]), writes=["ownm"])
        p.dma("sp", lambda: nc.sync.dma_start(out=self.Eb[0:32, :], in_=dr["EB"]), writes=["Eb"])
        for hb in self.heads_B:
            h = 12 + hb
            s = self.load_head(h, h, h)
            ms = hb % 2
            self.moba_prologue(s, ms)
            for m in range(4):
                kts = list(range(0, 16 * m + 16))
                acc = self.acc_i % 2
                self.acc_i += 1

                def post(h=h, m=m, acc=acc):
                    p.op("dve", lambda: nc.vector.reciprocal(out=self.rden[:], in_=self.pacc[acc][64:128, :]), reads=[f"pacc{acc}"], writes=["rden"])
                    self.write_out(h, m, (self.pacc[acc][0:64, :], f"pacc{acc}"), (self.rden[:], "rden"))
                self.add_units(s, m, kts, 16 * m - 12, h, acc, mask=(32, ms), post=post)
            self.flush_units()


    def compress(self, kv, t):
        nc, p, dr = self.nc, self.p, self.dr
        s = 0
        if self.fused:
            self.load_kT_gathered(self.kTb[s], dr["cmpT_g"], (t * 3 + kv) * 64, f"kTb{s}")
        else:
            p.dma("sp", lambda: nc.sync.dma_start(out=self.kTb[s], in_=dr["cmpTf"][t * 3 + kv]), writes=[f"kTb{s}"])
        stg = self.band1[0:64, 0:4096].rearrange("e (l j) -> e l j", j=128)
        p.dma("sp", lambda: nc.sync.dma_start(out=stg, in_=dr["w1"][t]), writes=["bandb"])
        p.op("act", lambda: nc.scalar.copy(out=self.w1b[:], in_=stg), reads=["bandb"], writes=["w1b"])
        ph = self.pm[0]
        base = self.kTb[s]
        for l in range(32):
            rhs = bass.AP(tensor=base.tensor, offset=base.offset + l, ap=[list(base.ap[0]), [16, 511]])
            p.op("pe", lambda l=l, rhs=rhs: nc.tensor.matmul(ph[:, 0:511], lhsT=self.w1b[:, l, :], rhs=rhs, start=(l == 0), stop=(l == 31)),
                 reads=["w1b", f"kTb{s}"], writes=["pm0"])
        pc = self.pm[1]
        for l in range(32):
            p.op("pe", lambda l=l: nc.tensor.matmul(pc[:, 0:2], lhsT=self.w1b[:, l, :], rhs=self.peTb[:, t, l, :], start=(l == 0), stop=(l == 31)),
                 reads=["w1b", "peTb"], writes=["pS3"])
        p.op("dve", lambda: nc.vector.tensor_copy(out=self.cbias[:], in_=pc[:, 0:1]), reads=["pS3"], writes=["cbias"])
        x, y = self.scr[0], self.scr[1]
        p.op("act", lambda: nc.scalar.activation(out=x[:, 0:511], in_=ph[:, 0:511], func=AF.Identity, bias=self.cbias[:, 0:1]),
             reads=["pm0", "cbias"], writes=["scr0"])
        p.op("dve", lambda: nc.vector.tensor_tensor(out=y[:, 0:511], in0=x[:, 0:511], in1=x[:, 0:511], op=ALU.mult), reads=["scr0"], writes=["scr1"])
        p.op("dve", lambda: nc.vector.tensor_scalar(out=y[:, 0:511], in0=y[:, 0:511], scalar1=0.044715, scalar2=1.0, op0=ALU.mult, op1=ALU.add), reads=["scr1"], writes=["scr1"])
        p.op("dve", lambda: nc.vector.tensor_tensor(out=y[:, 0:511], in0=y[:, 0:511], in1=x[:, 0:511], op=ALU.mult), reads=["scr0", "scr1"], writes=["scr1"])
        p.op("act", lambda: nc.scalar.activation(out=y[:, 0:511], in_=y[:, 0:511], func=AF.Tanh, scale=0.7978845608028654), reads=["scr1"], writes=["scr1"])
        p.op("dve", lambda: nc.vector.scalar_tensor_tensor(out=y[:, 0:511], in0=y[:, 0:511], scalar=1.0, in1=x[:, 0:511], op0=ALU.add, op1=ALU.mult), reads=["scr0", "scr1"], writes=["scr1"])
        p.op("dve", lambda: nc.vector.tensor_scalar(out=self.hid[:, 0:511], in0=y[:, 0:511], scalar1=0.5, scalar2=None, op0=ALU.mult), reads=["scr1"], writes=["hid"])
        if t == 0:
            pk = self.pS[0]
            p.op("pe", lambda: nc.tensor.matmul(pk[0:64, :], lhsT=self.w2b[:, 0, :], rhs=self.hid[:], start=True, stop=True),
                 reads=["w2b", "hid"], writes=["pS0"])
            p.op("act", lambda: nc.scalar.copy(out=self.kcTb[:], in_=pk[0:64, :]), reads=["pS0"], writes=["kcTb"])
        else:
            pv = self.pS[1]
            for it in range(4):
                p.op("pe", lambda it=it: nc.tensor.matmul(pv[:, it * 64:(it + 1) * 64], lhsT=self.hid[:, it * 128:(it + 1) * 128], rhs=self.w2b[:, 1, :], start=True, stop=True),
                     reads=["w2b", "hid"], writes=["pS1"])
            p.op("act", lambda: nc.scalar.copy(out=self.vcb[:], in_=pv[:, 0:256].rearrange("p (a b) -> p a b", b=64)), reads=["pS1"], writes=["vcb"])

    def cmp_stage(self, kv, m):
        nc, p, dr = self.nc, self.p, self.dr
        pden, poc, pgt, pimp, ptr = self.pS[0], self.pS[1], self.pS[2], self.pacc[0], self.pacc[1]
        nit = min(m, 3) + 1
        for gq in range(4):
            hc = kv * 4 + gq
            qs = self.qc_i % 3
            self.qc_i += 1
            p.dma("sp", lambda qs=qs, hc=hc: nc.sync.dma_start(out=self.qcm[qs][:], in_=dr["qT"][20 + hc][:, m * 512:(m + 1) * 512]), writes=[f"qcm{qs}"])
            p.dma("sp", lambda hc=hc: nc.sync.dma_start(out=self.sel3[:], in_=dr["selg"][:, 3 * hc:3 * hc + 3, :]), writes=["sel3"])
            for it in range(nit):
                ps = self.pm[it % 2]
                psn = self.pmn[it % 2]
                p.op("pe", lambda it=it, ps=ps, qs=qs: nc.tensor.matmul(ps[:], lhsT=self.kcTb[:, it * 128:(it + 1) * 128], rhs=self.qcm[qs][:], start=True, stop=True),
                     reads=["kcTb", f"qcm{qs}"], writes=[psn])
                if it >= m - 1:
                    cm, cmn = (self.cmA, "cmA") if it == m else (self.cmB, "cmB")
                    sbi = it % 2
                    p.op("dve", lambda ps=ps, cm=cm, sbi=sbi: nc.vector.tensor_tensor(out=self.sb[sbi][:], in0=ps[:], in1=cm[:], op=ALU.add),
                         reads=[psn, cmn], writes=[f"sb{sbi}"])
                    p.op("act", lambda it=it, sbi=sbi: nc.scalar.activation(out=self.ef[it][:], in_=self.sb[sbi][:], func=AF.Exp), reads=[f"sb{sbi}"], writes=[f"ef{it}"])
                else:
                    p.op("act", lambda it=it, ps=ps: nc.scalar.activation(out=self.ef[it][:], in_=ps[:], func=AF.Exp), reads=[psn], writes=[f"ef{it}"])
                p.op("pe", lambda it=it: nc.tensor.matmul(pden[:], lhsT=self.onesf[:], rhs=self.ef[it][:], start=(it == 0), stop=(it == nit - 1)),
                     reads=["onesf", f"ef{it}"], writes=["pS0"])
            rd = self.scr[0]
            p.op("dve", lambda: nc.vector.tensor_scalar(out=rd[:], in0=pden[:], scalar1=1e-30, scalar2=None, op0=ALU.max), reads=["pS0"], writes=["scr0"])
            p.op("dve", lambda: nc.vector.reciprocal(out=rd[:], in_=rd[:]), reads=["scr0"], writes=["scr0"])
            for it in range(nit):
                p.op("dve", lambda it=it: nc.vector.tensor_tensor(out=self.ef[it][:], in0=self.ef[it][:], in1=rd[:], op=ALU.mult), reads=[f"ef{it}", "scr0"], writes=[f"ef{it}"])
                p.op("pool", lambda it=it: nc.gpsimd.tensor_copy(out=self.pcb[it][:], in_=self.ef[it][:]), reads=[f"ef{it}"], writes=[f"pcb{it}"])
                p.op("pe", lambda it=it, gq=gq: nc.tensor.matmul(pimp[:], lhsT=self.ovl[:, it, :], rhs=self.ef[it][:], start=(gq == 0 and it == 0), stop=(gq == 3 and it == nit - 1)),
                     reads=["ovl", f"ef{it}"], writes=["pacc0"])
            for it in range(nit):
                p.op("pe", lambda it=it: nc.tensor.matmul(poc[0:64, :], lhsT=self.vcb[:, it, :], rhs=self.pcb[it][:], start=(it == 0), stop=(it == nit - 1)),
                     reads=["vcb", f"pcb{it}"], writes=["pS1"])
            p.op("pe", lambda: nc.tensor.matmul(pgt[0:64, :], lhsT=self.sel3[:, 0, :], rhs=self.gTs[:, m * 512:(m + 1) * 512], start=True, stop=True),
                 reads=["sel3", "gTs"], writes=["pS2"])
            p.op("act", lambda: nc.scalar.copy(out=self.ocs, in_=poc[0:64, :]), reads=["pS1"], writes=["scr1"])
            half, idx = gq // 2, (gq % 2) * 4 + m
            if half == 0:
                p.op("dve", lambda idx=idx: nc.vector.tensor_tensor(out=self.nd[0:64, idx, :], in0=self.ocs, in1=pgt[0:64, :], op=ALU.mult),
                     reads=["scr1", "pS2"], writes=[f"oc{gq}_{m}"])
            else:
                p.op("dve", lambda: nc.vector.tensor_tensor(out=self.tmpf[:], in0=self.ocs, in1=pgt[0:64, :], op=ALU.mult),
                     reads=["scr1", "pS2"], writes=["tmpf"])
                p.op("dve", lambda idx=idx: nc.vector.tensor_copy(out=self.nd[64:128, idx, :], in_=self.tmpf[:]), reads=["tmpf"], writes=[f"oc{gq}_{m}"])
        p.dma("sp", lambda: nc.sync.dma_start(out=self.amb[:], in_=dr["AM"][:, 4 * m:4 * m + 4, :]), writes=["amb"])
        p.dma("sp", lambda: nc.sync.dma_start(out=self.bab[:], in_=dr["BA"][:, 4 * m:4 * m + 4, :]), writes=["bab"])
        p.op("act", lambda: nc.scalar.copy(out=self.impS[:], in_=pimp[:]), reads=["pacc0"], writes=["impS"])
        for qs in range(4):
            p.op("pe", lambda qs=qs: nc.tensor.transpose(out=ptr[:, qs * 128:(qs + 1) * 128], in_=self.impS[:, qs * 128:(qs + 1) * 128], identity=self.identf[:]),
                 reads=["impS", "identf"], writes=["pacc1"])
        p.op("dve", lambda: nc.vector.tensor_tensor(out=self.score[:], in0=ptr[:].rearrange("p (a b) -> p a b", b=128), in1=self.amb[:], op=ALU.mult),
             reads=["pacc1", "amb"], writes=["score"])
        p.op("dve", lambda: nc.vector.tensor_tensor(out=self.score[:], in0=self.score[:], in1=self.bab[:], op=ALU.add), reads=["score", "bab"], writes=["score"])
        for qs in range(4):
            p.op("dve", lambda qs=qs: nc.vector.max(out=self.m16[:, qs, 0:8], in_=self.score[:, qs, :]), reads=["score"], writes=["m16"])
            p.op("dve", lambda qs=qs: nc.vector.match_replace(out=self.sc2[:], in_to_replace=self.m16[:, qs, 0:8], in_values=self.score[:, qs, :], imm_value=-1e30),
                 reads=["score", "m16"], writes=["sc2"])
            p.op("dve", lambda qs=qs: nc.vector.max(out=self.m16[:, qs, 8:16], in_=self.sc2[:]), reads=["sc2"], writes=["m16"])
            p.op("dve", lambda qs=qs: nc.vector.tensor_scalar(out=self.score[:, qs, :], in0=self.score[:, qs, :], scalar1=self.m16[:, qs, 15:16], scalar2=None, op0=ALU.is_ge),
                 reads=["score", "m16"], writes=["score"])
        p.op("dve", lambda: nc.vector.tensor_scalar(out=self.Msel[:], in0=self.score[:], scalar1=-1.0, scalar2=-NEGM, op0=ALU.add, op1=ALU.mult), reads=["score"], writes=["Msel"])
        ms = kv % 2
        for qs in range(4):
            p.op("pe", lambda qs=qs: nc.tensor.transpose(out=self.pmb[:, qs * 128:(qs + 1) * 128], in_=self.Msel[:, qs, :], identity=self.identb[:]),
                 reads=["Msel", "identb"], writes=["pmb"])
        p.op("act", lambda: nc.scalar.copy(out=self.MTb[ms][:, m * 512:(m + 1) * 512], in_=self.pmb[:, 0:512]), reads=["pmb"], writes=[f"MTb{ms}"])

    def mixer_C(self):
        nc, p, dr = self.nc, self.p, self.dr
        self.qc_i = 0
        p.dma("sp", lambda: nc.sync.dma_start(out=self.Eb[:], in_=dr["EC"]), writes=["Eb"])
        for nm in ("cmA", "cmB", "ovl", "gTs"):
            p.dma("sp", lambda nm=nm: nc.sync.dma_start(out=getattr(self, nm)[:], in_=dr[nm]), writes=[nm])
        p.dma("sp", lambda: nc.sync.dma_start(out=self.w2f[:], in_=dr["w2"]), writes=["w2f"])
        p.dma("sp", lambda: nc.sync.dma_start(out=self.peTf[:], in_=dr["peT"]), writes=["peTf"])
        p.op("dve", lambda: nc.vector.tensor_copy(out=self.w2b[:], in_=self.w2f[:]), reads=["w2f"], writes=["w2b"])
        p.op("dve", lambda: nc.vector.tensor_copy(out=self.peTb[:], in_=self.peTf[:]), reads=["peTf"], writes=["peTb"])
        p.op("pool", lambda: nc.gpsimd.memset(self.onesf[:], 1.0), writes=["onesf"])
        p.op("pool", lambda: nc.gpsimd.memset(self.hid[:], 0.0), writes=["hid"])
        for kv in self.kvs_C:
            self.compress(kv, 0)
            self.compress(kv, 1)
            for m in range(4):
                self.cmp_stage(kv, m)
            ms = kv % 2
            for gq in range(4):
                hc = kv * 4 + gq
                s = self.load_head(20 + hc, 20 + kv, 20 + hc)
                p.dma("sp", lambda hc=hc: nc.sync.dma_start(out=self.sel3[:], in_=dr["selg"][:, 3 * hc:3 * hc + 3, :]), writes=["sel3"])
                for m in range(4):
                    kts = list(range(0, 16 * m + 16))
                    acc = self.acc_i % 2
                    self.acc_i += 1

                    def post(gq=gq, m=m, acc=acc):
                        pg = self.pm[0]
                        p.op("dve", lambda: nc.vector.reciprocal(out=self.rden[:], in_=self.pacc[acc][64:128, :]), reads=[f"pacc{acc}"], writes=["rden"])
                        p.op("pe", lambda: nc.tensor.matmul(pg[0:64, :], lhsT=self.sel3[:, 1, :], rhs=self.gTs[:, m * 512:(m + 1) * 512], start=True, stop=True),
                             reads=["sel3", "gTs"], writes=["pm0"])
                        p.op("dve", lambda: nc.vector.tensor_tensor(out=self.tmpf[:], in0=self.pacc[acc][0:64, :], in1=self.rden[:], op=ALU.mult),
                             reads=[f"pacc{acc}", "rden"], writes=["tmpf"])
                        p.op("dve", lambda: nc.vector.tensor_tensor(out=self.nd[0:64, 8 + m, :], in0=self.tmpf[:], in1=pg[0:64, :], op=ALU.mult),
                             reads=["tmpf", "pm0"], writes=[f"res{m}"])
                        half, idx = gq // 2, (gq % 2) * 4 + m
                        if half == 0:
                            p.op("pool", lambda: nc.gpsimd.tensor_tensor(out=self.nd[0:64, 8 + m, :], in0=self.nd[0:64, 8 + m, :], in1=self.nd[0:64, idx, :], op=ALU.add),
                                 reads=[f"res{m}", f"oc{gq}_{m}"], writes=[f"res{m}"])
                        else:
                            p.op("dve", lambda: nc.vector.tensor_copy(out=self.tmpf[:], in_=self.nd[64:128, idx, :]), reads=[f"oc{gq}_{m}"], writes=["tmpf"])
                            p.op("pool", lambda: nc.gpsimd.tensor_tensor(out=self.nd[0:64, 8 + m, :], in0=self.nd[0:64, 8 + m, :], in1=self.tmpf[:], op=ALU.add),
                                 reads=[f"res{m}", "tmpf"], writes=[f"res{m}"])
                    self.add_units(s, m, kts, 16 * m - 12, 20 + hc, acc, mask=(128, ms), post=post)
                self.flush_units()
                s = self.load_head(20 + hc, 23 + kv, 32 + hc)
                for m in range(4):
                    kts = list(range(max(0, 16 * m - 4), 16 * m + 16))
                    acc = self.acc_i % 2
                    self.acc_i += 1

                    def post(hc=hc, m=m, acc=acc):
                        pg = self.pm[1]
                        p.op("dve", lambda: nc.vector.reciprocal(out=self.rden[:], in_=self.pacc[acc][64:128, :]), reads=[f"pacc{acc}"], writes=["rden"])
                        p.op("pe", lambda: nc.tensor.matmul(pg[0:64, :], lhsT=self.sel3[:, 2, :], rhs=self.gTs[:, m * 512:(m + 1) * 512], start=True, stop=True),
                             reads=["sel3", "gTs"], writes=["pS3"])
                        p.op("dve", lambda: nc.vector.tensor_tensor(out=self.tmpf[:], in0=self.pacc[acc][0:64, :], in1=self.rden[:], op=ALU.mult),
                             reads=[f"pacc{acc}", "rden"], writes=["tmpf"])
                        p.op("dve", lambda: nc.vector.tensor_tensor(out=self.tmpf[:], in0=self.tmpf[:], in1=pg[0:64, :], op=ALU.mult),
                             reads=["tmpf", "pS3"], writes=["tmpf"])
                        o = self.ost_i % 3
                        self.ost_i += 1
                        p.op("dve", lambda: nc.vector.tensor_tensor(out=self.ost[o][:], in0=self.tmpf[:], in1=self.nd[0:64, 8 + m, :], op=ALU.add),
                             reads=["tmpf", f"res{m}"], writes=[f"ost{o}"])
                        dst = self.dr["OT"][(20 + hc) * 64:(21 + hc) * 64, m * 512:(m + 1) * 512]
                        self.out_ops.append(p.dma("pool", lambda: nc.gpsimd.dma_start(out=dst, in_=self.ost[o][:]), reads=[f"ost{o}"]))
                    self.add_units(s, m, kts, 0, 32 + hc, acc, post=post)
                self.flush_units()


def dram_p2(nc):
    dr = {}
    I = lambda name, shape, dt: nc.dram_tensor(name, shape, dt, kind="ExternalInput").ap()
    dr["qT"] = I("qT", [32, 64, NT], BF16)
    dr["kTf"] = I("kTf", [26, 64, S], BF16)
    dr["vf"] = I("vf", [26, 128, 64, 64], BF16)
    dr["G"] = I("G", [44, GL], F32)
    dr["cb"] = I("cb", [128, 44], F32)
    dr["negm"] = I("negm", [128, 16, 32], F32)
    dr["ownm"] = I("ownm", [128, 16, 32], F32)
    dr["EB"] = I("EB", [32, S], BF16)
    dr["cmpTf"] = I("cmpTf", [6, 64, S], BF16)
    dr["w1"] = I("w1", [2, 64, 32, 128], F32)
    dr["w2"] = I("w2", [128, 2, 64], F32)
    dr["peT"] = I("peT", [64, 2, 32, 2], F32)
    dr["EC"] = I("EC", [128, S], BF16)
    dr["AM"] = I("AM", [128, 16, 128], F32)
    dr["BA"] = I("BA", [128, 16, 128], F32)
    dr["cmA"] = I("cmA", [128, 512], F32)
    dr["cmB"] = I("cmB", [128, 512], F32)
    dr["ovl"] = I("ovl", [128, 4, 128], F32)
    dr["selg"] = I("selg", [36, 36, 64], F32)
    dr["gTs"] = I("gTs", [36, NT], F32)
    dr["OT"] = nc.dram_tensor("OT", [2048, NT], BF16, kind="ExternalOutput").ap()
    return dr


def build_p2(**kw):
    nc = bass.Bass("TRN2", target_bir_lowering=False)
    dr = dram_p2(nc)
    with contextlib.ExitStack() as st:
        T = lambda name, shape, dt: st.enter_context(nc.sbuf_tensor("s_" + name, shape, dt))
        PS = lambda name, shape, dt: st.enter_context(nc.psum_tensor("p_" + name, shape, dt))
        p = Prog(nc)
        P = P2(nc, p, T, PS, dr, **kw)
        P.setup()
        P.mixer_A()
        P.mixer_B()
        P.mixer_C()
        p.emit(final_wait_ops=P.out_ops)
    return nc

import contextlib
import numpy as np

D = 2048
NT = 2048
DFF = 5632
EPS = 1e-6
TW = 514


def build_p3a():
    nc = bass.Bass("TRN2", target_bir_lowering=False)
    I = lambda name, shape, dt: nc.dram_tensor(name, shape, dt, kind="ExternalInput").ap()
    xT = I("xT", [D, 4, TW], F32)
    OT = I("OT", [D, 4, TW], BF16)
    w = I("w", [D, D], F32)
    gn = I("gn", [128, 16], F32)
    x1T = nc.dram_tensor("x1T", [D, 4, TW], F32, kind="ExternalOutput").ap()
    hT = nc.dram_tensor("hT", [128, 16, 4, TW], BF16, kind="ExternalOutput").ap()
    with contextlib.ExitStack() as st:
        T = lambda name, shape, dt: st.enter_context(nc.sbuf_tensor("s_" + name, shape, dt))
        PS = lambda name, shape, dt: st.enter_context(nc.psum_tensor("p_" + name, shape, dt))
        p = Prog(nc)
        outs = []
        OTb = T("OTb", [128, 16, 4, TW], BF16)
        gsb = T("gsb", [128, 16], F32)
        ones = T("ones", [128, 128], F32)
        wst = [T(f"wst{i}", [128, 16, 128], F32) for i in range(2)]
        wbf = [T(f"wbf{i}", [128, 16, 128], BF16) for i in range(2)]
        xc = [T(f"xc{i}", [128, 4, TW], F32) for i in range(2)]
        x1c = [T(f"x1c{i}", [128, 4, TW], F32) for i in range(2)]
        sqt = T("sqt", [128, 4, TW], F32)
        accsq = T("accsq", [128, 4, TW], F32)
        rstd = T("rstd", [128, 4, TW], F32)
        hc = [T(f"hc{i}", [128, 4, TW], BF16) for i in range(2)]
        pacc = [PS(f"pacc{i}", [128, 512], F32) for i in range(4)]
        ph = PS("ph", [128, 512], F32)
        pss = [PS(f"pss{i}", [128, 512], F32) for i in range(2)]

        p.dma("sp", lambda: nc.sync.dma_start(out=gsb[:], in_=gn), writes=["gsb"])
        p.op("pool", lambda: nc.gpsimd.memset(ones[:], 1.0), writes=["ones"])
        p.op("pool", lambda: nc.gpsimd.memset(accsq[:], 0.0), writes=["accsq"])
        OTv = OT.rearrange("(k p) m t -> p k m t", p=128)
        for k4 in range(4):
            p.dma("sp", lambda k4=k4: nc.sync.dma_start(out=OTb[:, k4 * 4:(k4 + 1) * 4], in_=OTv[:, k4 * 4:(k4 + 1) * 4]), writes=[f"OTb{k4}"])
        OTres = [f"OTb{k4}" for k4 in range(4)]
        ai = 0
        for c in range(16):
            s = c % 2
            src = w[:, c * 128:(c + 1) * 128].rearrange("(k p) n -> p k n", p=128)
            p.dma("sp", lambda s=s, src=src: nc.sync.dma_start(out=wst[s][:], in_=src), writes=[f"wst{s}"])
            p.op("act", lambda s=s: nc.scalar.copy(out=wbf[s][:, 0:8], in_=wst[s][:, 0:8]), reads=[f"wst{s}"], writes=[f"wbfa{s}"])
            p.op("pool", lambda s=s: nc.gpsimd.tensor_copy(out=wbf[s][:, 8:16], in_=wst[s][:, 8:16]), reads=[f"wst{s}"], writes=[f"wbfb{s}"])
            p.dma("sp", lambda s=s, c=c: nc.sync.dma_start(out=xc[s][:], in_=xT[c * 128:(c + 1) * 128]), writes=[f"xc{s}"])
            for m in range(4):
                a = ai % 4
                ai += 1
                for k in range(16):
                    p.op("pe", lambda a=a, s=s, k=k, m=m: nc.tensor.matmul(pacc[a][:], lhsT=wbf[s][:, k, :], rhs=OTb[:, k, m, 2:TW], start=(k == 0), stop=(k == 15)),
                         reads=[f"wbfa{s}", f"wbfb{s}"] + OTres, writes=[f"pacc{a}"])
                p.op("dve", lambda a=a, s=s, m=m: nc.vector.tensor_tensor(out=x1c[s][:, m, 2:TW], in0=pacc[a][:], in1=xc[s][:, m, 2:TW], op=ALU.add),
                     reads=[f"pacc{a}", f"xc{s}"], writes=[f"x1c{s}"])
            for k in range(16):
                p.op("pe", lambda s=s, k=k: nc.tensor.matmul(ph[:, 0:8], lhsT=wbf[s][:, k, :], rhs=OTb[:, k, :, 0:2], start=(k == 0), stop=(k == 15)),
                     reads=[f"wbfa{s}", f"wbfb{s}"] + OTres, writes=["ph"])
            p.op("dve", lambda s=s: nc.vector.tensor_tensor(out=x1c[s][:, :, 0:2], in0=ph[:, 0:8].rearrange("p (m h) -> p m h", h=2), in1=xc[s][:, :, 0:2], op=ALU.add),
                 reads=["ph", f"xc{s}"], writes=[f"x1c{s}"])
            outs.append(p.dma("pool", lambda s=s, c=c: nc.gpsimd.dma_start(out=x1T[c * 128:(c + 1) * 128], in_=x1c[s][:]), reads=[f"x1c{s}"], writes=[f"x1T{c}"]))
            p.op("act", lambda s=s: nc.scalar.activation(out=sqt[:], in_=x1c[s][:], func=AF.Square), reads=[f"x1c{s}"], writes=["sqt"])
            p.op("pool", lambda: nc.gpsimd.tensor_tensor(out=accsq[:], in0=accsq[:], in1=sqt[:], op=ALU.add), reads=["sqt", "accsq"], writes=["accsq"])
        for m in range(4):
            q = m % 2
            p.op("pe", lambda q=q, m=m: nc.tensor.matmul(pss[q][:], lhsT=ones[:], rhs=accsq[:, m, 2:TW], start=True, stop=True), reads=["ones", "accsq"], writes=[f"pss{q}"])
            p.op("act", lambda q=q, m=m: nc.scalar.activation(out=rstd[:, m, 2:TW], in_=pss[q][:], func=AF.Sqrt, scale=1.0 / D, bias=EPS), reads=[f"pss{q}"], writes=["rstd"])
        p.op("pe", lambda: nc.tensor.matmul(ph[:, 0:8], lhsT=ones[:], rhs=accsq[:, :, 0:2], start=True, stop=True), reads=["ones", "accsq"], writes=["ph"])
        p.op("act", lambda: nc.scalar.activation(out=rstd[:, :, 0:2], in_=ph[:, 0:8].rearrange("p (m h) -> p m h", h=2), func=AF.Sqrt, scale=1.0 / D, bias=EPS), reads=["ph"], writes=["rstd"])
        p.op("dve", lambda: nc.vector.reciprocal(out=rstd[:], in_=rstd[:]), reads=["rstd"], writes=["rstd"])
        for c in range(16):
            s = c % 2
            p.dma("sp", lambda s=s, c=c: nc.sync.dma_start(out=x1c[s][:], in_=x1T[c * 128:(c + 1) * 128]), reads=[f"x1T{c}"], writes=[f"x1c{s}"])
            p.op("dve", lambda s=s, c=c: nc.vector.scalar_tensor_tensor(out=hc[s][:], in0=x1c[s][:], scalar=gsb[:, c:c + 1], in1=rstd[:], op0=ALU.mult, op1=ALU.mult),
                 reads=[f"x1c{s}", "gsb", "rstd"], writes=[f"hc{s}"])
            outs.append(p.dma("pool", lambda s=s, c=c: nc.gpsimd.dma_start(out=hT[:, c], in_=hc[s][:]), reads=[f"hc{s}"]))
        p.emit(final_wait_ops=outs)
    return nc


def emit_p3b(nc, p, T, PS, hT, x1get, wu, wd, cw, cbv, x2T, relayout=False):
    outs = []
    hTh = T("hTh", [128, 16, 2, TW], BF16)
    actT = T("actT", [128, 44, 1024], BF16)
    wst = [T(f"wst{i}", [128, 4096], F32) for i in range(2)]
    wbf = [T(f"wbf{i}", [128, 4096], BF16) for i in range(2)]
    cws = T("cws", [128, 88, 3], F32)
    cbs = T("cbs", [128, 88], F32)
    ua = [T(f"ua{i}", [128, TW], F32) for i in range(2)]
    ug = [T(f"ug{i}", [128, TW], F32) for i in range(2)]
    ya = [T(f"ya{i}", [128, 512], F32) for i in range(2)]
    yg = [T(f"yg{i}", [128, 512], F32) for i in range(2)]
    sg = [T(f"sg{i}", [128, 512], F32) for i in range(2)]
    x1c = [T(f"x1c{i}", [128, 2, 512], F32) for i in range(2)]
    pa = [PS(f"pa{i}", [128, 512], F32) for i in range(2)]
    pg = [PS(f"pg{i}", [128, 512], F32) for i in range(2)]
    ph = PS("ph", [128, 512], F32)
    pd = [PS(f"pd{i}", [128, 512], F32) for i in range(2)]
    p.dma("sp", lambda: nc.sync.dma_start(out=cws[:], in_=cw), writes=["cws"])
    p.dma("sp", lambda: nc.sync.dma_start(out=cbs[:], in_=cbv), writes=["cbs"])
    ui = [0]
    jobs = []
    for half in range(2):
        for c in range(44):
            jobs.append(("up", half, c, 0))
        for cc in range(16):
            for piece in range(2):
                jobs.append(("dn", half, cc, piece))

    def views(job, s):
        if job[0] == "up":
            return wst[s][:].rearrange("p (k n) -> p k n", n=256), wbf[s][:].rearrange("p (k n) -> p k n", n=256)
        return (wst[s][:, 0:22 * 128].rearrange("p (k n) -> p k n", n=128), wbf[s][:, 0:22 * 128].rearrange("p (k n) -> p k n", n=128))

    def load(job, s):
        wv, wb = views(job, s)
        if job[0] == "up":
            c = job[2]
            if relayout:
                srcp = wu[c].rearrange("p (k n) -> p k n", n=256)
                p.dma("sp", lambda wv=wv, srcp=srcp: nc.sync.dma_start(out=wv, in_=srcp), writes=[f"wst{s}_0", f"wst{s}_1"])
            else:
                for part in range(2):
                    col0 = part * DFF + c * 128
                    src = wu[:, col0:col0 + 128].rearrange("(k p) n -> p k n", p=128)
                    p.dma("sp", lambda wv=wv, src=src, part=part: nc.sync.dma_start(out=wv[:, :, part * 128:(part + 1) * 128], in_=src), writes=[f"wst{s}_{part}"])
            ka = 12
            kn = 16
        else:
            cc, piece = job[2], job[3]
            if relayout:
                src = wd[cc, piece].rearrange("p (k n) -> p k n", n=128)
            else:
                src = wd[piece * 2816:(piece + 1) * 2816, cc * 128:(cc + 1) * 128].rearrange("(k p) n -> p k n", p=128)
            p.dma("sp", lambda wv=wv, src=src: nc.sync.dma_start(out=wv, in_=src), writes=[f"wst{s}_0", f"wst{s}_1"])
            ka = 16
            kn = 22
        p.op("act", lambda wv=wv, wb=wb, ka=ka: nc.scalar.copy(out=wb[:, 0:ka], in_=wv[:, 0:ka]), reads=[f"wst{s}_0", f"wst{s}_1"], writes=[f"wbfa{s}"])
        p.op("pool", lambda wv=wv, wb=wb, ka=ka, kn=kn: nc.gpsimd.tensor_copy(out=wb[:, ka:kn], in_=wv[:, ka:kn]), reads=[f"wst{s}_0", f"wst{s}_1"], writes=[f"wbfb{s}"])

    def compute(job, s):
        wv, wb = views(job, s)
        wres = [f"wbfa{s}", f"wbfb{s}"]
        half = job[1]
        if job[0] == "up":
            c = job[2]
            if c == 0:
                p.dma("sp", lambda half=half: nc.sync.dma_start(out=hTh[:], in_=hT[:, :, 2 * half:2 * half + 2, :]), writes=["hTh"])
            for k in range(16):
                p.op("pe", lambda k=k, wb=wb: nc.tensor.matmul(ph[:, 0:4], lhsT=wb[:, k, 0:128], rhs=hTh[:, k, :, 0:2], start=(k == 0), stop=(k == 15)),
                     reads=wres + ["hTh"], writes=["ph"])
            for k in range(16):
                p.op("pe", lambda k=k, wb=wb: nc.tensor.matmul(ph[:, 4:8], lhsT=wb[:, k, 128:256], rhs=hTh[:, k, :, 0:2], start=(k == 0), stop=(k == 15)),
                     reads=wres + ["hTh"], writes=["ph"])
            for tt in range(2):
                u = ui[0] % 2
                ui[0] += 1
                for k in range(16):
                    p.op("pe", lambda u=u, k=k, tt=tt, wb=wb: nc.tensor.matmul(pa[u][:], lhsT=wb[:, k, 0:128], rhs=hTh[:, k, tt, 2:TW], start=(k == 0), stop=(k == 15)),
                         reads=wres + ["hTh"], writes=[f"pa{u}"])
                for k in range(16):
                    p.op("pe", lambda u=u, k=k, tt=tt, wb=wb: nc.tensor.matmul(pg[u][:], lhsT=wb[:, k, 128:256], rhs=hTh[:, k, tt, 2:TW], start=(k == 0), stop=(k == 15)),
                         reads=wres + ["hTh"], writes=[f"pg{u}"])
                p.op("act", lambda u=u: nc.scalar.copy(out=ua[u][:, 2:TW], in_=pa[u][:]), reads=[f"pa{u}"], writes=[f"ua{u}"])
                p.op("act", lambda u=u: nc.scalar.copy(out=ug[u][:, 2:TW], in_=pg[u][:]), reads=[f"pg{u}"], writes=[f"ug{u}"])
                p.op("act", lambda u=u, tt=tt: nc.scalar.copy(out=ua[u][:, 0:2], in_=ph[:, 2 * tt:2 * tt + 2]), reads=["ph"], writes=[f"ua{u}"])
                p.op("act", lambda u=u, tt=tt: nc.scalar.copy(out=ug[u][:, 0:2], in_=ph[:, 4 + 2 * tt:6 + 2 * tt]), reads=["ph"], writes=[f"ug{u}"])
                for (ub, yb, nm, ch) in ((ua, ya, "a", c), (ug, yg, "g", 44 + c)):
                    p.op("dve", lambda u=u, ub=ub, yb=yb, ch=ch: nc.vector.tensor_scalar(out=yb[u][:], in0=ub[u][:, 2:TW], scalar1=cws[:, ch, 0:1], scalar2=cbs[:, ch:ch + 1], op0=ALU.mult, op1=ALU.add),
                         reads=[f"u{nm}{u}", "cws", "cbs"], writes=[f"y{nm}{u}"])
                    p.op("dve", lambda u=u, ub=ub, yb=yb, ch=ch: nc.vector.scalar_tensor_tensor(out=yb[u][:], in0=ub[u][:, 1:TW - 1], scalar=cws[:, ch, 1:2], in1=yb[u][:], op0=ALU.mult, op1=ALU.add),
                         reads=[f"u{nm}{u}", "cws", f"y{nm}{u}"], writes=[f"y{nm}{u}"])
                    p.op("dve", lambda u=u, ub=ub, yb=yb, ch=ch: nc.vector.scalar_tensor_tensor(out=yb[u][:], in0=ub[u][:, 0:TW - 2], scalar=cws[:, ch, 2:3], in1=yb[u][:], op0=ALU.mult, op1=ALU.add),
                         reads=[f"u{nm}{u}", "cws", f"y{nm}{u}"], writes=[f"y{nm}{u}"])
                p.op("act", lambda u=u: nc.scalar.activation(out=sg[u][:], in_=yg[u][:], func=AF.Silu), reads=[f"yg{u}"], writes=[f"sg{u}"])
                p.op("pool", lambda u=u, c=c, tt=tt: nc.gpsimd.tensor_tensor(out=actT[:, c, tt * 512:(tt + 1) * 512], in0=sg[u][:], in1=ya[u][:], op=ALU.mult),
                     reads=[f"sg{u}", f"ya{u}"], writes=[f"actT{c}"])
        else:
            cc, piece = job[2], job[3]
            for tt in range(2):
                for k in range(22):
                    c = piece * 22 + k
                    p.op("pe", lambda wb=wb, k=k, c=c, tt=tt: nc.tensor.matmul(pd[tt][:], lhsT=wb[:, k, :], rhs=actT[:, c, tt * 512:(tt + 1) * 512], start=(c == 0), stop=(c == 43)),
                         reads=wres + [f"actT{c}"], writes=[f"pd{tt}"])
            if piece == 1:
                xs = cc % 2
                p.dma("sp", lambda xs=xs, cc=cc, half=half: nc.sync.dma_start(out=x1c[xs][:], in_=x1get(cc, half)), writes=[f"x1c{xs}"])
                for tt in range(2):
                    p.op("dve", lambda xs=xs, tt=tt: nc.vector.tensor_tensor(out=x1c[xs][:, tt, :], in0=pd[tt][:], in1=x1c[xs][:, tt, :], op=ALU.add),
                         reads=[f"pd{tt}", f"x1c{xs}"], writes=[f"x1c{xs}"])
                dst = x2T[cc * 128:(cc + 1) * 128, half * 1024:(half + 1) * 1024].rearrange("p (t n) -> p t n", n=512)
                outs.append(p.dma("pool", lambda xs=xs, dst=dst: nc.gpsimd.dma_start(out=dst, in_=x1c[xs][:]), reads=[f"x1c{xs}"]))

    load(jobs[0], 0)
    for i, job in enumerate(jobs):
        if i + 1 < len(jobs):
            load(jobs[i + 1], (i + 1) % 2)
        compute(job, i % 2)
    return outs


def build_p3b():
    nc = bass.Bass("TRN2", target_bir_lowering=False)
    I = lambda name, shape, dt: nc.dram_tensor(name, shape, dt, kind="ExternalInput").ap()
    hT = I("hT", [128, 16, 4, TW], BF16)
    x1T = I("x1T", [D, 4, TW], F32)
    wu = I("wu", [D, 2 * DFF], F32)
    wd = I("wd", [DFF, D], F32)
    cw = I("cw", [128, 88, 3], F32)
    cbv = I("cbv", [128, 88], F32)
    x2T = nc.dram_tensor("x2T", [D, NT], F32, kind="ExternalOutput").ap()
    with contextlib.ExitStack() as st:
        T = lambda name, shape, dt: st.enter_context(nc.sbuf_tensor("s_" + name, shape, dt))
        PS = lambda name, shape, dt: st.enter_context(nc.psum_tensor("p_" + name, shape, dt))
        p = Prog(nc)
        x1get = lambda cc, half: x1T[cc * 128:(cc + 1) * 128, 2 * half:2 * half + 2, 2:TW]
        outs = emit_p3b(nc, p, T, PS, hT, x1get, wu, wd, cw, cbv, x2T)
        p.emit(final_wait_ops=outs)
    return nc


def emit_kf(nc, p, T, PS, xT, gn, yT):
    outs = []
    gsb = T("gsb", [128, 16], F32)
    ones = T("ones", [128, 128], F32)
    xs = [T(f"xs{i}", [128, 16, 512], F32) for i in range(2)]
    ys = [T(f"ys{i}", [128, 16, 512], F32) for i in range(2)]
    sq = [T(f"sq{i}", [128, 512], F32) for i in range(2)]
    rstd = T("rstd", [128, 512], F32)
    pss = PS("pss", [128, 512], F32)
    p.dma("sp", lambda: nc.sync.dma_start(out=gsb[:], in_=gn), writes=["gsb"])
    p.op("pool", lambda: nc.gpsimd.memset(ones[:], 1.0), writes=["ones"])
    xv = xT.rearrange("(k p) t -> p k t", p=128)
    yv = yT.rearrange("(k p) t -> p k t", p=128)
    for m in range(4):
        s = m % 2
        p.dma("sp", lambda m=m, s=s: nc.sync.dma_start(out=xs[s][:], in_=xv[:, :, m * 512:(m + 1) * 512]), writes=[f"xs{s}"])
        for k in range(16):
            q = k % 2
            p.op("act", lambda s=s, k=k, q=q: nc.scalar.activation(out=sq[q][:], in_=xs[s][:, k, :], func=AF.Square), reads=[f"xs{s}"], writes=[f"sq{q}"])
            p.op("pe", lambda q=q, k=k: nc.tensor.matmul(pss[:], lhsT=ones[:], rhs=sq[q][:], start=(k == 0), stop=(k == 15)), reads=["ones", f"sq{q}"], writes=["pss"])
        p.op("act", lambda: nc.scalar.activation(out=rstd[:], in_=pss[:], func=AF.Sqrt, scale=1.0 / D, bias=EPS), reads=["pss"], writes=["rstd"])
        p.op("dve", lambda: nc.vector.reciprocal(out=rstd[:], in_=rstd[:]), reads=["rstd"], writes=["rstd"])
        for k in range(16):
            p.op("dve", lambda s=s, k=k: nc.vector.scalar_tensor_tensor(out=ys[s][:, k, :], in0=xs[s][:, k, :], scalar=gsb[:, k:k + 1], in1=rstd[:], op0=ALU.mult, op1=ALU.mult),
                 reads=[f"xs{s}", "gsb", "rstd"], writes=[f"ys{s}"])
        outs.append(p.dma("pool", lambda m=m, s=s: nc.gpsimd.dma_start(out=yv[:, :, m * 512:(m + 1) * 512], in_=ys[s][:]), reads=[f"ys{s}"]))
    return outs


def build_kf():
    nc = bass.Bass("TRN2", target_bir_lowering=False)
    xT = nc.dram_tensor("xT", [D, NT], F32, kind="ExternalInput").ap()
    gn = nc.dram_tensor("gn", [128, 16], F32, kind="ExternalInput").ap()
    yT = nc.dram_tensor("yT", [D, NT], F32, kind="ExternalOutput").ap()
    with contextlib.ExitStack() as st:
        T = lambda name, shape, dt: st.enter_context(nc.sbuf_tensor("s_" + name, shape, dt))
        PS = lambda name, shape, dt: st.enter_context(nc.psum_tensor("p_" + name, shape, dt))
        p = Prog(nc)
        outs = emit_kf(nc, p, T, PS, xT, gn, yT)
        p.emit(final_wait_ops=outs)
    return nc

import contextlib, math
import numpy as np

DEPTH = 4
RG = [[0, 1, 2, 3], [4, 5, 6, 7]]


def ec_const_nat():
    c = np.arange(S)
    return (c[None, :] // 64 == np.arange(128)[:, None]).astype(np.float32)


def halo_coef(j):
    co = np.zeros((128, 5), np.float32)
    if j >= 1:
        co[:, j - 1] = 1.0
    else:
        co[:, 4] = 1.0
    return co


def emit_p3a_f(nc, p, T, PS, xT, OT, w, gn, x1T, hT, ht_s):
    OTb = T("OTb", [128, 16, NT], BF16)
    gsb = T("gsb", [128, 16], F32)
    ones = T("ones", [128, 128], F32)
    wst = [T(f"wst{i}", [128, 16, 128], F32) for i in range(2)]
    wbf = [T(f"wbf{i}", [128, 16, 128], BF16) for i in range(2)]
    xc = [T(f"xc{i}", [128, NT], F32) for i in range(2)]
    x1c = [T(f"x1c{i}", [128, NT], F32) for i in range(2)]
    sqt = T("sqt", [128, NT], F32)
    accsq = T("accsq", [128, NT], F32)
    rstd = T("rstd", [128, NT], F32)
    hc = [T(f"hc{i}", [128, NT], BF16) for i in range(2)]
    pacc = [PS(f"pacc{i}", [128, 512], F32) for i in range(4)]
    pss = [PS(f"pss{i}", [128, 512], F32) for i in range(2)]
    p.dma("sp", lambda: nc.sync.dma_start(out=gsb[:], in_=gn), writes=["gsb"])
    p.op("pool", lambda: nc.gpsimd.memset(ones[:], 1.0), writes=["ones"])
    p.op("pool", lambda: nc.gpsimd.memset(accsq[:], 0.0), writes=["accsq"])
    OTv = OT.rearrange("(k p) t -> p k t", p=128)
    for k4 in range(4):
        p.dma("sp", lambda k4=k4: nc.sync.dma_start(out=OTb[:, k4 * 4:(k4 + 1) * 4], in_=OTv[:, k4 * 4:(k4 + 1) * 4]), writes=[f"OTb{k4}"])
    OTres = [f"OTb{k4}" for k4 in range(4)]
    ai = 0

    def load_wo(c):
        s = c % 2
        src = w[c].rearrange("p (k n) -> p k n", n=128)
        p.dma("sp", lambda s=s, src=src: nc.sync.dma_start(out=wst[s][:], in_=src), writes=[f"wst{s}"])
        p.op("act", lambda s=s: nc.scalar.copy(out=wbf[s][:, 0:12], in_=wst[s][:, 0:12]), reads=[f"wst{s}"], writes=[f"wbfa{s}"])
        p.op("pool", lambda s=s: nc.gpsimd.tensor_copy(out=wbf[s][:, 12:16], in_=wst[s][:, 12:16]), reads=[f"wst{s}"], writes=[f"wbfb{s}"])
        p.dma("sp", lambda s=s, c=c: nc.sync.dma_start(out=xc[s][:], in_=xT[c * 128:(c + 1) * 128]), writes=[f"xc{s}"])
    load_wo(0)
    for c in range(16):
        s = c % 2
        if c + 1 < 16:
            load_wo(c + 1)
        for m in range(4):
            a = ai % 4
            ai += 1
            for k in range(16):
                p.op("pe", lambda a=a, s=s, k=k, m=m: nc.tensor.matmul(pacc[a][:], lhsT=wbf[s][:, k, :], rhs=OTb[:, k, m * 512:(m + 1) * 512], start=(k == 0), stop=(k == 15)),
                     reads=[f"wbfa{s}", f"wbfb{s}"] + OTres, writes=[f"pacc{a}"])
            p.op("dve", lambda a=a, s=s, m=m: nc.vector.tensor_tensor(out=x1c[s][:, m * 512:(m + 1) * 512], in0=pacc[a][:], in1=xc[s][:, m * 512:(m + 1) * 512], op=ALU.add),
                 reads=[f"pacc{a}", f"xc{s}"], writes=[f"x1c{s}"])
        p.dma("pool", lambda s=s, c=c: nc.gpsimd.dma_start(out=x1T[c * 128:(c + 1) * 128], in_=x1c[s][:]), reads=[f"x1c{s}"], writes=[f"x1T{c}"])
        p.op("act", lambda s=s: nc.scalar.activation(out=sqt[:], in_=x1c[s][:], func=AF.Square), reads=[f"x1c{s}"], writes=["sqt"])
        p.op("pool", lambda: nc.gpsimd.tensor_tensor(out=accsq[:], in0=accsq[:], in1=sqt[:], op=ALU.add), reads=["sqt", "accsq"], writes=["accsq"])
    for m in range(4):
        q = m % 2
        p.op("pe", lambda q=q, m=m: nc.tensor.matmul(pss[q][:], lhsT=ones[:], rhs=accsq[:, m * 512:(m + 1) * 512], start=True, stop=True), reads=["ones", "accsq"], writes=[f"pss{q}"])
        p.op("act", lambda q=q, m=m: nc.scalar.activation(out=rstd[:, m * 512:(m + 1) * 512], in_=pss[q][:], func=AF.Sqrt, scale=1.0 / D, bias=EPS), reads=[f"pss{q}"], writes=["rstd"])
    p.op("dve", lambda: nc.vector.reciprocal(out=rstd[:], in_=rstd[:]), reads=["rstd"], writes=["rstd"])
    for c in range(16):
        s = c % 2
        p.dma("sp", lambda s=s, c=c: nc.sync.dma_start(out=x1c[s][:], in_=x1T[c * 128:(c + 1) * 128]), reads=[f"x1T{c}"], writes=[f"x1c{s}"])
        p.op("dve", lambda s=s, c=c: nc.vector.scalar_tensor_tensor(out=hc[s][:], in0=x1c[s][:], scalar=gsb[:, c:c + 1], in1=rstd[:], op0=ALU.mult, op1=ALU.mult),
             reads=[f"x1c{s}", "gsb", "rstd"], writes=[f"hc{s}"])
        p.dma("pool", lambda s=s, c=c: nc.gpsimd.dma_start(out=hT[:, c, :, 2:TW], in_=hc[s][:].rearrange("p (m t) -> p m t", t=512)), reads=[f"hc{s}"])
        p.dma("pool", lambda s=s, c=c: nc.gpsimd.dma_start(out=ht_s[c * 128:(c + 1) * 128, :].rearrange("p (m h) -> p m h", h=2),
                                                            in_=hc[s][:].rearrange("p (m t) -> p m t", t=512)[:, :, 510:512]), reads=[f"hc{s}"])


def emit_halo(nc, p, T, PS, ht_g, hco_d, hT):
    Hg = T("Hg", [128, 4, 16, 8], BF16)
    hco = T("hco", [128, 5], F32)
    acc = T("hacc", [128, 4, 16, 2], F32)
    hb = T("hb", [128, 4, 16, 2], BF16)
    p.dma("sp", lambda: nc.sync.dma_start(out=Hg[:], in_=ht_g.rearrange("(r k p) c -> p r k c", r=4, p=128)), writes=["Hg"])
    p.dma("sp", lambda: nc.sync.dma_start(out=hco[:], in_=hco_d), writes=["hco"])
    for m in range(4):
        p.op("dve", lambda m=m: nc.vector.tensor_scalar(out=acc[:, m], in0=Hg[:, 0, :, 2 * m:2 * m + 2], scalar1=hco[:, 0:1], scalar2=None, op0=ALU.mult),
             reads=["Hg", "hco"], writes=[f"hacc{m}"])
        for r in range(1, 4):
            p.op("dve", lambda m=m, r=r: nc.vector.scalar_tensor_tensor(out=acc[:, m], in0=Hg[:, r, :, 2 * m:2 * m + 2], scalar=hco[:, r:r + 1], in1=acc[:, m], op0=ALU.mult, op1=ALU.add),
                 reads=["Hg", "hco", f"hacc{m}"], writes=[f"hacc{m}"])
        if m >= 1:
            p.op("dve", lambda m=m: nc.vector.scalar_tensor_tensor(out=acc[:, m], in0=Hg[:, 3, :, 2 * m - 2:2 * m], scalar=hco[:, 4:5], in1=acc[:, m], op0=ALU.mult, op1=ALU.add),
                 reads=["Hg", "hco", f"hacc{m}"], writes=[f"hacc{m}"])
        p.op("dve", lambda m=m: nc.vector.tensor_copy(out=hb[:, m], in_=acc[:, m]), reads=[f"hacc{m}"], writes=[f"hb{m}"])
        p.dma("sp", lambda m=m: nc.sync.dma_start(out=hT[:, :, m, 0:2], in_=hb[:, m]), reads=[f"hb{m}"])


def build_fused(depth=DEPTH, debug=False, stop=10**9):
    nc = bass.Bass("TRN2", target_bir_lowering=False)
    I = lambda name, shape, dt: nc.dram_tensor(name, shape, dt, kind="ExternalInput").ap()
    N = lambda name, shape, dt, **kw: nc.dram_tensor(name, shape, dt, kind="Internal", **kw).ap()
    xT0 = I("xT0", [D, NT], F32)
    NTC = len(t_chunks()) + 1
    NNC = len(n_chunks())
    wiT = I("wiT", [depth, NTC, 128, 2048], F32)
    wiN = I("wiN", [depth, NNC, 128, 4096], F32)
    w_out = I("woR", [depth, 16, 128, 2048], F32)
    w_up = I("wuR", [depth, 44, 128, 4096], F32)
    w_down = I("wdR", [depth, 16, 2, 128, 2816], F32)
    gn_attn = I("gn_attn", [depth, 128, 16], F32)
    gn_mlp = I("gn_mlp", [depth, 128, 16], F32)
    gn_fin = I("gn_fin", [128, 16], F32)
    w1d = I("w1", [depth, 2, 64, 32, 128], F32)
    w2d = I("w2", [depth, 128, 2, 64], F32)
    peTd = I("peT", [depth, 64, 2, 32, 2], F32)
    cwd = I("cw", [depth, 128, 88, 3], F32)
    cbd = I("cbv", [depth, 128, 88], F32)
    hco_d = I("hco", [128, 5], F32)
    consts = {}
    for nm, shape, dt in (("G", [44, GL], F32), ("cb", [128, 44], F32), ("negm", [128, 16, 32], F32), ("ownm", [128, 16, 32], F32),
                          ("EB", [32, S], BF16), ("EC", [128, S], BF16), ("AM", [128, 16, 128], F32), ("BA", [128, 16, 128], F32),
                          ("cmA", [128, 512], F32), ("cmB", [128, 512], F32), ("ovl", [128, 4, 128], F32), ("selg", [36, 36, 64], F32)):
        consts[nm] = I(nm, shape, dt)
    yT = nc.dram_tensor("yT", [D, NT], F32, kind="ExternalOutput").ap()
    qT_s = N("qT_s", [32 * 64, NT], BF16)
    kT_s = N("kT_s", [26 * 64, NT], BF16, addr_space="Local")
    cmpT_s = N("cmpT_s", [6 * 64, NT], BF16, addr_space="Local")
    v_s = N("v_s", [26 * 128, 1024], BF16, addr_space="Local")
    def chunked(name, total, step, cols):
        out = []
        r0 = 0
        while r0 < total:
            nr = min(step, total - r0)
            out.append((r0, nr, N(f"{name}_{r0}", [4 * nr, cols], BF16, addr_space="Local")))
            r0 += nr
        return out
    kT_g = chunked("kT_g", 26 * 64, 256, NT)
    cmpT_g = chunked("cmpT_g", 6 * 64, 256, NT)
    v_g = chunked("v_g", 26 * 128, 512, 1024)
    gT_s = N("gT_s", [36, NT], F32)
    N2 = (lambda name, shape, dt: nc.dram_tensor(name, shape, dt, kind="ExternalOutput").ap()) if debug else N
    OT_s = N2("OT_s", [D, NT], BF16)
    x1T_s = N("x1T_s", [D, NT], F32)
    hT_s = N2("hT_s", [128, 16, 4, TW], BF16)
    ht_s = N("ht_s", [D, 8], BF16, addr_space="Local")
    ht_g = N("ht_g", [4 * D, 8], BF16, addr_space="Local")
    x2T_s = N2("x2T_s", [D, NT], F32)

    phase = [0]
    with contextlib.ExitStack() as top:
        ctx = Ctx(nc, top)

        def run_phase(fn, final=False):
            ph = phase[0]
            phase[0] += 1
            if ph >= stop and not final:
                return
            with contextlib.ExitStack() as st:
                T = lambda name, shape, dt: st.enter_context(nc.sbuf_tensor(f"s{ph}_" + name, shape, dt))
                PS = lambda name, shape, dt: st.enter_context(nc.psum_tensor(f"p{ph}_" + name, shape, dt))
                p = PProg(ctx)
                fin = fn(p, T, PS)
                p.emit(final_wait_ops=fin if final else ())

        xcur = xT0
        for l in range(depth):
            def ph_p1(p, T, PS, l=l, xcur=xcur):
                outs = {"qT": qT_s.rearrange("(h e) t -> h e t", e=64), "kT": kT_s.rearrange("(h e) t -> h e t", e=64),
                        "cmpT": cmpT_s.rearrange("(h e) t -> h e t", e=64)}
                v4 = v_s.rearrange("(h p) (ts e) -> h p ts e", p=128, e=64)

                def vdst(ts, vc0, ncols):
                    h0, nh = vc0 // 64, ncols // 64
                    return v4[h0:h0 + nh, :, ts, :].rearrange("h p e -> p h e")
                def wsrc(kind, idx):
                    if kind == "T":
                        return wiT[l, idx].rearrange("p (k n) -> p k n", n=128)
                    return wiN[l, idx].rearrange("p (k n) -> p k n", n=256)
                emit_p1(nc, p, T, PS, xcur, gn_attn[l], None, outs, None, gT_s, vdst=vdst, wsrc=wsrc)
                return ()
            run_phase(ph_p1)

            def ph_cc1(p, T, PS):
                for (a, chs) in ((kT_s, kT_g), (cmpT_s, cmpT_g), (v_s, v_g)):
                    for (r0, nr, b) in chs:
                        p.cc(lambda a=a, b=b, r0=r0, nr=nr: nc.gpsimd.collective_compute("AllGather", ALU.bypass, replica_groups=RG, ins=[a[r0:r0 + nr]], outs=[b]))
                return ()
            run_phase(ph_cc1)

            def ph_p2(p, T, PS, l=l):
                dr = dict(consts)
                dr.update({"qT": qT_s.rearrange("(h e) t -> h e t", e=64), "kT_g": kT_g, "cmpT_g": cmpT_g, "v_g": v_g, "gTs": gT_s,
                           "w1": w1d[l], "w2": w2d[l], "peT": peTd[l], "OT": OT_s})
                P = P2(nc, p, T, PS, dr, fused=True)
                P.setup()
                P.mixer_A()
                P.mixer_B()
                P.mixer_C()
                return ()
            run_phase(ph_p2)

            def ph_p3a(p, T, PS, l=l, xcur=xcur):
                emit_p3a_f(nc, p, T, PS, xcur, OT_s, w_out[l], gn_mlp[l], x1T_s, hT_s, ht_s)
                return ()
            run_phase(ph_p3a)

            def ph_cc2(p, T, PS):
                p.cc(lambda: nc.gpsimd.collective_compute("AllGather", ALU.bypass, replica_groups=RG, ins=[ht_s], outs=[ht_g]))
                return ()
            run_phase(ph_cc2)

            def ph_halo(p, T, PS):
                emit_halo(nc, p, T, PS, ht_g, hco_d, hT_s)
                return ()
            run_phase(ph_halo)

            def ph_p3b(p, T, PS, l=l):
                x1get = lambda cc, half: x1T_s[cc * 128:(cc + 1) * 128, half * 1024:(half + 1) * 1024].rearrange("p (t n) -> p t n", n=512)
                emit_p3b(nc, p, T, PS, hT_s, x1get, w_up[l], w_down[l], cwd[l], cbd[l], x2T_s, relayout=True)
                return ()
            run_phase(ph_p3b)
            xcur = x2T_s

        def ph_kf(p, T, PS):
            return emit_kf(nc, p, T, PS, x2T_s, gn_fin, yT)
        run_phase(ph_kf, final=True)
    return nc


def _relayout_in_T(w_in):
    L = w_in.shape[0]
    tch = t_chunks() + [(CG, 36, None, 1.0)]
    out = np.zeros((L, len(tch), 128, 16, 128), np.float32)
    for ci, ch in enumerate(tch):
        c0, nc_ = ch[0], ch[1]
        out[:, ci, :, :, 0:nc_] = w_in[:, :, c0:c0 + nc_].reshape(L, 16, 128, nc_).transpose(0, 2, 1, 3)
    return out.reshape(L, len(tch), 128, 2048)


def _relayout_in_N(w_in):
    L = w_in.shape[0]
    nch = n_chunks()
    out = np.zeros((L, len(nch), 128, 16, 256), np.float32)
    for ni, (c0, nc_, vc0) in enumerate(nch):
        out[:, ni, :, :, 0:nc_] = w_in[:, :, c0:c0 + nc_].reshape(L, 16, 128, nc_).transpose(0, 2, 1, 3)
    return out.reshape(L, len(nch), 128, 4096)


def fused_host_inputs(x, rel_table, w_in, w_out, cmp_w1, cmp_w2, cmp_pe, norm_attn, norm_mlp, w_up, conv_w, conv_b, w_down, norm_final):
    import ml_dtypes
    bf = ml_dtypes.bfloat16
    f32 = np.float32
    A = lambda a: np.ascontiguousarray(np.asarray(a, f32))
    x = np.asarray(x, f32)
    rel_table = np.asarray(rel_table, f32)
    L = np.asarray(w_in).shape[0]
    shared = {
        "wiT": _relayout_in_T(np.asarray(w_in, f32)), "wiN": _relayout_in_N(np.asarray(w_in, f32)),
        "woR": A(np.asarray(w_out, f32).reshape(L, 16, 128, 16, 128).transpose(0, 3, 2, 1, 4).reshape(L, 16, 128, 2048)),
        "wuR": A(np.asarray(w_up, f32).reshape(L, 16, 128, 2, 44, 128).transpose(0, 4, 2, 1, 3, 5).reshape(L, 44, 128, 4096)),
        "wdR": A(np.asarray(w_down, f32).reshape(L, 2, 22, 128, 16, 128).transpose(0, 4, 1, 3, 2, 5).reshape(L, 16, 2, 128, 2816)),
        "gn_attn": A(np.asarray(norm_attn, f32).reshape(L, 16, 128).transpose(0, 2, 1)),
        "gn_mlp": A(np.asarray(norm_mlp, f32).reshape(L, 16, 128).transpose(0, 2, 1)),
        "gn_fin": A(np.asarray(norm_final, f32).reshape(16, 128).T),
        "w1": A(np.asarray(cmp_w1, f32).reshape(L, 2, 32, 64, 128).transpose(0, 1, 3, 2, 4)),
        "w2": A(np.asarray(cmp_w2, f32).transpose(0, 2, 1, 3)),
        "peT": A(np.repeat(np.asarray(cmp_pe, f32).transpose(0, 3, 1, 2)[..., None], 2, axis=-1)),
        "cw": A(np.asarray(conv_w, f32).transpose(0, 2, 1).reshape(L, 88, 128, 3).transpose(0, 2, 1, 3)),
        "cbv": A(np.asarray(conv_b, f32).reshape(L, 88, 128).transpose(0, 2, 1)),
        "EB": eb_const().astype(bf), "EC": ec_const_nat().astype(bf), "ovl": ovl_const(), "selg": selg_const(),
    }
    per_j = []
    for j in range(4):
        G, cb = band_vectors(rel_table, j)
        negm, ownm = moba_consts(j)
        AM, BA, cmA, cmB = nsa_consts(j)
        per_j.append({"G": G, "cb": np.ascontiguousarray(np.broadcast_to(cb[None, :], (128, 44))), "negm": negm, "ownm": ownm,
                      "AM": AM, "BA": BA, "cmA": cmA, "cmB": cmB, "hco": halo_coef(j)})
    in_maps = []
    for c in range(8):
        b, j = c // 4, c % 4
        d = dict(shared)
        d.update(per_j[j])
        d["xT0"] = np.ascontiguousarray(x[b, core_tokens(j)].T)
        in_maps.append(d)
    return in_maps


from concourse.bass_utils import run_bass_kernel_spmd

_FUSED = {}


def kernel(x, rel_table, w_in, w_out, cmp_w1, cmp_w2, cmp_pe, norm_attn, norm_mlp,
           w_up, conv_w, conv_b, w_down, norm_final):
    if "nc" not in _FUSED:
        _FUSED["nc"] = build_fused()
    nc = _FUSED["nc"]
    in_maps = fused_host_inputs(x, rel_table, w_in, w_out, cmp_w1, cmp_w2, cmp_pe, norm_attn, norm_mlp,
                                w_up, conv_w, conv_b, w_down, norm_final)
    res = run_bass_kernel_spmd(nc, in_maps, core_ids=list(range(8)))
    out = np.empty((2, S, 2048), np.float32)
    for c in range(8):
        b, j = c // 4, c % 4
        out[b, core_tokens(j)] = np.asarray(res.results[c]["yT"]).T
    return out
```

```python
import contextlib, math
import numpy as np
import concourse.bass as bass
import concourse.mybir as mybir

F32 = mybir.dt.float32
BF16 = mybir.dt.bfloat16
I32 = mybir.dt.int32
AF = mybir.ActivationFunctionType
ALU = mybir.AluOpType
AX = mybir.AxisListType

SEM_CAP = 30000
N_DMA_SEMS = 8


class _Op:
    __slots__ = ("eng", "fn", "deps", "is_dma", "sig", "has_dependents", "idx", "dsem_prev", "is_cc", "inc")

    def __init__(self, eng, fn, is_dma):
        self.eng = eng
        self.fn = fn
        self.is_dma = is_dma
        self.deps = []
        self.sig = None
        self.has_dependents = False
        self.dsem_prev = None
        self.is_cc = False
        self.inc = 1


class Prog:
    ENGS = ("pe", "act", "dve", "pool", "sp")

    def __init__(self, nc):
        self.nc = nc
        self.ops = []
        self.last_writer = {}
        self.readers = {}

    def _add(self, eng, fn, reads, writes, is_dma):
        op = _Op(eng, fn, is_dma)
        deps = {}
        for r in reads:
            w = self.last_writer.get(r)
            if w is not None:
                deps[id(w)] = w
        for r in writes:
            w = self.last_writer.get(r)
            if w is not None:
                deps[id(w)] = w
            for rd in self.readers.get(r, ()):
                deps[id(rd)] = rd
        for r in reads:
            self.readers.setdefault(r, []).append(op)
        for r in writes:
            self.last_writer[r] = op
            self.readers[r] = []
        for d in deps.values():
            if d is op:
                continue
            if (not is_dma) and (not d.is_dma) and d.eng == eng and eng == "pe":
                continue
            op.deps.append(d)
            d.has_dependents = True
        self.ops.append(op)
        return op

    def op(self, eng, fn, reads=(), writes=()):
        return self._add(eng, fn, reads, writes, False)

    def dma(self, eng, fn, reads=(), writes=()):
        return self._add(eng, fn, reads, writes, True)

    def emit(self, final_wait_ops=()):
        nc = self.nc
        engs = {"pe": nc.tensor, "act": nc.scalar, "dve": nc.vector, "pool": nc.gpsimd, "sp": nc.sync}
        import contextlib
        with contextlib.ExitStack() as st:
            sem_lists = {e: [] for e in self.ENGS}
            counts = {e: 0 for e in self.ENGS}

            def new_sem(name):
                return st.enter_context(nc.semaphore(name))

            dma_sems = {}
            dma_state = {}
            for e in self.ENGS:
                dma_sems[e] = None
            for op in self.ops:
                if op.is_dma:
                    if dma_sems[op.eng] is None:
                        dma_sems[op.eng] = [new_sem(f"d_{op.eng}_{i}") for i in range(N_DMA_SEMS)]
                        dma_state[op.eng] = {"rr": 0, "cnt": [0] * N_DMA_SEMS}
                    stt = dma_state[op.eng]
                    i = stt["rr"]
                    stt["rr"] = (i + 1) % N_DMA_SEMS
                    prev = stt["cnt"][i]
                    stt["cnt"][i] = prev + 16
                    op.sig = (dma_sems[op.eng][i], prev + 16)
                    op.dsem_prev = (dma_sems[op.eng][i], prev) if prev > 0 else None
                elif op.has_dependents:
                    e = op.eng
                    if not sem_lists[e] or counts[e] >= SEM_CAP:
                        sem_lists[e].append(new_sem(f"c_{e}_{len(sem_lists[e])}"))
                        counts[e] = 0
                    counts[e] += 1
                    op.sig = (sem_lists[e][-1], counts[e])
            streams = {e: [] for e in self.ENGS}
            waited = {e: {} for e in self.ENGS}
            for op in self.ops:
                e = op.eng
                waits = []
                need = []
                if op.dsem_prev is not None:
                    need.append(op.dsem_prev)
                for d in op.deps:
                    need.append(d.sig)
                for (sem, val) in need:
                    k = id(sem)
                    if waited[e].get(k, 0) >= val:
                        continue
                    waited[e][k] = val
                    waits.append((sem, val))
                streams[e].append((waits, op))
            finals = [o.sig for o in final_wait_ops]
            blk = st.enter_context(nc.Block())

            def make(e):
                def body(engine):
                    for waits, op in streams[e]:
                        for (sem, val) in waits:
                            engine.wait_ge(sem, val)
                        ins = op.fn()
                        if op.sig is not None:
                            ins.then_inc(op.sig[0], 16 if op.is_dma else 1)
                    if e == "sp":
                        for (sem, val) in finals:
                            engine.wait_ge(sem, val)
                return body

            blk.tensor(make("pe"))
            blk.scalar(make("act"))
            blk.vector(make("dve"))
            blk.gpsimd(make("pool"))
            blk.sync(make("sp"))
        return self


class Ctx:
    ENGS = ("pe", "act", "dve", "pool", "sp")

    def __init__(self, nc, stack):
        self.nc = nc
        self.stack = stack
        self.sem_lists = {e: [] for e in self.ENGS}
        self.counts = {e: 0 for e in self.ENGS}
        self.dma_sems = {e: None for e in self.ENGS}
        self.dma_state = {}
        self.cc_sem = None
        self.cc_count = 0
        self.waited = {e: {} for e in self.ENGS}
        self.barrier_sigs = []
        self.nsem = 0

    def new_sem(self, name):
        self.nsem += 1
        return self.stack.enter_context(self.nc.semaphore(name))


class PProg(Prog):
    def __init__(self, ctx):
        super().__init__(ctx.nc)
        self.ctx = ctx

    def cc(self, fn, reads=(), writes=()):
        op = self._add("pool", fn, reads, writes, True)
        op.is_cc = True
        return op

    def emit(self, final_wait_ops=()):
        nc, ctx = self.nc, self.ctx
        engs = self.ENGS
        last_op = {e: None for e in engs}
        for op in self.ops:
            if not op.is_dma:
                last_op[op.eng] = op
        for e in engs:
            if last_op[e] is not None:
                last_op[e].has_dependents = True
        for op in self.ops:
            if getattr(op, "is_cc", False):
                if ctx.cc_sem is None:
                    ctx.cc_sem = ctx.new_sem("ccs")
                ctx.cc_count += 1
                op.sig = (ctx.cc_sem, ctx.cc_count)
                op.inc = 1
            elif op.is_dma:
                if ctx.dma_sems[op.eng] is None:
                    ctx.dma_sems[op.eng] = [ctx.new_sem(f"d_{op.eng}_{i}") for i in range(N_DMA_SEMS)]
                    ctx.dma_state[op.eng] = {"rr": 0, "cnt": [0] * N_DMA_SEMS}
                stt = ctx.dma_state[op.eng]
                i = stt["rr"]
                stt["rr"] = (i + 1) % N_DMA_SEMS
                prev = stt["cnt"][i]
                stt["cnt"][i] = prev + 16
                op.sig = (ctx.dma_sems[op.eng][i], prev + 16)
                op.dsem_prev = (ctx.dma_sems[op.eng][i], prev) if prev > 0 else None
                op.inc = 16
            elif op.has_dependents:
                e = op.eng
                if not ctx.sem_lists[e] or ctx.counts[e] >= SEM_CAP:
                    ctx.sem_lists[e].append(ctx.new_sem(f"c_{e}_{len(ctx.sem_lists[e])}"))
                    ctx.counts[e] = 0
                ctx.counts[e] += 1
                op.sig = (ctx.sem_lists[e][-1], ctx.counts[e])
                op.inc = 1
        streams = {e: [] for e in engs}
        start_waits = {e: [] for e in engs}
        for e in engs:
            for (sem, val) in ctx.barrier_sigs:
                k = id(sem)
                if ctx.waited[e].get(k, 0) >= val:
                    continue
                ctx.waited[e][k] = val
                start_waits[e].append((sem, val))
        for op in self.ops:
            e = op.eng
            waits = []
            need = []
            if op.dsem_prev is not None:
                need.append(op.dsem_prev)
            for d in op.deps:
                need.append(d.sig)
            for (sem, val) in need:
                k = id(sem)
                if ctx.waited[e].get(k, 0) >= val:
                    continue
                ctx.waited[e][k] = val
                waits.append((sem, val))
            streams[e].append((waits, op))
        sigs = []
        for e in engs:
            if last_op[e] is not None:
                sigs.append(last_op[e].sig)
            if ctx.dma_sems[e] is not None:
                for i, sem in enumerate(ctx.dma_sems[e]):
                    c = ctx.dma_state[e]["cnt"][i]
                    if c > 0:
                        sigs.append((sem, c))
        if ctx.cc_sem is not None and ctx.cc_count > 0:
            sigs.append((ctx.cc_sem, ctx.cc_count))
        ctx.barrier_sigs = sigs
        finals = [o.sig for o in final_wait_ops]
        with nc.Block() as blk:
            def make(e):
                def body(engine):
                    for (sem, val) in start_waits[e]:
                        engine.wait_ge(sem, val)
                    for waits, op in streams[e]:
                        for (sem, val) in waits:
                            engine.wait_ge(sem, val)
                        ins = op.fn()
                        if op.sig is not None:
                            if op.inc == 1 and getattr(op, "is_cc", False):
                                ins.then_inc(op.sig[0])
                            else:
                                ins.then_inc(op.sig[0], op.inc)
                    if e == "sp":
                        for (sem, val) in finals:
                            engine.wait_ge(sem, val)
                return body
            blk.tensor(make("pe"))
            blk.scalar(make("act"))
            blk.vector(make("dve"))
            blk.gpsimd(make("pool"))
            blk.sync(make("sp"))
        return self

import contextlib
import numpy as np

D = 2048
NT = 2048
IN_W = 5796
EPS = 1e-6

def a_col(g, part, hg):
    return g * 768 + part * 256 + hg * 64
B0 = 2304
C0 = 3840
CKV = C0 + 768
CG = CKV + 1152


def t_chunks():
    ch = []
    for g in range(3):
        for pair in range(2):
            ch.append((a_col(g, 0, 2 * pair), 128, [("qT", g * 4 + 2 * pair, 0, 64), ("qT", g * 4 + 2 * pair + 1, 64, 64)], 0.125))
        for pair in range(2):
            ch.append((a_col(g, 1, 2 * pair), 128, [("kT", g * 4 + 2 * pair, 0, 64), ("kT", g * 4 + 2 * pair + 1, 64, 64)], 1.0))
    for pair in range(4):
        ch.append((B0 + pair * 128, 128, [("qT", 12 + 2 * pair, 0, 64), ("qT", 13 + 2 * pair, 64, 64)], 0.125))
    for pair in range(4):
        ch.append((B0 + 512 + pair * 128, 128, [("kT", 12 + 2 * pair, 0, 64), ("kT", 13 + 2 * pair, 64, 64)], 1.0))
    for pair in range(6):
        ch.append((C0 + pair * 128, 128, [("qT", 20 + 2 * pair, 0, 64), ("qT", 21 + 2 * pair, 64, 64)], 0.125))
    for pair in range(3):
        ch.append((CKV + pair * 128, 128, [("cmpT", 2 * pair, 0, 64), ("cmpT", 2 * pair + 1, 64, 64)], 1.0))
    ch.append((CKV + 384, 128, [("kT", 20, 0, 64), ("kT", 21, 64, 64)], 1.0))
    ch.append((CKV + 384 + 128, 64, [("kT", 22, 0, 64)], 1.0))
    ch.append((CKV + 768, 128, [("kT", 23, 0, 64), ("kT", 24, 64, 64)], 1.0))
    ch.append((CKV + 768 + 128, 64, [("kT", 25, 0, 64)], 1.0))
    return ch


def n_chunks():
    ch = []
    for g in range(3):
        ch.append((a_col(g, 2, 0), 256, g * 256))
    ch.append((B0 + 1024, 256, 768))
    ch.append((B0 + 1024 + 256, 256, 1024))
    ch.append((CKV + 576, 192, 1280))
    ch.append((CKV + 960, 192, 1472))
    return ch


def build_p1():
    nc = bass.Bass("TRN2", target_bir_lowering=False)
    xT = nc.dram_tensor("xT", [D, NT], F32, kind="ExternalInput").ap()
    gn = nc.dram_tensor("gn", [128, 16], F32, kind="ExternalInput").ap()
    w = nc.dram_tensor("w", [D, IN_W], F32, kind="ExternalInput").ap()
    outs = {
        "qT": nc.dram_tensor("qT", [32, 64, NT], BF16, kind="ExternalOutput").ap(),
        "kT": nc.dram_tensor("kT", [26, 64, NT], BF16, kind="ExternalOutput").ap(),
        "cmpT": nc.dram_tensor("cmpT", [6, 64, NT], BF16, kind="ExternalOutput").ap(),
    }
    vO = nc.dram_tensor("v", [NT, 1664], BF16, kind="ExternalOutput").ap()
    gT = nc.dram_tensor("gT", [36, NT], F32, kind="ExternalOutput").ap()
    with contextlib.ExitStack() as st:
        T = lambda name, shape, dt: st.enter_context(nc.sbuf_tensor("s_" + name, shape, dt))
        PS = lambda name, shape, dt: st.enter_context(nc.psum_tensor("p_" + name, shape, dt))
        p = Prog(nc)
        emit_p1(nc, p, T, PS, xT, gn, w, outs, vO, gT)
        p.emit(final_wait_ops=p.final_ops)
    return nc


def emit_p1(nc, p, T, PS, xT, gn, w, outs, vO, gT, vdst=None, wsrc=None):
    p.final_ops = getattr(p, "final_ops", [])
    hT = T("hT", [128, 16, NT], BF16)
    gsb = T("gsb", [128, 16], F32)
    ones = T("ones", [128, 128], F32)
    xs = [T(f"xs{i}", [128, 16, 512], F32) for i in range(2)]
    sq = [T(f"sq{i}", [128, 512], F32) for i in range(2)]
    rstd = T("rstd", [128, 512], F32)
    wst = [T(f"wst{i}", [128, 16, 256], F32) for i in range(2)]
    wbf = [T(f"wbf{i}", [128, 16, 256], BF16) for i in range(2)]
    ost = [T(f"ost{i}", [128, NT], BF16) for i in range(2)]
    gst = T("gst", [36, NT], F32)
    vst = [T(f"vst{i}", [128, 256], BF16) for i in range(3)]
    pss = PS("pss", [128, 512], F32)
    pacc = [PS(f"pacc{i}", [128, 512], F32) for i in range(3)]

    p.dma("sp", lambda: nc.sync.dma_start(out=gsb[:], in_=gn), writes=["gsb"])
    p.op("pool", lambda: nc.gpsimd.memset(ones[:], 1.0), writes=["ones"])
    xv = xT.rearrange("(k p) t -> p k t", p=128)
    for m in range(4):
        s = m % 2
        p.dma("sp", lambda m=m, s=s: nc.sync.dma_start(out=xs[s][:], in_=xv[:, :, m * 512:(m + 1) * 512]), writes=[f"xs{s}"])
        for k in range(16):
            q = k % 2
            p.op("act", lambda s=s, k=k, q=q: nc.scalar.activation(out=sq[q][:], in_=xs[s][:, k, :], func=AF.Square),
                 reads=[f"xs{s}"], writes=[f"sq{q}"])
            p.op("pe", lambda q=q, k=k: nc.tensor.matmul(pss[:], lhsT=ones[:], rhs=sq[q][:], start=(k == 0), stop=(k == 15)),
                 reads=["ones", f"sq{q}"], writes=["pss"])
        p.op("act", lambda: nc.scalar.activation(out=rstd[:], in_=pss[:], func=AF.Sqrt, scale=1.0 / D, bias=EPS),
             reads=["pss"], writes=["rstd"])
        p.op("dve", lambda: nc.vector.reciprocal(out=rstd[:], in_=rstd[:]), reads=["rstd"], writes=["rstd"])
        for k in range(16):
            eng = "dve" if k % 2 == 0 else "pool"
            E = nc.vector if eng == "dve" else nc.gpsimd
            if eng == "dve":
                p.op("dve", lambda s=s, k=k, m=m: nc.vector.scalar_tensor_tensor(
                    out=hT[:, k, m * 512:(m + 1) * 512], in0=xs[s][:, k, :], scalar=gsb[:, k:k + 1], in1=rstd[:],
                    op0=ALU.mult, op1=ALU.mult), reads=[f"xs{s}", "gsb", "rstd"], writes=[f"hT{m}"])
            else:
                p.op("dve", lambda s=s, k=k, m=m: nc.vector.scalar_tensor_tensor(
                    out=hT[:, k, m * 512:(m + 1) * 512], in0=xs[s][:, k, :], scalar=gsb[:, k:k + 1], in1=rstd[:],
                    op0=ALU.mult, op1=ALU.mult), reads=[f"xs{s}", "gsb", "rstd"], writes=[f"hT{m}"])
    hT_all = [f"hT{m}" for m in range(4)]

    wcount = [0]

    def load_w(c0, ncols, kind=None, idx=0):
        s = wcount[0] % 2
        wcount[0] += 1
        if wsrc is None:
            src = w[:, c0:c0 + ncols].rearrange("(k p) n -> p k n", p=128)
        else:
            src = wsrc(kind, idx)[:, :, 0:ncols]
        p.dma("sp", lambda: nc.sync.dma_start(out=wst[s][:, :, 0:ncols], in_=src), writes=[f"wst{s}"])
        h = 12
        p.op("act", lambda: nc.scalar.copy(out=wbf[s][:, 0:h, 0:ncols], in_=wst[s][:, 0:h, 0:ncols]),
             reads=[f"wst{s}"], writes=[f"wbfa{s}"])
        p.op("pool", lambda: nc.gpsimd.tensor_copy(out=wbf[s][:, h:16, 0:ncols], in_=wst[s][:, h:16, 0:ncols]),
             reads=[f"wst{s}"], writes=[f"wbfb{s}"])
        return s

    tch = t_chunks() + [(CG, 36, [("gT", 0, 0, 36)], 1.0)]
    nch = n_chunks()
    wjobs = [(ch[0], ch[1], "T", i) for i, ch in enumerate(tch)] + [(ch[0], ch[1], "N", i) for i, ch in enumerate(nch)]
    wslot = {0: load_w(*wjobs[0])}

    def get_w(j):
        if j + 1 < len(wjobs):
            wslot[j + 1] = load_w(*wjobs[j + 1])
        return wslot[j]
    acc_i = [0]
    for ci, (c0, ncols, dests, scale) in enumerate(tch):
        s = get_w(ci)
        o = ci % 2
        is_gate = dests[0][0] == "gT"
        for m in range(4):
            a = acc_i[0] % 3
            acc_i[0] += 1
            for k in range(16):
                p.op("pe", lambda a=a, s=s, k=k, m=m, ncols=ncols: nc.tensor.matmul(
                    pacc[a][0:ncols, :], lhsT=wbf[s][:, k, 0:ncols], rhs=hT[:, k, m * 512:(m + 1) * 512],
                    start=(k == 0), stop=(k == 15)),
                    reads=[f"wbfa{s}", f"wbfb{s}", f"hT{m}"], writes=[f"pacc{a}"])
            if is_gate:
                p.op("act", lambda a=a, m=m: nc.scalar.activation(out=gst[:, m * 512:(m + 1) * 512], in_=pacc[a][0:36, :], func=AF.Sigmoid),
                     reads=[f"pacc{a}"], writes=["gst"])
            elif m % 2 == 0:
                p.op("act", lambda a=a, m=m, o=o, ncols=ncols, scale=scale: nc.scalar.activation(
                    out=ost[o][0:ncols, m * 512:(m + 1) * 512], in_=pacc[a][0:ncols, :], func=AF.Copy, scale=scale),
                    reads=[f"pacc{a}"], writes=[f"ost{o}"])
            else:
                p.op("dve", lambda a=a, m=m, o=o, ncols=ncols, scale=scale: nc.vector.tensor_scalar(
                    out=ost[o][0:ncols, m * 512:(m + 1) * 512], in0=pacc[a][0:ncols, :], scalar1=scale, scalar2=None, op0=ALU.mult),
                    reads=[f"pacc{a}"], writes=[f"ost{o}"])
        if is_gate:
            p.final_ops.append(p.dma("pool", lambda: nc.gpsimd.dma_start(out=gT, in_=gst[:]), reads=["gst"]))
        else:
            for (dn, dh, r0, nr) in dests:
                p.final_ops.append(p.dma("pool", lambda dn=dn, dh=dh, r0=r0, nr=nr, o=o: nc.gpsimd.dma_start(
                    out=outs[dn][dh], in_=ost[o][r0:r0 + nr, :]), reads=[f"ost{o}"]))

    vi = [0]
    for ni, (c0, ncols, vc0) in enumerate(nch):
        s = get_w(len(tch) + ni)
        for ts in range(16):
            a = acc_i[0] % 3
            acc_i[0] += 1
            m = ts // 4
            for k in range(16):
                p.op("pe", lambda a=a, s=s, k=k, ts=ts, ncols=ncols: nc.tensor.matmul(
                    pacc[a][:, 0:ncols], lhsT=hT[:, k, ts * 128:(ts + 1) * 128], rhs=wbf[s][:, k, 0:ncols],
                    start=(k == 0), stop=(k == 15)),
                    reads=[f"wbfa{s}", f"wbfb{s}", f"hT{m}"], writes=[f"pacc{a}"])
            vs = vi[0] % 3
            vi[0] += 1
            if ts % 2 == 0:
                p.op("act", lambda a=a, vs=vs, ncols=ncols: nc.scalar.copy(out=vst[vs][:, 0:ncols], in_=pacc[a][:, 0:ncols]),
                     reads=[f"pacc{a}"], writes=[f"vst{vs}"])
            else:
                p.op("dve", lambda a=a, vs=vs, ncols=ncols: nc.vector.tensor_copy(out=vst[vs][:, 0:ncols], in_=pacc[a][:, 0:ncols]),
                     reads=[f"pacc{a}"], writes=[f"vst{vs}"])
            if vdst is None:
                p.final_ops.append(p.dma("pool", lambda ts=ts, vs=vs, vc0=vc0, ncols=ncols: nc.gpsimd.dma_start(
                    out=vO[ts * 128:(ts + 1) * 128, vc0:vc0 + ncols], in_=vst[vs][:, 0:ncols]), reads=[f"vst{vs}"]))
            else:
                p.final_ops.append(p.dma("pool", lambda ts=ts, vs=vs, vc0=vc0, ncols=ncols: nc.gpsimd.dma_start(
                    out=vdst(ts, vc0, ncols), in_=vst[vs][:, 0:ncols].rearrange("p (h e) -> p h e", e=64)), reads=[f"vst{vs}"]))


import contextlib, math
import numpy as np

S = 8192
NT = 2048
BW = 4480
GL = BW + 128
NEGM = -30000.0
A_CFG = ((128, 1), (512, 4), (2048, 16))
U16 = mybir.dt.uint16


def t5_bucket_np(dist):
    n = np.maximum(dist, 0)
    nf = np.maximum(n, 1).astype(np.float32)
    large = 16 + (np.log(nf / np.float32(16)) / np.float32(math.log(128.0)) * np.float32(16)).astype(np.int32)
    large = np.minimum(large, 31)
    return np.where(n < 16, n, large)


def band_vectors(rel_table, j):
    v = np.arange(GL)
    dist = v + 512 * j - 2047
    bk = t5_bucket_np(dist)
    G = np.empty((44, GL), np.float32)
    cb = np.empty((44,), np.float32)
    for b in range(44):
        if b < 12:
            h = b
            W, d = A_CFG[b // 4]
            ok = (dist >= 0) & (dist <= W) & (dist % d == 0)
        elif b < 20:
            h = b
            ok = dist >= 0
        elif b < 32:
            h = b
            ok = dist >= 0
        else:
            h = 20 + (b - 32)
            ok = (dist >= 0) & (dist < 512)
        G[b] = np.where(ok, rel_table[h, bk], np.float32(NEGM))
        cb[b] = rel_table[h, 31]
    return G, cb


def core_tokens(j):
    return np.concatenate([np.arange(512 * (4 * m + j), 512 * (4 * m + j) + 512) for m in range(4)])


def moba_consts(j):
    t = core_tokens(j)
    ob = t // 256
    n = np.arange(32)[None, :]
    neg = np.where(n >= ob[:, None], np.float32(-1e30), np.float32(0)).astype(np.float32)
    own = (n >= ob[:, None]).astype(np.float32)
    f = lambda a: np.ascontiguousarray(a.reshape(16, 128, 32).transpose(1, 0, 2))
    return f(neg), f(own)


def eb_const():
    k = np.arange(S)
    return (k[None, :] // 256 == np.arange(32)[:, None]).astype(np.float32)


def ec_const():
    c = np.arange(S)
    key = (c // 128) * 128 + 127 - (c % 128)
    return (key[None, :] // 64 == np.arange(128)[:, None]).astype(np.float32)


def rev_blocks(a, axis):
    a = np.moveaxis(a, axis, -1)
    sh = a.shape
    a = a.reshape(sh[:-1] + (sh[-1] // 128, 128))[..., ::-1].reshape(sh)
    return np.moveaxis(a, -1, axis)


def nsa_consts(j):
    t = core_tokens(j)
    own = (t // 64)[:, None]
    jb = np.arange(128)[None, :]
    valid = jb <= own
    forced = (jb == 0) | (jb == own) | (jb == own - 1)
    am = (valid & ~forced).astype(np.float32)
    ba = np.where(valid, np.where(forced, np.float32(1e9), np.float32(0)), np.float32(-1)).astype(np.float32)
    f = lambda a: np.ascontiguousarray(a.reshape(16, 128, 128).transpose(1, 0, 2))
    pp = np.arange(128)[:, None]
    q = np.arange(512)[None, :]
    cmA = np.where(16 * pp + 31 <= 512 * j + q, np.float32(0), np.float32(NEGM)).astype(np.float32)
    cmB = np.where(16 * pp + 31 - 2048 <= 512 * j + q, np.float32(0), np.float32(NEGM)).astype(np.float32)
    return f(am), f(ba), cmA, cmB


def ovl_const():
    i = np.arange(512)[:, None]
    jb = np.arange(128)[None, :]
    ov = ((16 * i < 64 * jb + 64) & (16 * i + 32 > 64 * jb) & (i < 511)).astype(np.float32)
    return np.ascontiguousarray(ov.reshape(4, 128, 128).transpose(1, 0, 2))


def selg_const():
    sg = np.zeros((36, 36, 64), np.float32)
    for r in range(36):
        sg[r, r, :] = 1.0
    return sg

class P2:
    def __init__(self, nc, p, T, PS, dr, heads_A=(0, 1, 2, 3), heads_B=tuple(range(8)), kvs_C=(0, 1, 2), fused=False):
        self.nc, self.p, self.T, self.PS, self.dr = nc, p, T, PS, dr
        self.fused = fused
        self.heads_A, self.heads_B, self.kvs_C = heads_A, heads_B, kvs_C
        self.units = []
        self.alloc()

    def alloc(self):
        T, PS = self.T, self.PS
        self.kTall = T("kTall", [128, S], BF16)
        self.qTz = [T(f"qTz{i}", [128, NT], BF16) for i in range(2)]
        self.kTb = [self.kTall[0:64], self.kTall[64:128]]
        self.qTb = [self.qTz[0][0:64], self.qTz[1][64:128]]
        self.Vb = [T(f"Vb{i}", [128, 64, 128], BF16) for i in range(2)]
        self.band1 = T("band1", [128, BW], F32)
        self.bandb = [self.band1, self.band1]
        self.cbb = T("cbb", [128, 44], F32)
        self.Eb = T("Eb", [128, S], BF16)
        self.MTb = [T(f"MTb{i}", [128, NT], BF16) for i in range(2)]
        self.sb = [T(f"sb{i}", [128, 512], F32) for i in range(2)]
        self.PT = [T(f"PT{i}", [128, 512], BF16) for i in range(3)]
        self.dsum = self.sb[1]
        self.nd = T("nd", [128, 12, 512], F32)
        self.rden = T("rden", [64, 512], F32)
        self.ost = [T(f"ost{i}", [64, 512], BF16) for i in range(3)]
        self.identb = T("identb", [128, 128], BF16)
        self.identf = T("identf", [128, 128], F32)
        self.negm = T("negm", [128, 16, 32], F32)
        self.ownm = T("ownm", [128, 16, 32], F32)
        self.km = T("km", [128, 32], F32)
        self.kmb = T("kmb", [128, 32], BF16)
        self.gm = T("gm", [128, 16, 32], F32)
        self.m8 = T("m8", [128, 16, 8], F32)
        self.selt = T("selt", [128, 16, 32], F32)
        self.Mq = T("Mq", [128, 16, 32], BF16)
        self.qcm = [T(f"qcm{i}", [64, 512], BF16) for i in range(3)]
        self.ef = [T(f"ef{i}", [128, 512], F32) for i in range(4)]
        self.pcb = [T(f"pcb{i}", [128, 512], BF16) for i in range(4)]
        self.w1b = T("w1b", [64, 32, 128], BF16)
        self.w2f = T("w2f", [128, 2, 64], F32)
        self.w2b = T("w2b", [128, 2, 64], BF16)
        self.peTf = T("peTf", [64, 2, 32, 2], F32)
        self.peTb = T("peTb", [64, 2, 32, 2], BF16)
        self.kcTb = T("kcTb", [64, 512], BF16)
        self.vcb = T("vcb", [128, 4, 64], BF16)
        self.scr = [T(f"scr{i}", [128, 512], F32) for i in range(2)]
        self.ocs = self.scr[1][0:64]
        self.cbias = T("cbias", [128, 1], F32)
        self.hid = T("hid", [128, 512], BF16)
        self.amb = T("amb", [128, 4, 128], F32)
        self.bab = T("bab", [128, 4, 128], F32)
        self.cmA = T("cmA", [128, 512], F32)
        self.cmB = T("cmB", [128, 512], F32)
        self.ovl = T("ovl", [128, 4, 128], F32)
        self.sel3 = T("sel3", [36, 3, 64], F32)
        self.gTs = T("gTs", [36, NT], F32)
        self.impS = T("impS", [128, 512], F32)
        self.score = T("score", [128, 4, 128], F32)
        self.sc2 = T("sc2", [128, 128], F32)
        self.m16 = T("m16", [128, 4, 16], F32)
        self.Msel = T("Msel", [128, 4, 128], BF16)
        self.onesf = T("onesf", [128, 128], F32)
        self.tmpf = T("tmpf", [64, 512], F32)
        self.pS = [PS(f"pS{i}", [128, 512], F32) for i in range(4)]
        self.pacc = [PS(f"pacc{i}", [128, 512], F32) for i in range(2)]
        self.pm = [PS("pm0", [128, 512], F32), self.pS[3]]
        self.pmn = ["pm0", "pS3"]
        self.pmb = PS("pmb", [128, 1024], BF16)
        if self.fused:
            self.hst = [T(f"hst{i}", [128, 512], F32) for i in range(2)]
            self.Jf = T("Jf", [128, 128], F32)
        self.ost_i = 0
        self.slot = 0
        self.acc_i = 0
        self.out_ops = []

    def setup(self):
        nc, p = self.nc, self.p
        p.op("pool", lambda: nc.gpsimd.memset(self.identf[:], 1.0), writes=["identf"])
        p.op("pool", lambda: nc.gpsimd.affine_select(out=self.identf[:], in_=self.identf[:], pattern=[[-1, 128]],
                                                     compare_op=ALU.is_equal, fill=0.0, base=0, channel_multiplier=1),
             reads=["identf"], writes=["identf"])
        p.op("dve", lambda: nc.vector.tensor_copy(out=self.identb[:], in_=self.identf[:]), reads=["identf"], writes=["identb"])
        for i in range(2):
            p.op("pool", lambda i=i: nc.gpsimd.memset(self.Vb[i][:, :, 64:128], 1.0), writes=[f"Vones{i}"])
        p.dma("sp", lambda: nc.sync.dma_start(out=self.cbb[:], in_=self.dr["cb"]), writes=["cbb"])
        p.op("pool", lambda: nc.gpsimd.memset(self.kTall[:], 0.0), writes=["kTb0", "kTb1"])
        for i in range(2):
            p.op("pool", lambda i=i: nc.gpsimd.memset(self.qTz[i][:], 0.0), writes=[f"qTb{i}"])
            p.op("pool", lambda i=i: nc.gpsimd.memset(self.MTb[i][:], 0.0), writes=[f"MTb{i}"])
        p.op("pool", lambda: nc.gpsimd.memset(self.Eb[:], 0.0), writes=["Eb"])
        if self.fused:
            p.op("pool", lambda: nc.gpsimd.memset(self.Jf[:], 1.0), writes=["Jf"])
            p.op("pool", lambda: nc.gpsimd.affine_select(out=self.Jf[:], in_=self.Jf[:], pattern=[[1, 128]],
                                                         compare_op=ALU.is_equal, fill=0.0, base=-127, channel_multiplier=1),
                 reads=["Jf"], writes=["Jf"])

    def load_kT_gathered(self, dst, chunks, row0, res):
        nc, p = self.nc, self.p
        src2d = None
        for (r0, nr, ap) in chunks:
            if r0 <= row0 < r0 + nr:
                src2d, row0 = ap, row0 - r0
                break
        nrows = src2d.shape[0] // 4
        for m in range(4):
            src = bass.AP(tensor=src2d.tensor, offset=src2d[row0:row0 + 1, m * 512:m * 512 + 1].offset,
                          ap=[[2048, 64], [nrows * 2048, 4], [1, 512]])
            d = dst[:, m * 2048:(m + 1) * 2048].rearrange("e (r i) -> e r i", i=512)
            p.dma("sp", lambda src=src, d=d: nc.sync.dma_start(out=d, in_=src), reads=["gathered"], writes=[res])

    def load_v_gathered(self, s, ki):
        nc, p = self.nc, self.p
        vg, lrow, nr = None, 0, 0
        for (r0, nr_, ap) in self.dr["v_g"]:
            if r0 <= ki * 128 < r0 + nr_:
                vg, lrow, nr = ap, ki * 128 - r0, nr_
                break
        for m in range(4):
            for r in range(4):
                src = bass.AP(tensor=vg.tensor, offset=vg[r * nr + lrow:r * nr + lrow + 1, m * 256:m * 256 + 1].offset,
                              ap=[[1024, 128], [64, 4], [1, 64]])
                d = self.Vb[s][:, m * 16 + r * 4:m * 16 + r * 4 + 4, 0:64]
                p.dma("sp", lambda src=src, d=d: nc.sync.dma_start(out=d, in_=src), reads=["gathered"], writes=[f"Vb{s}"])

    def load_band_flipped(self, bi):
        nc, p = self.nc, self.p
        g = self.dr["G"]
        if bi < 4:
            width = 2560
        elif bi < 8 or bi >= 32:
            width = 2944
        elif bi < 12:
            width = BW
        else:
            width = 3968
        nch = (width + 511) // 512
        stg = [(self.hst[0], "hst0"), (self.hst[1], "hst1"), (self.sb[0], "sb0"), (self.sb[1], "sb1")]
        for ch in range(nch):
            w_ = min(512, BW - ch * 512)
            st_, stn = stg[ch % 4]
            src = bass.AP(tensor=g.tensor, offset=g[bi:bi + 1, ch * 512:ch * 512 + 1].offset, ap=[[1, 128], [1, w_]])
            p.dma("sp", lambda src=src, st_=st_, w_=w_: nc.sync.dma_start(out=st_[:, 0:w_], in_=src), writes=[stn])
            pb = self.pm[ch % 2]
            p.op("pe", lambda st_=st_, w_=w_, pb=pb: nc.tensor.matmul(pb[:, 0:w_], lhsT=self.Jf[:], rhs=st_[:, 0:w_], start=True, stop=True),
                 reads=["Jf", stn], writes=[self.pmn[ch % 2]])
            p.op("act", lambda ch=ch, w_=w_, pb=pb: nc.scalar.copy(out=self.band1[:, ch * 512:ch * 512 + w_], in_=pb[:, 0:w_]),
                 reads=[self.pmn[ch % 2]], writes=["bandb"])

    def load_head(self, qi, ki, bi):
        nc, p, dr = self.nc, self.p, self.dr
        s = self.slot % 2
        self.slot += 1
        if self.fused:
            p.dma("sp", lambda: nc.sync.dma_start(out=self.qTb[s], in_=dr["qT"][qi]), reads=["qT_s"], writes=[f"qTb{s}"])
            self.load_kT_gathered(self.kTb[s], dr["kT_g"], ki * 64, f"kTb{s}")
            self.load_v_gathered(s, ki)
            self.load_band_flipped(bi)
            return s
        p.dma("sp", lambda: nc.sync.dma_start(out=self.qTb[s], in_=dr["qT"][qi]), writes=[f"qTb{s}"])
        p.dma("sp", lambda: nc.sync.dma_start(out=self.kTb[s], in_=dr["kTf"][ki]), writes=[f"kTb{s}"])
        p.dma("sp", lambda: nc.sync.dma_start(out=self.Vb[s][:, :, 0:64], in_=dr["vf"][ki]), writes=[f"Vb{s}"])
        g = dr["G"]
        src = bass.AP(tensor=g.tensor, offset=g[bi:bi + 1, 0:1].offset, ap=[[1, 128], [1, BW]])
        p.dma("sp", lambda: nc.sync.dma_start(out=self.bandb[s][:], in_=src), writes=["bandb"])
        return s

    def add_units(self, s, m, kts, near_lo, bi, acc, mask=None, post=None, pre=None):
        n = len(kts)
        for idx, kt in enumerate(kts):
            self.units.append(dict(s=s, m=m, kt=kt, near=(kt >= near_lo), bi=bi, acc=acc, first=(idx == 0), last=(idx == n - 1),
                                   mask=mask, post=post if idx == n - 1 else None, pre=pre if idx == 0 else None))

    def flush_units(self, LA=3):
        nc, p = self.nc, self.p
        U = self.units
        n = len(U)

        def qk(i):
            u = U[i]
            if u["pre"] is not None:
                u["pre"]()
            b = i % 4
            s, m, kt = u["s"], u["m"], u["kt"]
            rd = [f"kTb{s}", f"qTb{s}"]
            if u["mask"] is None:
                p.op("pe", lambda: nc.tensor.matmul(self.pS[b][:], lhsT=self.kTall[:, kt * 128:(kt + 1) * 128],
                                                    rhs=self.qTz[s][:, m * 512:(m + 1) * 512], start=True, stop=True),
                     reads=rd, writes=[f"pS{b}"])
            else:
                nr, ms = u["mask"]
                p.op("pe", lambda: nc.tensor.matmul(self.pS[b][:], lhsT=self.kTall[:, kt * 128:(kt + 1) * 128],
                                                    rhs=self.qTz[s][:, m * 512:(m + 1) * 512], start=True, stop=False),
                     reads=rd, writes=[f"pS{b}"])
                p.op("pe", lambda: nc.tensor.matmul(self.pS[b][:], lhsT=self.Eb[:, kt * 128:(kt + 1) * 128],
                                                    rhs=self.MTb[ms][:, m * 512:(m + 1) * 512], start=False, stop=True),
                     reads=["Eb", f"MTb{ms}"], writes=[f"pS{b}"])

        def rest(i):
            u = U[i]
            b = i % 4
            s, m, kt, bi, acc = u["s"], u["m"], u["kt"], u["bi"], u["acc"]
            pt = i % 3
            if u["near"]:
                sbi = i % 2
                u0 = 2048 * m - 128 * kt + 1920
                assert 0 <= u0 and u0 + 512 <= BW, (m, kt, u0)
                p.op("dve", lambda: nc.vector.tensor_tensor(out=self.sb[sbi][:], in0=self.pS[b][:], in1=self.bandb[s][:, u0:u0 + 512], op=ALU.add),
                     reads=[f"pS{b}", "bandb"], writes=[f"sb{sbi}"])
                p.op("act", lambda: nc.scalar.activation(out=self.PT[pt][:], in_=self.sb[sbi][:], func=AF.Exp),
                     reads=[f"sb{sbi}"], writes=[f"PT{pt}"])
            else:
                p.op("act", lambda: nc.scalar.activation(out=self.PT[pt][:], in_=self.pS[b][:], func=AF.Exp, bias=self.cbb[:, bi:bi + 1]),
                     reads=[f"pS{b}", "cbb"], writes=[f"PT{pt}"])
            p.op("pe", lambda: nc.tensor.matmul(self.pacc[acc][:], lhsT=self.Vb[s][:, kt, :], rhs=self.PT[pt][:],
                                                start=u["first"], stop=u["last"]),
                 reads=[f"Vb{s}", f"Vones{s}", f"PT{pt}"], writes=[f"pacc{acc}"])
            if u["post"] is not None:
                u["post"]()

        for i in range(n + LA):
            if i < n:
                qk(i)
            if i - LA >= 0:
                rest(i - LA)
        self.units = []

    def write_out(self, head_feat, m, num_ap, rden_ap):
        nc, p = self.nc, self.p
        o = self.ost_i % 3
        self.ost_i += 1
        num, nres = num_ap
        rd, rres = rden_ap
        p.op("dve", lambda: nc.vector.tensor_tensor(out=self.ost[o][:], in0=num, in1=rd, op=ALU.mult),
             reads=[nres, rres], writes=[f"ost{o}"])
        dst = self.dr["OT"][head_feat * 64:(head_feat + 1) * 64, m * 512:(m + 1) * 512]
        self.out_ops.append(p.dma("pool", lambda: nc.gpsimd.dma_start(out=dst, in_=self.ost[o][:]), reads=[f"ost{o}"]))

    def mixer_A(self):
        nc, p = self.nc, self.p
        for hg in self.heads_A:
            for g in range(3):
                W, d = A_CFG[g]
                h = g * 4 + hg
                s = self.load_head(h, h, h)
                for m in range(4):
                    lo = max(0, (2048 * m - W) // 128)
                    kts = list(range(lo, 16 * m + 16))
                    acc = self.acc_i % 2
                    self.acc_i += 1

                    def post(g=g, m=m, acc=acc):
                        p.op("act", lambda: nc.scalar.copy(out=self.nd[:, g * 4 + m, :], in_=self.pacc[acc][:]),
                             reads=[f"pacc{acc}"], writes=[f"nd{g}_{m}"])
                    self.add_units(s, m, kts, 0, h, acc, post=post)
                self.flush_units()
            for m in range(4):
                p.op("pool", lambda m=m: nc.gpsimd.tensor_tensor(out=self.dsum[64:128, :], in0=self.nd[64:128, 0 * 4 + m, :], in1=self.nd[64:128, 1 * 4 + m, :], op=ALU.add),
                     reads=[f"nd0_{m}", f"nd1_{m}"], writes=["sb1"])
                p.op("pool", lambda m=m: nc.gpsimd.tensor_tensor(out=self.dsum[64:128, :], in0=self.dsum[64:128, :], in1=self.nd[64:128, 2 * 4 + m, :], op=ALU.add),
                     reads=["sb1", f"nd2_{m}"], writes=["sb1"])
                p.op("dve", lambda: nc.vector.reciprocal(out=self.rden[:], in_=self.dsum[64:128, :]), reads=["sb1"], writes=["rden"])
                for g in range(3):
                    self.write_out(g * 4 + hg, m, (self.nd[0:64, g * 4 + m, :], f"nd{g}_{m}"), (self.rden[:], "rden"))

    def moba_prologue(self, s, ms):
        nc, p = self.nc, self.p
        p.op("dve", lambda: nc.vector.tensor_reduce(out=self.km[64 * s:64 * s + 64, :], in_=self.kTb[s].rearrange("e (n k) -> e n k", k=256), axis=AX.X, op=ALU.add),
             reads=[f"kTb{s}"], writes=["km"])
        p.op("dve", lambda: nc.vector.tensor_scalar(out=self.kmb[64 * s:64 * s + 64, :], in0=self.km[64 * s:64 * s + 64, :], scalar1=1.0 / 256, scalar2=None, op0=ALU.mult),
             reads=["km"], writes=["kmb"])
        pg = self.pm[0]
        for qs in range(16):
            p.op("pe", lambda qs=qs: nc.tensor.matmul(pg[:, qs * 32:(qs + 1) * 32], lhsT=self.qTb[s][:, qs * 128:(qs + 1) * 128], rhs=self.kmb[64 * s:64 * s + 64, :], start=True, stop=True),
                 reads=[f"qTb{s}", "kmb"], writes=["pm0"])
        p.op("dve", lambda: nc.vector.tensor_tensor(out=self.gm[:], in0=pg[:].rearrange("p (a b) -> p a b", b=32), in1=self.negm[:], op=ALU.add),
             reads=["pm0", "negm"], writes=["gm"])
        for qs in range(16):
            p.op("dve", lambda qs=qs: nc.vector.max(out=self.m8[:, qs, :], in_=self.gm[:, qs, :]), reads=["gm"], writes=["m8"])
        for qs in range(16):
            p.op("dve", lambda qs=qs: nc.vector.tensor_scalar(out=self.selt[:, qs, :], in0=self.gm[:, qs, :], scalar1=self.m8[:, qs, 2:3], scalar2=None, op0=ALU.is_ge),
                 reads=["gm", "m8"], writes=["selt"])
        p.op("dve", lambda: nc.vector.tensor_tensor(out=self.selt[:], in0=self.selt[:], in1=self.ownm[:], op=ALU.max), reads=["selt", "ownm"], writes=["selt"])
        p.op("dve", lambda: nc.vector.tensor_scalar(out=self.Mq[:], in0=self.selt[:], scalar1=-1.0, scalar2=-NEGM, op0=ALU.add, op1=ALU.mult),
             reads=["selt"], writes=["Mq"])
        for half in range(2):
            for q8 in range(8):
                qs = half * 8 + q8
                p.op("pe", lambda qs=qs, q8=q8: nc.tensor.transpose(out=self.pmb[0:32, q8 * 128:(q8 + 1) * 128], in_=self.Mq[:, qs, :], identity=self.identb[:]),
                     reads=["Mq", "identb"], writes=["pmb"])
            p.op("act", lambda half=half: nc.scalar.copy(out=self.MTb[ms][0:32, half * 1024:(half + 1) * 1024], in_=self.pmb[0:32, :]),
                 reads=["pmb"], writes=[f"MTb{ms}"])

    def mixer_B(self):
        nc, p, dr = self.nc, self.p, self.dr
        p.dma("sp", lambda: nc.sync.dma_start(out=self.negm[:], in_=dr["negm"]), writes=["negm"])
        p.dma("sp", lambda: nc.sync.dma_start(out=self.ownm[:], in_=dr["ownm"]), writes=["ownm"])
        p.dma("sp", lambda: nc.sync.dma_start(out=self.Eb[0:32, :], in_=dr["EB"]), writes=["Eb"])
        for hb in self.heads_B:
            h = 12 + hb
            s = self.load_head(h, h, h)
            ms = hb % 2
            self.moba_prologue(s, ms)
            for m in range(4):
                kts = list(range(0, 16 * m + 16))
                acc = self.acc_i % 2
                self.acc_i += 1

                def post(h=h, m=m, acc=acc):
                    p.op("dve", lambda: nc.vector.reciprocal(out=self.rden[:], in_=self.pacc[acc][64:128, :]), reads=[f"pacc{acc}"], writes=["rden"])
                    self.write_out(h, m, (self.pacc[acc][0:64, :], f"pacc{acc}"), (self.rden[:], "rden"))
                self.add_units(s, m, kts, 16 * m - 12, h, acc, mask=(32, ms), post=post)
            self.flush_units()


    def compress(self, kv, t):
        nc, p, dr = self.nc, self.p, self.dr
        s = 0
        if self.fused:
            self.load_kT_gathered(self.kTb[s], dr["cmpT_g"], (t * 3 + kv) * 64, f"kTb{s}")
        else:
            p.dma("sp", lambda: nc.sync.dma_start(out=self.kTb[s], in_=dr["cmpTf"][t * 3 + kv]), writes=[f"kTb{s}"])
        stg = self.band1[0:64, 0:4096].rearrange("e (l j) -> e l j", j=128)
        p.dma("sp", lambda: nc.sync.dma_start(out=stg, in_=dr["w1"][t]), writes=["bandb"])
        p.op("act", lambda: nc.scalar.copy(out=self.w1b[:], in_=stg), reads=["bandb"], writes=["w1b"])
        ph = self.pm[0]
        base = self.kTb[s]
        for l in range(32):
            rhs = bass.AP(tensor=base.tensor, offset=base.offset + l, ap=[list(base.ap[0]), [16, 511]])
            p.op("pe", lambda l=l, rhs=rhs: nc.tensor.matmul(ph[:, 0:511], lhsT=self.w1b[:, l, :], rhs=rhs, start=(l == 0), stop=(l == 31)),
                 reads=["w1b", f"kTb{s}"], writes=["pm0"])
        pc = self.pm[1]
        for l in range(32):
            p.op("pe", lambda l=l: nc.tensor.matmul(pc[:, 0:2], lhsT=self.w1b[:, l, :], rhs=self.peTb[:, t, l, :], start=(l == 0), stop=(l == 31)),
                 reads=["w1b", "peTb"], writes=["pS3"])
        p.op("dve", lambda: nc.vector.tensor_copy(out=self.cbias[:], in_=pc[:, 0:1]), reads=["pS3"], writes=["cbias"])
        x, y = self.scr[0], self.scr[1]
        p.op("act", lambda: nc.scalar.activation(out=x[:, 0:511], in_=ph[:, 0:511], func=AF.Identity, bias=self.cbias[:, 0:1]),
             reads=["pm0", "cbias"], writes=["scr0"])
        p.op("dve", lambda: nc.vector.tensor_tensor(out=y[:, 0:511], in0=x[:, 0:511], in1=x[:, 0:511], op=ALU.mult), reads=["scr0"], writes=["scr1"])
        p.op("dve", lambda: nc.vector.tensor_scalar(out=y[:, 0:511], in0=y[:, 0:511], scalar1=0.044715, scalar2=1.0, op0=ALU.mult, op1=ALU.add), reads=["scr1"], writes=["scr1"])
        p.op("dve", lambda: nc.vector.tensor_tensor(out=y[:, 0:511], in0=y[:, 0:511], in1=x[:, 0:511], op=ALU.mult), reads=["scr0", "scr1"], writes=["scr1"])
        p.op("act", lambda: nc.scalar.activation(out=y[:, 0:511], in_=y[:, 0:511], func=AF.Tanh, scale=0.7978845608028654), reads=["scr1"], writes=["scr1"])
        p.op("dve", lambda: nc.vector.scalar_tensor_tensor(out=y[:, 0:511], in0=y[:, 0:511], scalar=1.0, in1=x[:, 0:511], op0=ALU.add, op1=ALU.mult), reads=["scr0", "scr1"], writes=["scr1"])
        p.op("dve", lambda: nc.vector.tensor_scalar(out=self.hid[:, 0:511], in0=y[:, 0:511], scalar1=0.5, scalar2=None, op0=ALU.mult), reads=["scr1"], writes=["hid"])
        if t == 0:
            pk = self.pS[0]
            p.op("pe", lambda: nc.tensor.matmul(pk[0:64, :], lhsT=self.w2b[:, 0, :], rhs=self.hid[:], start=True, stop=True),
                 reads=["w2b", "hid"], writes=["pS0"])
            p.op("act", lambda: nc.scalar.copy(out=self.kcTb[:], in_=pk[0:64, :]), reads=["pS0"], writes=["kcTb"])
        else:
            pv = self.pS[1]
            for it in range(4):
                p.op("pe", lambda it=it: nc.tensor.matmul(pv[:, it * 64:(it + 1) * 64], lhsT=self.hid[:, it * 128:(it + 1) * 128], rhs=self.w2b[:, 1, :], start=True, stop=True),
                     reads=["w2b", "hid"], writes=["pS1"])
            p.op("act", lambda: nc.scalar.copy(out=self.vcb[:], in_=pv[:, 0:256].rearrange("p (a b) -> p a b", b=64)), reads=["pS1"], writes=["vcb"])

    def cmp_stage(self, kv, m):
        nc, p, dr = self.nc, self.p, self.dr
        pden, poc, pgt, pimp, ptr = self.pS[0], self.pS[1], self.pS[2], self.pacc[0], self.pacc[1]
        nit = min(m, 3) + 1
        for gq in range(4):
            hc = kv * 4 + gq
            qs = self.qc_i % 3
            self.qc_i += 1
            p.dma("sp", lambda qs=qs, hc=hc: nc.sync.dma_start(out=self.qcm[qs][:], in_=dr["qT"][20 + hc][:, m * 512:(m + 1) * 512]), writes=[f"qcm{qs}"])
            p.dma("sp", lambda hc=hc: nc.sync.dma_start(out=self.sel3[:], in_=dr["selg"][:, 3 * hc:3 * hc + 3, :]), writes=["sel3"])
            for it in range(nit):
                ps = self.pm[it % 2]
                psn = self.pmn[it % 2]
                p.op("pe", lambda it=it, ps=ps, qs=qs: nc.tensor.matmul(ps[:], lhsT=self.kcTb[:, it * 128:(it + 1) * 128], rhs=self.qcm[qs][:], start=True, stop=True),
                     reads=["kcTb", f"qcm{qs}"], writes=[psn])
                if it >= m - 1:
                    cm, cmn = (self.cmA, "cmA") if it == m else (self.cmB, "cmB")
                    sbi = it % 2
                    p.op("dve", lambda ps=ps, cm=cm, sbi=sbi: nc.vector.tensor_tensor(out=self.sb[sbi][:], in0=ps[:], in1=cm[:], op=ALU.add),
                         reads=[psn, cmn], writes=[f"sb{sbi}"])
                    p.op("act", lambda it=it, sbi=sbi: nc.scalar.activation(out=self.ef[it][:], in_=self.sb[sbi][:], func=AF.Exp), reads=[f"sb{sbi}"], writes=[f"ef{it}"])
                else:
                    p.op("act", lambda it=it, ps=ps: nc.scalar.activation(out=self.ef[it][:], in_=ps[:], func=AF.Exp), reads=[psn], writes=[f"ef{it}"])
                p.op("pe", lambda it=it: nc.tensor.matmul(pden[:], lhsT=self.onesf[:], rhs=self.ef[it][:], start=(it == 0), stop=(it == nit - 1)),
                     reads=["onesf", f"ef{it}"], writes=["pS0"])
            rd = self.scr[0]
            p.op("dve", lambda: nc.vector.tensor_scalar(out=rd[:], in0=pden[:], scalar1=1e-30, scalar2=None, op0=ALU.max), reads=["pS0"], writes=["scr0"])
            p.op("dve", lambda: nc.vector.reciprocal(out=rd[:], in_=rd[:]), reads=["scr0"], writes=["scr0"])
            for it in range(nit):
                p.op("dve", lambda it=it: nc.vector.tensor_tensor(out=self.ef[it][:], in0=self.ef[it][:], in1=rd[:], op=ALU.mult), reads=[f"ef{it}", "scr0"], writes=[f"ef{it}"])
                p.op("pool", lambda it=it: nc.gpsimd.tensor_copy(out=self.pcb[it][:], in_=self.ef[it][:]), reads=[f"ef{it}"], writes=[f"pcb{it}"])
                p.op("pe", lambda it=it, gq=gq: nc.tensor.matmul(pimp[:], lhsT=self.ovl[:, it, :], rhs=self.ef[it][:], start=(gq == 0 and it == 0), stop=(gq == 3 and it == nit - 1)),
                     reads=["ovl", f"ef{it}"], writes=["pacc0"])
            for it in range(nit):
                p.op("pe", lambda it=it: nc.tensor.matmul(poc[0:64, :], lhsT=self.vcb[:, it, :], rhs=self.pcb[it][:], start=(it == 0), stop=(it == nit - 1)),
                     reads=["vcb", f"pcb{it}"], writes=["pS1"])
            p.op("pe", lambda: nc.tensor.matmul(pgt[0:64, :], lhsT=self.sel3[:, 0, :], rhs=self.gTs[:, m * 512:(m + 1) * 512], start=True, stop=True),
                 reads=["sel3", "gTs"], writes=["pS2"])
            p.op("act", lambda: nc.scalar.copy(out=self.ocs, in_=poc[0:64, :]), reads=["pS1"], writes=["scr1"])
            half, idx = gq // 2, (gq % 2) * 4 + m
            if half == 0:
                p.op("dve", lambda idx=idx: nc.vector.tensor_tensor(out=self.nd[0:64, idx, :], in0=self.ocs, in1=pgt[0:64, :], op=ALU.mult),
                     reads=["scr1", "pS2"], writes=[f"oc{gq}_{m}"])
            else:
                p.op("dve", lambda: nc.vector.tensor_tensor(out=self.tmpf[:], in0=self.ocs, in1=pgt[0:64, :], op=ALU.mult),
                     reads=["scr1", "pS2"], writes=["tmpf"])
                p.op("dve", lambda idx=idx: nc.vector.tensor_copy(out=self.nd[64:128, idx, :], in_=self.tmpf[:]), reads=["tmpf"], writes=[f"oc{gq}_{m}"])
        p.dma("sp", lambda: nc.sync.dma_start(out=self.amb[:], in_=dr["AM"][:, 4 * m:4 * m + 4, :]), writes=["amb"])
        p.dma("sp", lambda: nc.sync.dma_start(out=self.bab[:], in_=dr["BA"][:, 4 * m:4 * m + 4, :]), writes=["bab"])
        p.op("act", lambda: nc.scalar.copy(out=self.impS[:], in_=pimp[:]), reads=["pacc0"], writes=["impS"])
        for qs in range(4):
            p.op("pe", lambda qs=qs: nc.tensor.transpose(out=ptr[:, qs * 128:(qs + 1) * 128], in_=self.impS[:, qs * 128:(qs + 1) * 128], identity=self.identf[:]),
                 reads=["impS", "identf"], writes=["pacc1"])
        p.op("dve", lambda: nc.vector.tensor_tensor(out=self.score[:], in0=ptr[:].rearrange("p (a b) -> p a b", b=128), in1=self.amb[:], op=ALU.mult),
             reads=["pacc1", "amb"], writes=["score"])
        p.op("dve", lambda: nc.vector.tensor_tensor(out=self.score[:], in0=self.score[:], in1=self.bab[:], op=ALU.add), reads=["score", "bab"], writes=["score"])
        for qs in range(4):
            p.op("dve", lambda qs=qs: nc.vector.max(out=self.m16[:, qs, 0:8], in_=self.score[:, qs, :]), reads=["score"], writes=["m16"])
            p.op("dve", lambda qs=qs: nc.vector.match_replace(out=self.sc2[:], in_to_replace=self.m16[:, qs, 0:8], in_values=self.score[:, qs, :], imm_value=-1e30),
                 reads=["score", "m16"], writes=["sc2"])
            p.op("dve", lambda qs=qs: nc.vector.max(out=self.m16[:, qs, 8:16], in_=self.sc2[:]), reads=["sc2"], writes=["m16"])
            p.op("dve", lambda qs=qs: nc.vector.tensor_scalar(out=self.score[:, qs, :], in0=self.score[:, qs, :], scalar1=self.m16[:, qs, 15:16], scalar2=None, op0=ALU.is_ge),
                 reads=["score", "m16"], writes=["score"])
        p.op("dve", lambda: nc.vector.tensor_scalar(out=self.Msel[:], in0=self.score[:], scalar1=-1.0, scalar2=-NEGM, op0=ALU.add, op1=ALU.mult), reads=["score"], writes=["Msel"])
        ms = kv % 2
        for qs in range(4):
            p.op("pe", lambda qs=qs: nc.tensor.transpose(out=self.pmb[:, qs * 128:(qs + 1) * 128], in_=self.Msel[:, qs, :], identity=self.identb[:]),
                 reads=["Msel", "identb"], writes=["pmb"])
        p.op("act", lambda: nc.scalar.copy(out=self.MTb[ms][:, m * 512:(m + 1) * 512], in_=self.pmb[:, 0:512]), reads=["pmb"], writes=[f"MTb{ms}"])

    def mixer_C(self):
        nc, p, dr = self.nc, self.p, self.dr
        self.qc_i = 0
        p.dma("sp", lambda: nc.sync.dma_start(out=self.Eb[:], in_=dr["EC"]), writes=["Eb"])
        for nm in ("cmA", "cmB", "ovl", "gTs"):
            p.dma("sp", lambda nm=nm: nc.sync.dma_start(out=getattr(self, nm)[:], in_=dr[nm]), writes=[nm])
        p.dma("sp", lambda: nc.sync.dma_start(out=self.w2f[:], in_=dr["w2"]), writes=["w2f"])
        p.dma("sp", lambda: nc.sync.dma_start(out=self.peTf[:], in_=dr["peT"]), writes=["peTf"])
        p.op("dve", lambda: nc.vector.tensor_copy(out=self.w2b[:], in_=self.w2f[:]), reads=["w2f"], writes=["w2b"])
        p.op("dve", lambda: nc.vector.tensor_copy(out=self.peTb[:], in_=self.peTf[:]), reads=["peTf"], writes=["peTb"])
        p.op("pool", lambda: nc.gpsimd.memset(self.onesf[:], 1.0), writes=["onesf"])
        p.op("pool", lambda: nc.gpsimd.memset(self.hid[:], 0.0), writes=["hid"])
        for kv in self.kvs_C:
            self.compress(kv, 0)
            self.compress(kv, 1)
            for m in range(4):
                self.cmp_stage(kv, m)
            ms = kv % 2
            for gq in range(4):
                hc = kv * 4 + gq
                s = self.load_head(20 + hc, 20 + kv, 20 + hc)
                p.dma("sp", lambda hc=hc: nc.sync.dma_start(out=self.sel3[:], in_=dr["selg"][:, 3 * hc:3 * hc + 3, :]), writes=["sel3"])
                for m in range(4):
                    kts = list(range(0, 16 * m + 16))
                    acc = self.acc_i % 2
                    self.acc_i += 1

                    def post(gq=gq, m=m, acc=acc):
                        pg = self.pm[0]
                        p.op("dve", lambda: nc.vector.reciprocal(out=self.rden[:], in_=self.pacc[acc][64:128, :]), reads=[f"pacc{acc}"], writes=["rden"])
                        p.op("pe", lambda: nc.tensor.matmul(pg[0:64, :], lhsT=self.sel3[:, 1, :], rhs=self.gTs[:, m * 512:(m + 1) * 512], start=True, stop=True),
                             reads=["sel3", "gTs"], writes=["pm0"])
                        p.op("dve", lambda: nc.vector.tensor_tensor(out=self.tmpf[:], in0=self.pacc[acc][0:64, :], in1=self.rden[:], op=ALU.mult),
                             reads=[f"pacc{acc}", "rden"], writes=["tmpf"])
                        p.op("dve", lambda: nc.vector.tensor_tensor(out=self.nd[0:64, 8 + m, :], in0=self.tmpf[:], in1=pg[0:64, :], op=ALU.mult),
                             reads=["tmpf", "pm0"], writes=[f"res{m}"])
                        half, idx = gq // 2, (gq % 2) * 4 + m
                        if half == 0:
                            p.op("pool", lambda: nc.gpsimd.tensor_tensor(out=self.nd[0:64, 8 + m, :], in0=self.nd[0:64, 8 + m, :], in1=self.nd[0:64, idx, :], op=ALU.add),
                                 reads=[f"res{m}", f"oc{gq}_{m}"], writes=[f"res{m}"])
                        else:
                            p.op("dve", lambda: nc.vector.tensor_copy(out=self.tmpf[:], in_=self.nd[64:128, idx, :]), reads=[f"oc{gq}_{m}"], writes=["tmpf"])
                            p.op("pool", lambda: nc.gpsimd.tensor_tensor(out=self.nd[0:64, 8 + m, :], in0=self.nd[0:64, 8 + m, :], in1=self.tmpf[:], op=ALU.add),
                                 reads=[f"res{m}", "tmpf"], writes=[f"res{m}"])
                    self.add_units(s, m, kts, 16 * m - 12, 20 + hc, acc, mask=(128, ms), post=post)
                self.flush_units()
                s = self.load_head(20 + hc, 23 + kv, 32 + hc)
                for m in range(4):
                    kts = list(range(max(0, 16 * m - 4), 16 * m + 16))
                    acc = self.acc_i % 2
                    self.acc_i += 1

                    def post(hc=hc, m=m, acc=acc):
                        pg = self.pm[1]
                        p.op("dve", lambda: nc.vector.reciprocal(out=self.rden[:], in_=self.pacc[acc][64:128, :]), reads=[f"pacc{acc}"], writes=["rden"])
                        p.op("pe", lambda: nc.tensor.matmul(pg[0:64, :], lhsT=self.sel3[:, 2, :], rhs=self.gTs[:, m * 512:(m + 1) * 512], start=True, stop=True),
                             reads=["sel3", "gTs"], writes=["pS3"])
                        p.op("dve", lambda: nc.vector.tensor_tensor(out=self.tmpf[:], in0=self.pacc[acc][0:64, :], in1=self.rden[:], op=ALU.mult),
                             reads=[f"pacc{acc}", "rden"], writes=["tmpf"])
                        p.op("dve", lambda: nc.vector.tensor_tensor(out=self.tmpf[:], in0=self.tmpf[:], in1=pg[0:64, :], op=ALU.mult),
                             reads=["tmpf", "pS3"], writes=["tmpf"])
                        o = self.ost_i % 3
                        self.ost_i += 1
                        p.op("dve", lambda: nc.vector.tensor_tensor(out=self.ost[o][:], in0=self.tmpf[:], in1=self.nd[0:64, 8 + m, :], op=ALU.add),
                             reads=["tmpf", f"res{m}"], writes=[f"ost{o}"])
                        dst = self.dr["OT"][(20 + hc) * 64:(21 + hc) * 64, m * 512:(m + 1) * 512]
                        self.out_ops.append(p.dma("pool", lambda: nc.gpsimd.dma_start(out=dst, in_=self.ost[o][:]), reads=[f"ost{o}"]))
                    self.add_units(s, m, kts, 0, 32 + hc, acc, post=post)
                self.flush_units()


def dram_p2(nc):
    dr = {}
    I = lambda name, shape, dt: nc.dram_tensor(name, shape, dt, kind="ExternalInput").ap()
    dr["qT"] = I("qT", [32, 64, NT], BF16)
    dr["kTf"] = I("kTf", [26, 64, S], BF16)
    dr["vf"] = I("vf", [26, 128, 64, 64], BF16)
    dr["G"] = I("G", [44, GL], F32)
    dr["cb"] = I("cb", [128, 44], F32)
    dr["negm"] = I("negm", [128, 16, 32], F32)
    dr["ownm"] = I("ownm", [128, 16, 32], F32)
    dr["EB"] = I("EB", [32, S], BF16)
    dr["cmpTf"] = I("cmpTf", [6, 64, S], BF16)
    dr["w1"] = I("w1", [2, 64, 32, 128], F32)
    dr["w2"] = I("w2", [128, 2, 64], F32)
    dr["peT"] = I("peT", [64, 2, 32, 2], F32)
    dr["EC"] = I("EC", [128, S], BF16)
    dr["AM"] = I("AM", [128, 16, 128], F32)
    dr["BA"] = I("BA", [128, 16, 128], F32)
    dr["cmA"] = I("cmA", [128, 512], F32)
    dr["cmB"] = I("cmB", [128, 512], F32)
    dr["ovl"] = I("ovl", [128, 4, 128], F32)
    dr["selg"] = I("selg", [36, 36, 64], F32)
    dr["gTs"] = I("gTs", [36, NT], F32)
    dr["OT"] = nc.dram_tensor("OT", [2048, NT], BF16, kind="ExternalOutput").ap()
    return dr


def build_p2(**kw):
    nc = bass.Bass("TRN2", target_bir_lowering=False)
    dr = dram_p2(nc)
    with contextlib.ExitStack() as st:
        T = lambda name, shape, dt: st.enter_context(nc.sbuf_tensor("s_" + name, shape, dt))
        PS = lambda name, shape, dt: st.enter_context(nc.psum_tensor("p_" + name, shape, dt))
        p = Prog(nc)
        P = P2(nc, p, T, PS, dr, **kw)
        P.setup()
        P.mixer_A()
        P.mixer_B()
        P.mixer_C()
        p.emit(final_wait_ops=P.out_ops)
    return nc

import contextlib
import numpy as np

D = 2048
NT = 2048
DFF = 5632
EPS = 1e-6
TW = 514


def build_p3a():
    nc = bass.Bass("TRN2", target_bir_lowering=False)
    I = lambda name, shape, dt: nc.dram_tensor(name, shape, dt, kind="ExternalInput").ap()
    xT = I("xT", [D, 4, TW], F32)
    OT = I("OT", [D, 4, TW], BF16)
    w = I("w", [D, D], F32)
    gn = I("gn", [128, 16], F32)
    x1T = nc.dram_tensor("x1T", [D, 4, TW], F32, kind="ExternalOutput").ap()
    hT = nc.dram_tensor("hT", [128, 16, 4, TW], BF16, kind="ExternalOutput").ap()
    with contextlib.ExitStack() as st:
        T = lambda name, shape, dt: st.enter_context(nc.sbuf_tensor("s_" + name, shape, dt))
        PS = lambda name, shape, dt: st.enter_context(nc.psum_tensor("p_" + name, shape, dt))
        p = Prog(nc)
        outs = []
        OTb = T("OTb", [128, 16, 4, TW], BF16)
        gsb = T("gsb", [128, 16], F32)
        ones = T("ones", [128, 128], F32)
        wst = [T(f"wst{i}", [128, 16, 128], F32) for i in range(2)]
        wbf = [T(f"wbf{i}", [128, 16, 128], BF16) for i in range(2)]
        xc = [T(f"xc{i}", [128, 4, TW], F32) for i in range(2)]
        x1c = [T(f"x1c{i}", [128, 4, TW], F32) for i in range(2)]
        sqt = T("sqt", [128, 4, TW], F32)
        accsq = T("accsq", [128, 4, TW], F32)
        rstd = T("rstd", [128, 4, TW], F32)
        hc = [T(f"hc{i}", [128, 4, TW], BF16) for i in range(2)]
        pacc = [PS(f"pacc{i}", [128, 512], F32) for i in range(4)]
        ph = PS("ph", [128, 512], F32)
        pss = [PS(f"pss{i}", [128, 512], F32) for i in range(2)]

        p.dma("sp", lambda: nc.sync.dma_start(out=gsb[:], in_=gn), writes=["gsb"])
        p.op("pool", lambda: nc.gpsimd.memset(ones[:], 1.0), writes=["ones"])
        p.op("pool", lambda: nc.gpsimd.memset(accsq[:], 0.0), writes=["accsq"])
        OTv = OT.rearrange("(k p) m t -> p k m t", p=128)
        for k4 in range(4):
            p.dma("sp", lambda k4=k4: nc.sync.dma_start(out=OTb[:, k4 * 4:(k4 + 1) * 4], in_=OTv[:, k4 * 4:(k4 + 1) * 4]), writes=[f"OTb{k4}"])
        OTres = [f"OTb{k4}" for k4 in range(4)]
        ai = 0
        for c in range(16):
            s = c % 2
            src = w[:, c * 128:(c + 1) * 128].rearrange("(k p) n -> p k n", p=128)
            p.dma("sp", lambda s=s, src=src: nc.sync.dma_start(out=wst[s][:], in_=src), writes=[f"wst{s}"])
            p.op("act", lambda s=s: nc.scalar.copy(out=wbf[s][:, 0:8], in_=wst[s][:, 0:8]), reads=[f"wst{s}"], writes=[f"wbfa{s}"])
            p.op("pool", lambda s=s: nc.gpsimd.tensor_copy(out=wbf[s][:, 8:16], in_=wst[s][:, 8:16]), reads=[f"wst{s}"], writes=[f"wbfb{s}"])
            p.dma("sp", lambda s=s, c=c: nc.sync.dma_start(out=xc[s][:], in_=xT[c * 128:(c + 1) * 128]), writes=[f"xc{s}"])
            for m in range(4):
                a = ai % 4
                ai += 1
                for k in range(16):
                    p.op("pe", lambda a=a, s=s, k=k, m=m: nc.tensor.matmul(pacc[a][:], lhsT=wbf[s][:, k, :], rhs=OTb[:, k, m, 2:TW], start=(k == 0), stop=(k == 15)),
                         reads=[f"wbfa{s}", f"wbfb{s}"] + OTres, writes=[f"pacc{a}"])
                p.op("dve", lambda a=a, s=s, m=m: nc.vector.tensor_tensor(out=x1c[s][:, m, 2:TW], in0=pacc[a][:], in1=xc[s][:, m, 2:TW], op=ALU.add),
                     reads=[f"pacc{a}", f"xc{s}"], writes=[f"x1c{s}"])
            for k in range(16):
                p.op("pe", lambda s=s, k=k: nc.tensor.matmul(ph[:, 0:8], lhsT=wbf[s][:, k, :], rhs=OTb[:, k, :, 0:2], start=(k == 0), stop=(k == 15)),
                     reads=[f"wbfa{s}", f"wbfb{s}"] + OTres, writes=["ph"])
            p.op("dve", lambda s=s: nc.vector.tensor_tensor(out=x1c[s][:, :, 0:2], in0=ph[:, 0:8].rearrange("p (m h) -> p m h", h=2), in1=xc[s][:, :, 0:2], op=ALU.add),
                 reads=["ph", f"xc{s}"], writes=[f"x1c{s}"])
            outs.append(p.dma("pool", lambda s=s, c=c: nc.gpsimd.dma_start(out=x1T[c * 128:(c + 1) * 128], in_=x1c[s][:]), reads=[f"x1c{s}"], writes=[f"x1T{c}"]))
            p.op("act", lambda s=s: nc.scalar.activation(out=sqt[:], in_=x1c[s][:], func=AF.Square), reads=[f"x1c{s}"], writes=["sqt"])
            p.op("pool", lambda: nc.gpsimd.tensor_tensor(out=accsq[:], in0=accsq[:], in1=sqt[:], op=ALU.add), reads=["sqt", "accsq"], writes=["accsq"])
        for m in range(4):
            q = m % 2
            p.op("pe", lambda q=q, m=m: nc.tensor.matmul(pss[q][:], lhsT=ones[:], rhs=accsq[:, m, 2:TW], start=True, stop=True), reads=["ones", "accsq"], writes=[f"pss{q}"])
            p.op("act", lambda q=q, m=m: nc.scalar.activation(out=rstd[:, m, 2:TW], in_=pss[q][:], func=AF.Sqrt, scale=1.0 / D, bias=EPS), reads=[f"pss{q}"], writes=["rstd"])
        p.op("pe", lambda: nc.tensor.matmul(ph[:, 0:8], lhsT=ones[:], rhs=accsq[:, :, 0:2], start=True, stop=True), reads=["ones", "accsq"], writes=["ph"])
        p.op("act", lambda: nc.scalar.activation(out=rstd[:, :, 0:2], in_=ph[:, 0:8].rearrange("p (m h) -> p m h", h=2), func=AF.Sqrt, scale=1.0 / D, bias=EPS), reads=["ph"], writes=["rstd"])
        p.op("dve", lambda: nc.vector.reciprocal(out=rstd[:], in_=rstd[:]), reads=["rstd"], writes=["rstd"])
        for c in range(16):
            s = c % 2
            p.dma("sp", lambda s=s, c=c: nc.sync.dma_start(out=x1c[s][:], in_=x1T[c * 128:(c + 1) * 128]), reads=[f"x1T{c}"], writes=[f"x1c{s}"])
            p.op("dve", lambda s=s, c=c: nc.vector.scalar_tensor_tensor(out=hc[s][:], in0=x1c[s][:], scalar=gsb[:, c:c + 1], in1=rstd[:], op0=ALU.mult, op1=ALU.mult),
                 reads=[f"x1c{s}", "gsb", "rstd"], writes=[f"hc{s}"])
            outs.append(p.dma("pool", lambda s=s, c=c: nc.gpsimd.dma_start(out=hT[:, c], in_=hc[s][:]), reads=[f"hc{s}"]))
        p.emit(final_wait_ops=outs)
    return nc


def emit_p3b(nc, p, T, PS, hT, x1get, wu, wd, cw, cbv, x2T, relayout=False):
    outs = []
    hTh = T("hTh", [128, 16, 2, TW], BF16)
    actT = T("actT", [128, 44, 1024], BF16)
    wst = [T(f"wst{i}", [128, 4096], F32) for i in range(2)]
    wbf = [T(f"wbf{i}", [128, 4096], BF16) for i in range(2)]
    cws = T("cws", [128, 88, 3], F32)
    cbs = T("cbs", [128, 88], F32)
    ua = [T(f"ua{i}", [128, TW], F32) for i in range(2)]
    ug = [T(f"ug{i}", [128, TW], F32) for i in range(2)]
    ya = [T(f"ya{i}", [128, 512], F32) for i in range(2)]
    yg = [T(f"yg{i}", [128, 512], F32) for i in range(2)]
    sg = [T(f"sg{i}", [128, 512], F32) for i in range(2)]
    x1c = [T(f"x1c{i}", [128, 2, 512], F32) for i in range(2)]
    pa = [PS(f"pa{i}", [128, 512], F32) for i in range(2)]
    pg = [PS(f"pg{i}", [128, 512], F32) for i in range(2)]
    ph = PS("ph", [128, 512], F32)
    pd = [PS(f"pd{i}", [128, 512], F32) for i in range(2)]
    p.dma("sp", lambda: nc.sync.dma_start(out=cws[:], in_=cw), writes=["cws"])
    p.dma("sp", lambda: nc.sync.dma_start(out=cbs[:], in_=cbv), writes=["cbs"])
    ui = [0]
    jobs = []
    for half in range(2):
        for c in range(44):
            jobs.append(("up", half, c, 0))
        for cc in range(16):
            for piece in range(2):
                jobs.append(("dn", half, cc, piece))

    def views(job, s):
        if job[0] == "up":
            return wst[s][:].rearrange("p (k n) -> p k n", n=256), wbf[s][:].rearrange("p (k n) -> p k n", n=256)
        return (wst[s][:, 0:22 * 128].rearrange("p (k n) -> p k n", n=128), wbf[s][:, 0:22 * 128].rearrange("p (k n) -> p k n", n=128))

    def load(job, s):
        wv, wb = views(job, s)
        if job[0] == "up":
            c = job[2]
            if relayout:
                srcp = wu[c].rearrange("p (k n) -> p k n", n=256)
                p.dma("sp", lambda wv=wv, srcp=srcp: nc.sync.dma_start(out=wv, in_=srcp), writes=[f"wst{s}_0", f"wst{s}_1"])
            else:
                for part in range(2):
                    col0 = part * DFF + c * 128
                    src = wu[:, col0:col0 + 128].rearrange("(k p) n -> p k n", p=128)
                    p.dma("sp", lambda wv=wv, src=src, part=part: nc.sync.dma_start(out=wv[:, :, part * 128:(part + 1) * 128], in_=src), writes=[f"wst{s}_{part}"])
            ka = 12
            kn = 16
        else:
            cc, piece = job[2], job[3]
            if relayout:
                src = wd[cc, piece].rearrange("p (k n) -> p k n", n=128)
            else:
                src = wd[piece * 2816:(piece + 1) * 2816, cc * 128:(cc + 1) * 128].rearrange("(k p) n -> p k n", p=128)
            p.dma("sp", lambda wv=wv, src=src: nc.sync.dma_start(out=wv, in_=src), writes=[f"wst{s}_0", f"wst{s}_1"])
            ka = 16
            kn = 22
        p.op("act", lambda wv=wv, wb=wb, ka=ka: nc.scalar.copy(out=wb[:, 0:ka], in_=wv[:, 0:ka]), reads=[f"wst{s}_0", f"wst{s}_1"], writes=[f"wbfa{s}"])
        p.op("pool", lambda wv=wv, wb=wb, ka=ka, kn=kn: nc.gpsimd.tensor_copy(out=wb[:, ka:kn], in_=wv[:, ka:kn]), reads=[f"wst{s}_0", f"wst{s}_1"], writes=[f"wbfb{s}"])

    def compute(job, s):
        wv, wb = views(job, s)
        wres = [f"wbfa{s}", f"wbfb{s}"]
        half = job[1]
        if job[0] == "up":
            c = job[2]
            if c == 0:
                p.dma("sp", lambda half=half: nc.sync.dma_start(out=hTh[:], in_=hT[:, :, 2 * half:2 * half + 2, :]), writes=["hTh"])
            for k in range(16):
                p.op("pe", lambda k=k, wb=wb: nc.tensor.matmul(ph[:, 0:4], lhsT=wb[:, k, 0:128], rhs=hTh[:, k, :, 0:2], start=(k == 0), stop=(k == 15)),
                     reads=wres + ["hTh"], writes=["ph"])
            for k in range(16):
                p.op("pe", lambda k=k, wb=wb: nc.tensor.matmul(ph[:, 4:8], lhsT=wb[:, k, 128:256], rhs=hTh[:, k, :, 0:2], start=(k == 0), stop=(k == 15)),
                     reads=wres + ["hTh"], writes=["ph"])
            for tt in range(2):
                u = ui[0] % 2
                ui[0] += 1
                for k in range(16):
                    p.op("pe", lambda u=u, k=k, tt=tt, wb=wb: nc.tensor.matmul(pa[u][:], lhsT=wb[:, k, 0:128], rhs=hTh[:, k, tt, 2:TW], start=(k == 0), stop=(k == 15)),
                         reads=wres + ["hTh"], writes=[f"pa{u}"])
                for k in range(16):
                    p.op("pe", lambda u=u, k=k, tt=tt, wb=wb: nc.tensor.matmul(pg[u][:], lhsT=wb[:, k, 128:256], rhs=hTh[:, k, tt, 2:TW], start=(k == 0), stop=(k == 15)),
                         reads=wres + ["hTh"], writes=[f"pg{u}"])
                p.op("act", lambda u=u: nc.scalar.copy(out=ua[u][:, 2:TW], in_=pa[u][:]), reads=[f"pa{u}"], writes=[f"ua{u}"])
                p.op("act", lambda u=u: nc.scalar.copy(out=ug[u][:, 2:TW], in_=pg[u][:]), reads=[f"pg{u}"], writes=[f"ug{u}"])
                p.op("act", lambda u=u, tt=tt: nc.scalar.copy(out=ua[u][:, 0:2], in_=ph[:, 2 * tt:2 * tt + 2]), reads=["ph"], writes=[f"ua{u}"])
                p.op("act", lambda u=u, tt=tt: nc.scalar.copy(out=ug[u][:, 0:2], in_=ph[:, 4 + 2 * tt:6 + 2 * tt]), reads=["ph"], writes=[f"ug{u}"])
                for (ub, yb, nm, ch) in ((ua, ya, "a", c), (ug, yg, "g", 44 + c)):
                    p.op("dve", lambda u=u, ub=ub, yb=yb, ch=ch: nc.vector.tensor_scalar(out=yb[u][:], in0=ub[u][:, 2:TW], scalar1=cws[:, ch, 0:1], scalar2=cbs[:, ch:ch + 1], op0=ALU.mult, op1=ALU.add),
                         reads=[f"u{nm}{u}", "cws", "cbs"], writes=[f"y{nm}{u}"])
                    p.op("dve", lambda u=u, ub=ub, yb=yb, ch=ch: nc.vector.scalar_tensor_tensor(out=yb[u][:], in0=ub[u][:, 1:TW - 1], scalar=cws[:, ch, 1:2], in1=yb[u][:], op0=ALU.mult, op1=ALU.add),
                         reads=[f"u{nm}{u}", "cws", f"y{nm}{u}"], writes=[f"y{nm}{u}"])
                    p.op("dve", lambda u=u, ub=ub, yb=yb, ch=ch: nc.vector.scalar_tensor_tensor(out=yb[u][:], in0=ub[u][:, 0:TW - 2], scalar=cws[:, ch, 2:3], in1=yb[u][:], op0=ALU.mult, op1=ALU.add),
                         reads=[f"u{nm}{u}", "cws", f"y{nm}{u}"], writes=[f"y{nm}{u}"])
                p.op("act", lambda u=u: nc.scalar.activation(out=sg[u][:], in_=yg[u][:], func=AF.Silu), reads=[f"yg{u}"], writes=[f"sg{u}"])
                p.op("pool", lambda u=u, c=c, tt=tt: nc.gpsimd.tensor_tensor(out=actT[:, c, tt * 512:(tt + 1) * 512], in0=sg[u][:], in1=ya[u][:], op=ALU.mult),
                     reads=[f"sg{u}", f"ya{u}"], writes=[f"actT{c}"])
        else:
            cc, piece = job[2], job[3]
            for tt in range(2):
                for k in range(22):
                    c = piece * 22 + k
                    p.op("pe", lambda wb=wb, k=k, c=c, tt=tt: nc.tensor.matmul(pd[tt][:], lhsT=wb[:, k, :], rhs=actT[:, c, tt * 512:(tt + 1) * 512], start=(c == 0), stop=(c == 43)),
                         reads=wres + [f"actT{c}"], writes=[f"pd{tt}"])
            if piece == 1:
                xs = cc % 2
                p.dma("sp", lambda xs=xs, cc=cc, half=half: nc.sync.dma_start(out=x1c[xs][:], in_=x1get(cc, half)), writes=[f"x1c{xs}"])
                for tt in range(2):
                    p.op("dve", lambda xs=xs, tt=tt: nc.vector.tensor_tensor(out=x1c[xs][:, tt, :], in0=pd[tt][:], in1=x1c[xs][:, tt, :], op=ALU.add),
                         reads=[f"pd{tt}", f"x1c{xs}"], writes=[f"x1c{xs}"])
                dst = x2T[cc * 128:(cc + 1) * 128, half * 1024:(half + 1) * 1024].rearrange("p (t n) -> p t n", n=512)
                outs.append(p.dma("pool", lambda xs=xs, dst=dst: nc.gpsimd.dma_start(out=dst, in_=x1c[xs][:]), reads=[f"x1c{xs}"]))

    load(jobs[0], 0)
    for i, job in enumerate(jobs):
        if i + 1 < len(jobs):
            load(jobs[i + 1], (i + 1) % 2)
        compute(job, i % 2)
    return outs


def build_p3b():
    nc = bass.Bass("TRN2", target_bir_lowering=False)
    I = lambda name, shape, dt: nc.dram_tensor(name, shape, dt, kind="ExternalInput").ap()
    hT = I("hT", [128, 16, 4, TW], BF16)
    x1T = I("x1T", [D, 4, TW], F32)
    wu = I("wu", [D, 2 * DFF], F32)
    wd = I("wd", [DFF, D], F32)
    cw = I("cw", [128, 88, 3], F32)
    cbv = I("cbv", [128, 88], F32)
    x2T = nc.dram_tensor("x2T", [D, NT], F32, kind="ExternalOutput").ap()
    with contextlib.ExitStack() as st:
        T = lambda name, shape, dt: st.enter_context(nc.sbuf_tensor("s_" + name, shape, dt))
        PS = lambda name, shape, dt: st.enter_context(nc.psum_tensor("p_" + name, shape, dt))
        p = Prog(nc)
        x1get = lambda cc, half: x1T[cc * 128:(cc + 1) * 128, 2 * half:2 * half + 2, 2:TW]
        outs = emit_p3b(nc, p, T, PS, hT, x1get, wu, wd, cw, cbv, x2T)
        p.emit(final_wait_ops=outs)
    return nc


def emit_kf(nc, p, T, PS, xT, gn, yT):
    outs = []
    gsb = T("gsb", [128, 16], F32)
    ones = T("ones", [128, 128], F32)
    xs = [T(f"xs{i}", [128, 16, 512], F32) for i in range(2)]
    ys = [T(f"ys{i}", [128, 16, 512], F32) for i in range(2)]
    sq = [T(f"sq{i}", [128, 512], F32) for i in range(2)]
    rstd = T("rstd", [128, 512], F32)
    pss = PS("pss", [128, 512], F32)
    p.dma("sp", lambda: nc.sync.dma_start(out=gsb[:], in_=gn), writes=["gsb"])
    p.op("pool", lambda: nc.gpsimd.memset(ones[:], 1.0), writes=["ones"])
    xv = xT.rearrange("(k p) t -> p k t", p=128)
    yv = yT.rearrange("(k p) t -> p k t", p=128)
    for m in range(4):
        s = m % 2
        p.dma("sp", lambda m=m, s=s: nc.sync.dma_start(out=xs[s][:], in_=xv[:, :, m * 512:(m + 1) * 512]), writes=[f"xs{s}"])
        for k in range(16):
            q = k % 2
            p.op("act", lambda s=s, k=k, q=q: nc.scalar.activation(out=sq[q][:], in_=xs[s][:, k, :], func=AF.Square), reads=[f"xs{s}"], writes=[f"sq{q}"])
            p.op("pe", lambda q=q, k=k: nc.tensor.matmul(pss[:], lhsT=ones[:], rhs=sq[q][:], start=(k == 0), stop=(k == 15)), reads=["ones", f"sq{q}"], writes=["pss"])
        p.op("act", lambda: nc.scalar.activation(out=rstd[:], in_=pss[:], func=AF.Sqrt, scale=1.0 / D, bias=EPS), reads=["pss"], writes=["rstd"])
        p.op("dve", lambda: nc.vector.reciprocal(out=rstd[:], in_=rstd[:]), reads=["rstd"], writes=["rstd"])
        for k in range(16):
            p.op("dve", lambda s=s, k=k: nc.vector.scalar_tensor_tensor(out=ys[s][:, k, :], in0=xs[s][:, k, :], scalar=gsb[:, k:k + 1], in1=rstd[:], op0=ALU.mult, op1=ALU.mult),
                 reads=[f"xs{s}", "gsb", "rstd"], writes=[f"ys{s}"])
        outs.append(p.dma("pool", lambda m=m, s=s: nc.gpsimd.dma_start(out=yv[:, :, m * 512:(m + 1) * 512], in_=ys[s][:]), reads=[f"ys{s}"]))
    return outs


def build_kf():
    nc = bass.Bass("TRN2", target_bir_lowering=False)
    xT = nc.dram_tensor("xT", [D, NT], F32, kind="ExternalInput").ap()
    gn = nc.dram_tensor("gn", [128, 16], F32, kind="ExternalInput").ap()
    yT = nc.dram_tensor("yT", [D, NT], F32, kind="ExternalOutput").ap()
    with contextlib.ExitStack() as st:
        T = lambda name, shape, dt: st.enter_context(nc.sbuf_tensor("s_" + name, shape, dt))
        PS = lambda name, shape, dt: st.enter_context(nc.psum_tensor("p_" + name, shape, dt))
        p = Prog(nc)
        outs = emit_kf(nc, p, T, PS, xT, gn, yT)
        p.emit(final_wait_ops=outs)
    return nc

import contextlib, math
import numpy as np

DEPTH = 4
RG = [[0, 1, 2, 3], [4, 5, 6, 7]]


def ec_const_nat():
    c = np.arange(S)
    return (c[None, :] // 64 == np.arange(128)[:, None]).astype(np.float32)


def halo_coef(j):
    co = np.zeros((128, 5), np.float32)
    if j >= 1:
        co[:, j - 1] = 1.0
    else:
        co[:, 4] = 1.0
    return co


def emit_p3a_f(nc, p, T, PS, xT, OT, w, gn, x1T, hT, ht_s):
    OTb = T("OTb", [128, 16, NT], BF16)
    gsb = T("gsb", [128, 16], F32)
    ones = T("ones", [128, 128], F32)
    wst = [T(f"wst{i}", [128, 16, 128], F32) for i in range(2)]
    wbf = [T(f"wbf{i}", [128, 16, 128], BF16) for i in range(2)]
    xc = [T(f"xc{i}", [128, NT], F32) for i in range(2)]
    x1c = [T(f"x1c{i}", [128, NT], F32) for i in range(2)]
    sqt = T("sqt", [128, NT], F32)
    accsq = T("accsq", [128, NT], F32)
    rstd = T("rstd", [128, NT], F32)
    hc = [T(f"hc{i}", [128, NT], BF16) for i in range(2)]
    pacc = [PS(f"pacc{i}", [128, 512], F32) for i in range(4)]
    pss = [PS(f"pss{i}", [128, 512], F32) for i in range(2)]
    p.dma("sp", lambda: nc.sync.dma_start(out=gsb[:], in_=gn), writes=["gsb"])
    p.op("pool", lambda: nc.gpsimd.memset(ones[:], 1.0), writes=["ones"])
    p.op("pool", lambda: nc.gpsimd.memset(accsq[:], 0.0), writes=["accsq"])
    OTv = OT.rearrange("(k p) t -> p k t", p=128)
    for k4 in range(4):
        p.dma("sp", lambda k4=k4: nc.sync.dma_start(out=OTb[:, k4 * 4:(k4 + 1) * 4], in_=OTv[:, k4 * 4:(k4 + 1) * 4]), writes=[f"OTb{k4}"])
    OTres = [f"OTb{k4}" for k4 in range(4)]
    ai = 0

    def load_wo(c):
        s = c % 2
        src = w[c].rearrange("p (k n) -> p k n", n=128)
        p.dma("sp", lambda s=s, src=src: nc.sync.dma_start(out=wst[s][:], in_=src), writes=[f"wst{s}"])
        p.op("act", lambda s=s: nc.scalar.copy(out=wbf[s][:, 0:12], in_=wst[s][:, 0:12]), reads=[f"wst{s}"], writes=[f"wbfa{s}"])
        p.op("pool", lambda s=s: nc.gpsimd.tensor_copy(out=wbf[s][:, 12:16], in_=wst[s][:, 12:16]), reads=[f"wst{s}"], writes=[f"wbfb{s}"])
        p.dma("sp", lambda s=s, c=c: nc.sync.dma_start(out=xc[s][:], in_=xT[c * 128:(c + 1) * 128]), writes=[f"xc{s}"])
    load_wo(0)
    for c in range(16):
        s = c % 2
        if c + 1 < 16:
            load_wo(c + 1)
        for m in range(4):
            a = ai % 4
            ai += 1
            for k in range(16):
                p.op("pe", lambda a=a, s=s, k=k, m=m: nc.tensor.matmul(pacc[a][:], lhsT=wbf[s][:, k, :], rhs=OTb[:, k, m * 512:(m + 1) * 512], start=(k == 0), stop=(k == 15)),
                     reads=[f"wbfa{s}", f"wbfb{s}"] + OTres, writes=[f"pacc{a}"])
            p.op("dve", lambda a=a, s=s, m=m: nc.vector.tensor_tensor(out=x1c[s][:, m * 512:(m + 1) * 512], in0=pacc[a][:], in1=xc[s][:, m * 512:(m + 1) * 512], op=ALU.add),
                 reads=[f"pacc{a}", f"xc{s}"], writes=[f"x1c{s}"])
        p.dma("pool", lambda s=s, c=c: nc.gpsimd.dma_start(out=x1T[c * 128:(c + 1) * 128], in_=x1c[s][:]), reads=[f"x1c{s}"], writes=[f"x1T{c}"])
        p.op("act", lambda s=s: nc.scalar.activation(out=sqt[:], in_=x1c[s][:], func=AF.Square), reads=[f"x1c{s}"], writes=["sqt"])
        p.op("pool", lambda: nc.gpsimd.tensor_tensor(out=accsq[:], in0=accsq[:], in1=sqt[:], op=ALU.add), reads=["sqt", "accsq"], writes=["accsq"])
    for m in range(4):
        q = m % 2
        p.op("pe", lambda q=q, m=m: nc.tensor.matmul(pss[q][:], lhsT=ones[:], rhs=accsq[:, m * 512:(m + 1) * 512], start=True, stop=True), reads=["ones", "accsq"], writes=[f"pss{q}"])
        p.op("act", lambda q=q, m=m: nc.scalar.activation(out=rstd[:, m * 512:(m + 1) * 512], in_=pss[q][:], func=AF.Sqrt, scale=1.0 / D, bias=EPS), reads=[f"pss{q}"], writes=["rstd"])
    p.op("dve", lambda: nc.vector.reciprocal(out=rstd[:], in_=rstd[:]), reads=["rstd"], writes=["rstd"])
    for c in range(16):
        s = c % 2
        p.dma("sp", lambda s=s, c=c: nc.sync.dma_start(out=x1c[s][:], in_=x1T[c * 128:(c + 1) * 128]), reads=[f"x1T{c}"], writes=[f"x1c{s}"])
        p.op("dve", lambda s=s, c=c: nc.vector.scalar_tensor_tensor(out=hc[s][:], in0=x1c[s][:], scalar=gsb[:, c:c + 1], in1=rstd[:], op0=ALU.mult, op1=ALU.mult),
             reads=[f"x1c{s}", "gsb", "rstd"], writes=[f"hc{s}"])
        p.dma("pool", lambda s=s, c=c: nc.gpsimd.dma_start(out=hT[:, c, :, 2:TW], in_=hc[s][:].rearrange("p (m t) -> p m t", t=512)), reads=[f"hc{s}"])
        p.dma("pool", lambda s=s, c=c: nc.gpsimd.dma_start(out=ht_s[c * 128:(c + 1) * 128, :].rearrange("p (m h) -> p m h", h=2),
                                                            in_=hc[s][:].rearrange("p (m t) -> p m t", t=512)[:, :, 510:512]), reads=[f"hc{s}"])


def emit_halo(nc, p, T, PS, ht_g, hco_d, hT):
    Hg = T("Hg", [128, 4, 16, 8], BF16)
    hco = T("hco", [128, 5], F32)
    acc = T("hacc", [128, 4, 16, 2], F32)
    hb = T("hb", [128, 4, 16, 2], BF16)
    p.dma("sp", lambda: nc.sync.dma_start(out=Hg[:], in_=ht_g.rearrange("(r k p) c -> p r k c", r=4, p=128)), writes=["Hg"])
    p.dma("sp", lambda: nc.sync.dma_start(out=hco[:], in_=hco_d), writes=["hco"])
    for m in range(4):
        p.op("dve", lambda m=m: nc.vector.tensor_scalar(out=acc[:, m], in0=Hg[:, 0, :, 2 * m:2 * m + 2], scalar1=hco[:, 0:1], scalar2=None, op0=ALU.mult),
             reads=["Hg", "hco"], writes=[f"hacc{m}"])
        for r in range(1, 4):
            p.op("dve", lambda m=m, r=r: nc.vector.scalar_tensor_tensor(out=acc[:, m], in0=Hg[:, r, :, 2 * m:2 * m + 2], scalar=hco[:, r:r + 1], in1=acc[:, m], op0=ALU.mult, op1=ALU.add),
                 reads=["Hg", "hco", f"hacc{m}"], writes=[f"hacc{m}"])
        if m >= 1:
            p.op("dve", lambda m=m: nc.vector.scalar_tensor_tensor(out=acc[:, m], in0=Hg[:, 3, :, 2 * m - 2:2 * m], scalar=hco[:, 4:5], in1=acc[:, m], op0=ALU.mult, op1=ALU.add),
                 reads=["Hg", "hco", f"hacc{m}"], writes=[f"hacc{m}"])
        p.op("dve", lambda m=m: nc.vector.tensor_copy(out=hb[:, m], in_=acc[:, m]), reads=[f"hacc{m}"], writes=[f"hb{m}"])
        p.dma("sp", lambda m=m: nc.sync.dma_start(out=hT[:, :, m, 0:2], in_=hb[:, m]), reads=[f"hb{m}"])


def build_fused(depth=DEPTH, debug=False, stop=10**9):
    nc = bass.Bass("TRN2", target_bir_lowering=False)
    I = lambda name, shape, dt: nc.dram_tensor(name, shape, dt, kind="ExternalInput").ap()
    N = lambda name, shape, dt, **kw: nc.dram_tensor(name, shape, dt, kind="Internal", **kw).ap()
    xT0 = I("xT0", [D, NT], F32)
    NTC = len(t_chunks()) + 1
    NNC = len(n_chunks())
    wiT = I("wiT", [depth, NTC, 128, 2048], F32)
    wiN = I("wiN", [depth, NNC, 128, 4096], F32)
    w_out = I("woR", [depth, 16, 128, 2048], F32)
    w_up = I("wuR", [depth, 44, 128, 4096], F32)
    w_down = I("wdR", [depth, 16, 2, 128, 2816], F32)
    gn_attn = I("gn_attn", [depth, 128, 16], F32)
    gn_mlp = I("gn_mlp", [depth, 128, 16], F32)
    gn_fin = I("gn_fin", [128, 16], F32)
    w1d = I("w1", [depth, 2, 64, 32, 128], F32)
    w2d = I("w2", [depth, 128, 2, 64], F32)
    peTd = I("peT", [depth, 64, 2, 32, 2], F32)
    cwd = I("cw", [depth, 128, 88, 3], F32)
    cbd = I("cbv", [depth, 128, 88], F32)
    hco_d = I("hco", [128, 5], F32)
    consts = {}
    for nm, shape, dt in (("G", [44, GL], F32), ("cb", [128, 44], F32), ("negm", [128, 16, 32], F32), ("ownm", [128, 16, 32], F32),
                          ("EB", [32, S], BF16), ("EC", [128, S], BF16), ("AM", [128, 16, 128], F32), ("BA", [128, 16, 128], F32),
                          ("cmA", [128, 512], F32), ("cmB", [128, 512], F32), ("ovl", [128, 4, 128], F32), ("selg", [36, 36, 64], F32)):
        consts[nm] = I(nm, shape, dt)
    yT = nc.dram_tensor("yT", [D, NT], F32, kind="ExternalOutput").ap()
    qT_s = N("qT_s", [32 * 64, NT], BF16)
    kT_s = N("kT_s", [26 * 64, NT], BF16, addr_space="Local")
    cmpT_s = N("cmpT_s", [6 * 64, NT], BF16, addr_space="Local")
    v_s = N("v_s", [26 * 128, 1024], BF16, addr_space="Local")
    def chunked(name, total, step, cols):
        out = []
        r0 = 0
        while r0 < total:
            nr = min(step, total - r0)
            out.append((r0, nr, N(f"{name}_{r0}", [4 * nr, cols], BF16, addr_space="Local")))
            r0 += nr
        return out
    kT_g = chunked("kT_g", 26 * 64, 256, NT)
    cmpT_g = chunked("cmpT_g", 6 * 64, 256, NT)
    v_g = chunked("v_g", 26 * 128, 512, 1024)
    gT_s = N("gT_s", [36, NT], F32)
    N2 = (lambda name, shape, dt: nc.dram_tensor(name, shape, dt, kind="ExternalOutput").ap()) if debug else N
    OT_s = N2("OT_s", [D, NT], BF16)
    x1T_s = N("x1T_s", [D, NT], F32)
    hT_s = N2("hT_s", [128, 16, 4, TW], BF16)
    ht_s = N("ht_s", [D, 8], BF16, addr_space="Local")
    ht_g = N("ht_g", [4 * D, 8], BF16, addr_space="Local")
    x2T_s = N2("x2T_s", [D, NT], F32)

    phase = [0]
    with contextlib.ExitStack() as top:
        ctx = Ctx(nc, top)

        def run_phase(fn, final=False):
            ph = phase[0]
            phase[0] += 1
            if ph >= stop and not final:
                return
            with contextlib.ExitStack() as st:
                T = lambda name, shape, dt: st.enter_context(nc.sbuf_tensor(f"s{ph}_" + name, shape, dt))
                PS = lambda name, shape, dt: st.enter_context(nc.psum_tensor(f"p{ph}_" + name, shape, dt))
                p = PProg(ctx)
                fin = fn(p, T, PS)
                p.emit(final_wait_ops=fin if final else ())

        xcur = xT0
        for l in range(depth):
            def ph_p1(p, T, PS, l=l, xcur=xcur):
                outs = {"qT": qT_s.rearrange("(h e) t -> h e t", e=64), "kT": kT_s.rearrange("(h e) t -> h e t", e=64),
                        "cmpT": cmpT_s.rearrange("(h e) t -> h e t", e=64)}
                v4 = v_s.rearrange("(h p) (ts e) -> h p ts e", p=128, e=64)

                def vdst(ts, vc0, ncols):
                    h0, nh = vc0 // 64, ncols // 64
                    return v4[h0:h0 + nh, :, ts, :].rearrange("h p e -> p h e")
                def wsrc(kind, idx):
                    if kind == "T":
                        return wiT[l, idx].rearrange("p (k n) -> p k n", n=128)
                    return wiN[l, idx].rearrange("p (k n) -> p k n", n=256)
                emit_p1(nc, p, T, PS, xcur, gn_attn[l], None, outs, None, gT_s, vdst=vdst, wsrc=wsrc)
                return ()
            run_phase(ph_p1)

            def ph_cc1(p, T, PS):
                for (a, chs) in ((kT_s, kT_g), (cmpT_s, cmpT_g), (v_s, v_g)):
                    for (r0, nr, b) in chs:
                        p.cc(lambda a=a, b=b, r0=r0, nr=nr: nc.gpsimd.collective_compute("AllGather", ALU.bypass, replica_groups=RG, ins=[a[r0:r0 + nr]], outs=[b]))
                return ()
            run_phase(ph_cc1)

            def ph_p2(p, T, PS, l=l):
                dr = dict(consts)
                dr.update({"qT": qT_s.rearrange("(h e) t -> h e t", e=64), "kT_g": kT_g, "cmpT_g": cmpT_g, "v_g": v_g, "gTs": gT_s,
                           "w1": w1d[l], "w2": w2d[l], "peT": peTd[l], "OT": OT_s})
                P = P2(nc, p, T, PS, dr, fused=True)
                P.setup()
                P.mixer_A()
                P.mixer_B()
                P.mixer_C()
                return ()
            run_phase(ph_p2)

            def ph_p3a(p, T, PS, l=l, xcur=xcur):
                emit_p3a_f(nc, p, T, PS, xcur, OT_s, w_out[l], gn_mlp[l], x1T_s, hT_s, ht_s)
                return ()
            run_phase(ph_p3a)

            def ph_cc2(p, T, PS):
                p.cc(lambda: nc.gpsimd.collective_compute("AllGather", ALU.bypass, replica_groups=RG, ins=[ht_s], outs=[ht_g]))
                return ()
            run_phase(ph_cc2)

            def ph_halo(p, T, PS):
                emit_halo(nc, p, T, PS, ht_g, hco_d, hT_s)
                return ()
            run_phase(ph_halo)

            def ph_p3b(p, T, PS, l=l):
                x1get = lambda cc, half: x1T_s[cc * 128:(cc + 1) * 128, half * 1024:(half + 1) * 1024].rearrange("p (t n) -> p t n", n=512)
                emit_p3b(nc, p, T, PS, hT_s, x1get, w_up[l], w_down[l], cwd[l], cbd[l], x2T_s, relayout=True)
                return ()
            run_phase(ph_p3b)
            xcur = x2T_s

        def ph_kf(p, T, PS):
            return emit_kf(nc, p, T, PS, x2T_s, gn_fin, yT)
        run_phase(ph_kf, final=True)
    return nc


def _relayout_in_T(w_in):
    L = w_in.shape[0]
    tch = t_chunks() + [(CG, 36, None, 1.0)]
    out = np.zeros((L, len(tch), 128, 16, 128), np.float32)
    for ci, ch in enumerate(tch):
        c0, nc_ = ch[0], ch[1]
        out[:, ci, :, :, 0:nc_] = w_in[:, :, c0:c0 + nc_].reshape(L, 16, 128, nc_).transpose(0, 2, 1, 3)
    return out.reshape(L, len(tch), 128, 2048)


def _relayout_in_N(w_in):
    L = w_in.shape[0]
    nch = n_chunks()
    out = np.zeros((L, len(nch), 128, 16, 256), np.float32)
    for ni, (c0, nc_, vc0) in enumerate(nch):
        out[:, ni, :, :, 0:nc_] = w_in[:, :, c0:c0 + nc_].reshape(L, 16, 128, nc_).transpose(0, 2, 1, 3)
    return out.reshape(L, len(nch), 128, 4096)


def fused_host_inputs(x, rel_table, w_in, w_out, cmp_w1, cmp_w2, cmp_pe, norm_attn, norm_mlp, w_up, conv_w, conv_b, w_down, norm_final):
    import ml_dtypes
    bf = ml_dtypes.bfloat16
    f32 = np.float32
    A = lambda a: np.ascontiguousarray(np.asarray(a, f32))
    x = np.asarray(x, f32)
    rel_table = np.asarray(rel_table, f32)
    L = np.asarray(w_in).shape[0]
    shared = {
        "wiT": _relayout_in_T(np.asarray(w_in, f32)), "wiN": _relayout_in_N(np.asarray(w_in, f32)),
        "woR": A(np.asarray(w_out, f32).reshape(L, 16, 128, 16, 128).transpose(0, 3, 2, 1, 4).reshape(L, 16, 128, 2048)),
        "wuR": A(np.asarray(w_up, f32).reshape(L, 16, 128, 2, 44, 128).transpose(0, 4, 2, 1, 3, 5).reshape(L, 44, 128, 4096)),
        "wdR": A(np.asarray(w_down, f32).reshape(L, 2, 22, 128, 16, 128).transpose(0, 4, 1, 3, 2, 5).reshape(L, 16, 2, 128, 2816)),
        "gn_attn": A(np.asarray(norm_attn, f32).reshape(L, 16, 128).transpose(0, 2, 1)),
        "gn_mlp": A(np.asarray(norm_mlp, f32).reshape(L, 16, 128).transpose(0, 2, 1)),
        "gn_fin": A(np.asarray(norm_final, f32).reshape(16, 128).T),
        "w1": A(np.asarray(cmp_w1, f32).reshape(L, 2, 32, 64, 128).transpose(0, 1, 3, 2, 4)),
        "w2": A(np.asarray(cmp_w2, f32).transpose(0, 2, 1, 3)),
        "peT": A(np.repeat(np.asarray(cmp_pe, f32).transpose(0, 3, 1, 2)[..., None], 2, axis=-1)),
        "cw": A(np.asarray(conv_w, f32).transpose(0, 2, 1).reshape(L, 88, 128, 3).transpose(0, 2, 1, 3)),
        "cbv": A(np.asarray(conv_b, f32).reshape(L, 88, 128).transpose(0, 2, 1)),
        "EB": eb_const().astype(bf), "EC": ec_const_nat().astype(bf), "ovl": ovl_const(), "selg": selg_const(),
    }
    per_j = []
    for j in range(4):
        G, cb = band_vectors(rel_table, j)
        negm, ownm = moba_consts(j)
        AM, BA, cmA, cmB = nsa_consts(j)
        per_j.append({"G": G, "cb": np.ascontiguousarray(np.broadcast_to(cb[None, :], (128, 44))), "negm": negm, "ownm": ownm,
                      "AM": AM, "BA": BA, "cmA": cmA, "cmB": cmB, "hco": halo_coef(j)})
    in_maps = []
    for c in range(8):
        b, j = c // 4, c % 4
        d = dict(shared)
        d.update(per_j[j])
        d["xT0"] = np.ascontiguousarray(x[b, core_tokens(j)].T)
        in_maps.append(d)
    return in_maps


from concourse.bass_utils import run_bass_kernel_spmd

_FUSED = {}


def kernel(x, rel_table, w_in, w_out, cmp_w1, cmp_w2, cmp_pe, norm_attn, norm_mlp,
           w_up, conv_w, conv_b, w_down, norm_final):
    if "nc" not in _FUSED:
        _FUSED["nc"] = build_fused()
    nc = _FUSED["nc"]
    in_maps = fused_host_inputs(x, rel_table, w_in, w_out, cmp_w1, cmp_w2, cmp_pe, norm_attn, norm_mlp,
                                w_up, conv_w, conv_b, w_down, norm_final)
    res = run_bass_kernel_spmd(nc, in_maps, core_ids=list(range(8)))
    out = np.empty((2, S, 2048), np.float32)
    for c in range(8):
        b, j = c // 4, c % 4
        out[b, core_tokens(j)] = np.asarray(res.results[c]["yT"]).T
    return out
```
